# Optimizing a Trainium2 kernel written in Bass

```python
import math
import jax
import jax.numpy as jnp
from jax import lax
import numpy as np

D_MODEL = 1024
BATCH = 2
SEQ = 8192
DEPTH = 4

CHUNK = 64
N_META = 16
QBLOCK = 128
FRONT_PAD = QBLOCK - N_META
N_MIXERS = 3
N_DN_LAYERS = (DEPTH + 2) // 3
N_DA_LAYERS = (DEPTH + 1) // 3
N_LRU_LAYERS = DEPTH // 3
EPS = 1e-6

DN_HEADS = 8
DN_HEAD_K = 128
DN_HEAD_V = 128
DN_QK = DN_HEADS * DN_HEAD_K
DN_V = DN_HEADS * DN_HEAD_V
DN_CONV = 4
DN_CONV_CH = 2 * DN_QK + DN_V
DN_IN = DN_CONV_CH + DN_V + 2 * DN_HEADS

DA_HEADS = 8
DA_HEAD = D_MODEL // (2 * DA_HEADS)
DA_QK = DA_HEADS * 2 * DA_HEAD
DA_V = DA_HEADS * 2 * DA_HEAD
DA_IN = 2 * DA_QK + DA_V
N_BUCKETS = 32
MAX_DISTANCE = 128
NEG_INF = -1e30

LRU_WIDTH = D_MODEL
LRU_BLOCKS = 4
LRU_BLOCK = LRU_WIDTH // LRU_BLOCKS
LRU_CONV = 4
LRU_C = 8.0

D_FF = 4 * D_MODEL

kernel_name = "hybrid_deltanet_diffattn_rglru_trunk"


def rmsnorm(x, w):
    xf = x.astype(jnp.float32)
    y = xf * lax.rsqrt(jnp.mean(xf * xf, axis=-1, keepdims=True) + EPS)
    return (y * w.astype(jnp.float32)).astype(x.dtype)


def l2norm(x):
    xf = x.astype(jnp.float32)
    return (xf * lax.rsqrt(jnp.sum(xf * xf, axis=-1, keepdims=True) + EPS)).astype(x.dtype)


def causal_depthwise_conv(x, w):
    width, ch = w.shape
    return lax.conv_general_dilated(
        x, w[:, None, :].astype(x.dtype), window_strides=(1,),
        padding=[(width - 1, 0)], dimension_numbers=("NWC", "WIO", "NWC"),
        feature_group_count=ch)


def t5_bucket(rel):
    nb = N_BUCKETS // 2
    ret = jnp.where(rel > 0, nb, 0)
    n = jnp.abs(rel)
    max_exact = nb // 2
    nf = jnp.maximum(n, 1).astype(jnp.float32)
    large = max_exact + (jnp.log(nf / max_exact) / math.log(MAX_DISTANCE / max_exact)
                         * (nb - max_exact)).astype(jnp.int32)
    large = jnp.minimum(large, nb - 1)
    return ret + jnp.where(n < max_exact, n, large)


def chunk_gated_delta_rule(q, k, v, beta, g):
    f32 = jnp.float32
    out_dtype = v.dtype
    b, L, H, dk = q.shape
    dv = v.shape[-1]
    nc = L // CHUNK

    def chunks(t):
        t = t.astype(f32).reshape(b, nc, CHUNK, H, -1)
        return t.transpose(1, 0, 3, 2, 4)

    qc, kc, vc = chunks(q), chunks(k), chunks(v)
    bc = chunks(beta[..., None])[..., 0]
    gc = jnp.cumsum(chunks(g[..., None])[..., 0], axis=-1)
    idx = jnp.arange(CHUNK)
    incl = idx[:, None] >= idx[None, :]
    strict = idx[:, None] > idx[None, :]
    decay = jnp.exp(jnp.where(incl, gc[..., :, None] - gc[..., None, :], -jnp.inf))
    kb = kc * bc[..., None]
    a_strict = jnp.where(strict, jnp.einsum("nbhid,nbhjd->nbhij", kb, kc) * decay, 0.0)
    m = a_strict + jnp.eye(CHUNK, dtype=f32)
    u = lax.linalg.triangular_solve(m, vc * bc[..., None], left_side=True,
                                    lower=True, unit_diagonal=True)
    w = lax.linalg.triangular_solve(m, kb * jnp.exp(gc)[..., None], left_side=True,
                                    lower=True, unit_diagonal=True)

    def step(state, xs):
        qi, ki, ui, wi, gi, di = xs
        attn = jnp.einsum("bhid,bhjd->bhij", qi, ki) * di
        v_new = ui - jnp.einsum("bhid,bhde->bhie", wi, state)
        o = (jnp.einsum("bhid,bhde->bhie", qi * jnp.exp(gi)[..., None], state)
             + jnp.einsum("bhij,bhje->bhie", attn, v_new))
        g_last = gi[..., -1]
        state = (state * jnp.exp(g_last)[..., None, None]
                 + jnp.einsum("bhid,bhie->bhde",
                              ki * jnp.exp(g_last[..., None] - gi)[..., None], v_new))
        return state, o

    s0 = jnp.zeros((b, H, dk, dv), f32)
    _, o = lax.scan(step, s0, (qc, kc, u, w, gc, decay))
    o = o.transpose(1, 0, 3, 2, 4).reshape(b, L, H, dv)
    return o.astype(out_dtype)


def gated_deltanet(u, valid, w_in, conv_w, a_log, dt_bias, norm_w, w_out):
    f32 = jnp.float32
    b, L, _ = u.shape
    proj = u @ w_in
    qkv = jax.nn.silu(causal_depthwise_conv(proj[..., :DN_CONV_CH], conv_w))
    z = proj[..., DN_CONV_CH:DN_CONV_CH + DN_V].reshape(b, L, DN_HEADS, DN_HEAD_V)
    beta_logit = proj[..., DN_CONV_CH + DN_V:DN_CONV_CH + DN_V + DN_HEADS]
    a = proj[..., DN_CONV_CH + DN_V + DN_HEADS:]
    q = l2norm(qkv[..., :DN_QK].reshape(b, L, DN_HEADS, DN_HEAD_K)) * (DN_HEAD_K ** -0.5)
    k = l2norm(qkv[..., DN_QK:2 * DN_QK].reshape(b, L, DN_HEADS, DN_HEAD_K))
    v = qkv[..., 2 * DN_QK:].reshape(b, L, DN_HEADS, DN_HEAD_V)
    m = valid[None, :, None].astype(f32)
    beta = jax.nn.sigmoid(beta_logit.astype(f32))
    g = -jnp.exp(a_log.astype(f32)) * jax.nn.softplus(a.astype(f32) + dt_bias.astype(f32)) * m
    k = k * m[..., None].astype(k.dtype)
    v = v * m[..., None].astype(v.dtype)
    o = chunk_gated_delta_rule(q, k, v, beta, g)
    o = rmsnorm(o, norm_w) * jax.nn.silu(z)
    return o.reshape(b, L, DN_V) @ w_out


def diff_attention(u, valid, w_in, lam_q1, lam_k1, lam_q2, lam_k2, subln_w, w_out,
                   rel_bias, lambda_init):
    f32 = jnp.float32
    b, L, _ = u.shape
    proj = u @ w_in
    q = proj[..., :DA_QK].reshape(b, L, DA_HEADS, 2, DA_HEAD)
    k = proj[..., DA_QK:2 * DA_QK].reshape(b, L, DA_HEADS, 2, DA_HEAD)
    v = proj[..., 2 * DA_QK:].reshape(b, L, DA_HEADS, 2 * DA_HEAD)
    lam = (jnp.exp(jnp.sum(lam_q1.astype(f32) * lam_k1.astype(f32)))
           - jnp.exp(jnp.sum(lam_q2.astype(f32) * lam_k2.astype(f32))) + lambda_init)
    scale = DA_HEAD ** -0.5
    pos = jnp.arange(L)
    kchunk = pos // CHUNK
    nq = L // QBLOCK
    qb = q.reshape(b, nq, QBLOCK, DA_HEADS, 2, DA_HEAD).transpose(1, 0, 2, 3, 4, 5)
    table = rel_bias.astype(f32)

    def block(args):
        qi, bi = args
        qpos = bi * QBLOCK + jnp.arange(QBLOCK)
        s = jnp.einsum("bqhcd,bkhcd->bhcqk", qi, k).astype(f32) * scale
        bias = table[t5_bucket(pos[None, :] - qpos[:, None])]
        s = s + jnp.moveaxis(bias, -1, 0)[None, :, None]
        allowed = (kchunk[None, :] <= (qpos // CHUNK)[:, None]) & valid[None, :]
        s = jnp.where(allowed, s, NEG_INF)
        p = jax.nn.softmax(s, axis=-1)
        attn = p[:, :, 0] - lam * p[:, :, 1]
        return jnp.einsum("bhqk,bkhe->bqhe", attn.astype(v.dtype), v)

    o = lax.map(block, (qb, jnp.arange(nq)))
    o = o.transpose(1, 0, 2, 3, 4).reshape(b, L, DA_HEADS, 2 * DA_HEAD)
    o = rmsnorm(o, subln_w) * (1.0 - lambda_init)
    return o.reshape(b, L, DA_V) @ w_out


def rglru_block(u, valid, w_in, conv_w, conv_b, w_rgate, b_rgate, w_igate, b_igate,
                lam, w_out):
    f32 = jnp.float32
    b, L, _ = u.shape
    proj = u @ w_in
    gate = jax.nn.gelu(proj[..., :LRU_WIDTH], approximate=True)
    xr = (causal_depthwise_conv(proj[..., LRU_WIDTH:], conv_w) + conv_b) \
        * valid[None, :, None].astype(u.dtype)
    xb = xr.reshape(b, L, LRU_BLOCKS, LRU_BLOCK)
    r = jax.nn.sigmoid((jnp.einsum("blgi,gio->blgo", xb, w_rgate).reshape(b, L, LRU_WIDTH)
                        + b_rgate).astype(f32))
    i = jax.nn.sigmoid((jnp.einsum("blgi,gio->blgo", xb, w_igate).reshape(b, L, LRU_WIDTH)
                        + b_igate).astype(f32))
    log_a = -LRU_C * r * jax.nn.softplus(-lam.astype(f32))
    a = jnp.exp(log_a)
    inp = jnp.sqrt(-jnp.expm1(2.0 * log_a)) * (i * xr.astype(f32))

    def combine(c1, c2):
        a1, b1 = c1
        a2, b2 = c2
        return a1 * a2, a2 * b1 + b2

    _, hs = lax.associative_scan(combine, (a, inp), axis=1)
    y = hs.astype(u.dtype) * gate
    return y @ w_out


def sq_relu_mlp(u, w1, w2):
    return jnp.square(jax.nn.relu(u @ w1)) @ w2


def setup_inputs(seed: int = 0) -> dict:
    key = jax.random.key(seed)
    ks = list(jax.random.split(key, 32))
    f32 = jnp.float32

    def nrm(i, shape, scale):
        return scale * jax.random.normal(ks[i], shape, f32)

    def uni(i, shape, lo, hi):
        return jax.random.uniform(ks[i], shape, f32, lo, hi)

    dt = jnp.exp(uni(9, (N_DN_LAYERS, DN_HEADS), math.log(1e-3), math.log(1e-1)))
    s = uni(26, (N_LRU_LAYERS, LRU_WIDTH), 0.9, 0.999) ** (1.0 / LRU_C)
    return {
        "x": nrm(0, (BATCH, SEQ, D_MODEL), 1.0),
        "meta_tokens": nrm(1, (N_META, D_MODEL), 1.0),
        "rel_bias": nrm(2, (N_BUCKETS, DA_HEADS), 0.5),
        "norm_mix_w": 1.0 + nrm(3, (DEPTH, D_MODEL), 0.1),
        "norm_mlp_w": 1.0 + nrm(4, (DEPTH, D_MODEL), 0.1),
        "final_norm_w": 1.0 + nrm(5, (D_MODEL,), 0.1),
        "dn_w_in": nrm(6, (N_DN_LAYERS, D_MODEL, DN_IN), D_MODEL ** -0.5),
        "dn_conv_w": nrm(7, (N_DN_LAYERS, DN_CONV, DN_CONV_CH), DN_CONV ** -0.5),
        "dn_a_log": jnp.log(uni(8, (N_DN_LAYERS, DN_HEADS), 1.0, 16.0)),
        "dn_dt_bias": dt + jnp.log(-jnp.expm1(-dt)),
        "dn_norm_w": 1.0 + nrm(10, (N_DN_LAYERS, DN_HEAD_V), 0.1),
        "dn_w_out": nrm(11, (N_DN_LAYERS, DN_V, D_MODEL), DN_V ** -0.5),
        "da_w_in": nrm(12, (N_DA_LAYERS, D_MODEL, DA_IN), D_MODEL ** -0.5),
        "da_lam_q1": nrm(13, (N_DA_LAYERS, DA_HEAD), 0.1),
        "da_lam_k1": nrm(14, (N_DA_LAYERS, DA_HEAD), 0.1),
        "da_lam_q2": nrm(15, (N_DA_LAYERS, DA_HEAD), 0.1),
        "da_lam_k2": nrm(16, (N_DA_LAYERS, DA_HEAD), 0.1),
        "da_subln_w": 1.0 + nrm(17, (N_DA_LAYERS, 2 * DA_HEAD), 0.1),
        "da_w_out": nrm(18, (N_DA_LAYERS, DA_V, D_MODEL), DA_V ** -0.5),
        "lru_w_in": nrm(19, (N_LRU_LAYERS, D_MODEL, 2 * LRU_WIDTH), D_MODEL ** -0.5),
        "lru_conv_w": nrm(20, (N_LRU_LAYERS, LRU_CONV, LRU_WIDTH), LRU_CONV ** -0.5),
        "lru_conv_b": nrm(21, (N_LRU_LAYERS, LRU_WIDTH), 0.01),
        "lru_w_rgate": nrm(22, (N_LRU_LAYERS, LRU_BLOCKS, LRU_BLOCK, LRU_BLOCK), LRU_BLOCK ** -0.5),
        "lru_b_rgate": nrm(23, (N_LRU_LAYERS, LRU_WIDTH), 0.01),
        "lru_w_igate": nrm(24, (N_LRU_LAYERS, LRU_BLOCKS, LRU_BLOCK, LRU_BLOCK), LRU_BLOCK ** -0.5),
        "lru_b_igate": nrm(25, (N_LRU_LAYERS, LRU_WIDTH), 0.01),
        "lru_lambda": jnp.log(s) - jnp.log1p(-s),
        "lru_w_out": nrm(27, (N_LRU_LAYERS, LRU_WIDTH, D_MODEL), LRU_WIDTH ** -0.5),
        "mlp_w1": nrm(28, (DEPTH, D_MODEL, D_FF), D_MODEL ** -0.5),
        "mlp_w2": nrm(29, (DEPTH, D_FF, D_MODEL), D_FF ** -0.5),
    }


def reference(x, meta_tokens, rel_bias, norm_mix_w, norm_mlp_w, final_norm_w,
              dn_w_in, dn_conv_w, dn_a_log, dn_dt_bias, dn_norm_w, dn_w_out,
              da_w_in, da_lam_q1, da_lam_k1, da_lam_q2, da_lam_k2, da_subln_w, da_w_out,
              lru_w_in, lru_conv_w, lru_conv_b, lru_w_rgate, lru_b_rgate, lru_w_igate,
              lru_b_igate, lru_lambda, lru_w_out, mlp_w1, mlp_w2):
    b = x.shape[0]
    h = jnp.concatenate([
        jnp.zeros((b, FRONT_PAD, D_MODEL), x.dtype),
        jnp.broadcast_to(meta_tokens[None].astype(x.dtype), (b, N_META, D_MODEL)),
        x,
    ], axis=1)
    L = h.shape[1]
    valid = jnp.arange(L) >= FRONT_PAD
    keep = valid[None, :, None].astype(h.dtype)

    for layer in range(DEPTH):
        kind = layer % N_MIXERS
        slot = layer // N_MIXERS
        u = rmsnorm(h, norm_mix_w[layer])
        if kind == 0:
            y = gated_deltanet(u, valid, dn_w_in[slot], dn_conv_w[slot], dn_a_log[slot],
                               dn_dt_bias[slot], dn_norm_w[slot], dn_w_out[slot])
        elif kind == 1:
            lambda_init = 0.8 - 0.6 * math.exp(-0.3 * layer)
            y = diff_attention(u, valid, da_w_in[slot], da_lam_q1[slot], da_lam_k1[slot],
                               da_lam_q2[slot], da_lam_k2[slot], da_subln_w[slot],
                               da_w_out[slot], rel_bias, lambda_init)
        else:
            y = rglru_block(u, valid, lru_w_in[slot], lru_conv_w[slot], lru_conv_b[slot],
                            lru_w_rgate[slot], lru_b_rgate[slot], lru_w_igate[slot],
                            lru_b_igate[slot], lru_lambda[slot], lru_w_out[slot])
        h = h + keep * y
        u = rmsnorm(h, norm_mlp_w[layer])
        h = h + keep * sq_relu_mlp(u, mlp_w1[layer], mlp_w2[layer])

    h = rmsnorm(h, final_norm_w)
    return h[:, FRONT_PAD + N_META:]
```

```python
import math
from contextlib import ExitStack

import numpy as np
import ml_dtypes
import concourse.bass as bass
import concourse.mybir as mybir
from concourse.bass_utils import run_bass_kernel_spmd

F32 = mybir.dt.float32
BF16 = mybir.dt.bfloat16
AF = mybir.ActivationFunctionType
ALU = mybir.AluOpType
AX = mybir.AxisListType

D = 1024
B = 2
SEQ = 8192
NMETA = 16
PADF = 48
T = PADF + NMETA + SEQ
NTC = T // 4
EPS = 1e-6
DFF = 4096
NCORES = 8


class Tl:
    def __init__(self, t, name, st=None):
        self.t = t
        self.name = name
        self.st = st if st is not None else [None, {}]

    @property
    def w(self):
        return self.st[0]

    @w.setter
    def w(self, v):
        self.st[0] = v

    @property
    def r(self):
        return self.st[1]

    @r.setter
    def r(self, v):
        self.st[1] = v

    def view(self, ap):
        return Tl(ap, self.name, self.st)

    def __getitem__(self, idx):
        return self.t[idx]


class Ctx:
    NDMA = 8

    def __init__(self, nc, es):
        self.nc = nc
        self.es = es
        self.eng = {"pe": nc.tensor, "act": nc.scalar, "dve": nc.vector,
                    "pool": nc.gpsimd, "sp": nc.sync}
        self.sem = {k: es.enter_context(nc.semaphore("s_" + k)) for k in self.eng}
        self.cnt = {k: 0 for k in self.eng}
        self.seen = {k: {} for k in self.eng}
        self.dsem = {}
        self.dcnt = {}
        for q in ("sp", "pool"):
            self.dsem[q] = [es.enter_context(nc.semaphore("d_%s%d" % (q, i)))
                            for i in range(self.NDMA)]
            self.dcnt[q] = 0
        self.ntile = 0

    def sb(self, shape, dt=F32, name=None, es=None):
        self.ntile += 1
        name = "%s_%d" % (name or "t", self.ntile)
        t = (es or self.es).enter_context(self.nc.sbuf_tensor(name, list(shape), dt))
        return Tl(t, name)

    def ps(self, shape, dt=F32, name=None, es=None):
        self.ntile += 1
        name = "%s_%d" % (name or "p", self.ntile)
        t = (es or self.es).enter_context(self.nc.psum_tensor(name, list(shape), dt))
        return Tl(t, name)

    def dram(self, name, shape, dt, kind):
        t = self.nc.dram_tensor(name, list(shape), dt, kind=kind)
        return Tl(t.ap(), name)

    def _wait(self, e, sem, val):
        key = id(sem)
        if self.seen[e].get(key, 0) >= val:
            return
        self.eng[e].wait_ge(sem, val)
        self.seen[e][key] = val

    def _deps(self, e, reads, writes):
        deps = {}

        def add(d):
            if d is None:
                return
            s, v = d
            if deps.get(id(s), (None, 0))[1] < v:
                deps[id(s)] = (s, v)
        for t in reads:
            add(t.w)
        for t in writes:
            add(t.w)
            for s_v in t.r.values():
                add(s_v)
        for s, v in deps.values():
            if e == "pe" and s is self.sem["pe"]:
                continue
            self._wait(e, s, v)

    def _mark(self, token, reads, writes):
        s, v = token
        for t in reads:
            t.r[id(s)] = (s, v)
        for t in writes:
            t.w = (s, v)
            t.r = {}

    def op(self, e, fn, reads=(), writes=(), inc=True):
        self._deps(e, reads, writes)
        ins = fn(self.eng[e])
        if inc:
            self.cnt[e] += 1
            ins.then_inc(self.sem[e], 1)
            tok = (self.sem[e], self.cnt[e])
        else:
            tok = (self.sem[e], self.cnt[e] + 1)
        self._mark(tok, reads, writes)
        return ins

    def dma(self, q, out, in_, reads=(), writes=(), **kw):
        self._deps(q, reads, writes)
        i = self.dcnt[q]
        s = self.dsem[q][i % self.NDMA]
        prev = 16 * (i // self.NDMA)
        if prev:
            self._wait(q, s, prev)
        ins = self.eng[q].dma_start(out=out, in_=in_, **kw)
        ins.then_inc(s, 16)
        self.dcnt[q] += 1
        self._mark((s, prev + 16), reads, writes)

    def barrier(self):
        for e in self.eng:
            for e2 in self.eng:
                if e2 != e and self.cnt[e2] > 0:
                    self._wait(e, self.sem[e2], self.cnt[e2])
            for q in self.dsem:
                n = self.dcnt[q]
                for j, s in enumerate(self.dsem[q]):
                    k = (n - j + self.NDMA - 1) // self.NDMA
                    if k > 0:
                        self._wait(e, s, 16 * k)

    def finish(self):
        for q in self.dsem:
            n = self.dcnt[q]
            for j, s in enumerate(self.dsem[q]):
                k = (n - j + self.NDMA - 1) // self.NDMA
                if k > 0:
                    self._wait("sp", s, 16 * k)


class Rot:
    def __init__(self, tiles):
        self.tiles = tiles
        self.i = 0

    def get(self):
        t = self.tiles[self.i % len(self.tiles)]
        self.i += 1
        return t


def rms_stats(c, ones, src_tiles_fn, nk, n, pstat, sqrot, rstd, scale_div):
    for k in range(nk):
        src, src_t = src_tiles_fn(k)
        sq = sqrot.get()
        c.op("act", lambda e: e.activation(out=sq[:, :n], in_=src, func=AF.Square),
             reads=[src_t], writes=[sq])
        c.op("pe", lambda e: e.matmul(pstat[:, :n], lhsT=ones[:], rhs=sq[:, :n],
                                      start=(k == 0), stop=(k == nk - 1)),
             reads=[ones, sq], writes=[pstat])
    c.op("act", lambda e: e.activation(out=rstd[:, :n], in_=pstat[:, :n], func=AF.Sqrt,
                                       bias=c.eps_t[:, 0:1], scale=1.0 / scale_div),
         reads=[pstat, c.eps_t], writes=[rstd])
    c.op("dve", lambda e: e.reciprocal(out=rstd[:, :n], in_=rstd[:, :n]),
         reads=[rstd], writes=[rstd])


def make_consts(c):
    c.eps_t = c.sb([128, 1], F32, "eps_t")
    c.op("pool", lambda e: e.memset(c.eps_t[:], EPS), writes=[c.eps_t])
    c.ones = c.sb([128, 128], F32, "ones")
    c.op("pool", lambda e: e.memset(c.ones[:], 1.0), writes=[c.ones])


GS = 344
NG = NTC // GS


def build_post(final):
    nc = bass.Bass("TRN2", target_bir_lowering=False)
    with ExitStack() as es:
        c = Ctx(nc, es)
        io = {}
        io["hT"] = c.dram("hT", [D, NTC], F32, "ExternalInput")
        io["oT"] = c.dram("oT", [D, NTC], BF16, "ExternalInput")
        io["wo"] = c.dram("wo", [D, D], F32, "ExternalInput")
        io["w1"] = c.dram("w1", [D, DFF], F32, "ExternalInput")
        io["w2"] = c.dram("w2", [DFF, D], F32, "ExternalInput")
        io["nw"] = c.dram("nw", [128, 16], F32, "ExternalInput")
        io["hout"] = c.dram("hout", [D, NTC], F32, "ExternalOutput")
        io["uout"] = c.dram("uout", [D, NTC], F32 if final else BF16, "ExternalOutput")
        make_consts(c)
        emit_post(c, io, final)
        c.finish()
    return nc


def emit_post(c, io, final):
    if True:
        hT, oT, wo, w1, w2, nw, hout, uout = (io[k] for k in ("hT", "oT", "wo", "w1", "w2", "nw", "hout", "uout"))
        nws = c.sb([128, 16], F32, "nws")
        c.dma("sp", nws[:], nw[:], writes=[nws])

        H = [c.sb([128, 8, GS], F32, "H%d" % g) for g in range(NG)]
        XB = [c.sb([128, 8, GS], BF16, "XB%d" % g) for g in range(NG)]
        wbuf = Rot([c.sb([128, 8192], BF16, "wb%d" % i) for i in range(2)])
        abuf = Rot([c.sb([128, 4, GS], BF16, "ab%d" % i) for i in range(2)])
        rbuf = Rot([c.sb([128, GS], F32, "rb%d" % i) for i in range(3)])
        sqb = Rot([c.sb([128, GS], F32, "sq%d" % i) for i in range(2)])
        rstd = c.sb([128, GS], F32, "rstd")
        uo = Rot([c.sb([128, 8, GS], F32 if final else BF16, "uo%d" % i) for i in range(2)])
        pa = Rot([c.ps([128, 512], F32, "pa%d" % i) for i in range(4)])
        py = Rot([c.ps([128, 512], F32, "py%d" % i) for i in range(3)])
        pstat = c.ps([128, 512], F32, "pstat")

        hT3 = hT[:].rearrange("(k p) n -> p k n", p=128)
        oT3 = oT[:].rearrange("(k p) n -> p k n", p=128)
        ho3 = hout[:].rearrange("(k p) n -> p k n", p=128)
        uo3 = uout[:].rearrange("(k p) n -> p k n", p=128)

        wob = wbuf.get()
        wo3 = wo[:].rearrange("(k p) n -> p k n", p=128)
        for k0 in range(0, 8, 2):
            c.dma("pool", wob[:, k0 * 1024:(k0 + 2) * 1024].rearrange("p (k n) -> p k n", k=2), wo3[:, k0:k0 + 2, :], writes=[wob])
        for g in range(NG):
            c.dma("sp", XB[g][:], oT3[:, :, g * GS:(g + 1) * GS], writes=[XB[g]])
            c.dma("sp", H[g][:], hT3[:, :, g * GS:(g + 1) * GS], writes=[H[g]])

        def load_eighth(e8):
            wb = wbuf.get()
            w13 = w1[:].rearrange("(k p) n -> p k n", p=128)
            for k0 in range(0, 8, 4):
                c.dma("pool", wb[:, k0 * 512:(k0 + 4) * 512].rearrange("p (k n) -> p k n", k=4),
                      w13[:, k0:k0 + 4, e8 * 512:(e8 + 1) * 512], writes=[wb])
            w23 = w2[:].rearrange("(f p) n -> p f n", p=128)
            for f0 in range(0, 4, 2):
                c.dma("pool", wb[:, 4096 + f0 * 1024:4096 + (f0 + 2) * 1024].rearrange("p (f n) -> p f n", f=2),
                      w23[:, e8 * 4 + f0:e8 * 4 + f0 + 2, :], writes=[wb])
            return wb

        def norm_to(g, col, dst, dst_is_f32):
            rms_stats(c, c.ones, lambda k: (H[g][:, k, :], H[g]), 8, GS, pstat, sqb, rstd, float(D))
            for m in range(8):
                c.op("dve", lambda e: e.scalar_tensor_tensor(
                    out=dst[:, m, :], in0=H[g][:, m, :], scalar=nws[:, col + m:col + m + 1],
                    in1=rstd[:], op0=ALU.mult, op1=ALU.mult),
                    reads=[H[g], nws, rstd], writes=[dst])

        wnext = load_eighth(0)
        for g in range(NG):
            for m in range(8):
                p = py.get()
                for k in range(8):
                    c.op("pe", lambda e: e.matmul(p[:, :GS], lhsT=wob[:, k * 1024 + m * 128:k * 1024 + (m + 1) * 128],
                                                  rhs=XB[g][:, k, :], start=(k == 0), stop=(k == 7)),
                         reads=[wob, XB[g]], writes=[p], inc=(k == 7))
                c.op("dve", lambda e: e.tensor_tensor(out=H[g][:, m, :], in0=p[:, :GS], in1=H[g][:, m, :], op=ALU.add),
                     reads=[p, H[g]], writes=[H[g]])
            norm_to(g, 0, XB[g], False)
        for e8 in range(8):
            wb = wnext
            if e8 + 1 < 8:
                wnext = load_eighth(e8 + 1)
            for g in range(NG):
                ab = abuf.get()
                for f in range(4):
                    p = pa.get()
                    for k in range(8):
                        c.op("pe", lambda e: e.matmul(p[:, :GS], lhsT=wb[:, k * 512 + f * 128:k * 512 + (f + 1) * 128],
                                                      rhs=XB[g][:, k, :], start=(k == 0), stop=(k == 7)),
                             reads=[wb, XB[g]], writes=[p], inc=(k == 7))
                    r = rbuf.get()
                    c.op("act", lambda e: e.activation(out=r[:], in_=p[:, :GS], func=AF.Relu),
                         reads=[p], writes=[r])
                    c.op("dve", lambda e: e.tensor_tensor(out=ab[:, f, :], in0=r[:], in1=r[:], op=ALU.mult),
                         reads=[r], writes=[ab])
                for m in range(8):
                    p = py.get()
                    for f in range(4):
                        c.op("pe", lambda e: e.matmul(p[:, :GS], lhsT=wb[:, 4096 + f * 1024 + m * 128:4096 + f * 1024 + (m + 1) * 128],
                                                      rhs=ab[:, f, :], start=(f == 0), stop=(f == 3)),
                             reads=[wb, ab], writes=[p], inc=(f == 3))
                    c.op("dve", lambda e: e.tensor_tensor(out=H[g][:, m, :], in0=p[:, :GS], in1=H[g][:, m, :], op=ALU.add),
                         reads=[p, H[g]], writes=[H[g]])
                if e8 == 7:
                    c.dma("sp", ho3[:, :, g * GS:(g + 1) * GS], H[g][:], reads=[H[g]], writes=[hout])
                    u = uo.get()
                    norm_to(g, 8, u, final)
                    c.dma("sp", uo3[:, :, g * GS:(g + 1) * GS], u[:], reads=[u], writes=[uout])


def load_cast(c, dst_t, dst_ap, src_ap):
    c.dma("pool", dst_ap, src_ap, writes=[dst_t])


LB = 342
LNB = (T - PADF) // LB


def build_lru():
    nc = bass.Bass("TRN2", target_bir_lowering=False)
    with ExitStack() as es:
        c = Ctx(nc, es)
        io = {}
        io["uT"] = c.dram("uT", [D, T], BF16, "ExternalInput")
        io["wg"] = c.dram("wg", [D, 256], F32, "ExternalInput")
        io["wx"] = c.dram("wx", [D, 256], F32, "ExternalInput")
        io["wr"] = c.dram("wr", [256, 256], F32, "ExternalInput")
        io["wi"] = c.dram("wi", [256, 256], F32, "ExternalInput")
        io["prm"] = c.dram("prm", [128, 16], F32, "ExternalInput")
        io["yT"] = c.dram("yT", [256, T], BF16, "ExternalOutput")
        make_consts(c)
        emit_lru(c, io)
        c.finish()
    return nc


def emit_lru(c, io):
    if True:
        uT, wg, wx, wr, wi, prm, yT = (io[k] for k in ("uT", "wg", "wx", "wr", "wi", "prm", "yT"))
        one_t = c.sb([128, 1], F32, "one_t")
        c.op("pool", lambda e: e.memset(one_t[:], 1.0), writes=[one_t])
        ps_ = c.sb([128, 16], F32, "prm_s")
        c.dma("sp", ps_[:], prm[:], writes=[ps_])
        wgb = c.sb([128, 8, 256], BF16, "wgb")
        wxb = c.sb([128, 8, 256], BF16, "wxb")
        wrb = c.sb([128, 2, 256], BF16, "wrb")
        wib = c.sb([128, 2, 256], BF16, "wib")
        for (dst, src, nk) in ((wgb, wg, 8), (wxb, wx, 8), (wrb, wr, 2), (wib, wi, 2)):
            load_cast(c, dst, dst[:], src[:].rearrange("(k p) n -> p k n", p=128))
        cch = c.sb([128, 2], F32, "cch")
        ee = c.sb([128, 2], F32, "ee")
        acc = c.sb([128, 2], F32, "lacc")
        lam_ap = lambda: ps_[:, 7:16:8]
        c.op("act", lambda e: e.activation(out=ee[:], in_=lam_ap(), func=AF.Exp, scale=-1.0),
             reads=[ps_], writes=[ee])
        c.op("dve", lambda e: e.tensor_scalar(out=acc[:], in0=ee[:], scalar1=-1.0 / 6, scalar2=1.0 / 5,
                                              op0=ALU.mult, op1=ALU.add), reads=[ee], writes=[acc])
        for coef in (1.0 / 4, 1.0 / 3, 1.0 / 2, 1.0):
            c.op("dve", lambda e: e.tensor_tensor(out=acc[:], in0=acc[:], in1=ee[:], op=ALU.mult),
                 reads=[acc, ee], writes=[acc])
            c.op("dve", lambda e: e.tensor_scalar(out=acc[:], in0=acc[:], scalar1=-1.0, scalar2=coef,
                                                  op0=ALU.mult, op1=ALU.add), reads=[acc], writes=[acc])
        c.op("dve", lambda e: e.tensor_tensor(out=acc[:], in0=acc[:], in1=ee[:], op=ALU.mult),
             reads=[acc, ee], writes=[acc])
        c.op("dve", lambda e: e.tensor_scalar(out=cch[:], in0=acc[:], scalar1=-8.0, scalar2=None,
                                              op0=ALU.mult), reads=[acc], writes=[cch])

        ub = Rot([c.sb([128, 8, LB], BF16, "ub%d" % i) for i in range(3)])
        xc = [c.sb([128, LB + 3], F32, "xc%d" % t) for t in range(2)]
        for t in range(2):
            c.op("pool", lambda e: e.memset(xc[t][:], 0.0), writes=[xc[t]])
        xr = Rot([c.sb([128, LB], F32, "xr%d" % i) for i in range(2)])
        xrb = Rot([c.sb([128, 2, LB], BF16, "xrb%d" % i) for i in range(2)])
        gt = Rot([c.sb([128, LB], F32, "gt%d" % i) for i in range(4)])
        xrs = Rot([c.sb([128, LB], F32, "xrs%d" % i) for i in range(4)])
        tmp = Rot([c.sb([128, LB], F32, "tmp%d" % i) for i in range(6)])
        hs = [Rot([c.sb([128, LB], F32, "hs%d_%d" % (t, i)) for i in range(2)]) for t in range(2)]
        yb = Rot([c.sb([128, 2, LB], BF16, "yb%d" % i) for i in range(2)])
        pp = Rot([c.ps([128, 512], F32, "pp%d" % i) for i in range(4)])
        pg_ = Rot([c.ps([128, 512], F32, "pq%d" % i) for i in range(4)])
        zz = c.sb([128, 2, PADF], BF16, "zz")
        c.op("pool", lambda e: e.memset(zz[:], 0.0), writes=[zz])
        y3 = yT[:].rearrange("(t p) n -> p t n", p=128)
        c.dma("sp", y3[:, :, 0:PADF], zz[:], reads=[zz], writes=[yT])
        u3 = uT[:].rearrange("(k p) n -> p k n", p=128)
        prev_hs = [None, None]
        P = lambda t, j: ps_[:, t * 8 + j:t * 8 + j + 1]
        n = LB
        for blk in range(LNB):
            t0 = PADF + blk * LB
            u = ub.get()
            c.dma("sp", u[:], u3[:, :, t0:t0 + n], writes=[u])
            gts, xrf = [], []
            xb_ = xrb.get()
            for t in range(2):
                pgt = pp.get()
                for k in range(8):
                    c.op("pe", lambda e: e.matmul(pgt[:, :n], lhsT=wgb[:, k, t * 128:(t + 1) * 128], rhs=u[:, k, :],
                                                  start=(k == 0), stop=(k == 7)), reads=[wgb, u], writes=[pgt], inc=(k == 7))
                pxt = pp.get()
                for k in range(8):
                    c.op("pe", lambda e: e.matmul(pxt[:, :n], lhsT=wxb[:, k, t * 128:(t + 1) * 128], rhs=u[:, k, :],
                                                  start=(k == 0), stop=(k == 7)), reads=[wxb, u], writes=[pxt], inc=(k == 7))
                s = tmp.get()
                c.op("act", lambda e: e.activation(out=s[:], in_=pgt[:, :n], func=AF.Square), reads=[pgt], writes=[s])
                c.op("dve", lambda e: e.tensor_scalar(out=s[:], in0=s[:], scalar1=0.044715, scalar2=1.0,
                                                      op0=ALU.mult, op1=ALU.add), reads=[s], writes=[s])
                c.op("dve", lambda e: e.tensor_tensor(out=s[:], in0=s[:], in1=pgt[:, :n], op=ALU.mult),
                     reads=[s, pgt], writes=[s])
                c.op("act", lambda e: e.activation(out=s[:], in_=s[:], func=AF.Sigmoid, scale=1.5957691216057308),
                     reads=[s], writes=[s])
                g_ = gt.get()
                c.op("dve", lambda e: e.tensor_tensor(out=g_[:], in0=s[:], in1=pgt[:, :n], op=ALU.mult),
                     reads=[s, pgt], writes=[g_])
                gts.append(g_)
                c.op("act", lambda e: e.activation(out=xc[t][:, 3:3 + n], in_=pxt[:, :n], func=AF.Copy),
                     reads=[pxt], writes=[xc[t]])
                x_ = xrs.get()
                c.op("dve", lambda e: e.tensor_scalar(out=x_[:], in0=xc[t][:, 0:n], scalar1=P(t, 0), scalar2=P(t, 4),
                                                      op0=ALU.mult, op1=ALU.add), reads=[xc[t], ps_], writes=[x_])
                for j in range(1, 4):
                    c.op("dve", lambda e: e.scalar_tensor_tensor(out=x_[:], in0=xc[t][:, j:j + n], scalar=P(t, j),
                                                                 in1=x_[:], op0=ALU.mult, op1=ALU.add),
                         reads=[xc[t], ps_, x_], writes=[x_])
                c.op("pool", lambda e: e.tensor_copy(out=xc[t][:, 0:3], in_=xc[t][:, n:n + 3]),
                     reads=[xc[t]], writes=[xc[t]])
                c.op("pool", lambda e: e.tensor_copy(out=xb_[:, t, :], in_=x_[:]), reads=[x_], writes=[xb_])
                xrf.append(x_)
            y_ = yb.get()
            for t in range(2):
                pr = pg_.get()
                for k in range(2):
                    c.op("pe", lambda e: e.matmul(pr[:, :n], lhsT=wrb[:, k, t * 128:(t + 1) * 128], rhs=xb_[:, k, :],
                                                  start=(k == 0), stop=(k == 1)), reads=[wrb, xb_], writes=[pr], inc=(k == 1))
                pi_ = pg_.get()
                for k in range(2):
                    c.op("pe", lambda e: e.matmul(pi_[:, :n], lhsT=wib[:, k, t * 128:(t + 1) * 128], rhs=xb_[:, k, :],
                                                  start=(k == 0), stop=(k == 1)), reads=[wib, xb_], writes=[pi_], inc=(k == 1))
                a_ = tmp.get()
                c.op("act", lambda e: e.activation(out=a_[:], in_=pr[:, :n], func=AF.Sigmoid, bias=P(t, 5)),
                     reads=[pr, ps_], writes=[a_])
                c.op("act", lambda e: e.activation(out=a_[:], in_=a_[:], func=AF.Exp, scale=cch[:, t:t + 1]),
                     reads=[a_, cch], writes=[a_])
                i_ = tmp.get()
                c.op("act", lambda e: e.activation(out=i_[:], in_=pi_[:, :n], func=AF.Sigmoid, bias=P(t, 6)),
                     reads=[pi_, ps_], writes=[i_])
                m_ = tmp.get()
                c.op("dve", lambda e: e.tensor_tensor(out=m_[:], in0=a_[:], in1=a_[:], op=ALU.mult), reads=[a_], writes=[m_])
                c.op("act", lambda e: e.activation(out=m_[:], in_=m_[:], func=AF.Sqrt, bias=one_t[:, 0:1], scale=-1.0),
                     reads=[m_, one_t], writes=[m_])
                c.op("dve", lambda e: e.tensor_tensor(out=i_[:], in0=i_[:], in1=xrf[t][:], op=ALU.mult),
                     reads=[i_, xrf[t]], writes=[i_])
                c.op("dve", lambda e: e.tensor_tensor(out=i_[:], in0=i_[:], in1=m_[:], op=ALU.mult),
                     reads=[i_, m_], writes=[i_])
                h_ = hs[t].get()
                if prev_hs[t] is None:
                    c.op("dve", lambda e: e.tensor_tensor_scan(out=h_[:], data0=a_[:], data1=i_[:], initial=0.0,
                                                               op0=ALU.mult, op1=ALU.add), reads=[a_, i_], writes=[h_])
                else:
                    ph = prev_hs[t]
                    c.op("dve", lambda e: e.tensor_tensor_scan(out=h_[:], data0=a_[:], data1=i_[:], initial=ph[:, n - 1:n],
                                                               op0=ALU.mult, op1=ALU.add), reads=[a_, i_, ph], writes=[h_])
                prev_hs[t] = h_
                c.op("pool", lambda e: e.tensor_tensor(out=y_[:, t, :], in0=h_[:], in1=gts[t][:], op=ALU.mult),
                     reads=[h_, gts[t]], writes=[y_])
            c.dma("sp", y3[:, :, t0:t0 + n], y_[:], reads=[y_], writes=[yT])


def lru_params(cw, cb, br, bi, lam):
    prm = np.zeros((128, 2, 8), np.float32)
    for t in range(2):
        sl = slice(t * 128, (t + 1) * 128)
        prm[:, t, 0:4] = cw[:, sl].T
        prm[:, t, 4] = cb[sl]
        prm[:, t, 5] = br[sl]
        prm[:, t, 6] = bi[sl]
        prm[:, t, 7] = lam[sl]
    return np.ascontiguousarray(prm.reshape(128, 16))


TD = T + 64
DN_BLOCKS = [(i * 512, 4) for i in range(16)] + [(8192, 1)]


def dn_consts():
    i = np.arange(128)
    same = (i[:, None] // 64) == (i[None, :] // 64)
    cst = np.zeros((128, 6, 128), np.float32)
    cst[:, 0, :] = np.eye(128)
    cst[:, 1, :] = (same & (i[:, None] <= i[None, :]))
    cst[:, 2, :] = (same & (i[:, None] > i[None, :]))
    cst[:, 3, :] = np.where(same & (i[:, None] >= i[None, :]), 0.0, -30000.0)
    cst[:, 4, :] = (same & (i[:, None] > i[None, :]))
    cst[:, 5, :] = -1.0
    return cst


def build_dn(blocks=None, TD=TD, dbg=99):
    blocks = blocks or DN_BLOCKS
    nc = bass.Bass("TRN2", target_bir_lowering=False)
    with ExitStack() as es:
        c = Ctx(nc, es)
        io = {}
        io["uT"] = c.dram("uT", [D, TD], BF16, "ExternalInput")
        io["wcat"] = c.dram("wcat", [D, 1028], F32, "ExternalInput")
        io["cst"] = c.dram("cst", [128, 6, 128], F32, "ExternalInput")
        io["prm"] = c.dram("prm", [128, 32], F32, "ExternalInput")
        io["oT"] = c.dram("oT", [256, TD], BF16, "ExternalOutput")
        make_consts(c)
        emit_dn(c, io, blocks, dbg)
        c.finish()
    return nc


def emit_dn(c, io, blocks=None, dbg=99):
    blocks = blocks or DN_BLOCKS
    if True:
        uT, wcat, cstd, prm, oT = (io[k] for k in ("uT", "wcat", "cst", "prm", "oT"))
        one_t = c.sb([128, 1], F32, "one_t")
        c.op("pool", lambda e: e.memset(one_t[:], 1.0), writes=[one_t])
        cst = c.sb([128, 6, 128], F32, "cst_s")
        c.dma("sp", cst[:], cstd[:], writes=[cst])
        ident = cst[:, 0, :]
        U2 = cst[:, 1, :]
        R2 = cst[:, 2, :]
        negmask = cst[:, 3, :]
        smask = cst[:, 4, :]
        negones = cst[:, 5, :]
        ps_ = c.sb([128, 32], F32, "prm_s")
        c.dma("sp", ps_[:], prm[:], writes=[ps_])
        nea = c.sb([128, 2], F32, "nea")
        c.op("act", lambda e: e.activation(out=nea[:], in_=ps_[:, 24:26], func=AF.Exp), reads=[ps_], writes=[nea])
        c.op("dve", lambda e: e.tensor_scalar(out=nea[:], in0=nea[:], scalar1=-1.0, scalar2=None, op0=ALU.mult),
             reads=[nea], writes=[nea])
        wb = c.sb([128, 8, 1028], BF16, "wb")
        w3 = wcat[:].rearrange("(k p) n -> p k n", p=128)
        for k0 in range(0, 8, 2):
            load_cast(c, wb, wb[:, k0:k0 + 2, :], w3[:, k0:k0 + 2, :])

        ub = Rot([c.sb([128, 8, 512], BF16, "ub%d" % i) for i in range(2)])
        xc = [c.sb([128, 515], F32, "xc%d" % j) for j in range(6)]
        for j in range(6):
            c.op("pool", lambda e: e.memset(xc[j][:], 0.0), writes=[xc[j]])
        ft = [Rot([c.sb([128, 512], F32, "ft%d_%d" % (j, i)) for i in range(2)]) for j in range(6)]
        szr = [Rot([c.sb([128, 512], F32, "sz%d_%d" % (h, i)) for i in range(2)]) for h in range(2)]
        sqb = Rot([c.sb([128, 512], F32, "sq%d" % i) for i in range(2)])
        rstd = c.sb([128, 512], F32, "rstd")
        obr = Rot([c.sb([128, 2, 512], BF16, "ob%d" % i) for i in range(2)])
        S = [c.sb([128, 128], F32, "S%d" % h) for h in range(2)]
        for h in range(2):
            c.op("pool", lambda e: e.memset(S[h][:], 0.0), writes=[S[h]])
        bt = Rot([c.sb([128, 4, 2], F32, "bt%d" % i) for i in range(2)])
        gg = Rot([c.sb([128, 4, 2], F32, "gg%d" % i) for i in range(2)])
        gtmp = Rot([c.sb([128, 4, 2], F32, "gtmp%d" % i) for i in range(6)])
        sm2 = Rot([c.sb([128, 2], F32, "sm2_%d" % i) for i in range(12)])
        sm1 = Rot([c.sb([128, 1], F32, "sm1_%d" % i) for i in range(8)])
        NSQ = 80
        sq128 = Rot([c.sb([128, 128], F32, "m%d" % i) for i in range(NSQ)])
        pbig = Rot([c.ps([128, 512], F32, "pbig%d" % i) for i in range(2)])
        banks = [c.ps([128, 512], F32, "pbank%d" % i) for i in range(6)]
        psm = Rot([Tl(banks[i].t[:, 0:128], "psm%d" % i) for i in range(6)])

        u3 = uT[:].rearrange("(k p) n -> p k n", p=128)
        o3 = oT[:].rearrange("(h p) n -> p h n", p=128)

        def mm(out_t, out_ap, lhsT, rhs, reads, start=True, stop=True):
            c.op("pe", lambda e: e.matmul(out_ap, lhsT=lhsT, rhs=rhs, start=start, stop=stop),
                 reads=reads, writes=[out_t], inc=stop)

        def evac(eng, dst_t, dst_ap, src_t, src_ap, scale=None, extra=()):
            if eng == "act":
                if scale is None:
                    c.op("act", lambda e: e.activation(out=dst_ap, in_=src_ap, func=AF.Copy),
                         reads=[src_t], writes=[dst_t])
                else:
                    c.op("act", lambda e: e.activation(out=dst_ap, in_=src_ap, func=AF.Copy, scale=scale),
                         reads=[src_t] + list(extra), writes=[dst_t])
            else:
                c.op("dve", lambda e: e.tensor_copy(out=dst_ap, in_=src_ap), reads=[src_t], writes=[dst_t])

        for (b0, ntl) in blocks:
            n = ntl * 128
            u = ub.get()
            c.dma("sp", u[:, :, :n], u3[:, :, b0:b0 + n], writes=[u])
            F = []
            for j in range(8):
                p = pbig.get()
                for k in range(8):
                    mm(p, p[:, :n], wb[:, k, j * 128:(j + 1) * 128], u[:, k, :n], [wb, u], start=(k == 0), stop=(k == 7))
                if j < 6:
                    c.op("act", lambda e: e.activation(out=xc[j][:, 3:3 + n], in_=p[:, :n], func=AF.Copy),
                         reads=[p], writes=[xc[j]])
                    f = ft[j].get()
                    c.op("dve", lambda e: e.tensor_scalar(out=f[:, :n], in0=xc[j][:, 0:n], scalar1=ps_[:, j * 4:j * 4 + 1],
                                                          scalar2=None, op0=ALU.mult), reads=[xc[j], ps_], writes=[f])
                    for tp in range(1, 4):
                        c.op("dve", lambda e: e.scalar_tensor_tensor(out=f[:, :n], in0=xc[j][:, tp:tp + n],
                                                                     scalar=ps_[:, j * 4 + tp:j * 4 + tp + 1], in1=f[:, :n],
                                                                     op0=ALU.mult, op1=ALU.add),
                             reads=[xc[j], ps_, f], writes=[f])
                    c.op("pool", lambda e: e.tensor_copy(out=xc[j][:, 0:3], in_=xc[j][:, n:n + 3]),
                         reads=[xc[j]], writes=[xc[j]])
                    c.op("act", lambda e: e.activation(out=f[:, :n], in_=f[:, :n], func=AF.Silu), reads=[f], writes=[f])
                    F.append(f)
                else:
                    z = szr[j - 6].get()
                    c.op("act", lambda e: e.activation(out=z[:, :n], in_=p[:, :n], func=AF.Silu), reads=[p], writes=[z])
                    F.append(z)
            for j in range(4 if dbg >= 1 else 0):
                rms_stats(c, c.ones, lambda k, j=j: (F[j][:, :n], F[j]), 1, n, pbig.get(), sqb, rstd, 1.0)
                if j < 2:
                    c.op("dve", lambda e: e.scalar_tensor_tensor(out=F[j][:, :n], in0=F[j][:, :n], scalar=128.0 ** -0.5,
                                                                 in1=rstd[:, :n], op0=ALU.mult, op1=ALU.mult),
                         reads=[F[j], rstd], writes=[F[j]])
                else:
                    c.op("dve", lambda e: e.tensor_tensor(out=F[j][:, :n], in0=F[j][:, :n], in1=rstd[:, :n], op=ALU.mult),
                         reads=[F[j], rstd], writes=[F[j]])
            if dbg < 2:
                continue
            pba_t = psm.get()
            for tt in range(ntl):
                for k in range(8):
                    mm(pba_t, pba_t[:, tt * 4:(tt + 1) * 4], u[:, k, tt * 128:(tt + 1) * 128], wb[:, k, 1024:1028],
                       [u, wb], start=(k == 0), stop=(k == 7))
            pba = pba_t[:, 0:4 * ntl].rearrange("p (t f) -> p t f", f=4)
            btb = bt.get()
            ggb = gg.get()
            c.op("act", lambda e: e.activation(out=btb[:, :ntl, :], in_=pba[:, :, 0:2], func=AF.Sigmoid),
                 reads=[pba_t], writes=[btb])
            x_ = gtmp.get(); ax = gtmp.get(); rl = gtmp.get()
            for h in range(2):
                c.op("dve", lambda e: e.tensor_scalar(out=x_[:, :ntl, h:h + 1], in0=pba[:, :, 2 + h:3 + h],
                                                      scalar1=ps_[:, 26 + h:27 + h], scalar2=None, op0=ALU.add),
                     reads=[pba_t, ps_], writes=[x_])
            c.op("act", lambda e: e.activation(out=ax[:, :ntl, :], in_=x_[:, :ntl, :], func=AF.Abs),
                 reads=[x_], writes=[ax])
            c.op("act", lambda e: e.activation(out=ax[:, :ntl, :], in_=ax[:, :ntl, :], func=AF.Exp, scale=-1.0),
                 reads=[ax], writes=[ax])
            c.op("act", lambda e: e.activation(out=ax[:, :ntl, :], in_=ax[:, :ntl, :], func=AF.Ln, bias=one_t[:, 0:1]),
                 reads=[ax, one_t], writes=[ax])
            c.op("dve", lambda e: e.tensor_scalar(out=rl[:, :ntl, :], in0=x_[:, :ntl, :], scalar1=0.0, scalar2=None,
                                                  op0=ALU.max), reads=[x_], writes=[rl])
            c.op("dve", lambda e: e.tensor_tensor(out=rl[:, :ntl, :], in0=rl[:, :ntl, :], in1=ax[:, :ntl, :], op=ALU.add),
                 reads=[rl, ax], writes=[rl])
            for h in range(2):
                c.op("dve", lambda e: e.tensor_scalar(out=ggb[:, :ntl, h:h + 1], in0=rl[:, :ntl, h:h + 1],
                                                      scalar1=nea[:, h:h + 1], scalar2=None, op0=ALU.mult),
                     reads=[rl, nea], writes=[ggb])
            ob = obr.get()
            for tt in range(ntl if dbg >= 3 else 0):
                cs = slice(tt * 128, (tt + 1) * 128)
                HS = range(2)
                pg1 = psm.get(); pg2 = psm.get()
                mm(pg1, pg1[:, 0:2], U2, ggb[:, tt, :], [cst, ggb])
                mm(pg2, pg2[:, 0:2], R2, ggb[:, tt, :], [cst, ggb])
                egc = sm2.get(); ed = sm2.get(); be = sm2.get()
                c.op("act", lambda e: e.activation(out=egc[:], in_=pg1[:, 0:2], func=AF.Exp), reads=[pg1], writes=[egc])
                c.op("act", lambda e: e.activation(out=ed[:], in_=pg2[:, 0:2], func=AF.Exp), reads=[pg2], writes=[ed])
                c.op("dve", lambda e: e.tensor_tensor(out=be[:], in0=egc[:], in1=btb[:, tt, :], op=ALU.mult),
                     reads=[egc, btb], writes=[be])
                Ug, dec, decs, egrow, P, Q, Y = {}, {}, {}, {}, {}, {}, {}
                for h in HS:
                    Ug[h] = sq128.get()
                    c.op("dve", lambda e: e.tensor_scalar(out=Ug[h][:], in0=U2, scalar1=ggb[:, tt, h:h + 1], scalar2=None,
                                                          op0=ALU.mult), reads=[cst, ggb], writes=[Ug[h]])
                for h in HS:
                    pd = psm.get()
                    mm(pd, pd[:], Ug[h][:], c.ones[:], [Ug[h], c.ones], start=True, stop=False)
                    mm(pd, pd[:], negones, Ug[h][:], [cst, Ug[h]], start=False, stop=True)
                    dec[h] = sq128.get()
                    c.op("dve", lambda e: e.tensor_tensor(out=dec[h][:], in0=pd[:], in1=negmask, op=ALU.add),
                         reads=[pd, cst], writes=[dec[h]])
                    c.op("act", lambda e: e.activation(out=dec[h][:], in_=dec[h][:], func=AF.Exp),
                         reads=[dec[h]], writes=[dec[h]])
                    decs[h] = sq128.get()
                    c.op("pool", lambda e: e.tensor_tensor(out=decs[h][:], in0=dec[h][:], in1=smask, op=ALU.mult),
                         reads=[dec[h], cst], writes=[decs[h]])
                    pe_ = psm.get()
                    mm(pe_, pe_[:], c.ones[:], Ug[h][:], [c.ones, Ug[h]])
                    egrow[h] = sq128.get()
                    c.op("act", lambda e: e.activation(out=egrow[h][:], in_=pe_[:], func=AF.Exp),
                         reads=[pe_], writes=[egrow[h]])
                if dbg < 4:
                    continue
                for h in HS:
                    kT = F[2 + h][:, cs]
                    pk = psm.get()
                    mm(pk, pk[:], kT, kT, [F[2 + h]])
                    P[h] = sq128.get()
                    c.op("dve", lambda e: e.scalar_tensor_tensor(out=P[h][:], in0=pk[:], scalar=btb[:, tt, h:h + 1],
                                                                 in1=decs[h][:], op0=ALU.mult, op1=ALU.mult),
                         reads=[pk, btb, decs[h]], writes=[P[h]])
                if dbg < 4.3:
                    continue
                for h in HS:
                    pb = psm.get()
                    mm(pb, pb[:], P[h][:], ident, [P[h], cst])
                    Q[h] = sq128.get(); Y[h] = sq128.get()
                    evac("act", Q[h], Q[h][:], pb, pb[:])
                    c.op("pool", lambda e: e.tensor_tensor(out=Y[h][:], in0=ident, in1=Q[h][:], op=ALU.subtract),
                         reads=[Q[h], cst], writes=[Y[h]])
                if dbg < 4.6:
                    continue
                for s in range(5 if dbg >= 4.7 else 1):
                    Pn, Qn = {}, {}
                    for h in HS:
                        pp_ = psm.get()
                        mm(pp_, pp_[:], Q[h][:], P[h][:], [Q[h], P[h]])
                        Pn[h] = sq128.get()
                        evac("act", Pn[h], Pn[h][:], pp_, pp_[:])
                        if s < 4:
                            pq = psm.get()
                            mm(pq, pq[:], P[h][:], Q[h][:], [P[h], Q[h]])
                            Qn[h] = sq128.get()
                            evac("dve", Qn[h], Qn[h][:], pq, pq[:])
                    for h in HS:
                        py_ = psm.get()
                        mm(py_, py_[:], Pn[h][:], Y[h][:], [Pn[h], Y[h]])
                        Yn = sq128.get()
                        c.op("dve", lambda e: e.tensor_tensor(out=Yn[:], in0=py_[:], in1=Y[h][:], op=ALU.add),
                             reads=[py_, Y[h]], writes=[Yn])
                        Y[h] = Yn
                        P[h] = Pn[h]
                        if s < 4:
                            Q[h] = Qn[h]
                if dbg < 5:
                    continue
                kbg, kd, vb, usb, wTs, attT, qg = {}, {}, {}, {}, {}, {}, {}
                for h in HS:
                    kT = F[2 + h][:, cs]; vT = F[4 + h][:, cs]; qT = F[h][:, cs]
                    pkt = psm.get()
                    mm(pkt, pkt[:], kT, ident, [F[2 + h], cst])
                    kbg[h] = sq128.get(); kd[h] = sq128.get()
                    evac("act", kbg[h], kbg[h][:], pkt, pkt[:], scale=be[:, h:h + 1], extra=[be])
                    evac("act", kd[h], kd[h][:], pkt, pkt[:], scale=ed[:, h:h + 1], extra=[ed])
                    pvt = psm.get()
                    mm(pvt, pvt[:], vT, ident, [F[4 + h], cst])
                    vb[h] = sq128.get()
                    evac("act", vb[h], vb[h][:], pvt, pvt[:], scale=btb[:, tt, h:h + 1], extra=[btb])
                    pqk = psm.get()
                    mm(pqk, pqk[:], qT, kT, [F[h], F[2 + h]])
                    att = sq128.get()
                    c.op("dve", lambda e: e.tensor_tensor(out=att[:], in0=pqk[:], in1=dec[h][:], op=ALU.mult),
                         reads=[pqk, dec[h]], writes=[att])
                    pat = psm.get()
                    mm(pat, pat[:], att[:], ident, [att, cst])
                    attT[h] = sq128.get()
                    evac("dve", attT[h], attT[h][:], pat, pat[:])
                    qg[h] = sq128.get()
                    c.op("pool", lambda e: e.tensor_tensor(out=qg[h][:], in0=qT, in1=egrow[h][:], op=ALU.mult),
                         reads=[F[h], egrow[h]], writes=[qg[h]])
                for h in HS:
                    pu = psm.get()
                    mm(pu, pu[:], Y[h][:], vb[h][:], [Y[h], vb[h]])
                    usb[h] = sq128.get()
                    evac("act", usb[h], usb[h][:], pu, pu[:])
                    pw = psm.get()
                    mm(pw, pw[:], kbg[h][:], Y[h][:], [kbg[h], Y[h]])
                    wTs[h] = sq128.get()
                    evac("dve", wTs[h], wTs[h][:], pw, pw[:])
                if dbg < 6:
                    continue
                otok = {h: sq128.get() for h in HS}
                vnew = {h: sq128.get() for h in HS}
                for half in range(2):
                    r = slice(half * 64, half * 64 + 64)
                    for h in HS:
                        pws = psm.get()
                        mm(pws, pws[:], wTs[h][:], S[h][:], [wTs[h], S[h]])
                        c.op("dve", lambda e: e.tensor_tensor(out=vnew[h][r, :], in0=usb[h][r, :], in1=pws[r, :], op=ALU.subtract),
                             reads=[usb[h], pws], writes=[vnew[h]])
                    for h in HS:
                        po = psm.get()
                        mm(po, po[:], qg[h][:], S[h][:], [qg[h], S[h]], start=True, stop=False)
                        mm(po, po[:], attT[h][r, :], vnew[h][r, :], [attT[h], vnew[h]], start=False, stop=True)
                        evac("act", otok[h], otok[h][r, :], po, po[r, :])
                        pst = psm.get()
                        mm(pst, pst[:], kd[h][r, :], vnew[h][r, :], [kd[h], vnew[h]])
                        c.op("dve", lambda e: e.scalar_tensor_tensor(out=S[h][:], in0=S[h][:],
                                                                     scalar=egrow[h][:, half * 64 + 63:half * 64 + 64],
                                                                     in1=pst[:], op0=ALU.mult, op1=ALU.add),
                             reads=[S[h], egrow[h], pst], writes=[S[h]])
                if dbg < 7:
                    continue
                for h in HS:
                    junk = sq128.get()
                    ss = sm1.get()
                    c.op("act", lambda e: e.activation(out=junk[:], in_=otok[h][:], func=AF.Square, accum_out=ss[:]),
                         reads=[otok[h]], writes=[junk, ss])
                    c.op("act", lambda e: e.activation(out=ss[:], in_=ss[:], func=AF.Sqrt, bias=c.eps_t[:, 0:1], scale=1.0 / 128),
                         reads=[ss, c.eps_t], writes=[ss])
                    c.op("dve", lambda e: e.reciprocal(out=ss[:], in_=ss[:]), reads=[ss], writes=[ss])
                    on = sq128.get()
                    c.op("dve", lambda e: e.tensor_scalar(out=on[:], in0=otok[h][:], scalar1=ss[:, 0:1], scalar2=None, op0=ALU.mult),
                         reads=[otok[h], ss], writes=[on])
                    pot = psm.get()
                    mm(pot, pot[:], on[:], ident, [on, cst])
                    c.op("dve", lambda e: e.scalar_tensor_tensor(out=ob[:, h, cs], in0=pot[:], scalar=ps_[:, 28:29],
                                                                 in1=F[6 + h][:, cs], op0=ALU.mult, op1=ALU.mult),
                         reads=[pot, ps_, F[6 + h]], writes=[ob])
            c.dma("sp", o3[:, :, b0:b0 + n], ob[:, :, :n], reads=[ob], writes=[oT])


def dn_params(conv_w6, a_log2, dt_bias2, norm_w):
    prm = np.zeros((128, 32), np.float32)
    for j in range(6):
        prm[:, j * 4:(j + 1) * 4] = conv_w6[:, j * 128:(j + 1) * 128].T
    prm[:, 24:26] = a_log2[None, :]
    prm[:, 26:28] = dt_bias2[None, :]
    prm[:, 28] = norm_w
    return prm


DA_DS = (-128, 0, 128, 256, 384)
DA_NEGM = -240000.0
LAMBDA_INIT_L1 = 0.8 - 0.6 * math.exp(-0.3 * 1)


def _t5_bucket_np(rel):
    import jax
    import jax.numpy as jnp
    with jax.default_device(jax.devices("cpu")[0]):
        rel = jnp.asarray(rel, jnp.int32)
        nb = 16
        ret = jnp.where(rel > 0, nb, 0)
        n = jnp.abs(rel)
        max_exact = nb // 2
        nf = jnp.maximum(n, 1).astype(jnp.float32)
        large = max_exact + (jnp.log(nf / max_exact) / math.log(128 / max_exact)
                             * (nb - max_exact)).astype(jnp.int32)
        large = jnp.minimum(large, nb - 1)
        return np.asarray(ret + jnp.where(n < max_exact, n, large))


def da_consts():
    r = np.arange(-639, 513)
    bk = _t5_bucket_np(r)
    oh = np.zeros((32, 1152), np.float32)
    oh[bk, np.arange(1152)] = 1.0
    oh[15, :] -= 1.0
    kk = np.arange(128)[:, None]
    qq = np.arange(512)[None, :]
    md = np.zeros((128, 5, 512), np.float32)
    for i, d in enumerate(DA_DS):
        allowed = ((d + kk) // 64) <= (qq // 64)
        md[:, i, :] = np.where(allowed, 0.0, DA_NEGM)
    pm = np.zeros((128, 1), np.float32)
    pm[:112] = -30000.0
    return oh, md, pm


def build_da(lambda_init=LAMBDA_INIT_L1):
    NKT = TD // 128
    qtiles = [(0, 128)] + [(128 + 512 * i, 512) for i in range(16)]
    nc = bass.Bass("TRN2", target_bir_lowering=False)
    with ExitStack() as es:
        c = Ctx(nc, es)
        io = {}
        io["uT"] = c.dram("uT", [D, TD], BF16, "ExternalInput")
        io["wcat"] = c.dram("wcat", [D, 768], F32, "ExternalInput")
        io["oh"] = c.dram("oh", [32, 1152], F32, "ExternalInput")
        io["md"] = c.dram("md", [128, 5, 512], F32, "ExternalInput")
        io["pm"] = c.dram("pm", [128, 1], F32, "ExternalInput")
        io["rb"] = c.dram("rb", [32, 2], F32, "ExternalInput")
        io["rb15"] = c.dram("rb15", [128, 2], F32, "ExternalInput")
        io["lamv"] = c.dram("lamv", [128, 4, 64], F32, "ExternalInput")
        io["sw"] = c.dram("sw", [128, 1], F32, "ExternalInput")
        io["identf"] = c.dram("identf", [128, 128], F32, "ExternalInput")
        io["oT"] = c.dram("oT", [256, TD], BF16, "ExternalOutput")
        make_consts(c)
        emit_da(c, io, lambda_init, "")
        c.finish()
    return nc


def emit_da(c, io, lambda_init, tag):
    NKT = TD // 128
    qtiles = [(0, 128)] + [(128 + 512 * i, 512) for i in range(16)]
    if True:
        uT, wcat, ohd, mdd, pmd, rb, rb15, lamv, sw, idd, oT = (io[k] for k in (
            "uT", "wcat", "oh", "md", "pm", "rb", "rb15", "lamv", "sw", "identf", "oT"))
        tvd = c.dram("tvscr" + tag, [2, 1152], F32, "Internal")
        identb = c.sb([128, 128], BF16, "identb")
        onesb = c.sb([128, 128], BF16, "onesb")
        c.op("pool", lambda e: e.memset(onesb[:], 1.0), writes=[onesb])
        QT = [c.sb([128, TD], BF16, "QT%d" % h) for h in range(2)]
        KT = [c.sb([128, TD], BF16, "KT%d" % h) for h in range(2)]
        V = c.sb([128, NKT, 256], BF16, "V")
        BH = [[c.sb([128, 512], BF16, "BH%d_%d" % (h, i)) for i in range(5)] for h in range(2)]
        BL = [[c.sb([128, 512], BF16, "BL%d_%d" % (h, i)) for i in range(5)] for h in range(2)]
        biasc = c.sb([128, 2], F32, "biasc")
        bias0 = c.sb([128, 2], F32, "bias0")
        neglam = c.sb([128, 1], F32, "neglam")
        swp = c.sb([128, 1], F32, "swp")

        with ExitStack() as es1:
            c.dma("sp", biasc[:], rb15[:], writes=[biasc])
            pms = c.sb([128, 1], F32, "pms", es=es1)
            c.dma("sp", pms[:], pmd[:], writes=[pms])
            c.op("dve", lambda e: e.tensor_scalar(out=bias0[:], in0=biasc[:], scalar1=pms[:, 0:1], scalar2=None, op0=ALU.add),
                 reads=[biasc, pms], writes=[bias0])
            sws = c.sb([128, 1], F32, "sws", es=es1)
            c.dma("sp", sws[:], sw[:], writes=[sws])
            c.op("dve", lambda e: e.tensor_scalar(out=swp[:], in0=sws[:], scalar1=1.0 - lambda_init, scalar2=None, op0=ALU.mult),
                 reads=[sws], writes=[swp])
            lv = c.sb([128, 4, 64], F32, "lv", es=es1)
            c.dma("sp", lv[:], lamv[:], writes=[lv])
            pr = c.sb([128, 2, 64], F32, "lpr", es=es1)
            sm = c.sb([128, 2], F32, "lsm", es=es1)
            for i in range(2):
                c.op("dve", lambda e: e.tensor_tensor(out=pr[:, i, :], in0=lv[:, 2 * i, :], in1=lv[:, 2 * i + 1, :], op=ALU.mult),
                     reads=[lv], writes=[pr])
                c.op("dve", lambda e: e.reduce_sum(out=sm[:, i:i + 1], in_=pr[:, i, :], axis=AX.X), reads=[pr], writes=[sm])
            c.op("act", lambda e: e.activation(out=sm[:], in_=sm[:], func=AF.Exp), reads=[sm], writes=[sm])
            c.op("dve", lambda e: e.tensor_tensor(out=neglam[:], in0=sm[:, 1:2], in1=sm[:, 0:1], op=ALU.subtract),
                 reads=[sm], writes=[neglam])
            c.op("dve", lambda e: e.tensor_scalar(out=neglam[:], in0=neglam[:], scalar1=-lambda_init, scalar2=None, op0=ALU.add),
                 reads=[neglam], writes=[neglam])
            idf = c.sb([128, 128], F32, "idf", es=es1)
            c.dma("sp", idf[:], idd[:], writes=[idf])
            c.op("dve", lambda e: e.tensor_copy(out=identb[:], in_=idf[:]), reads=[idf], writes=[identb])
            ohs = c.sb([32, 1152], F32, "ohs", es=es1)
            c.dma("sp", ohs[:], ohd[:], writes=[ohs])
            rbs = c.sb([32, 2], F32, "rbs", es=es1)
            c.dma("sp", rbs[:], rb[:], writes=[rbs])
            tvs = c.sb([2, 1152], F32, "tvs", es=es1)
            ptv = c.ps([128, 512], F32, "ptv", es=es1)
            for j in range(3):
                c.op("pe", lambda e: e.matmul(ptv[0:2, 0:384], lhsT=rbs[:], rhs=ohs[:, j * 384:(j + 1) * 384], start=True, stop=True),
                     reads=[rbs, ohs], writes=[ptv])
                c.op("act", lambda e: e.activation(out=tvs[:, j * 384:(j + 1) * 384], in_=ptv[0:2, 0:384], func=AF.Copy),
                     reads=[ptv], writes=[tvs])
            c.dma("sp", tvd[:], tvs[:], reads=[tvs], writes=[tvd])
            mds = c.sb([128, 5, 512], F32, "mds", es=es1)
            c.dma("sp", mds[:], mdd[:], writes=[mds])
            G = Rot([c.sb([128, 512], F32, "G%d" % i, es=es1) for i in range(2)])
            Bt = Rot([c.sb([128, 512], F32, "Bt%d" % i, es=es1) for i in range(2)])
            for h in range(2):
                for i, d in enumerate(DA_DS):
                    g_ = G.get()
                    src = bass.AP(tensor=tvd.t.tensor, offset=h * 1152 + d + 128, ap=[[1, 128], [1, 512]])
                    c.dma("sp", g_[:], src, reads=[tvd], writes=[g_])
                    b_ = Bt.get()
                    c.op("dve", lambda e: e.scalar_tensor_tensor(out=b_[:], in0=g_[:, ::-1], scalar=8.0, in1=mds[:, i, :],
                                                                 op0=ALU.mult, op1=ALU.add), reads=[g_, mds], writes=[b_])
                    c.op("act", lambda e: e.activation(out=BH[h][i][:], in_=b_[:], func=AF.Copy), reads=[b_], writes=[BH[h][i]])
                    c.op("dve", lambda e: e.tensor_tensor(out=BL[h][i][:], in0=b_[:], in1=BH[h][i][:], op=ALU.subtract),
                         reads=[b_, BH[h][i]], writes=[BL[h][i]])

        c.barrier()
        with ExitStack() as es2:
            wb = c.sb([128, 8, 768], BF16, "wb", es=es2)
            w3 = wcat[:].rearrange("(k p) n -> p k n", p=128)
            for k0 in range(0, 8, 2):
                load_cast(c, wb, wb[:, k0:k0 + 2, :], w3[:, k0:k0 + 2, :])
            ub = Rot([c.sb([128, 8, 512], BF16, "ub%d" % i, es=es2) for i in range(2)])
            pbig = Rot([c.ps([128, 512], F32, "pb%d" % i, es=es2) for i in range(6)])
            u3 = uT[:].rearrange("(k p) n -> p k n", p=128)
            for (b0, ntl) in DN_BLOCKS:
                n = ntl * 128
                u = ub.get()
                c.dma("sp", u[:, :, :n], u3[:, :, b0:b0 + n], writes=[u])
                for j in range(4):
                    p = pbig.get()
                    for k in range(8):
                        c.op("pe", lambda e: e.matmul(p[:, :n], lhsT=wb[:, k, j * 128:(j + 1) * 128], rhs=u[:, k, :n],
                                                      start=(k == 0), stop=(k == 7)), reads=[wb, u], writes=[p], inc=(k == 7))
                    dst = (QT[j] if j < 2 else KT[j - 2])
                    if j % 2 == 0:
                        c.op("act", lambda e: e.activation(out=dst[:, b0:b0 + n], in_=p[:, :n], func=AF.Copy), reads=[p], writes=[dst])
                    else:
                        c.op("dve", lambda e: e.tensor_copy(out=dst[:, b0:b0 + n], in_=p[:, :n]), reads=[p], writes=[dst])
                for tt in range(ntl):
                    p = pbig.get()
                    for k in range(8):
                        c.op("pe", lambda e: e.matmul(p[:, 0:256], lhsT=u[:, k, tt * 128:(tt + 1) * 128], rhs=wb[:, k, 512:768],
                                                      start=(k == 0), stop=(k == 7)), reads=[wb, u], writes=[p], inc=(k == 7))
                    kt = b0 // 128 + tt
                    if tt % 2 == 0:
                        c.op("act", lambda e: e.activation(out=V[:, kt, :], in_=p[:, 0:256], func=AF.Copy), reads=[p], writes=[V])
                    else:
                        c.op("dve", lambda e: e.tensor_copy(out=V[:, kt, :], in_=p[:, 0:256]), reads=[p], writes=[V])

        c.barrier()
        with ExitStack() as es3:
            sps = Rot([c.ps([128, 512], F32, "sps%d" % i, es=es3) for i in range(4)])
            oacc = [c.ps([128, 512], F32, "oacc%d" % i, es=es3) for i in range(2)]
            dacc = [c.ps([128, 512], F32, "dacc%d" % i, es=es3) for i in range(2)]
            ptb = Rot([c.sb([128, 512], BF16, "pt%d" % i, es=es3) for i in range(4)])
            rr = [c.sb([128, 512], F32, "rr%d" % i, es=es3) for i in range(2)]
            aa = [c.sb([128, 512], F32, "aa%d" % i, es=es3) for i in range(2)]
            sqb = Rot([c.sb([128, 512], F32, "sq%d" % i, es=es3) for i in range(2)])
            rstd = c.sb([128, 512], F32, "rstd", es=es3)
            obr = Rot([c.sb([128, 512], BF16, "ob%d" % i, es=es3) for i in range(2)])
            dsr = [Rot([c.sb([128, 512], F32, "dsum%d_%d" % (cc, i), es=es3) for i in range(2)]) for cc in range(2)]
            for (q0, nq) in qtiles:
                ktmax = (q0 + nq) // 128 - 1
                for h in range(2):
                    units = [(kt, cc) for kt in range(ktmax + 1) for cc in range(2)]
                    dsum = [dsr[0].get(), dsr[1].get()]

                    def emit_s(kt, cc):
                        ps = sps.get()
                        d = kt * 128 - q0
                        near = d in DA_DS
                        rs = slice(cc * 64, cc * 64 + 64)
                        c.op("pe", lambda e: e.matmul(ps[:, :nq], lhsT=KT[h][rs, kt * 128:(kt + 1) * 128], rhs=QT[h][rs, q0:q0 + nq],
                                                      start=True, stop=not near), reads=[KT[h], QT[h]], writes=[ps], inc=not near)
                        if near:
                            i = DA_DS.index(d)
                            c.op("pe", lambda e: e.matmul(ps[:, :nq], lhsT=identb[:], rhs=BH[h][i][:, :nq], start=False, stop=False),
                                 reads=[identb, BH[h][i]], writes=[ps], inc=False)
                            c.op("pe", lambda e: e.matmul(ps[:, :nq], lhsT=identb[:], rhs=BL[h][i][:, :nq], start=False, stop=True),
                                 reads=[identb, BL[h][i]], writes=[ps])
                        return ps

                    pend = [emit_s(*units[0])]
                    if len(units) > 1:
                        pend.append(emit_s(*units[1]))
                    for ui, (kt, cc) in enumerate(units):
                        ps = pend[ui]
                        pt = ptb.get()
                        bsrc = bias0 if kt == 0 else biasc
                        c.op("act", lambda e: e.activation(out=pt[:, :nq], in_=ps[:, :nq], func=AF.Exp, bias=bsrc[:, h:h + 1], scale=0.125),
                             reads=[ps, bsrc], writes=[pt])
                        if ui + 2 < len(units):
                            pend.append(emit_s(*units[ui + 2]))
                        c.op("pe", lambda e: e.matmul(oacc[cc][:, :nq], lhsT=V[:, kt, h * 128:(h + 1) * 128], rhs=pt[:, :nq],
                                                      start=(kt == 0), stop=(kt == ktmax)), reads=[V, pt], writes=[oacc[cc]])
                        if kt == 0:
                            c.op("dve", lambda e: e.tensor_copy(out=dsum[cc][:, :nq], in_=pt[:, :nq]), reads=[pt], writes=[dsum[cc]])
                        else:
                            c.op("dve", lambda e: e.tensor_tensor(out=dsum[cc][:, :nq], in0=dsum[cc][:, :nq], in1=pt[:, :nq], op=ALU.add),
                                 reads=[pt, dsum[cc]], writes=[dsum[cc]])
                    for cc in range(2):
                        c.op("pe", lambda e: e.matmul(dacc[cc][:, :nq], lhsT=c.ones[:], rhs=dsum[cc][:, :nq], start=True, stop=True),
                             reads=[c.ones, dsum[cc]], writes=[dacc[cc]])
                    for cc in range(2):
                        if q0 == 0:
                            c.op("dve", lambda e: e.tensor_scalar(out=rr[cc][:, :nq], in0=dacc[cc][:, :nq], scalar1=1e-30, scalar2=None,
                                                                  op0=ALU.max), reads=[dacc[cc]], writes=[rr[cc]])
                            c.op("dve", lambda e: e.reciprocal(out=rr[cc][:, :nq], in_=rr[cc][:, :nq]), reads=[rr[cc]], writes=[rr[cc]])
                        else:
                            c.op("dve", lambda e: e.reciprocal(out=rr[cc][:, :nq], in_=dacc[cc][:, :nq]), reads=[dacc[cc]], writes=[rr[cc]])
                        c.op("dve", lambda e: e.tensor_tensor(out=aa[cc][:, :nq], in0=oacc[cc][:, :nq], in1=rr[cc][:, :nq], op=ALU.mult),
                             reads=[oacc[cc], rr[cc]], writes=[aa[cc]])
                    c.op("dve", lambda e: e.scalar_tensor_tensor(out=aa[0][:, :nq], in0=aa[1][:, :nq], scalar=neglam[:, 0:1],
                                                                 in1=aa[0][:, :nq], op0=ALU.mult, op1=ALU.add),
                         reads=[aa[0], aa[1], neglam], writes=[aa[0]])
                    rms_stats(c, c.ones, lambda k: (aa[0][:, :nq], aa[0]), 1, nq, sps.get(), sqb, rstd, 128.0)
                    ob = obr.get()
                    c.op("dve", lambda e: e.scalar_tensor_tensor(out=ob[:, :nq], in0=aa[0][:, :nq], scalar=swp[:, 0:1],
                                                                 in1=rstd[:, :nq], op0=ALU.mult, op1=ALU.mult),
                         reads=[aa[0], swp, rstd], writes=[ob])
                    if q0 == 0:
                        c.op("dve", lambda e: e.memset(ob[:, 0:112], 0.0), writes=[ob])
                    c.dma("sp", oT[h * 128:(h + 1) * 128, q0:q0 + nq], ob[:, :nq], reads=[ob], writes=[oT])


def build_pre():
    nc = bass.Bass("TRN2", target_bir_lowering=False)
    with ExitStack() as es:
        c = Ctx(nc, es)
        io = {}
        io["hT"] = c.dram("hT", [D, NTC], F32, "ExternalInput")
        io["nw"] = c.dram("nw", [128, 8], F32, "ExternalInput")
        io["uout"] = c.dram("uout", [D, NTC], BF16, "ExternalOutput")
        make_consts(c)
        emit_pre(c, io)
        c.finish()
    return nc


def emit_pre(c, io):
    if True:
        hT, nw, uout = io["hT"], io["nw"], io["uout"]
        nws = c.sb([128, 8], F32, "nws")
        c.dma("sp", nws[:], nw[:], writes=[nws])
        H = Rot([c.sb([128, 8, GS], F32, "H%d" % g) for g in range(2)])
        U = Rot([c.sb([128, 8, GS], BF16, "U%d" % g) for g in range(2)])
        sqb = Rot([c.sb([128, GS], F32, "sq%d" % i) for i in range(2)])
        rstd = c.sb([128, GS], F32, "rstd")
        pstat = c.ps([128, 512], F32, "pstat")
        hT3 = hT[:].rearrange("(k p) n -> p k n", p=128)
        uo3 = uout[:].rearrange("(k p) n -> p k n", p=128)
        for g in range(NG):
            h = H.get()
            c.dma("sp", h[:], hT3[:, :, g * GS:(g + 1) * GS], writes=[h])
            rms_stats(c, c.ones, lambda k: (h[:, k, :], h), 8, GS, pstat, sqb, rstd, float(D))
            u = U.get()
            for m in range(8):
                c.op("dve", lambda e: e.scalar_tensor_tensor(out=u[:, m, :], in0=h[:, m, :], scalar=nws[:, m:m + 1],
                                                             in1=rstd[:], op0=ALU.mult, op1=ALU.mult),
                     reads=[h, nws, rstd], writes=[u])
            c.dma("sp", uo3[:, :, g * GS:(g + 1) * GS], u[:], reads=[u], writes=[uout])


def _run(nc, in_maps):
    res = run_bass_kernel_spmd(nc, in_maps, core_ids=list(range(NCORES)))
    return res.results


def _nwcols(w):
    return np.ascontiguousarray(np.asarray(w, np.float32).reshape(8, 128).T)


def _tok_shards(fullT):
    out = []
    for b in range(B):
        for j in range(4):
            out.append(np.ascontiguousarray(fullT[b][:, j * NTC:(j + 1) * NTC]))
    return out


def _gather_tok(shards):
    return [np.concatenate([shards[4 * b + j] for j in range(4)], axis=1) for b in range(B)]


def kernel_unfused(x, meta_tokens, rel_bias, norm_mix_w, norm_mlp_w, final_norm_w,
           dn_w_in, dn_conv_w, dn_a_log, dn_dt_bias, dn_norm_w, dn_w_out,
           da_w_in, da_lam_q1, da_lam_k1, da_lam_q2, da_lam_k2, da_subln_w, da_w_out,
           lru_w_in, lru_conv_w, lru_conv_b, lru_w_rgate, lru_b_rgate, lru_w_igate,
           lru_b_igate, lru_lambda, lru_w_out, mlp_w1, mlp_w2):
    f32 = np.float32
    x = np.asarray(x, f32)
    meta = np.asarray(meta_tokens, f32)
    bf = ml_dtypes.bfloat16
    hT_full = []
    for b in range(B):
        seq = np.concatenate([np.zeros((PADF, D), f32), meta, x[b]], axis=0)
        hT_full.append(np.ascontiguousarray(seq.T))
    h_sh = _tok_shards(hT_full)
    nc = build_pre()
    r = _run(nc, [{"hT": h_sh[c], "nw": _nwcols(norm_mix_w[0])} for c in range(NCORES)])
    u_sh = [r[c]["uout"] for c in range(NCORES)]
    depth = 4
    zpad = np.zeros((D, 64), bf)
    for layer in range(depth):
        kind = layer % 3
        slot = layer // 3
        u_full = _gather_tok(u_sh)
        ims = []
        if kind == 0:
            w_in = np.asarray(dn_w_in[slot], f32)
            cw = np.asarray(dn_conv_w[slot], f32)
            cst = dn_consts()
            for c in range(NCORES):
                b, g = divmod(c, 4)
                h0 = 2 * g
                s = slice(h0 * 128, h0 * 128 + 256)
                wcat = np.concatenate([w_in[:, 0:1024][:, s], w_in[:, 1024:2048][:, s], w_in[:, 2048:3072][:, s],
                                       w_in[:, 3072:4096][:, s], w_in[:, 4096 + h0:4096 + h0 + 2],
                                       w_in[:, 4104 + h0:4104 + h0 + 2]], axis=1)
                cw6 = np.concatenate([cw[:, 0:1024][:, s], cw[:, 1024:2048][:, s], cw[:, 2048:3072][:, s]], axis=1)
                prm = dn_params(cw6, np.asarray(dn_a_log[slot], f32)[h0:h0 + 2],
                                np.asarray(dn_dt_bias[slot], f32)[h0:h0 + 2], np.asarray(dn_norm_w[slot], f32))
                ims.append({"uT": np.ascontiguousarray(np.concatenate([zpad, u_full[b]], axis=1)),
                            "wcat": np.ascontiguousarray(wcat), "cst": cst, "prm": prm})
            r = _run(build_dn(), ims)
            o_full = [np.concatenate([r[4 * b + g]["oT"][:, 64:] for g in range(4)], axis=0) for b in range(B)]
            wo = np.asarray(dn_w_out[slot], f32)
        elif kind == 1:
            w_in = np.asarray(da_w_in[slot], f32)
            oh, md, pm = da_consts()
            rbt = np.asarray(rel_bias, f32)
            lamv = np.stack([np.asarray(v[slot], f32) for v in (da_lam_q1, da_lam_k1, da_lam_q2, da_lam_k2)], axis=0)
            lamv = np.ascontiguousarray(np.broadcast_to(lamv[None], (128, 4, 64)))
            sw = np.ascontiguousarray(np.asarray(da_subln_w[slot], f32)[:, None])
            ident = np.eye(128, dtype=f32)
            for c in range(NCORES):
                b, g = divmod(c, 4)
                h0 = 2 * g
                s = slice(h0 * 128, h0 * 128 + 256)
                wcat = np.concatenate([w_in[:, 0:1024][:, s], w_in[:, 1024:2048][:, s], w_in[:, 2048:3072][:, s]], axis=1)
                ims.append({"uT": np.ascontiguousarray(np.concatenate([zpad, u_full[b]], axis=1)),
                            "wcat": np.ascontiguousarray(wcat), "oh": oh, "md": md, "pm": pm,
                            "rb": np.ascontiguousarray(rbt[:, h0:h0 + 2]),
                            "rb15": np.ascontiguousarray(np.broadcast_to(rbt[15:16, h0:h0 + 2], (128, 2))),
                            "lamv": lamv, "sw": sw, "identf": ident})
            lam_init = 0.8 - 0.6 * math.exp(-0.3 * layer)
            r = _run(build_da(lam_init), ims)
            o_full = [np.concatenate([r[4 * b + g]["oT"][:, 64:] for g in range(4)], axis=0) for b in range(B)]
            wo = np.asarray(da_w_out[slot], f32)
        else:
            w_in = np.asarray(lru_w_in[slot], f32)
            cw = np.asarray(lru_conv_w[slot], f32)
            for c in range(NCORES):
                b, g = divmod(c, 4)
                s = slice(g * 256, (g + 1) * 256)
                prm = lru_params(cw[:, s], np.asarray(lru_conv_b[slot], f32)[s], np.asarray(lru_b_rgate[slot], f32)[s],
                                 np.asarray(lru_b_igate[slot], f32)[s], np.asarray(lru_lambda[slot], f32)[s])
                ims.append({"uT": u_full[b], "wg": np.ascontiguousarray(w_in[:, 0:1024][:, s]),
                            "wx": np.ascontiguousarray(w_in[:, 1024:2048][:, s]),
                            "wr": np.ascontiguousarray(np.asarray(lru_w_rgate[slot], f32)[g]),
                            "wi": np.ascontiguousarray(np.asarray(lru_w_igate[slot], f32)[g]), "prm": prm})
            r = _run(build_lru(), ims)
            o_full = [np.concatenate([r[4 * b + g]["yT"] for g in range(4)], axis=0) for b in range(B)]
            wo = np.asarray(lru_w_out[slot], f32)
        o_sh = _tok_shards(o_full)
        final = (layer == depth - 1)
        nxt = final_norm_w if final else norm_mix_w[layer + 1]
        nw = np.ascontiguousarray(np.concatenate([_nwcols(norm_mlp_w[layer]), _nwcols(nxt)], axis=1))
        w1 = np.asarray(mlp_w1[layer], f32)
        w2 = np.asarray(mlp_w2[layer], f32)
        r = _run(build_post(final), [{"hT": h_sh[c], "oT": o_sh[c], "wo": wo, "w1": w1, "w2": w2, "nw": nw}
                                     for c in range(NCORES)])
        h_sh = [r[c]["hout"] for c in range(NCORES)]
        u_sh = [r[c]["uout"] for c in range(NCORES)]
    out_full = _gather_tok(u_sh)
    out = np.stack([np.ascontiguousarray(out_full[b][:, PADF + NMETA:].T) for b in range(B)], axis=0)
    return out.astype(f32)


DEPTH = 4


def _phase(c, fn):
    base = c.es
    with ExitStack() as pes:
        c.es = pes
        fn()
        c.barrier()
    c.es = base


def build_fused():
    nc = bass.Bass("TRN2", target_bir_lowering=False)
    with ExitStack() as es:
        c = Ctx(nc, es)
        h0T = c.dram("h0T", [D, T], F32, "ExternalInput")
        outT = c.dram("outT", [D, T], F32, "ExternalOutput")
        HT = c.dram("HT", [D, T], F32, "Internal")
        UT = c.dram("UT", [D, TD], BF16, "Internal")
        OT = c.dram("OT", [D, TD], BF16, "Internal")
        nw0 = c.dram("nw0", [128, 8], F32, "ExternalInput")
        W = {}
        for l in range(DEPTH):
            kind = l % 3
            p = "L%d_" % l
            if kind == 0:
                W[p + "wcat"] = c.dram(p + "wcat", [4, D, 1028], F32, "ExternalInput")
                W[p + "prm"] = c.dram(p + "prm", [4, 128, 32], F32, "ExternalInput")
            elif kind == 1:
                W[p + "wcat"] = c.dram(p + "wcat", [4, D, 768], F32, "ExternalInput")
                W[p + "rb"] = c.dram(p + "rb", [4, 32, 2], F32, "ExternalInput")
                W[p + "rb15"] = c.dram(p + "rb15", [4, 128, 2], F32, "ExternalInput")
                W[p + "lamv"] = c.dram(p + "lamv", [128, 4, 64], F32, "ExternalInput")
                W[p + "sw"] = c.dram(p + "sw", [128, 1], F32, "ExternalInput")
            else:
                W[p + "wg"] = c.dram(p + "wg", [4, D, 256], F32, "ExternalInput")
                W[p + "wx"] = c.dram(p + "wx", [4, D, 256], F32, "ExternalInput")
                W[p + "wr"] = c.dram(p + "wr", [4, 256, 256], F32, "ExternalInput")
                W[p + "wi"] = c.dram(p + "wi", [4, 256, 256], F32, "ExternalInput")
                W[p + "prm"] = c.dram(p + "prm", [4, 128, 16], F32, "ExternalInput")
            W[p + "wo"] = c.dram(p + "wo", [D, D], F32, "ExternalInput")
            W[p + "w1"] = c.dram(p + "w1", [D, DFF], F32, "ExternalInput")
            W[p + "w2"] = c.dram(p + "w2", [DFF, D], F32, "ExternalInput")
            W[p + "nw"] = c.dram(p + "nw", [128, 16], F32, "ExternalInput")
        dn_cst = c.dram("dn_cst", [128, 6, 128], F32, "ExternalInput")
        da_oh = c.dram("da_oh", [32, 1152], F32, "ExternalInput")
        da_md = c.dram("da_md", [128, 5, 512], F32, "ExternalInput")
        da_pm = c.dram("da_pm", [128, 1], F32, "ExternalInput")
        identf = c.dram("identf", [128, 128], F32, "ExternalInput")
        make_consts(c)

        def sh(t, j, off=0):
            return t.view(t.t[:, off + j * NTC:off + (j + 1) * NTC])

        def zero_front():
            z = c.sb([128, 8, 64], BF16, "zfront")
            c.op("pool", lambda e: e.memset(z[:], 0.0), writes=[z])
            c.dma("sp", UT[:].rearrange("(k p) n -> p k n", p=128)[:, :, 0:64], z[:], reads=[z], writes=[UT])
        _phase(c, zero_front)
        for j in range(4):
            _phase(c, lambda j=j: emit_pre(c, {"hT": sh(h0T, j), "nw": nw0, "uout": sh(UT, j, 64)}))
        for l in range(DEPTH):
            kind = l % 3
            p = "L%d_" % l
            for g in range(4):
                rows = slice(g * 256, (g + 1) * 256)
                if kind == 0:
                    io = {"uT": UT, "wcat": W[p + "wcat"].view(W[p + "wcat"].t[g]), "cst": dn_cst,
                          "prm": W[p + "prm"].view(W[p + "prm"].t[g]), "oT": OT.view(OT.t[rows, :])}
                    _phase(c, lambda io=io: emit_dn(c, io))
                elif kind == 1:
                    io = {"uT": UT, "wcat": W[p + "wcat"].view(W[p + "wcat"].t[g]), "oh": da_oh, "md": da_md, "pm": da_pm,
                          "rb": W[p + "rb"].view(W[p + "rb"].t[g]), "rb15": W[p + "rb15"].view(W[p + "rb15"].t[g]),
                          "lamv": W[p + "lamv"], "sw": W[p + "sw"], "identf": identf, "oT": OT.view(OT.t[rows, :])}
                    lam_init = 0.8 - 0.6 * math.exp(-0.3 * l)
                    _phase(c, lambda io=io, lam_init=lam_init, tag="_%d_%d" % (l, g): emit_da(c, io, lam_init, tag))
                else:
                    io = {"uT": UT.view(UT.t[:, 64:64 + T]), "wg": W[p + "wg"].view(W[p + "wg"].t[g]),
                          "wx": W[p + "wx"].view(W[p + "wx"].t[g]), "wr": W[p + "wr"].view(W[p + "wr"].t[g]),
                          "wi": W[p + "wi"].view(W[p + "wi"].t[g]), "prm": W[p + "prm"].view(W[p + "prm"].t[g]),
                          "yT": OT.view(OT.t[rows, 64:64 + T])}
                    _phase(c, lambda io=io: emit_lru(c, io))
            final = (l == DEPTH - 1)
            for j in range(4):
                io = {"hT": sh(h0T if l == 0 else HT, j), "oT": sh(OT, j, 64), "wo": W[p + "wo"], "w1": W[p + "w1"],
                      "w2": W[p + "w2"], "nw": W[p + "nw"], "hout": sh(HT, j),
                      "uout": sh(outT, j) if final else sh(UT, j, 64)}
                _phase(c, lambda io=io, final=final: emit_post(c, io, final))
        c.finish()
    return nc


def fused_inputs(b, x, meta_tokens, rel_bias, norm_mix_w, norm_mlp_w, final_norm_w,
                 dn_w_in, dn_conv_w, dn_a_log, dn_dt_bias, dn_norm_w, dn_w_out,
                 da_w_in, da_lam_q1, da_lam_k1, da_lam_q2, da_lam_k2, da_subln_w, da_w_out,
                 lru_w_in, lru_conv_w, lru_conv_b, lru_w_rgate, lru_b_rgate, lru_w_igate,
                 lru_b_igate, lru_lambda, lru_w_out, mlp_w1, mlp_w2, shared=None):
    f32 = np.float32
    im = {}
    seq = np.concatenate([np.zeros((PADF, D), f32), np.asarray(meta_tokens, f32), np.asarray(x[b], f32)], axis=0)
    im["h0T"] = np.ascontiguousarray(seq.T)
    if shared is not None:
        im.update(shared)
        return im
    sh = {}
    sh["nw0"] = _nwcols(norm_mix_w[0])
    oh, md, pm = da_consts()
    sh["dn_cst"] = dn_consts()
    sh["da_oh"], sh["da_md"], sh["da_pm"] = oh, md, pm
    sh["identf"] = np.eye(128, dtype=f32)
    rbt = np.asarray(rel_bias, f32)
    for l in range(DEPTH):
        kind, slot = l % 3, l // 3
        p = "L%d_" % l
        if kind == 0:
            w_in = np.asarray(dn_w_in[slot], f32)
            cw = np.asarray(dn_conv_w[slot], f32)
            wc, pr = [], []
            for g in range(4):
                h0 = 2 * g
                s = slice(h0 * 128, h0 * 128 + 256)
                wc.append(np.concatenate([w_in[:, 0:1024][:, s], w_in[:, 1024:2048][:, s], w_in[:, 2048:3072][:, s],
                                          w_in[:, 3072:4096][:, s], w_in[:, 4096 + h0:4096 + h0 + 2],
                                          w_in[:, 4104 + h0:4104 + h0 + 2]], axis=1))
                cw6 = np.concatenate([cw[:, 0:1024][:, s], cw[:, 1024:2048][:, s], cw[:, 2048:3072][:, s]], axis=1)
                pr.append(dn_params(cw6, np.asarray(dn_a_log[slot], f32)[h0:h0 + 2],
                                    np.asarray(dn_dt_bias[slot], f32)[h0:h0 + 2], np.asarray(dn_norm_w[slot], f32)))
            sh[p + "wcat"] = np.ascontiguousarray(np.stack(wc))
            sh[p + "prm"] = np.ascontiguousarray(np.stack(pr))
            wo = dn_w_out[slot]
        elif kind == 1:
            w_in = np.asarray(da_w_in[slot], f32)
            wc, rb, rb15 = [], [], []
            for g in range(4):
                h0 = 2 * g
                s = slice(h0 * 128, h0 * 128 + 256)
                wc.append(np.concatenate([w_in[:, 0:1024][:, s], w_in[:, 1024:2048][:, s], w_in[:, 2048:3072][:, s]], axis=1))
                rb.append(rbt[:, h0:h0 + 2])
                rb15.append(np.broadcast_to(rbt[15:16, h0:h0 + 2], (128, 2)))
            sh[p + "wcat"] = np.ascontiguousarray(np.stack(wc))
            sh[p + "rb"] = np.ascontiguousarray(np.stack(rb))
            sh[p + "rb15"] = np.ascontiguousarray(np.stack(rb15))
            lamv = np.stack([np.asarray(v[slot], f32) for v in (da_lam_q1, da_lam_k1, da_lam_q2, da_lam_k2)], axis=0)
            sh[p + "lamv"] = np.ascontiguousarray(np.broadcast_to(lamv[None], (128, 4, 64)))
            sh[p + "sw"] = np.ascontiguousarray(np.asarray(da_subln_w[slot], f32)[:, None])
            wo = da_w_out[slot]
        else:
            w_in = np.asarray(lru_w_in[slot], f32)
            cw = np.asarray(lru_conv_w[slot], f32)
            wg, wx, pr = [], [], []
            for g in range(4):
                s = slice(g * 256, (g + 1) * 256)
                wg.append(w_in[:, 0:1024][:, s])
                wx.append(w_in[:, 1024:2048][:, s])
                pr.append(lru_params(cw[:, s], np.asarray(lru_conv_b[slot], f32)[s], np.asarray(lru_b_rgate[slot], f32)[s],
                                     np.asarray(lru_b_igate[slot], f32)[s], np.asarray(lru_lambda[slot], f32)[s]))
            sh[p + "wg"] = np.ascontiguousarray(np.stack(wg))
            sh[p + "wx"] = np.ascontiguousarray(np.stack(wx))
            sh[p + "wr"] = np.ascontiguousarray(np.asarray(lru_w_rgate[slot], f32))
            sh[p + "wi"] = np.ascontiguousarray(np.asarray(lru_w_igate[slot], f32))
            sh[p + "prm"] = np.ascontiguousarray(np.stack(pr))
            wo = lru_w_out[slot]
        final = (l == DEPTH - 1)
        nxt = final_norm_w if final else norm_mix_w[l + 1]
        sh[p + "wo"] = np.ascontiguousarray(np.asarray(wo, f32))
        sh[p + "w1"] = np.ascontiguousarray(np.asarray(mlp_w1[l], f32))
        sh[p + "w2"] = np.ascontiguousarray(np.asarray(mlp_w2[l], f32))
        sh[p + "nw"] = np.ascontiguousarray(np.concatenate([_nwcols(norm_mlp_w[l]), _nwcols(nxt)], axis=1))
    im.update(sh)
    im["_shared"] = sh
    return im


def kernel(**inputs):
    x = inputs["x"]
    im0 = fused_inputs(0, **inputs)
    shared = im0.pop("_shared")
    im1 = fused_inputs(1, **inputs, shared=shared)
    nc = build_fused()
    res = run_bass_kernel_spmd(nc, [im0, im1], core_ids=[0, 1])
    out = np.stack([np.ascontiguousarray(res.results[b]["outT"][:, PADF + NMETA:].T) for b in range(B)], axis=0)
    return out.astype(np.float32)
```

```python
import math
from contextlib import ExitStack

import numpy as np
import ml_dtypes
import concourse.bass as bass
import concourse.mybir as mybir
from concourse.bass_utils import run_bass_kernel_spmd

F32 = mybir.dt.float32
BF16 = mybir.dt.bfloat16
AF = mybir.ActivationFunctionType
ALU = mybir.AluOpType
AX = mybir.AxisListType

D = 1024
B = 2
SEQ = 8192
NMETA = 16
PADF = 48
T = PADF + NMETA + SEQ
NTC = T // 4
EPS = 1e-6
DFF = 4096
NCORES = 8


class Tl:
    def __init__(self, t, name, st=None):
        self.t = t
        self.name = name
        self.st = st if st is not None else [None, {}]

    @property
    def w(self):
        return self.st[0]

    @w.setter
    def w(self, v):
        self.st[0] = v

    @property
    def r(self):
        return self.st[1]

    @r.setter
    def r(self, v):
        self.st[1] = v

    def view(self, ap):
        return Tl(ap, self.name, self.st)

    def __getitem__(self, idx):
        return self.t[idx]


class Ctx:
    NDMA = 8

    def __init__(self, nc, es):
        self.nc = nc
        self.es = es
        self.eng = {"pe": nc.tensor, "act": nc.scalar, "dve": nc.vector,
                    "pool": nc.gpsimd, "sp": nc.sync}
        self.sem = {k: es.enter_context(nc.semaphore("s_" + k)) for k in self.eng}
        self.cnt = {k: 0 for k in self.eng}
        self.seen = {k: {} for k in self.eng}
        self.dsem = {}
        self.dcnt = {}
        for q in ("sp", "pool"):
            self.dsem[q] = [es.enter_context(nc.semaphore("d_%s%d" % (q, i)))
                            for i in range(self.NDMA)]
            self.dcnt[q] = 0
        self.ntile = 0

    def sb(self, shape, dt=F32, name=None, es=None):
        self.ntile += 1
        name = "%s_%d" % (name or "t", self.ntile)
        t = (es or self.es).enter_context(self.nc.sbuf_tensor(name, list(shape), dt))
        return Tl(t, name)

    def ps(self, shape, dt=F32, name=None, es=None):
        self.ntile += 1
        name = "%s_%d" % (name or "p", self.ntile)
        t = (es or self.es).enter_context(self.nc.psum_tensor(name, list(shape), dt))
        return Tl(t, name)

    def dram(self, name, shape, dt, kind):
        t = self.nc.dram_tensor(name, list(shape), dt, kind=kind)
        return Tl(t.ap(), name)

    def _wait(self, e, sem, val):
        key = id(sem)
        if self.seen[e].get(key, 0) >= val:
            return
        self.eng[e].wait_ge(sem, val)
        self.seen[e][key] = val

    def _deps(self, e, reads, writes):
        deps = {}

        def add(d):
            if d is None:
                return
            s, v = d
            if deps.get(id(s), (None, 0))[1] < v:
                deps[id(s)] = (s, v)
        for t in reads:
            add(t.w)
        for t in writes:
            add(t.w)
            for s_v in t.r.values():
                add(s_v)
        for s, v in deps.values():
            if e == "pe" and s is self.sem["pe"]:
                continue
            self._wait(e, s, v)

    def _mark(self, token, reads, writes):
        s, v = token
        for t in reads:
            t.r[id(s)] = (s, v)
        for t in writes:
            t.w = (s, v)
            t.r = {}

    def op(self, e, fn, reads=(), writes=(), inc=True):
        self._deps(e, reads, writes)
        ins = fn(self.eng[e])
        if inc:
            self.cnt[e] += 1
            ins.then_inc(self.sem[e], 1)
            tok = (self.sem[e], self.cnt[e])
        else:
            tok = (self.sem[e], self.cnt[e] + 1)
        self._mark(tok, reads, writes)
        return ins

    def dma(self, q, out, in_, reads=(), writes=(), **kw):
        self._deps(q, reads, writes)
        i = self.dcnt[q]
        s = self.dsem[q][i % self.NDMA]
        prev = 16 * (i // self.NDMA)
        if prev:
            self._wait(q, s, prev)
        ins = self.eng[q].dma_start(out=out, in_=in_, **kw)
        ins.then_inc(s, 16)
        self.dcnt[q] += 1
        self._mark((s, prev + 16), reads, writes)

    def switch_sem(self, e):
        self.nsw = getattr(self, "nsw", 0) + 1
        self.sem[e] = self.es.enter_context(self.nc.semaphore("s_%s_%d" % (e, self.nsw)))
        self.cnt[e] = 0

    def barrier(self):
        for e in self.eng:
            for e2 in self.eng:
                if e2 != e and self.cnt[e2] > 0:
                    self._wait(e, self.sem[e2], self.cnt[e2])
            for q in self.dsem:
                n = self.dcnt[q]
                for j, s in enumerate(self.dsem[q]):
                    k = (n - j + self.NDMA - 1) // self.NDMA
                    if k > 0:
                        self._wait(e, s, 16 * k)

    def finish(self):
        for q in self.dsem:
            n = self.dcnt[q]
            for j, s in enumerate(self.dsem[q]):
                k = (n - j + self.NDMA - 1) // self.NDMA
                if k > 0:
                    self._wait("sp", s, 16 * k)


class Rot:
    def __init__(self, tiles):
        self.tiles = tiles
        self.i = 0

    def get(self):
        t = self.tiles[self.i % len(self.tiles)]
        self.i += 1
        return t


def rms_stats(c, ones, src_tiles_fn, nk, n, pstat, sqrot, rstd, scale_div):
    for k in range(nk):
        src, src_t = src_tiles_fn(k)
        sq = sqrot.get()
        c.op("act", lambda e: e.activation(out=sq[:, :n], in_=src, func=AF.Square),
             reads=[src_t], writes=[sq])
        c.op("pe", lambda e: e.matmul(pstat[:, :n], lhsT=ones[:], rhs=sq[:, :n],
                                      start=(k == 0), stop=(k == nk - 1)),
             reads=[ones, sq], writes=[pstat])
    c.op("act", lambda e: e.activation(out=rstd[:, :n], in_=pstat[:, :n], func=AF.Sqrt,
                                       bias=c.eps_t[:, 0:1], scale=1.0 / scale_div),
         reads=[pstat, c.eps_t], writes=[rstd])
    c.op("dve", lambda e: e.reciprocal(out=rstd[:, :n], in_=rstd[:, :n]),
         reads=[rstd], writes=[rstd])


def make_consts(c):
    c.eps_t = c.sb([128, 1], F32, "eps_t")
    c.op("pool", lambda e: e.memset(c.eps_t[:], EPS), writes=[c.eps_t])
    c.ones = c.sb([128, 128], F32, "ones")
    c.op("pool", lambda e: e.memset(c.ones[:], 1.0), writes=[c.ones])


GS = 344
NG = NTC // GS


def build_post(final):
    nc = bass.Bass("TRN2", target_bir_lowering=False)
    with ExitStack() as es:
        c = Ctx(nc, es)
        io = {}
        io["hT"] = c.dram("hT", [D, NTC], F32, "ExternalInput")
        io["oT"] = c.dram("oT", [D, NTC], BF16, "ExternalInput")
        io["wo"] = c.dram("wo", [D, D], F32, "ExternalInput")
        io["w1"] = c.dram("w1", [D, DFF], F32, "ExternalInput")
        io["w2"] = c.dram("w2", [DFF, D], F32, "ExternalInput")
        io["nw"] = c.dram("nw", [128, 16], F32, "ExternalInput")
        io["hout"] = c.dram("hout", [D, NTC], F32, "ExternalOutput")
        io["uout"] = c.dram("uout", [D, NTC], F32 if final else BF16, "ExternalOutput")
        make_consts(c)
        emit_post(c, io, final)
        c.finish()
    return nc


def emit_post(c, io, final):
    if True:
        hT, oT, wo, w1, w2, nw, hout, uout = (io[k] for k in ("hT", "oT", "wo", "w1", "w2", "nw", "hout", "uout"))
        nws = c.sb([128, 16], F32, "nws")
        c.dma("sp", nws[:], nw[:], writes=[nws])

        H = [c.sb([128, 8, GS], F32, "H%d" % g) for g in range(NG)]
        XB = [c.sb([128, 8, GS], BF16, "XB%d" % g) for g in range(NG)]
        wbuf = Rot([c.sb([128, 8192], BF16, "wb%d" % i) for i in range(2)])
        abuf = Rot([c.sb([128, 4, GS], BF16, "ab%d" % i) for i in range(2)])
        rbuf = Rot([c.sb([128, GS], F32, "rb%d" % i) for i in range(3)])
        sqb = Rot([c.sb([128, GS], F32, "sq%d" % i) for i in range(2)])
        rstd = c.sb([128, GS], F32, "rstd")
        uo = Rot([c.sb([128, 8, GS], F32 if final else BF16, "uo%d" % i) for i in range(2)])
        pa = Rot([c.ps([128, 512], F32, "pa%d" % i) for i in range(4)])
        py = Rot([c.ps([128, 512], F32, "py%d" % i) for i in range(3)])
        pstat = c.ps([128, 512], F32, "pstat")

        hT3 = hT[:].rearrange("(k p) n -> p k n", p=128)
        oT3 = oT[:].rearrange("(k p) n -> p k n", p=128)
        ho3 = hout[:].rearrange("(k p) n -> p k n", p=128)
        uo3 = uout[:].rearrange("(k p) n -> p k n", p=128)

        wob = wbuf.get()
        wo3 = wo[:].rearrange("(k p) n -> p k n", p=128)
        for k0 in range(0, 8, 2):
            c.dma("pool", wob[:, k0 * 1024:(k0 + 2) * 1024].rearrange("p (k n) -> p k n", k=2), wo3[:, k0:k0 + 2, :], writes=[wob])
        for g in range(NG):
            c.dma("sp", XB[g][:], oT3[:, :, g * GS:(g + 1) * GS], writes=[XB[g]])
            c.dma("sp", H[g][:], hT3[:, :, g * GS:(g + 1) * GS], writes=[H[g]])

        def load_eighth(e8):
            wb = wbuf.get()
            w13 = w1[:].rearrange("(k p) n -> p k n", p=128)
            for k0 in range(0, 8, 4):
                c.dma("pool", wb[:, k0 * 512:(k0 + 4) * 512].rearrange("p (k n) -> p k n", k=4),
                      w13[:, k0:k0 + 4, e8 * 512:(e8 + 1) * 512], writes=[wb])
            w23 = w2[:].rearrange("(f p) n -> p f n", p=128)
            for f0 in range(0, 4, 2):
                c.dma("pool", wb[:, 4096 + f0 * 1024:4096 + (f0 + 2) * 1024].rearrange("p (f n) -> p f n", f=2),
                      w23[:, e8 * 4 + f0:e8 * 4 + f0 + 2, :], writes=[wb])
            return wb

        def norm_to(g, col, dst, dst_is_f32):
            rms_stats(c, c.ones, lambda k: (H[g][:, k, :], H[g]), 8, GS, pstat, sqb, rstd, float(D))
            for m in range(8):
                c.op("dve", lambda e: e.scalar_tensor_tensor(
                    out=dst[:, m, :], in0=H[g][:, m, :], scalar=nws[:, col + m:col + m + 1],
                    in1=rstd[:], op0=ALU.mult, op1=ALU.mult),
                    reads=[H[g], nws, rstd], writes=[dst])

        wnext = load_eighth(0)
        for g in range(NG):
            for m in range(8):
                p = py.get()
                for k in range(8):
                    c.op("pe", lambda e: e.matmul(p[:, :GS], lhsT=wob[:, k * 1024 + m * 128:k * 1024 + (m + 1) * 128],
                                                  rhs=XB[g][:, k, :], start=(k == 0), stop=(k == 7)),
                         reads=[wob, XB[g]], writes=[p], inc=(k == 7))
                c.op("dve", lambda e: e.tensor_tensor(out=H[g][:, m, :], in0=p[:, :GS], in1=H[g][:, m, :], op=ALU.add),
                     reads=[p, H[g]], writes=[H[g]])
            norm_to(g, 0, XB[g], False)
        for e8 in range(8):
            wb = wnext
            if e8 + 1 < 8:
                wnext = load_eighth(e8 + 1)
            for g in range(NG):
                ab = abuf.get()
                for f in range(4):
                    p = pa.get()
                    for k in range(8):
                        c.op("pe", lambda e: e.matmul(p[:, :GS], lhsT=wb[:, k * 512 + f * 128:k * 512 + (f + 1) * 128],
                                                      rhs=XB[g][:, k, :], start=(k == 0), stop=(k == 7)),
                             reads=[wb, XB[g]], writes=[p], inc=(k == 7))
                    r = rbuf.get()
                    c.op("act", lambda e: e.activation(out=r[:], in_=p[:, :GS], func=AF.Relu),
                         reads=[p], writes=[r])
                    c.op("dve", lambda e: e.tensor_tensor(out=ab[:, f, :], in0=r[:], in1=r[:], op=ALU.mult),
                         reads=[r], writes=[ab])
                for m in range(8):
                    p = py.get()
                    for f in range(4):
                        c.op("pe", lambda e: e.matmul(p[:, :GS], lhsT=wb[:, 4096 + f * 1024 + m * 128:4096 + f * 1024 + (m + 1) * 128],
                                                      rhs=ab[:, f, :], start=(f == 0), stop=(f == 3)),
                             reads=[wb, ab], writes=[p], inc=(f == 3))
                    c.op("dve", lambda e: e.tensor_tensor(out=H[g][:, m, :], in0=p[:, :GS], in1=H[g][:, m, :], op=ALU.add),
                         reads=[p, H[g]], writes=[H[g]])
                if e8 == 7:
                    c.dma("sp", ho3[:, :, g * GS:(g + 1) * GS], H[g][:], reads=[H[g]], writes=[hout])
                    u = uo.get()
                    norm_to(g, 8, u, final)
                    c.dma("sp", uo3[:, :, g * GS:(g + 1) * GS], u[:], reads=[u], writes=[uout])


def load_cast(c, dst_t, dst_ap, src_ap):
    c.dma("pool", dst_ap, src_ap, writes=[dst_t])


LB = 342
LNB = (T - PADF) // LB


def build_lru():
    nc = bass.Bass("TRN2", target_bir_lowering=False)
    with ExitStack() as es:
        c = Ctx(nc, es)
        io = {}
        io["uT"] = c.dram("uT", [D, T], BF16, "ExternalInput")
        io["wg"] = c.dram("wg", [D, 256], F32, "ExternalInput")
        io["wx"] = c.dram("wx", [D, 256], F32, "ExternalInput")
        io["wr"] = c.dram("wr", [256, 256], F32, "ExternalInput")
        io["wi"] = c.dram("wi", [256, 256], F32, "ExternalInput")
        io["prm"] = c.dram("prm", [128, 16], F32, "ExternalInput")
        io["yT"] = c.dram("yT", [256, T], BF16, "ExternalOutput")
        make_consts(c)
        emit_lru(c, io)
        c.finish()
    return nc


def emit_lru(c, io):
    if True:
        uT, wg, wx, wr, wi, prm, yT = (io[k] for k in ("uT", "wg", "wx", "wr", "wi", "prm", "yT"))
        one_t = c.sb([128, 1], F32, "one_t")
        c.op("pool", lambda e: e.memset(one_t[:], 1.0), writes=[one_t])
        ps_ = c.sb([128, 16], F32, "prm_s")
        c.dma("sp", ps_[:], prm[:], writes=[ps_])
        wgb = c.sb([128, 8, 256], BF16, "wgb")
        wxb = c.sb([128, 8, 256], BF16, "wxb")
        wrb = c.sb([128, 2, 256], BF16, "wrb")
        wib = c.sb([128, 2, 256], BF16, "wib")
        for (dst, src, nk) in ((wgb, wg, 8), (wxb, wx, 8), (wrb, wr, 2), (wib, wi, 2)):
            load_cast(c, dst, dst[:], src[:].rearrange("(k p) n -> p k n", p=128))
        cch = c.sb([128, 2], F32, "cch")
        ee = c.sb([128, 2], F32, "ee")
        acc = c.sb([128, 2], F32, "lacc")
        lam_ap = lambda: ps_[:, 7:16:8]
        c.op("act", lambda e: e.activation(out=ee[:], in_=lam_ap(), func=AF.Exp, scale=-1.0),
             reads=[ps_], writes=[ee])
        c.op("dve", lambda e: e.tensor_scalar(out=acc[:], in0=ee[:], scalar1=-1.0 / 6, scalar2=1.0 / 5,
                                              op0=ALU.mult, op1=ALU.add), reads=[ee], writes=[acc])
        for coef in (1.0 / 4, 1.0 / 3, 1.0 / 2, 1.0):
            c.op("dve", lambda e: e.tensor_tensor(out=acc[:], in0=acc[:], in1=ee[:], op=ALU.mult),
                 reads=[acc, ee], writes=[acc])
            c.op("dve", lambda e: e.tensor_scalar(out=acc[:], in0=acc[:], scalar1=-1.0, scalar2=coef,
                                                  op0=ALU.mult, op1=ALU.add), reads=[acc], writes=[acc])
        c.op("dve", lambda e: e.tensor_tensor(out=acc[:], in0=acc[:], in1=ee[:], op=ALU.mult),
             reads=[acc, ee], writes=[acc])
        c.op("dve", lambda e: e.tensor_scalar(out=cch[:], in0=acc[:], scalar1=-8.0, scalar2=None,
                                              op0=ALU.mult), reads=[acc], writes=[cch])

        ub = Rot([c.sb([128, 8, LB], BF16, "ub%d" % i) for i in range(3)])
        xc = [c.sb([128, LB + 3], F32, "xc%d" % t) for t in range(2)]
        for t in range(2):
            c.op("pool", lambda e: e.memset(xc[t][:], 0.0), writes=[xc[t]])
        xr = Rot([c.sb([128, LB], F32, "xr%d" % i) for i in range(2)])
        xrb = Rot([c.sb([128, 2, LB], BF16, "xrb%d" % i) for i in range(2)])
        gt = Rot([c.sb([128, LB], F32, "gt%d" % i) for i in range(4)])
        xrs = Rot([c.sb([128, LB], F32, "xrs%d" % i) for i in range(4)])
        tmp = Rot([c.sb([128, LB], F32, "tmp%d" % i) for i in range(6)])
        hs = [Rot([c.sb([128, LB], F32, "hs%d_%d" % (t, i)) for i in range(2)]) for t in range(2)]
        yb = Rot([c.sb([128, 2, LB], BF16, "yb%d" % i) for i in range(2)])
        pp = Rot([c.ps([128, 512], F32, "pp%d" % i) for i in range(4)])
        pg_ = Rot([c.ps([128, 512], F32, "pq%d" % i) for i in range(4)])
        zz = c.sb([128, 2, PADF], BF16, "zz")
        c.op("pool", lambda e: e.memset(zz[:], 0.0), writes=[zz])
        y3 = yT[:].rearrange("(t p) n -> p t n", p=128)
        c.dma("sp", y3[:, :, 0:PADF], zz[:], reads=[zz], writes=[yT])
        u3 = uT[:].rearrange("(k p) n -> p k n", p=128)
        prev_hs = [None, None]
        P = lambda t, j: ps_[:, t * 8 + j:t * 8 + j + 1]
        n = LB
        for blk in range(LNB):
            t0 = PADF + blk * LB
            u = ub.get()
            c.dma("sp", u[:], u3[:, :, t0:t0 + n], writes=[u])
            gts, xrf = [], []
            xb_ = xrb.get()
            for t in range(2):
                pgt = pp.get()
                for k in range(8):
                    c.op("pe", lambda e: e.matmul(pgt[:, :n], lhsT=wgb[:, k, t * 128:(t + 1) * 128], rhs=u[:, k, :],
                                                  start=(k == 0), stop=(k == 7)), reads=[wgb, u], writes=[pgt], inc=(k == 7))
                pxt = pp.get()
                for k in range(8):
                    c.op("pe", lambda e: e.matmul(pxt[:, :n], lhsT=wxb[:, k, t * 128:(t + 1) * 128], rhs=u[:, k, :],
                                                  start=(k == 0), stop=(k == 7)), reads=[wxb, u], writes=[pxt], inc=(k == 7))
                s = tmp.get()
                c.op("act", lambda e: e.activation(out=s[:], in_=pgt[:, :n], func=AF.Square), reads=[pgt], writes=[s])
                c.op("dve", lambda e: e.tensor_scalar(out=s[:], in0=s[:], scalar1=0.044715, scalar2=1.0,
                                                      op0=ALU.mult, op1=ALU.add), reads=[s], writes=[s])
                c.op("dve", lambda e: e.tensor_tensor(out=s[:], in0=s[:], in1=pgt[:, :n], op=ALU.mult),
                     reads=[s, pgt], writes=[s])
                c.op("act", lambda e: e.activation(out=s[:], in_=s[:], func=AF.Sigmoid, scale=1.5957691216057308),
                     reads=[s], writes=[s])
                g_ = gt.get()
                c.op("dve", lambda e: e.tensor_tensor(out=g_[:], in0=s[:], in1=pgt[:, :n], op=ALU.mult),
                     reads=[s, pgt], writes=[g_])
                gts.append(g_)
                c.op("act", lambda e: e.activation(out=xc[t][:, 3:3 + n], in_=pxt[:, :n], func=AF.Copy),
                     reads=[pxt], writes=[xc[t]])
                x_ = xrs.get()
                c.op("dve", lambda e: e.tensor_scalar(out=x_[:], in0=xc[t][:, 0:n], scalar1=P(t, 0), scalar2=P(t, 4),
                                                      op0=ALU.mult, op1=ALU.add), reads=[xc[t], ps_], writes=[x_])
                for j in range(1, 4):
                    c.op("dve", lambda e: e.scalar_tensor_tensor(out=x_[:], in0=xc[t][:, j:j + n], scalar=P(t, j),
                                                                 in1=x_[:], op0=ALU.mult, op1=ALU.add),
                         reads=[xc[t], ps_, x_], writes=[x_])
                c.op("pool", lambda e: e.tensor_copy(out=xc[t][:, 0:3], in_=xc[t][:, n:n + 3]),
                     reads=[xc[t]], writes=[xc[t]])
                c.op("pool", lambda e: e.tensor_copy(out=xb_[:, t, :], in_=x_[:]), reads=[x_], writes=[xb_])
                xrf.append(x_)
            y_ = yb.get()
            for t in range(2):
                pr = pg_.get()
                for k in range(2):
                    c.op("pe", lambda e: e.matmul(pr[:, :n], lhsT=wrb[:, k, t * 128:(t + 1) * 128], rhs=xb_[:, k, :],
                                                  start=(k == 0), stop=(k == 1)), reads=[wrb, xb_], writes=[pr], inc=(k == 1))
                pi_ = pg_.get()
                for k in range(2):
                    c.op("pe", lambda e: e.matmul(pi_[:, :n], lhsT=wib[:, k, t * 128:(t + 1) * 128], rhs=xb_[:, k, :],
                                                  start=(k == 0), stop=(k == 1)), reads=[wib, xb_], writes=[pi_], inc=(k == 1))
                a_ = tmp.get()
                c.op("act", lambda e: e.activation(out=a_[:], in_=pr[:, :n], func=AF.Sigmoid, bias=P(t, 5)),
                     reads=[pr, ps_], writes=[a_])
                c.op("act", lambda e: e.activation(out=a_[:], in_=a_[:], func=AF.Exp, scale=cch[:, t:t + 1]),
                     reads=[a_, cch], writes=[a_])
                i_ = tmp.get()
                c.op("act", lambda e: e.activation(out=i_[:], in_=pi_[:, :n], func=AF.Sigmoid, bias=P(t, 6)),
                     reads=[pi_, ps_], writes=[i_])
                m_ = tmp.get()
                c.op("dve", lambda e: e.tensor_tensor(out=m_[:], in0=a_[:], in1=a_[:], op=ALU.mult), reads=[a_], writes=[m_])
                c.op("act", lambda e: e.activation(out=m_[:], in_=m_[:], func=AF.Sqrt, bias=one_t[:, 0:1], scale=-1.0),
                     reads=[m_, one_t], writes=[m_])
                c.op("dve", lambda e: e.tensor_tensor(out=i_[:], in0=i_[:], in1=xrf[t][:], op=ALU.mult),
                     reads=[i_, xrf[t]], writes=[i_])
                c.op("dve", lambda e: e.tensor_tensor(out=i_[:], in0=i_[:], in1=m_[:], op=ALU.mult),
                     reads=[i_, m_], writes=[i_])
                h_ = hs[t].get()
                if prev_hs[t] is None:
                    c.op("dve", lambda e: e.tensor_tensor_scan(out=h_[:], data0=a_[:], data1=i_[:], initial=0.0,
                                                               op0=ALU.mult, op1=ALU.add), reads=[a_, i_], writes=[h_])
                else:
                    ph = prev_hs[t]
                    c.op("dve", lambda e: e.tensor_tensor_scan(out=h_[:], data0=a_[:], data1=i_[:], initial=ph[:, n - 1:n],
                                                               op0=ALU.mult, op1=ALU.add), reads=[a_, i_, ph], writes=[h_])
                prev_hs[t] = h_
                c.op("pool", lambda e: e.tensor_tensor(out=y_[:, t, :], in0=h_[:], in1=gts[t][:], op=ALU.mult),
                     reads=[h_, gts[t]], writes=[y_])
            c.dma("sp", y3[:, :, t0:t0 + n], y_[:], reads=[y_], writes=[yT])


def lru_params(cw, cb, br, bi, lam):
    prm = np.zeros((128, 2, 8), np.float32)
    for t in range(2):
        sl = slice(t * 128, (t + 1) * 128)
        prm[:, t, 0:4] = cw[:, sl].T
        prm[:, t, 4] = cb[sl]
        prm[:, t, 5] = br[sl]
        prm[:, t, 6] = bi[sl]
        prm[:, t, 7] = lam[sl]
    return np.ascontiguousarray(prm.reshape(128, 16))


TD = T + 64
DN_BLOCKS = [(i * 512, 4) for i in range(16)] + [(8192, 1)]


def dn_consts():
    i = np.arange(128)
    same = (i[:, None] // 64) == (i[None, :] // 64)
    cst = np.zeros((128, 6, 128), np.float32)
    cst[:, 0, :] = np.eye(128)
    cst[:, 1, :] = (same & (i[:, None] <= i[None, :]))
    cst[:, 2, :] = (same & (i[:, None] > i[None, :]))
    cst[:, 3, :] = np.where(same & (i[:, None] >= i[None, :]), 0.0, -30000.0)
    cst[:, 4, :] = (same & (i[:, None] > i[None, :]))
    cst[:, 5, :] = -1.0
    return cst


def build_dn(blocks=None, TD=TD, dbg=99):
    blocks = blocks or DN_BLOCKS
    nc = bass.Bass("TRN2", target_bir_lowering=False)
    with ExitStack() as es:
        c = Ctx(nc, es)
        io = {}
        io["uT"] = c.dram("uT", [D, TD], BF16, "ExternalInput")
        io["wcat"] = c.dram("wcat", [D, 1028], F32, "ExternalInput")
        io["cst"] = c.dram("cst", [128, 6, 128], F32, "ExternalInput")
        io["prm"] = c.dram("prm", [128, 32], F32, "ExternalInput")
        io["oT"] = c.dram("oT", [256, TD], BF16, "ExternalOutput")
        make_consts(c)
        emit_dn(c, io, blocks, dbg)
        c.finish()
    return nc


def emit_dn(c, io, blocks=None, dbg=99):
    blocks = blocks or DN_BLOCKS
    if True:
        uT, wcat, cstd, prm, oT = (io[k] for k in ("uT", "wcat", "cst", "prm", "oT"))
        one_t = c.sb([128, 1], F32, "one_t")
        c.op("pool", lambda e: e.memset(one_t[:], 1.0), writes=[one_t])
        cst = c.sb([128, 6, 128], F32, "cst_s")
        c.dma("sp", cst[:], cstd[:], writes=[cst])
        ident = cst[:, 0, :]
        U2 = cst[:, 1, :]
        R2 = cst[:, 2, :]
        negmask = cst[:, 3, :]
        smask = cst[:, 4, :]
        negones = cst[:, 5, :]
        ps_ = c.sb([128, 32], F32, "prm_s")
        c.dma("sp", ps_[:], prm[:], writes=[ps_])
        nea = c.sb([128, 2], F32, "nea")
        c.op("act", lambda e: e.activation(out=nea[:], in_=ps_[:, 24:26], func=AF.Exp), reads=[ps_], writes=[nea])
        c.op("dve", lambda e: e.tensor_scalar(out=nea[:], in0=nea[:], scalar1=-1.0, scalar2=None, op0=ALU.mult),
             reads=[nea], writes=[nea])
        wb = c.sb([128, 8, 1028], BF16, "wb")
        w3 = wcat[:].rearrange("(k p) n -> p k n", p=128)
        for k0 in range(0, 8, 2):
            load_cast(c, wb, wb[:, k0:k0 + 2, :], w3[:, k0:k0 + 2, :])

        ub = Rot([c.sb([128, 8, 512], BF16, "ub%d" % i) for i in range(2)])
        xc = [c.sb([128, 515], F32, "xc%d" % j) for j in range(6)]
        for j in range(6):
            c.op("pool", lambda e: e.memset(xc[j][:], 0.0), writes=[xc[j]])
        ft = [Rot([c.sb([128, 512], F32, "ft%d_%d" % (j, i)) for i in range(2)]) for j in range(6)]
        szr = [Rot([c.sb([128, 512], F32, "sz%d_%d" % (h, i)) for i in range(2)]) for h in range(2)]
        sqb = Rot([c.sb([128, 512], F32, "sq%d" % i) for i in range(2)])
        rstd = c.sb([128, 512], F32, "rstd")
        obr = Rot([c.sb([128, 2, 512], BF16, "ob%d" % i) for i in range(2)])
        S = [c.sb([128, 128], F32, "S%d" % h) for h in range(2)]
        for h in range(2):
            c.op("pool", lambda e: e.memset(S[h][:], 0.0), writes=[S[h]])
        bt = Rot([c.sb([128, 4, 2], F32, "bt%d" % i) for i in range(2)])
        gg = Rot([c.sb([128, 4, 2], F32, "gg%d" % i) for i in range(2)])
        gtmp = Rot([c.sb([128, 4, 2], F32, "gtmp%d" % i) for i in range(6)])
        sm2 = Rot([c.sb([128, 2], F32, "sm2_%d" % i) for i in range(24)])
        sm1 = Rot([c.sb([128, 1], F32, "sm1_%d" % i) for i in range(8)])
        NSQ = 96
        sq128 = Rot([c.sb([128, 128], F32, "m%d" % i) for i in range(NSQ)])
        LL = {nm: [c.sb([128, 128], F32, "%s%d" % (nm, i)) for i in range(8)]
              for nm in ("dec", "egrow", "kd", "attT", "qg", "usb", "wTs", "otok")}
        pbig = Rot([c.ps([128, 512], F32, "pbig%d" % i) for i in range(2)])
        banks = [c.ps([128, 512], F32, "pbank%d" % i) for i in range(6)]
        psm = Rot([Tl(banks[i].t[:, q * 128:(q + 1) * 128], "psm%d_%d" % (i, q)) for q in range(1) for i in range(6)])

        u3 = uT[:].rearrange("(k p) n -> p k n", p=128)
        o3 = oT[:].rearrange("(h p) n -> p h n", p=128)

        def mm(out_t, out_ap, lhsT, rhs, reads, start=True, stop=True):
            c.op("pe", lambda e: e.matmul(out_ap, lhsT=lhsT, rhs=rhs, start=start, stop=stop),
                 reads=reads, writes=[out_t], inc=stop)

        def evac(eng, dst_t, dst_ap, src_t, src_ap, scale=None, extra=()):
            if eng == "act":
                if scale is None:
                    c.op("act", lambda e: e.activation(out=dst_ap, in_=src_ap, func=AF.Copy),
                         reads=[src_t], writes=[dst_t])
                else:
                    c.op("act", lambda e: e.activation(out=dst_ap, in_=src_ap, func=AF.Copy, scale=scale),
                         reads=[src_t] + list(extra), writes=[dst_t])
            else:
                c.op("dve", lambda e: e.tensor_copy(out=dst_ap, in_=src_ap), reads=[src_t], writes=[dst_t])

        for (b0, ntl) in blocks:
            n = ntl * 128
            u = ub.get()
            c.dma("sp", u[:, :, :n], u3[:, :, b0:b0 + n], writes=[u])
            F = []
            for j in range(8):
                p = pbig.get()
                for k in range(8):
                    mm(p, p[:, :n], wb[:, k, j * 128:(j + 1) * 128], u[:, k, :n], [wb, u], start=(k == 0), stop=(k == 7))
                if j < 6:
                    c.op("act", lambda e: e.activation(out=xc[j][:, 3:3 + n], in_=p[:, :n], func=AF.Copy),
                         reads=[p], writes=[xc[j]])
                    f = ft[j].get()
                    c.op("dve", lambda e: e.tensor_scalar(out=f[:, :n], in0=xc[j][:, 0:n], scalar1=ps_[:, j * 4:j * 4 + 1],
                                                          scalar2=None, op0=ALU.mult), reads=[xc[j], ps_], writes=[f])
                    for tp in range(1, 4):
                        c.op("dve", lambda e: e.scalar_tensor_tensor(out=f[:, :n], in0=xc[j][:, tp:tp + n],
                                                                     scalar=ps_[:, j * 4 + tp:j * 4 + tp + 1], in1=f[:, :n],
                                                                     op0=ALU.mult, op1=ALU.add),
                             reads=[xc[j], ps_, f], writes=[f])
                    c.op("pool", lambda e: e.tensor_copy(out=xc[j][:, 0:3], in_=xc[j][:, n:n + 3]),
                         reads=[xc[j]], writes=[xc[j]])
                    c.op("act", lambda e: e.activation(out=f[:, :n], in_=f[:, :n], func=AF.Silu), reads=[f], writes=[f])
                    F.append(f)
                else:
                    z = szr[j - 6].get()
                    c.op("act", lambda e: e.activation(out=z[:, :n], in_=p[:, :n], func=AF.Silu), reads=[p], writes=[z])
                    F.append(z)
            for j in range(4 if dbg >= 1 else 0):
                rms_stats(c, c.ones, lambda k, j=j: (F[j][:, :n], F[j]), 1, n, pbig.get(), sqb, rstd, 1.0)
                if j < 2:
                    c.op("dve", lambda e: e.scalar_tensor_tensor(out=F[j][:, :n], in0=F[j][:, :n], scalar=128.0 ** -0.5,
                                                                 in1=rstd[:, :n], op0=ALU.mult, op1=ALU.mult),
                         reads=[F[j], rstd], writes=[F[j]])
                else:
                    c.op("dve", lambda e: e.tensor_tensor(out=F[j][:, :n], in0=F[j][:, :n], in1=rstd[:, :n], op=ALU.mult),
                         reads=[F[j], rstd], writes=[F[j]])
            if dbg < 2:
                continue
            pba_t = psm.get()
            for tt in range(ntl):
                for k in range(8):
                    mm(pba_t, pba_t[:, tt * 4:(tt + 1) * 4], u[:, k, tt * 128:(tt + 1) * 128], wb[:, k, 1024:1028],
                       [u, wb], start=(k == 0), stop=(k == 7))
            pba = pba_t[:, 0:4 * ntl].rearrange("p (t f) -> p t f", f=4)
            btb = bt.get()
            ggb = gg.get()
            c.op("act", lambda e: e.activation(out=btb[:, :ntl, :], in_=pba[:, :, 0:2], func=AF.Sigmoid),
                 reads=[pba_t], writes=[btb])
            x_ = gtmp.get(); ax = gtmp.get(); rl = gtmp.get()
            for h in range(2):
                c.op("dve", lambda e: e.tensor_scalar(out=x_[:, :ntl, h:h + 1], in0=pba[:, :, 2 + h:3 + h],
                                                      scalar1=ps_[:, 26 + h:27 + h], scalar2=None, op0=ALU.add),
                     reads=[pba_t, ps_], writes=[x_])
            c.op("act", lambda e: e.activation(out=ax[:, :ntl, :], in_=x_[:, :ntl, :], func=AF.Abs),
                 reads=[x_], writes=[ax])
            c.op("act", lambda e: e.activation(out=ax[:, :ntl, :], in_=ax[:, :ntl, :], func=AF.Exp, scale=-1.0),
                 reads=[ax], writes=[ax])
            c.op("act", lambda e: e.activation(out=ax[:, :ntl, :], in_=ax[:, :ntl, :], func=AF.Ln, bias=one_t[:, 0:1]),
                 reads=[ax, one_t], writes=[ax])
            c.op("dve", lambda e: e.tensor_scalar(out=rl[:, :ntl, :], in0=x_[:, :ntl, :], scalar1=0.0, scalar2=None,
                                                  op0=ALU.max), reads=[x_], writes=[rl])
            c.op("dve", lambda e: e.tensor_tensor(out=rl[:, :ntl, :], in0=rl[:, :ntl, :], in1=ax[:, :ntl, :], op=ALU.add),
                 reads=[rl, ax], writes=[rl])
            for h in range(2):
                c.op("dve", lambda e: e.tensor_scalar(out=ggb[:, :ntl, h:h + 1], in0=rl[:, :ntl, h:h + 1],
                                                      scalar1=nea[:, h:h + 1], scalar2=None, op0=ALU.mult),
                     reads=[rl, nea], writes=[ggb])
            ob = obr.get()
            TT = list(range(ntl))
            CH = [(tt, h) for tt in TT for h in range(2)]
            ci = {ch: i for i, ch in enumerate(CH)}
            csl = {tt: slice(tt * 128, (tt + 1) * 128) for tt in TT}
            egc, ed, be = {}, {}, {}
            for tt in TT:
                pg1 = psm.get(); pg2 = psm.get()
                mm(pg1, pg1[:, 0:2], U2, ggb[:, tt, :], [cst, ggb])
                mm(pg2, pg2[:, 0:2], R2, ggb[:, tt, :], [cst, ggb])
                egc[tt] = sm2.get(); ed[tt] = sm2.get(); be[tt] = sm2.get()
                c.op("act", lambda e: e.activation(out=egc[tt][:], in_=pg1[:, 0:2], func=AF.Exp), reads=[pg1], writes=[egc[tt]])
                c.op("act", lambda e: e.activation(out=ed[tt][:], in_=pg2[:, 0:2], func=AF.Exp), reads=[pg2], writes=[ed[tt]])
                c.op("dve", lambda e: e.tensor_tensor(out=be[tt][:], in0=egc[tt][:], in1=btb[:, tt, :], op=ALU.mult),
                     reads=[egc[tt], btb], writes=[be[tt]])
            Ug, dec, decs, egrow, P, Q, Y = {}, {}, {}, {}, {}, {}, {}
            for ch in CH:
                tt, h = ch
                Ug[ch] = sq128.get()
                c.op("dve", lambda e: e.tensor_scalar(out=Ug[ch][:], in0=U2, scalar1=ggb[:, tt, h:h + 1], scalar2=None,
                                                      op0=ALU.mult), reads=[cst, ggb], writes=[Ug[ch]])
            for ch in CH:
                tt, h = ch
                pd = psm.get()
                mm(pd, pd[:], Ug[ch][:], c.ones[:], [Ug[ch], c.ones], start=True, stop=False)
                mm(pd, pd[:], negones, Ug[ch][:], [cst, Ug[ch]], start=False, stop=True)
                dec[ch] = LL["dec"][ci[ch]]
                c.op("dve", lambda e: e.tensor_tensor(out=dec[ch][:], in0=pd[:], in1=negmask, op=ALU.add),
                     reads=[pd, cst], writes=[dec[ch]])
                c.op("act", lambda e: e.activation(out=dec[ch][:], in_=dec[ch][:], func=AF.Exp),
                     reads=[dec[ch]], writes=[dec[ch]])
                decs[ch] = sq128.get()
                c.op("pool", lambda e: e.tensor_tensor(out=decs[ch][:], in0=dec[ch][:], in1=smask, op=ALU.mult),
                     reads=[dec[ch], cst], writes=[decs[ch]])
                pe_ = psm.get()
                mm(pe_, pe_[:], c.ones[:], Ug[ch][:], [c.ones, Ug[ch]])
                egrow[ch] = LL["egrow"][ci[ch]]
                c.op("act", lambda e: e.activation(out=egrow[ch][:], in_=pe_[:], func=AF.Exp),
                     reads=[pe_], writes=[egrow[ch]])
            for ch in CH:
                tt, h = ch
                kT = F[2 + h][:, csl[tt]]
                pk = psm.get()
                mm(pk, pk[:], kT, kT, [F[2 + h]])
                P[ch] = sq128.get()
                c.op("dve", lambda e: e.scalar_tensor_tensor(out=P[ch][:], in0=pk[:], scalar=btb[:, tt, h:h + 1],
                                                             in1=decs[ch][:], op0=ALU.mult, op1=ALU.mult),
                     reads=[pk, btb, decs[ch]], writes=[P[ch]])
            for ch in CH:
                pb = psm.get()
                mm(pb, pb[:], P[ch][:], ident, [P[ch], cst])
                Q[ch] = sq128.get(); Y[ch] = sq128.get()
                evac("act", Q[ch], Q[ch][:], pb, pb[:])
                c.op("pool", lambda e: e.tensor_tensor(out=Y[ch][:], in0=ident, in1=Q[ch][:], op=ALU.subtract),
                     reads=[Q[ch], cst], writes=[Y[ch]])
            for s in range(5):
                Pn, Qn = {}, {}
                for ch in CH:
                    pp_ = psm.get()
                    mm(pp_, pp_[:], Q[ch][:], P[ch][:], [Q[ch], P[ch]])
                    Pn[ch] = sq128.get()
                    evac("act", Pn[ch], Pn[ch][:], pp_, pp_[:])
                    if s < 4:
                        pq = psm.get()
                        mm(pq, pq[:], P[ch][:], Q[ch][:], [P[ch], Q[ch]])
                        Qn[ch] = sq128.get()
                        evac("dve", Qn[ch], Qn[ch][:], pq, pq[:])
                for ch in CH:
                    py_ = psm.get()
                    mm(py_, py_[:], Pn[ch][:], Y[ch][:], [Pn[ch], Y[ch]])
                    Yn = sq128.get()
                    c.op("dve", lambda e: e.tensor_tensor(out=Yn[:], in0=py_[:], in1=Y[ch][:], op=ALU.add),
                         reads=[py_, Y[ch]], writes=[Yn])
                    Y[ch] = Yn
                    P[ch] = Pn[ch]
                    if s < 4:
                        Q[ch] = Qn[ch]
            kbg, kd, vb, usb, wTs, attT, qg = {}, {}, {}, {}, {}, {}, {}
            for ch in CH:
                tt, h = ch
                kT = F[2 + h][:, csl[tt]]; vT = F[4 + h][:, csl[tt]]; qT = F[h][:, csl[tt]]
                pkt = psm.get()
                mm(pkt, pkt[:], kT, ident, [F[2 + h], cst])
                kbg[ch] = sq128.get(); kd[ch] = LL["kd"][ci[ch]]
                evac("act", kbg[ch], kbg[ch][:], pkt, pkt[:], scale=be[tt][:, h:h + 1], extra=[be[tt]])
                evac("act", kd[ch], kd[ch][:], pkt, pkt[:], scale=ed[tt][:, h:h + 1], extra=[ed[tt]])
                pvt = psm.get()
                mm(pvt, pvt[:], vT, ident, [F[4 + h], cst])
                vb[ch] = sq128.get()
                evac("act", vb[ch], vb[ch][:], pvt, pvt[:], scale=btb[:, tt, h:h + 1], extra=[btb])
                pqk = psm.get()
                mm(pqk, pqk[:], qT, kT, [F[h], F[2 + h]])
                att = sq128.get()
                c.op("dve", lambda e: e.tensor_tensor(out=att[:], in0=pqk[:], in1=dec[ch][:], op=ALU.mult),
                     reads=[pqk, dec[ch]], writes=[att])
                pat = psm.get()
                mm(pat, pat[:], att[:], ident, [att, cst])
                attT[ch] = LL["attT"][ci[ch]]
                evac("dve", attT[ch], attT[ch][:], pat, pat[:])
                qg[ch] = LL["qg"][ci[ch]]
                c.op("pool", lambda e: e.tensor_tensor(out=qg[ch][:], in0=qT, in1=egrow[ch][:], op=ALU.mult),
                     reads=[F[h], egrow[ch]], writes=[qg[ch]])
            for ch in CH:
                pu = psm.get()
                mm(pu, pu[:], Y[ch][:], vb[ch][:], [Y[ch], vb[ch]])
                usb[ch] = LL["usb"][ci[ch]]
                evac("act", usb[ch], usb[ch][:], pu, pu[:])
                pw = psm.get()
                mm(pw, pw[:], kbg[ch][:], Y[ch][:], [kbg[ch], Y[ch]])
                wTs[ch] = LL["wTs"][ci[ch]]
                evac("dve", wTs[ch], wTs[ch][:], pw, pw[:])
            otok = {ch: LL["otok"][ci[ch]] for ch in CH}
            for tt in TT:
                vnew = {h: sq128.get() for h in range(2)}
                for half in range(2):
                    r = slice(half * 64, half * 64 + 64)
                    for h in range(2):
                        ch = (tt, h)
                        pws = psm.get()
                        mm(pws, pws[:], wTs[ch][:], S[h][:], [wTs[ch], S[h]])
                        c.op("dve", lambda e: e.tensor_tensor(out=vnew[h][r, :], in0=usb[ch][r, :], in1=pws[r, :], op=ALU.subtract),
                             reads=[usb[ch], pws], writes=[vnew[h]])
                    for h in range(2):
                        ch = (tt, h)
                        po = psm.get()
                        mm(po, po[:], qg[ch][:], S[h][:], [qg[ch], S[h]], start=True, stop=False)
                        mm(po, po[:], attT[ch][r, :], vnew[h][r, :], [attT[ch], vnew[h]], start=False, stop=True)
                        evac("act", otok[ch], otok[ch][r, :], po, po[r, :])
                        pst = psm.get()
                        mm(pst, pst[:], kd[ch][r, :], vnew[h][r, :], [kd[ch], vnew[h]])
                        c.op("dve", lambda e: e.scalar_tensor_tensor(out=S[h][:], in0=S[h][:],
                                                                     scalar=egrow[ch][:, half * 64 + 63:half * 64 + 64],
                                                                     in1=pst[:], op0=ALU.mult, op1=ALU.add),
                             reads=[S[h], egrow[ch], pst], writes=[S[h]])
            for ch in CH:
                tt, h = ch
                junk = sq128.get()
                ss = sm1.get()
                c.op("act", lambda e: e.activation(out=junk[:], in_=otok[ch][:], func=AF.Square, accum_out=ss[:]),
                     reads=[otok[ch]], writes=[junk, ss])
                c.op("act", lambda e: e.activation(out=ss[:], in_=ss[:], func=AF.Sqrt, bias=c.eps_t[:, 0:1], scale=1.0 / 128),
                     reads=[ss, c.eps_t], writes=[ss])
                c.op("dve", lambda e: e.reciprocal(out=ss[:], in_=ss[:]), reads=[ss], writes=[ss])
                on = sq128.get()
                c.op("dve", lambda e: e.tensor_scalar(out=on[:], in0=otok[ch][:], scalar1=ss[:, 0:1], scalar2=None, op0=ALU.mult),
                     reads=[otok[ch], ss], writes=[on])
                pot = psm.get()
                mm(pot, pot[:], on[:], ident, [on, cst])
                c.op("dve", lambda e: e.scalar_tensor_tensor(out=ob[:, h, csl[tt]], in0=pot[:], scalar=ps_[:, 28:29],
                                                             in1=F[6 + h][:, csl[tt]], op0=ALU.mult, op1=ALU.mult),
                     reads=[pot, ps_, F[6 + h]], writes=[ob])
            c.dma("sp", o3[:, :, b0:b0 + n], ob[:, :, :n], reads=[ob], writes=[oT])


def dn_params(conv_w6, a_log2, dt_bias2, norm_w):
    prm = np.zeros((128, 32), np.float32)
    for j in range(6):
        prm[:, j * 4:(j + 1) * 4] = conv_w6[:, j * 128:(j + 1) * 128].T
    prm[:, 24:26] = a_log2[None, :]
    prm[:, 26:28] = dt_bias2[None, :]
    prm[:, 28] = norm_w
    return prm


DA_DS = (-128, 0, 128, 256, 384)
DA_NEGM = -240000.0
LAMBDA_INIT_L1 = 0.8 - 0.6 * math.exp(-0.3 * 1)


def _t5_bucket_np(rel):
    import jax
    import jax.numpy as jnp
    with jax.default_device(jax.devices("cpu")[0]):
        rel = jnp.asarray(rel, jnp.int32)
        nb = 16
        ret = jnp.where(rel > 0, nb, 0)
        n = jnp.abs(rel)
        max_exact = nb // 2
        nf = jnp.maximum(n, 1).astype(jnp.float32)
        large = max_exact + (jnp.log(nf / max_exact) / math.log(128 / max_exact)
                             * (nb - max_exact)).astype(jnp.int32)
        large = jnp.minimum(large, nb - 1)
        return np.asarray(ret + jnp.where(n < max_exact, n, large))


def da_consts():
    r = np.arange(-639, 513)
    bk = _t5_bucket_np(r)
    oh = np.zeros((32, 1152), np.float32)
    oh[bk, np.arange(1152)] = 1.0
    oh[15, :] -= 1.0
    kk = np.arange(128)[:, None]
    qq = np.arange(512)[None, :]
    md = np.zeros((128, 5, 512), np.float32)
    for i, d in enumerate(DA_DS):
        allowed = ((d + kk) // 64) <= (qq // 64)
        md[:, i, :] = np.where(allowed, 0.0, DA_NEGM)
    pm = np.zeros((128, 1), np.float32)
    pm[:112] = -30000.0
    return oh, md, pm


def build_da(lambda_init=LAMBDA_INIT_L1):
    NKT = TD // 128
    qtiles = [(0, 128)] + [(128 + 512 * i, 512) for i in range(16)]
    nc = bass.Bass("TRN2", target_bir_lowering=False)
    with ExitStack() as es:
        c = Ctx(nc, es)
        io = {}
        io["uT"] = c.dram("uT", [D, TD], BF16, "ExternalInput")
        io["wcat"] = c.dram("wcat", [D, 768], F32, "ExternalInput")
        io["oh"] = c.dram("oh", [32, 1152], F32, "ExternalInput")
        io["md"] = c.dram("md", [128, 5, 512], F32, "ExternalInput")
        io["pm"] = c.dram("pm", [128, 1], F32, "ExternalInput")
        io["rb"] = c.dram("rb", [32, 2], F32, "ExternalInput")
        io["rb15"] = c.dram("rb15", [128, 2], F32, "ExternalInput")
        io["lamv"] = c.dram("lamv", [128, 4, 64], F32, "ExternalInput")
        io["sw"] = c.dram("sw", [128, 1], F32, "ExternalInput")
        io["identf"] = c.dram("identf", [128, 128], F32, "ExternalInput")
        io["oT"] = c.dram("oT", [256, TD], BF16, "ExternalOutput")
        make_consts(c)
        emit_da(c, io, lambda_init, "")
        c.finish()
    return nc


def emit_da(c, io, lambda_init, tag):
    NKT = TD // 128
    qtiles = [(0, 128)] + [(128 + 512 * i, 512) for i in range(16)]
    if True:
        uT, wcat, ohd, mdd, pmd, rb, rb15, lamv, sw, idd, oT = (io[k] for k in (
            "uT", "wcat", "oh", "md", "pm", "rb", "rb15", "lamv", "sw", "identf", "oT"))
        tvd = c.dram("tvscr" + tag, [2, 1152], F32, "Internal")
        identb = c.sb([128, 128], BF16, "identb")
        onesb = c.sb([128, 128], BF16, "onesb")
        c.op("pool", lambda e: e.memset(onesb[:], 1.0), writes=[onesb])
        QT = [c.sb([128, TD], BF16, "QT%d" % h) for h in range(2)]
        KT = [c.sb([128, TD], BF16, "KT%d" % h) for h in range(2)]
        V = c.sb([128, NKT, 256], BF16, "V")
        BH = [[c.sb([128, 512], BF16, "BH%d_%d" % (h, i)) for i in range(5)] for h in range(2)]
        BL = [[c.sb([128, 512], BF16, "BL%d_%d" % (h, i)) for i in range(5)] for h in range(2)]
        biasc = c.sb([128, 2], F32, "biasc")
        bias0 = c.sb([128, 2], F32, "bias0")
        neglam = c.sb([128, 1], F32, "neglam")
        swp = c.sb([128, 1], F32, "swp")

        with ExitStack() as es1:
            c.dma("sp", biasc[:], rb15[:], writes=[biasc])
            pms = c.sb([128, 1], F32, "pms", es=es1)
            c.dma("sp", pms[:], pmd[:], writes=[pms])
            c.op("dve", lambda e: e.tensor_scalar(out=bias0[:], in0=biasc[:], scalar1=pms[:, 0:1], scalar2=None, op0=ALU.add),
                 reads=[biasc, pms], writes=[bias0])
            sws = c.sb([128, 1], F32, "sws", es=es1)
            c.dma("sp", sws[:], sw[:], writes=[sws])
            c.op("dve", lambda e: e.tensor_scalar(out=swp[:], in0=sws[:], scalar1=1.0 - lambda_init, scalar2=None, op0=ALU.mult),
                 reads=[sws], writes=[swp])
            lv = c.sb([128, 4, 64], F32, "lv", es=es1)
            c.dma("sp", lv[:], lamv[:], writes=[lv])
            pr = c.sb([128, 2, 64], F32, "lpr", es=es1)
            sm = c.sb([128, 2], F32, "lsm", es=es1)
            for i in range(2):
                c.op("dve", lambda e: e.tensor_tensor(out=pr[:, i, :], in0=lv[:, 2 * i, :], in1=lv[:, 2 * i + 1, :], op=ALU.mult),
                     reads=[lv], writes=[pr])
                c.op("dve", lambda e: e.reduce_sum(out=sm[:, i:i + 1], in_=pr[:, i, :], axis=AX.X), reads=[pr], writes=[sm])
            c.op("act", lambda e: e.activation(out=sm[:], in_=sm[:], func=AF.Exp), reads=[sm], writes=[sm])
            c.op("dve", lambda e: e.tensor_tensor(out=neglam[:], in0=sm[:, 1:2], in1=sm[:, 0:1], op=ALU.subtract),
                 reads=[sm], writes=[neglam])
            c.op("dve", lambda e: e.tensor_scalar(out=neglam[:], in0=neglam[:], scalar1=-lambda_init, scalar2=None, op0=ALU.add),
                 reads=[neglam], writes=[neglam])
            idf = c.sb([128, 128], F32, "idf", es=es1)
            c.dma("sp", idf[:], idd[:], writes=[idf])
            c.op("dve", lambda e: e.tensor_copy(out=identb[:], in_=idf[:]), reads=[idf], writes=[identb])
            ohs = c.sb([32, 1152], F32, "ohs", es=es1)
            c.dma("sp", ohs[:], ohd[:], writes=[ohs])
            rbs = c.sb([32, 2], F32, "rbs", es=es1)
            c.dma("sp", rbs[:], rb[:], writes=[rbs])
            tvs = c.sb([2, 1152], F32, "tvs", es=es1)
            ptv = c.ps([128, 512], F32, "ptv", es=es1)
            for j in range(3):
                c.op("pe", lambda e: e.matmul(ptv[0:2, 0:384], lhsT=rbs[:], rhs=ohs[:, j * 384:(j + 1) * 384], start=True, stop=True),
                     reads=[rbs, ohs], writes=[ptv])
                c.op("act", lambda e: e.activation(out=tvs[:, j * 384:(j + 1) * 384], in_=ptv[0:2, 0:384], func=AF.Copy),
                     reads=[ptv], writes=[tvs])
            c.dma("sp", tvd[:], tvs[:], reads=[tvs], writes=[tvd])
            mds = c.sb([128, 5, 512], F32, "mds", es=es1)
            c.dma("sp", mds[:], mdd[:], writes=[mds])
            G = Rot([c.sb([128, 512], F32, "G%d" % i, es=es1) for i in range(2)])
            Bt = Rot([c.sb([128, 512], F32, "Bt%d" % i, es=es1) for i in range(2)])
            for h in range(2):
                for i, d in enumerate(DA_DS):
                    g_ = G.get()
                    src = bass.AP(tensor=tvd.t.tensor, offset=h * 1152 + d + 128, ap=[[1, 128], [1, 512]])
                    c.dma("sp", g_[:], src, reads=[tvd], writes=[g_])
                    b_ = Bt.get()
                    c.op("dve", lambda e: e.scalar_tensor_tensor(out=b_[:], in0=g_[:, ::-1], scalar=8.0, in1=mds[:, i, :],
                                                                 op0=ALU.mult, op1=ALU.add), reads=[g_, mds], writes=[b_])
                    c.op("act", lambda e: e.activation(out=BH[h][i][:], in_=b_[:], func=AF.Copy), reads=[b_], writes=[BH[h][i]])
                    c.op("dve", lambda e: e.tensor_tensor(out=BL[h][i][:], in0=b_[:], in1=BH[h][i][:], op=ALU.subtract),
                         reads=[b_, BH[h][i]], writes=[BL[h][i]])

        c.barrier()
        with ExitStack() as es2:
            wb = c.sb([128, 8, 768], BF16, "wb", es=es2)
            w3 = wcat[:].rearrange("(k p) n -> p k n", p=128)
            for k0 in range(0, 8, 2):
                load_cast(c, wb, wb[:, k0:k0 + 2, :], w3[:, k0:k0 + 2, :])
            ub = Rot([c.sb([128, 8, 512], BF16, "ub%d" % i, es=es2) for i in range(2)])
            pbig = Rot([c.ps([128, 512], F32, "pb%d" % i, es=es2) for i in range(6)])
            u3 = uT[:].rearrange("(k p) n -> p k n", p=128)
            for (b0, ntl) in DN_BLOCKS:
                n = ntl * 128
                u = ub.get()
                c.dma("sp", u[:, :, :n], u3[:, :, b0:b0 + n], writes=[u])
                for j in range(4):
                    p = pbig.get()
                    for k in range(8):
                        c.op("pe", lambda e: e.matmul(p[:, :n], lhsT=wb[:, k, j * 128:(j + 1) * 128], rhs=u[:, k, :n],
                                                      start=(k == 0), stop=(k == 7)), reads=[wb, u], writes=[p], inc=(k == 7))
                    dst = (QT[j] if j < 2 else KT[j - 2])
                    if j % 2 == 0:
                        c.op("act", lambda e: e.activation(out=dst[:, b0:b0 + n], in_=p[:, :n], func=AF.Copy), reads=[p], writes=[dst])
                    else:
                        c.op("dve", lambda e: e.tensor_copy(out=dst[:, b0:b0 + n], in_=p[:, :n]), reads=[p], writes=[dst])
                for tt in range(ntl):
                    p = pbig.get()
                    for k in range(8):
                        c.op("pe", lambda e: e.matmul(p[:, 0:256], lhsT=u[:, k, tt * 128:(tt + 1) * 128], rhs=wb[:, k, 512:768],
                                                      start=(k == 0), stop=(k == 7)), reads=[wb, u], writes=[p], inc=(k == 7))
                    kt = b0 // 128 + tt
                    if tt % 2 == 0:
                        c.op("act", lambda e: e.activation(out=V[:, kt, :], in_=p[:, 0:256], func=AF.Copy), reads=[p], writes=[V])
                    else:
                        c.op("dve", lambda e: e.tensor_copy(out=V[:, kt, :], in_=p[:, 0:256]), reads=[p], writes=[V])

        c.barrier()
        with ExitStack() as es3:
            sps = Rot([c.ps([128, 512], F32, "sps%d" % i, es=es3) for i in range(4)])
            oacc = [c.ps([128, 512], F32, "oacc%d" % i, es=es3) for i in range(2)]
            dacc = [c.ps([128, 512], F32, "dacc%d" % i, es=es3) for i in range(2)]
            ptb = Rot([c.sb([128, 512], BF16, "pt%d" % i, es=es3) for i in range(4)])
            rr = [c.sb([128, 512], F32, "rr%d" % i, es=es3) for i in range(2)]
            aa = [c.sb([128, 512], F32, "aa%d" % i, es=es3) for i in range(2)]
            sqb = Rot([c.sb([128, 512], F32, "sq%d" % i, es=es3) for i in range(2)])
            rstd = c.sb([128, 512], F32, "rstd", es=es3)
            obr = Rot([c.sb([128, 512], BF16, "ob%d" % i, es=es3) for i in range(2)])
            dsr = [Rot([c.sb([128, 512], F32, "dsum%d_%d" % (cc, i), es=es3) for i in range(2)]) for cc in range(2)]
            for (q0, nq) in qtiles:
                ktmax = (q0 + nq) // 128 - 1
                for h in range(2):
                    units = [(kt, cc) for kt in range(ktmax + 1) for cc in range(2)]
                    dsum = [dsr[0].get(), dsr[1].get()]

                    def emit_s(kt, cc):
                        ps = sps.get()
                        d = kt * 128 - q0
                        near = d in DA_DS
                        rs = slice(cc * 64, cc * 64 + 64)
                        c.op("pe", lambda e: e.matmul(ps[:, :nq], lhsT=KT[h][rs, kt * 128:(kt + 1) * 128], rhs=QT[h][rs, q0:q0 + nq],
                                                      start=True, stop=not near), reads=[KT[h], QT[h]], writes=[ps], inc=not near)
                        if near:
                            i = DA_DS.index(d)
                            c.op("pe", lambda e: e.matmul(ps[:, :nq], lhsT=identb[:], rhs=BH[h][i][:, :nq], start=False, stop=False),
                                 reads=[identb, BH[h][i]], writes=[ps], inc=False)
                            c.op("pe", lambda e: e.matmul(ps[:, :nq], lhsT=identb[:], rhs=BL[h][i][:, :nq], start=False, stop=True),
                                 reads=[identb, BL[h][i]], writes=[ps])
                        return ps

                    pend = [emit_s(*units[0])]
                    if len(units) > 1:
                        pend.append(emit_s(*units[1]))
                    for ui, (kt, cc) in enumerate(units):
                        ps = pend[ui]
                        pt = ptb.get()
                        bsrc = bias0 if kt == 0 else biasc
                        c.op("act", lambda e: e.activation(out=pt[:, :nq], in_=ps[:, :nq], func=AF.Exp, bias=bsrc[:, h:h + 1], scale=0.125),
                             reads=[ps, bsrc], writes=[pt])
                        if ui + 2 < len(units):
                            pend.append(emit_s(*units[ui + 2]))
                        c.op("pe", lambda e: e.matmul(oacc[cc][:, :nq], lhsT=V[:, kt, h * 128:(h + 1) * 128], rhs=pt[:, :nq],
                                                      start=(kt == 0), stop=(kt == ktmax)), reads=[V, pt], writes=[oacc[cc]])
                        if kt == 0:
                            c.op("dve", lambda e: e.tensor_copy(out=dsum[cc][:, :nq], in_=pt[:, :nq]), reads=[pt], writes=[dsum[cc]])
                        else:
                            c.op("dve", lambda e: e.tensor_tensor(out=dsum[cc][:, :nq], in0=dsum[cc][:, :nq], in1=pt[:, :nq], op=ALU.add),
                                 reads=[pt, dsum[cc]], writes=[dsum[cc]])
                    for cc in range(2):
                        c.op("pe", lambda e: e.matmul(dacc[cc][:, :nq], lhsT=c.ones[:], rhs=dsum[cc][:, :nq], start=True, stop=True),
                             reads=[c.ones, dsum[cc]], writes=[dacc[cc]])
                    for cc in range(2):
                        if q0 == 0:
                            c.op("dve", lambda e: e.tensor_scalar(out=rr[cc][:, :nq], in0=dacc[cc][:, :nq], scalar1=1e-30, scalar2=None,
                                                                  op0=ALU.max), reads=[dacc[cc]], writes=[rr[cc]])
                            c.op("dve", lambda e: e.reciprocal(out=rr[cc][:, :nq], in_=rr[cc][:, :nq]), reads=[rr[cc]], writes=[rr[cc]])
                        else:
                            c.op("dve", lambda e: e.reciprocal(out=rr[cc][:, :nq], in_=dacc[cc][:, :nq]), reads=[dacc[cc]], writes=[rr[cc]])
                        c.op("dve", lambda e: e.tensor_tensor(out=aa[cc][:, :nq], in0=oacc[cc][:, :nq], in1=rr[cc][:, :nq], op=ALU.mult),
                             reads=[oacc[cc], rr[cc]], writes=[aa[cc]])
                    c.op("dve", lambda e: e.scalar_tensor_tensor(out=aa[0][:, :nq], in0=aa[1][:, :nq], scalar=neglam[:, 0:1],
                                                                 in1=aa[0][:, :nq], op0=ALU.mult, op1=ALU.add),
                         reads=[aa[0], aa[1], neglam], writes=[aa[0]])
                    rms_stats(c, c.ones, lambda k: (aa[0][:, :nq], aa[0]), 1, nq, sps.get(), sqb, rstd, 128.0)
                    ob = obr.get()
                    c.op("dve", lambda e: e.scalar_tensor_tensor(out=ob[:, :nq], in0=aa[0][:, :nq], scalar=swp[:, 0:1],
                                                                 in1=rstd[:, :nq], op0=ALU.mult, op1=ALU.mult),
                         reads=[aa[0], swp, rstd], writes=[ob])
                    if q0 == 0:
                        c.op("dve", lambda e: e.memset(ob[:, 0:112], 0.0), writes=[ob])
                    c.dma("sp", oT[h * 128:(h + 1) * 128, q0:q0 + nq], ob[:, :nq], reads=[ob], writes=[oT])


def build_pre():
    nc = bass.Bass("TRN2", target_bir_lowering=False)
    with ExitStack() as es:
        c = Ctx(nc, es)
        io = {}
        io["hT"] = c.dram("hT", [D, NTC], F32, "ExternalInput")
        io["nw"] = c.dram("nw", [128, 8], F32, "ExternalInput")
        io["uout"] = c.dram("uout", [D, NTC], BF16, "ExternalOutput")
        make_consts(c)
        emit_pre(c, io)
        c.finish()
    return nc


def emit_pre(c, io):
    if True:
        hT, nw, uout = io["hT"], io["nw"], io["uout"]
        nws = c.sb([128, 8], F32, "nws")
        c.dma("sp", nws[:], nw[:], writes=[nws])
        H = Rot([c.sb([128, 8, GS], F32, "H%d" % g) for g in range(2)])
        U = Rot([c.sb([128, 8, GS], BF16, "U%d" % g) for g in range(2)])
        sqb = Rot([c.sb([128, GS], F32, "sq%d" % i) for i in range(2)])
        rstd = c.sb([128, GS], F32, "rstd")
        pstat = c.ps([128, 512], F32, "pstat")
        hT3 = hT[:].rearrange("(k p) n -> p k n", p=128)
        uo3 = uout[:].rearrange("(k p) n -> p k n", p=128)
        for g in range(NG):
            h = H.get()
            c.dma("sp", h[:], hT3[:, :, g * GS:(g + 1) * GS], writes=[h])
            rms_stats(c, c.ones, lambda k: (h[:, k, :], h), 8, GS, pstat, sqb, rstd, float(D))
            u = U.get()
            for m in range(8):
                c.op("dve", lambda e: e.scalar_tensor_tensor(out=u[:, m, :], in0=h[:, m, :], scalar=nws[:, m:m + 1],
                                                             in1=rstd[:], op0=ALU.mult, op1=ALU.mult),
                     reads=[h, nws, rstd], writes=[u])
            c.dma("sp", uo3[:, :, g * GS:(g + 1) * GS], u[:], reads=[u], writes=[uout])


def _run(nc, in_maps):
    res = run_bass_kernel_spmd(nc, in_maps, core_ids=list(range(NCORES)))
    return res.results


def _nwcols(w):
    return np.ascontiguousarray(np.asarray(w, np.float32).reshape(8, 128).T)


def _tok_shards(fullT):
    out = []
    for b in range(B):
        for j in range(4):
            out.append(np.ascontiguousarray(fullT[b][:, j * NTC:(j + 1) * NTC]))
    return out


def _gather_tok(shards):
    return [np.concatenate([shards[4 * b + j] for j in range(4)], axis=1) for b in range(B)]


def kernel_unfused(x, meta_tokens, rel_bias, norm_mix_w, norm_mlp_w, final_norm_w,
           dn_w_in, dn_conv_w, dn_a_log, dn_dt_bias, dn_norm_w, dn_w_out,
           da_w_in, da_lam_q1, da_lam_k1, da_lam_q2, da_lam_k2, da_subln_w, da_w_out,
           lru_w_in, lru_conv_w, lru_conv_b, lru_w_rgate, lru_b_rgate, lru_w_igate,
           lru_b_igate, lru_lambda, lru_w_out, mlp_w1, mlp_w2):
    f32 = np.float32
    x = np.asarray(x, f32)
    meta = np.asarray(meta_tokens, f32)
    bf = ml_dtypes.bfloat16
    hT_full = []
    for b in range(B):
        seq = np.concatenate([np.zeros((PADF, D), f32), meta, x[b]], axis=0)
        hT_full.append(np.ascontiguousarray(seq.T))
    h_sh = _tok_shards(hT_full)
    nc = build_pre()
    r = _run(nc, [{"hT": h_sh[c], "nw": _nwcols(norm_mix_w[0])} for c in range(NCORES)])
    u_sh = [r[c]["uout"] for c in range(NCORES)]
    depth = 4
    zpad = np.zeros((D, 64), bf)
    for layer in range(depth):
        kind = layer % 3
        slot = layer // 3
        u_full = _gather_tok(u_sh)
        ims = []
        if kind == 0:
            w_in = np.asarray(dn_w_in[slot], f32)
            cw = np.asarray(dn_conv_w[slot], f32)
            cst = dn_consts()
            for c in range(NCORES):
                b, g = divmod(c, 4)
                h0 = 2 * g
                s = slice(h0 * 128, h0 * 128 + 256)
                wcat = np.concatenate([w_in[:, 0:1024][:, s], w_in[:, 1024:2048][:, s], w_in[:, 2048:3072][:, s],
                                       w_in[:, 3072:4096][:, s], w_in[:, 4096 + h0:4096 + h0 + 2],
                                       w_in[:, 4104 + h0:4104 + h0 + 2]], axis=1)
                cw6 = np.concatenate([cw[:, 0:1024][:, s], cw[:, 1024:2048][:, s], cw[:, 2048:3072][:, s]], axis=1)
                prm = dn_params(cw6, np.asarray(dn_a_log[slot], f32)[h0:h0 + 2],
                                np.asarray(dn_dt_bias[slot], f32)[h0:h0 + 2], np.asarray(dn_norm_w[slot], f32))
                ims.append({"uT": np.ascontiguousarray(np.concatenate([zpad, u_full[b]], axis=1)),
                            "wcat": np.ascontiguousarray(wcat), "cst": cst, "prm": prm})
            r = _run(build_dn(), ims)
            o_full = [np.concatenate([r[4 * b + g]["oT"][:, 64:] for g in range(4)], axis=0) for b in range(B)]
            wo = np.asarray(dn_w_out[slot], f32)
        elif kind == 1:
            w_in = np.asarray(da_w_in[slot], f32)
            oh, md, pm = da_consts()
            rbt = np.asarray(rel_bias, f32)
            lamv = np.stack([np.asarray(v[slot], f32) for v in (da_lam_q1, da_lam_k1, da_lam_q2, da_lam_k2)], axis=0)
            lamv = np.ascontiguousarray(np.broadcast_to(lamv[None], (128, 4, 64)))
            sw = np.ascontiguousarray(np.asarray(da_subln_w[slot], f32)[:, None])
            ident = np.eye(128, dtype=f32)
            for c in range(NCORES):
                b, g = divmod(c, 4)
                h0 = 2 * g
                s = slice(h0 * 128, h0 * 128 + 256)
                wcat = np.concatenate([w_in[:, 0:1024][:, s], w_in[:, 1024:2048][:, s], w_in[:, 2048:3072][:, s]], axis=1)
                ims.append({"uT": np.ascontiguousarray(np.concatenate([zpad, u_full[b]], axis=1)),
                            "wcat": np.ascontiguousarray(wcat), "oh": oh, "md": md, "pm": pm,
                            "rb": np.ascontiguousarray(rbt[:, h0:h0 + 2]),
                            "rb15": np.ascontiguousarray(np.broadcast_to(rbt[15:16, h0:h0 + 2], (128, 2))),
                            "lamv": lamv, "sw": sw, "identf": ident})
            lam_init = 0.8 - 0.6 * math.exp(-0.3 * layer)
            r = _run(build_da(lam_init), ims)
            o_full = [np.concatenate([r[4 * b + g]["oT"][:, 64:] for g in range(4)], axis=0) for b in range(B)]
            wo = np.asarray(da_w_out[slot], f32)
        else:
            w_in = np.asarray(lru_w_in[slot], f32)
            cw = np.asarray(lru_conv_w[slot], f32)
            for c in range(NCORES):
                b, g = divmod(c, 4)
                s = slice(g * 256, (g + 1) * 256)
                prm = lru_params(cw[:, s], np.asarray(lru_conv_b[slot], f32)[s], np.asarray(lru_b_rgate[slot], f32)[s],
                                 np.asarray(lru_b_igate[slot], f32)[s], np.asarray(lru_lambda[slot], f32)[s])
                ims.append({"uT": u_full[b], "wg": np.ascontiguousarray(w_in[:, 0:1024][:, s]),
                            "wx": np.ascontiguousarray(w_in[:, 1024:2048][:, s]),
                            "wr": np.ascontiguousarray(np.asarray(lru_w_rgate[slot], f32)[g]),
                            "wi": np.ascontiguousarray(np.asarray(lru_w_igate[slot], f32)[g]), "prm": prm})
            r = _run(build_lru(), ims)
            o_full = [np.concatenate([r[4 * b + g]["yT"] for g in range(4)], axis=0) for b in range(B)]
            wo = np.asarray(lru_w_out[slot], f32)
        o_sh = _tok_shards(o_full)
        final = (layer == depth - 1)
        nxt = final_norm_w if final else norm_mix_w[layer + 1]
        nw = np.ascontiguousarray(np.concatenate([_nwcols(norm_mlp_w[layer]), _nwcols(nxt)], axis=1))
        w1 = np.asarray(mlp_w1[layer], f32)
        w2 = np.asarray(mlp_w2[layer], f32)
        r = _run(build_post(final), [{"hT": h_sh[c], "oT": o_sh[c], "wo": wo, "w1": w1, "w2": w2, "nw": nw}
                                     for c in range(NCORES)])
        h_sh = [r[c]["hout"] for c in range(NCORES)]
        u_sh = [r[c]["uout"] for c in range(NCORES)]
    out_full = _gather_tok(u_sh)
    out = np.stack([np.ascontiguousarray(out_full[b][:, PADF + NMETA:].T) for b in range(B)], axis=0)
    return out.astype(f32)


DEPTH = 4


def _phase(c, fn):
    base = c.es
    with ExitStack() as pes:
        c.es = pes
        fn()
        c.barrier()
    c.es = base


def build_fused():
    nc = bass.Bass("TRN2", target_bir_lowering=False)
    with ExitStack() as es:
        c = Ctx(nc, es)
        h0T = c.dram("h0T", [D, T], F32, "ExternalInput")
        outT = c.dram("outT", [D, T], F32, "ExternalOutput")
        HT = c.dram("HT", [D, T], F32, "Internal")
        UT = c.dram("UT", [D, TD], BF16, "Internal")
        OT = c.dram("OT", [D, TD], BF16, "Internal")
        nw0 = c.dram("nw0", [128, 8], F32, "ExternalInput")
        W = {}
        for l in range(DEPTH):
            kind = l % 3
            p = "L%d_" % l
            if kind == 0:
                W[p + "wcat"] = c.dram(p + "wcat", [4, D, 1028], F32, "ExternalInput")
                W[p + "prm"] = c.dram(p + "prm", [4, 128, 32], F32, "ExternalInput")
            elif kind == 1:
                W[p + "wcat"] = c.dram(p + "wcat", [4, D, 768], F32, "ExternalInput")
                W[p + "rb"] = c.dram(p + "rb", [4, 32, 2], F32, "ExternalInput")
                W[p + "rb15"] = c.dram(p + "rb15", [4, 128, 2], F32, "ExternalInput")
                W[p + "lamv"] = c.dram(p + "lamv", [128, 4, 64], F32, "ExternalInput")
                W[p + "sw"] = c.dram(p + "sw", [128, 1], F32, "ExternalInput")
            else:
                W[p + "wg"] = c.dram(p + "wg", [4, D, 256], F32, "ExternalInput")
                W[p + "wx"] = c.dram(p + "wx", [4, D, 256], F32, "ExternalInput")
                W[p + "wr"] = c.dram(p + "wr", [4, 256, 256], F32, "ExternalInput")
                W[p + "wi"] = c.dram(p + "wi", [4, 256, 256], F32, "ExternalInput")
                W[p + "prm"] = c.dram(p + "prm", [4, 128, 16], F32, "ExternalInput")
            W[p + "wo"] = c.dram(p + "wo", [D, D], F32, "ExternalInput")
            W[p + "w1"] = c.dram(p + "w1", [D, DFF], F32, "ExternalInput")
            W[p + "w2"] = c.dram(p + "w2", [DFF, D], F32, "ExternalInput")
            W[p + "nw"] = c.dram(p + "nw", [128, 16], F32, "ExternalInput")
        dn_cst = c.dram("dn_cst", [128, 6, 128], F32, "ExternalInput")
        da_oh = c.dram("da_oh", [32, 1152], F32, "ExternalInput")
        da_md = c.dram("da_md", [128, 5, 512], F32, "ExternalInput")
        da_pm = c.dram("da_pm", [128, 1], F32, "ExternalInput")
        identf = c.dram("identf", [128, 128], F32, "ExternalInput")
        make_consts(c)

        def sh(t, j, off=0):
            return t.view(t.t[:, off + j * NTC:off + (j + 1) * NTC])

        def zero_front():
            z = c.sb([128, 8, 64], BF16, "zfront")
            c.op("pool", lambda e: e.memset(z[:], 0.0), writes=[z])
            c.dma("sp", UT[:].rearrange("(k p) n -> p k n", p=128)[:, :, 0:64], z[:], reads=[z], writes=[UT])
        _phase(c, zero_front)
        for j in range(4):
            _phase(c, lambda j=j: emit_pre(c, {"hT": sh(h0T, j), "nw": nw0, "uout": sh(UT, j, 64)}))
        for l in range(DEPTH):
            kind = l % 3
            p = "L%d_" % l
            for g in range(4):
                rows = slice(g * 256, (g + 1) * 256)
                if kind == 0:
                    io = {"uT": UT, "wcat": W[p + "wcat"].view(W[p + "wcat"].t[g]), "cst": dn_cst,
                          "prm": W[p + "prm"].view(W[p + "prm"].t[g]), "oT": OT.view(OT.t[rows, :])}
                    _phase(c, lambda io=io: emit_dn(c, io))
                elif kind == 1:
                    io = {"uT": UT, "wcat": W[p + "wcat"].view(W[p + "wcat"].t[g]), "oh": da_oh, "md": da_md, "pm": da_pm,
                          "rb": W[p + "rb"].view(W[p + "rb"].t[g]), "rb15": W[p + "rb15"].view(W[p + "rb15"].t[g]),
                          "lamv": W[p + "lamv"], "sw": W[p + "sw"], "identf": identf, "oT": OT.view(OT.t[rows, :])}
                    lam_init = 0.8 - 0.6 * math.exp(-0.3 * l)
                    _phase(c, lambda io=io, lam_init=lam_init, tag="_%d_%d" % (l, g): emit_da(c, io, lam_init, tag))
                else:
                    io = {"uT": UT.view(UT.t[:, 64:64 + T]), "wg": W[p + "wg"].view(W[p + "wg"].t[g]),
                          "wx": W[p + "wx"].view(W[p + "wx"].t[g]), "wr": W[p + "wr"].view(W[p + "wr"].t[g]),
                          "wi": W[p + "wi"].view(W[p + "wi"].t[g]), "prm": W[p + "prm"].view(W[p + "prm"].t[g]),
                          "yT": OT.view(OT.t[rows, 64:64 + T])}
                    _phase(c, lambda io=io: emit_lru(c, io))
            final = (l == DEPTH - 1)
            for j in range(4):
                io = {"hT": sh(h0T if l == 0 else HT, j), "oT": sh(OT, j, 64), "wo": W[p + "wo"], "w1": W[p + "w1"],
                      "w2": W[p + "w2"], "nw": W[p + "nw"], "hout": sh(HT, j),
                      "uout": sh(outT, j) if final else sh(UT, j, 64)}
                _phase(c, lambda io=io, final=final: emit_post(c, io, final))
            if l == 1:
                c.switch_sem("pe")
        c.finish()
    return nc


def fused_inputs(b, x, meta_tokens, rel_bias, norm_mix_w, norm_mlp_w, final_norm_w,
                 dn_w_in, dn_conv_w, dn_a_log, dn_dt_bias, dn_norm_w, dn_w_out,
                 da_w_in, da_lam_q1, da_lam_k1, da_lam_q2, da_lam_k2, da_subln_w, da_w_out,
                 lru_w_in, lru_conv_w, lru_conv_b, lru_w_rgate, lru_b_rgate, lru_w_igate,
                 lru_b_igate, lru_lambda, lru_w_out, mlp_w1, mlp_w2, shared=None):
    f32 = np.float32
    im = {}
    seq = np.concatenate([np.zeros((PADF, D), f32), np.asarray(meta_tokens, f32), np.asarray(x[b], f32)], axis=0)
    im["h0T"] = np.ascontiguousarray(seq.T)
    if shared is not None:
        im.update(shared)
        return im
    sh = {}
    sh["nw0"] = _nwcols(norm_mix_w[0])
    oh, md, pm = da_consts()
    sh["dn_cst"] = dn_consts()
    sh["da_oh"], sh["da_md"], sh["da_pm"] = oh, md, pm
    sh["identf"] = np.eye(128, dtype=f32)
    rbt = np.asarray(rel_bias, f32)
    for l in range(DEPTH):
        kind, slot = l % 3, l // 3
        p = "L%d_" % l
        if kind == 0:
            w_in = np.asarray(dn_w_in[slot], f32)
            cw = np.asarray(dn_conv_w[slot], f32)
            wc, pr = [], []
            for g in range(4):
                h0 = 2 * g
                s = slice(h0 * 128, h0 * 128 + 256)
                wc.append(np.concatenate([w_in[:, 0:1024][:, s], w_in[:, 1024:2048][:, s], w_in[:, 2048:3072][:, s],
                                          w_in[:, 3072:4096][:, s], w_in[:, 4096 + h0:4096 + h0 + 2],
                                          w_in[:, 4104 + h0:4104 + h0 + 2]], axis=1))
                cw6 = np.concatenate([cw[:, 0:1024][:, s], cw[:, 1024:2048][:, s], cw[:, 2048:3072][:, s]], axis=1)
                pr.append(dn_params(cw6, np.asarray(dn_a_log[slot], f32)[h0:h0 + 2],
                                    np.asarray(dn_dt_bias[slot], f32)[h0:h0 + 2], np.asarray(dn_norm_w[slot], f32)))
            sh[p + "wcat"] = np.ascontiguousarray(np.stack(wc))
            sh[p + "prm"] = np.ascontiguousarray(np.stack(pr))
            wo = dn_w_out[slot]
        elif kind == 1:
            w_in = np.asarray(da_w_in[slot], f32)
            wc, rb, rb15 = [], [], []
            for g in range(4):
                h0 = 2 * g
                s = slice(h0 * 128, h0 * 128 + 256)
                wc.append(np.concatenate([w_in[:, 0:1024][:, s], w_in[:, 1024:2048][:, s], w_in[:, 2048:3072][:, s]], axis=1))
                rb.append(rbt[:, h0:h0 + 2])
                rb15.append(np.broadcast_to(rbt[15:16, h0:h0 + 2], (128, 2)))
            sh[p + "wcat"] = np.ascontiguousarray(np.stack(wc))
            sh[p + "rb"] = np.ascontiguousarray(np.stack(rb))
            sh[p + "rb15"] = np.ascontiguousarray(np.stack(rb15))
            lamv = np.stack([np.asarray(v[slot], f32) for v in (da_lam_q1, da_lam_k1, da_lam_q2, da_lam_k2)], axis=0)
            sh[p + "lamv"] = np.ascontiguousarray(np.broadcast_to(lamv[None], (128, 4, 64)))
            sh[p + "sw"] = np.ascontiguousarray(np.asarray(da_subln_w[slot], f32)[:, None])
            wo = da_w_out[slot]
        else:
            w_in = np.asarray(lru_w_in[slot], f32)
            cw = np.asarray(lru_conv_w[slot], f32)
            wg, wx, pr = [], [], []
            for g in range(4):
                s = slice(g * 256, (g + 1) * 256)
                wg.append(w_in[:, 0:1024][:, s])
                wx.append(w_in[:, 1024:2048][:, s])
                pr.append(lru_params(cw[:, s], np.asarray(lru_conv_b[slot], f32)[s], np.asarray(lru_b_rgate[slot], f32)[s],
                                     np.asarray(lru_b_igate[slot], f32)[s], np.asarray(lru_lambda[slot], f32)[s]))
            sh[p + "wg"] = np.ascontiguousarray(np.stack(wg))
            sh[p + "wx"] = np.ascontiguousarray(np.stack(wx))
            sh[p + "wr"] = np.ascontiguousarray(np.asarray(lru_w_rgate[slot], f32))
            sh[p + "wi"] = np.ascontiguousarray(np.asarray(lru_w_igate[slot], f32))
            sh[p + "prm"] = np.ascontiguousarray(np.stack(pr))
            wo = lru_w_out[slot]
        final = (l == DEPTH - 1)
        nxt = final_norm_w if final else norm_mix_w[l + 1]
        sh[p + "wo"] = np.ascontiguousarray(np.asarray(wo, f32))
        sh[p + "w1"] = np.ascontiguousarray(np.asarray(mlp_w1[l], f32))
        sh[p + "w2"] = np.ascontiguousarray(np.asarray(mlp_w2[l], f32))
        sh[p + "nw"] = np.ascontiguousarray(np.concatenate([_nwcols(norm_mlp_w[l]), _nwcols(nxt)], axis=1))
    im.update(sh)
    im["_shared"] = sh
    return im


def kernel(**inputs):
    x = inputs["x"]
    im0 = fused_inputs(0, **inputs)
    shared = im0.pop("_shared")
    im1 = fused_inputs(1, **inputs, shared=shared)
    nc = build_fused()
    res = run_bass_kernel_spmd(nc, [im0, im1], core_ids=[0, 1])
    out = np.stack([np.ascontiguousarray(res.results[b]["outT"][:, PADF + NMETA:].T) for b in range(B)], axis=0)
    return out.astype(np.float32)
```

```python
import math
from contextlib import ExitStack

import numpy as np
import ml_dtypes
import concourse.bass as bass
import concourse.mybir as mybir
from concourse.bass_utils import run_bass_kernel_spmd

F32 = mybir.dt.float32
BF16 = mybir.dt.bfloat16
AF = mybir.ActivationFunctionType
ALU = mybir.AluOpType
AX = mybir.AxisListType

D = 1024
B = 2
SEQ = 8192
NMETA = 16
PADF = 48
T = PADF + NMETA + SEQ
NTC = T // 4
EPS = 1e-6
DFF = 4096
NCORES = 8


class Tl:
    def __init__(self, t, name, st=None):
        self.t = t
        self.name = name
        self.st = st if st is not None else [None, {}]

    @property
    def w(self):
        return self.st[0]

    @w.setter
    def w(self, v):
        self.st[0] = v

    @property
    def r(self):
        return self.st[1]

    @r.setter
    def r(self, v):
        self.st[1] = v

    def view(self, ap):
        return Tl(ap, self.name, self.st)

    def __getitem__(self, idx):
        return self.t[idx]


class Ctx:
    NDMA = 8

    def __init__(self, nc, es):
        self.nc = nc
        self.es = es
        self.eng = {"pe": nc.tensor, "act": nc.scalar, "dve": nc.vector,
                    "pool": nc.gpsimd, "sp": nc.sync}
        self.sem = {k: es.enter_context(nc.semaphore("s_" + k)) for k in self.eng}
        self.cnt = {k: 0 for k in self.eng}
        self.seen = {k: {} for k in self.eng}
        self.dsem = {}
        self.dcnt = {}
        for q in ("sp", "pool"):
            self.dsem[q] = [es.enter_context(nc.semaphore("d_%s%d" % (q, i)))
                            for i in range(self.NDMA)]
            self.dcnt[q] = 0
        self.ntile = 0

    def sb(self, shape, dt=F32, name=None, es=None):
        self.ntile += 1
        name = "%s_%d" % (name or "t", self.ntile)
        t = (es or self.es).enter_context(self.nc.sbuf_tensor(name, list(shape), dt))
        return Tl(t, name)

    def ps(self, shape, dt=F32, name=None, es=None):
        self.ntile += 1
        name = "%s_%d" % (name or "p", self.ntile)
        t = (es or self.es).enter_context(self.nc.psum_tensor(name, list(shape), dt))
        return Tl(t, name)

    def dram(self, name, shape, dt, kind):
        t = self.nc.dram_tensor(name, list(shape), dt, kind=kind)
        return Tl(t.ap(), name)

    def _wait(self, e, sem, val):
        key = id(sem)
        if self.seen[e].get(key, 0) >= val:
            return
        self.eng[e].wait_ge(sem, val)
        self.seen[e][key] = val

    def _deps(self, e, reads, writes):
        deps = {}

        def add(d):
            if d is None:
                return
            s, v = d
            if deps.get(id(s), (None, 0))[1] < v:
                deps[id(s)] = (s, v)
        for t in reads:
            add(t.w)
        for t in writes:
            add(t.w)
            for s_v in t.r.values():
                add(s_v)
        for s, v in deps.values():
            if e == "pe" and s is self.sem["pe"]:
                continue
            self._wait(e, s, v)

    def _mark(self, token, reads, writes):
        s, v = token
        for t in reads:
            t.r[id(s)] = (s, v)
        for t in writes:
            t.w = (s, v)
            t.r = {}

    def op(self, e, fn, reads=(), writes=(), inc=True):
        self._deps(e, reads, writes)
        ins = fn(self.eng[e])
        if inc:
            self.cnt[e] += 1
            ins.then_inc(self.sem[e], 1)
            tok = (self.sem[e], self.cnt[e])
        else:
            tok = (self.sem[e], self.cnt[e] + 1)
        self._mark(tok, reads, writes)
        return ins

    def dma(self, q, out, in_, reads=(), writes=(), **kw):
        self._deps(q, reads, writes)
        i = self.dcnt[q]
        s = self.dsem[q][i % self.NDMA]
        prev = 16 * (i // self.NDMA)
        if prev:
            self._wait(q, s, prev)
        ins = self.eng[q].dma_start(out=out, in_=in_, **kw)
        ins.then_inc(s, 16)
        self.dcnt[q] += 1
        self._mark((s, prev + 16), reads, writes)

    def switch_sem(self, e):
        self.nsw = getattr(self, "nsw", 0) + 1
        self.sem[e] = self.es.enter_context(self.nc.semaphore("s_%s_%d" % (e, self.nsw)))
        self.cnt[e] = 0

    def barrier(self):
        for e in self.eng:
            for e2 in self.eng:
                if e2 != e and self.cnt[e2] > 0:
                    self._wait(e, self.sem[e2], self.cnt[e2])
            for q in self.dsem:
                n = self.dcnt[q]
                for j, s in enumerate(self.dsem[q]):
                    k = (n - j + self.NDMA - 1) // self.NDMA
                    if k > 0:
                        self._wait(e, s, 16 * k)

    def finish(self):
        for q in self.dsem:
            n = self.dcnt[q]
            for j, s in enumerate(self.dsem[q]):
                k = (n - j + self.NDMA - 1) // self.NDMA
                if k > 0:
                    self._wait("sp", s, 16 * k)


class Rot:
    def __init__(self, tiles):
        self.tiles = tiles
        self.i = 0

    def get(self):
        t = self.tiles[self.i % len(self.tiles)]
        self.i += 1
        return t


def rms_stats(c, ones, src_tiles_fn, nk, n, pstat, sqrot, rstd, scale_div):
    for k in range(nk):
        src, src_t = src_tiles_fn(k)
        sq = sqrot.get()
        c.op("act", lambda e: e.activation(out=sq[:, :n], in_=src, func=AF.Square),
             reads=[src_t], writes=[sq])
        c.op("pe", lambda e: e.matmul(pstat[:, :n], lhsT=ones[:], rhs=sq[:, :n],
                                      start=(k == 0), stop=(k == nk - 1)),
             reads=[ones, sq], writes=[pstat])
    c.op("act", lambda e: e.activation(out=rstd[:, :n], in_=pstat[:, :n], func=AF.Sqrt,
                                       bias=c.eps_t[:, 0:1], scale=1.0 / scale_div),
         reads=[pstat, c.eps_t], writes=[rstd])
    c.op("dve", lambda e: e.reciprocal(out=rstd[:, :n], in_=rstd[:, :n]),
         reads=[rstd], writes=[rstd])


def make_consts(c):
    c.eps_t = c.sb([128, 1], F32, "eps_t")
    c.op("pool", lambda e: e.memset(c.eps_t[:], EPS), writes=[c.eps_t])
    c.ones = c.sb([128, 128], F32, "ones")
    c.op("pool", lambda e: e.memset(c.ones[:], 1.0), writes=[c.ones])


GS = 344
NG = NTC // GS


def build_post(final):
    nc = bass.Bass("TRN2", target_bir_lowering=False)
    with ExitStack() as es:
        c = Ctx(nc, es)
        io = {}
        io["hT"] = c.dram("hT", [D, NTC], F32, "ExternalInput")
        io["oT"] = c.dram("oT", [D, NTC], BF16, "ExternalInput")
        io["wo"] = c.dram("wo", [D, D], F32, "ExternalInput")
        io["w1"] = c.dram("w1", [D, DFF], F32, "ExternalInput")
        io["w2"] = c.dram("w2", [DFF, D], F32, "ExternalInput")
        io["nw"] = c.dram("nw", [128, 16], F32, "ExternalInput")
        io["hout"] = c.dram("hout", [D, NTC], F32, "ExternalOutput")
        io["uout"] = c.dram("uout", [D, NTC], F32 if final else BF16, "ExternalOutput")
        make_consts(c)
        emit_post(c, io, final)
        c.finish()
    return nc


def emit_post(c, io, final):
    if True:
        hT, oT, wo, w1, w2, nw, hout, uout = (io[k] for k in ("hT", "oT", "wo", "w1", "w2", "nw", "hout", "uout"))
        nws = c.sb([128, 16], F32, "nws")
        c.dma("sp", nws[:], nw[:], writes=[nws])

        H = [c.sb([128, 8, GS], F32, "H%d" % g) for g in range(NG)]
        XB = [c.sb([128, 8, GS], BF16, "XB%d" % g) for g in range(NG)]
        wbuf = Rot([c.sb([128, 8192], BF16, "wb%d" % i) for i in range(2)])
        abuf = Rot([c.sb([128, 4, GS], BF16, "ab%d" % i) for i in range(2)])
        rbuf = Rot([c.sb([128, GS], F32, "rb%d" % i) for i in range(3)])
        sqb = Rot([c.sb([128, GS], F32, "sq%d" % i) for i in range(2)])
        rstd = c.sb([128, GS], F32, "rstd")
        uo = Rot([c.sb([128, 8, GS], F32 if final else BF16, "uo%d" % i) for i in range(2)])
        pa = Rot([c.ps([128, 512], F32, "pa%d" % i) for i in range(4)])
        py = Rot([c.ps([128, 512], F32, "py%d" % i) for i in range(3)])
        pstat = c.ps([128, 512], F32, "pstat")

        hT3 = hT[:].rearrange("(k p) n -> p k n", p=128)
        oT3 = oT[:].rearrange("(k p) n -> p k n", p=128)
        ho3 = hout[:].rearrange("(k p) n -> p k n", p=128)
        uo3 = uout[:].rearrange("(k p) n -> p k n", p=128)

        wob = wbuf.get()
        wo3 = wo[:].rearrange("(k p) n -> p k n", p=128)
        for k0 in range(0, 8, 2):
            c.dma("pool", wob[:, k0 * 1024:(k0 + 2) * 1024].rearrange("p (k n) -> p k n", k=2), wo3[:, k0:k0 + 2, :], writes=[wob])
        for g in range(NG):
            c.dma("sp", XB[g][:], oT3[:, :, g * GS:(g + 1) * GS], writes=[XB[g]])
            c.dma("sp", H[g][:], hT3[:, :, g * GS:(g + 1) * GS], writes=[H[g]])

        def load_eighth(e8):
            wb = wbuf.get()
            w13 = w1[:].rearrange("(k p) n -> p k n", p=128)
            for k0 in range(0, 8, 4):
                c.dma("pool", wb[:, k0 * 512:(k0 + 4) * 512].rearrange("p (k n) -> p k n", k=4),
                      w13[:, k0:k0 + 4, e8 * 512:(e8 + 1) * 512], writes=[wb])
            w23 = w2[:].rearrange("(f p) n -> p f n", p=128)
            for f0 in range(0, 4, 2):
                c.dma("pool", wb[:, 4096 + f0 * 1024:4096 + (f0 + 2) * 1024].rearrange("p (f n) -> p f n", f=2),
                      w23[:, e8 * 4 + f0:e8 * 4 + f0 + 2, :], writes=[wb])
            return wb

        def norm_to(g, col, dst, dst_is_f32):
            rms_stats(c, c.ones, lambda k: (H[g][:, k, :], H[g]), 8, GS, pstat, sqb, rstd, float(D))
            for m in range(8):
                c.op("dve", lambda e: e.scalar_tensor_tensor(
                    out=dst[:, m, :], in0=H[g][:, m, :], scalar=nws[:, col + m:col + m + 1],
                    in1=rstd[:], op0=ALU.mult, op1=ALU.mult),
                    reads=[H[g], nws, rstd], writes=[dst])

        wnext = load_eighth(0)
        for g in range(NG):
            for m in range(8):
                p = py.get()
                for k in range(8):
                    c.op("pe", lambda e: e.matmul(p[:, :GS], lhsT=wob[:, k * 1024 + m * 128:k * 1024 + (m + 1) * 128],
                                                  rhs=XB[g][:, k, :], start=(k == 0), stop=(k == 7)),
                         reads=[wob, XB[g]], writes=[p], inc=(k == 7))
                c.op("dve", lambda e: e.tensor_tensor(out=H[g][:, m, :], in0=p[:, :GS], in1=H[g][:, m, :], op=ALU.add),
                     reads=[p, H[g]], writes=[H[g]])
            norm_to(g, 0, XB[g], False)
        for e8 in range(8):
            wb = wnext
            if e8 + 1 < 8:
                wnext = load_eighth(e8 + 1)
            for g in range(NG):
                ab = abuf.get()
                for f in range(4):
                    p = pa.get()
                    for k in range(8):
                        c.op("pe", lambda e: e.matmul(p[:, :GS], lhsT=wb[:, k * 512 + f * 128:k * 512 + (f + 1) * 128],
                                                      rhs=XB[g][:, k, :], start=(k == 0), stop=(k == 7)),
                             reads=[wb, XB[g]], writes=[p], inc=(k == 7))
                    r = rbuf.get()
                    c.op("act", lambda e: e.activation(out=r[:], in_=p[:, :GS], func=AF.Relu),
                         reads=[p], writes=[r])
                    c.op("dve", lambda e: e.tensor_tensor(out=ab[:, f, :], in0=r[:], in1=r[:], op=ALU.mult),
                         reads=[r], writes=[ab])
                for m in range(8):
                    p = py.get()
                    for f in range(4):
                        c.op("pe", lambda e: e.matmul(p[:, :GS], lhsT=wb[:, 4096 + f * 1024 + m * 128:4096 + f * 1024 + (m + 1) * 128],
                                                      rhs=ab[:, f, :], start=(f == 0), stop=(f == 3)),
                             reads=[wb, ab], writes=[p], inc=(f == 3))
                    c.op("dve", lambda e: e.tensor_tensor(out=H[g][:, m, :], in0=p[:, :GS], in1=H[g][:, m, :], op=ALU.add),
                         reads=[p, H[g]], writes=[H[g]])
                if e8 == 7:
                    c.dma("sp", ho3[:, :, g * GS:(g + 1) * GS], H[g][:], reads=[H[g]], writes=[hout])
                    u = uo.get()
                    norm_to(g, 8, u, final)
                    c.dma("sp", uo3[:, :, g * GS:(g + 1) * GS], u[:], reads=[u], writes=[uout])


def load_cast(c, dst_t, dst_ap, src_ap):
    c.dma("pool", dst_ap, src_ap, writes=[dst_t])


LB = 342
LNB = (T - PADF) // LB


def build_lru():
    nc = bass.Bass("TRN2", target_bir_lowering=False)
    with ExitStack() as es:
        c = Ctx(nc, es)
        io = {}
        io["uT"] = c.dram("uT", [D, T], BF16, "ExternalInput")
        io["wg"] = c.dram("wg", [D, 256], F32, "ExternalInput")
        io["wx"] = c.dram("wx", [D, 256], F32, "ExternalInput")
        io["wr"] = c.dram("wr", [256, 256], F32, "ExternalInput")
        io["wi"] = c.dram("wi", [256, 256], F32, "ExternalInput")
        io["prm"] = c.dram("prm", [128, 16], F32, "ExternalInput")
        io["yT"] = c.dram("yT", [256, T], BF16, "ExternalOutput")
        make_consts(c)
        emit_lru(c, io)
        c.finish()
    return nc


def emit_lru(c, io):
    if True:
        uT, wg, wx, wr, wi, prm, yT = (io[k] for k in ("uT", "wg", "wx", "wr", "wi", "prm", "yT"))
        one_t = c.sb([128, 1], F32, "one_t")
        c.op("pool", lambda e: e.memset(one_t[:], 1.0), writes=[one_t])
        ps_ = c.sb([128, 16], F32, "prm_s")
        c.dma("sp", ps_[:], prm[:], writes=[ps_])
        wgb = c.sb([128, 8, 256], BF16, "wgb")
        wxb = c.sb([128, 8, 256], BF16, "wxb")
        wrb = c.sb([128, 2, 256], BF16, "wrb")
        wib = c.sb([128, 2, 256], BF16, "wib")
        for (dst, src, nk) in ((wgb, wg, 8), (wxb, wx, 8), (wrb, wr, 2), (wib, wi, 2)):
            load_cast(c, dst, dst[:], src[:].rearrange("(k p) n -> p k n", p=128))
        cch = c.sb([128, 2], F32, "cch")
        ee = c.sb([128, 2], F32, "ee")
        acc = c.sb([128, 2], F32, "lacc")
        lam_ap = lambda: ps_[:, 7:16:8]
        c.op("act", lambda e: e.activation(out=ee[:], in_=lam_ap(), func=AF.Exp, scale=-1.0),
             reads=[ps_], writes=[ee])
        c.op("dve", lambda e: e.tensor_scalar(out=acc[:], in0=ee[:], scalar1=-1.0 / 6, scalar2=1.0 / 5,
                                              op0=ALU.mult, op1=ALU.add), reads=[ee], writes=[acc])
        for coef in (1.0 / 4, 1.0 / 3, 1.0 / 2, 1.0):
            c.op("dve", lambda e: e.tensor_tensor(out=acc[:], in0=acc[:], in1=ee[:], op=ALU.mult),
                 reads=[acc, ee], writes=[acc])
            c.op("dve", lambda e: e.tensor_scalar(out=acc[:], in0=acc[:], scalar1=-1.0, scalar2=coef,
                                                  op0=ALU.mult, op1=ALU.add), reads=[acc], writes=[acc])
        c.op("dve", lambda e: e.tensor_tensor(out=acc[:], in0=acc[:], in1=ee[:], op=ALU.mult),
             reads=[acc, ee], writes=[acc])
        c.op("dve", lambda e: e.tensor_scalar(out=cch[:], in0=acc[:], scalar1=-8.0, scalar2=None,
                                              op0=ALU.mult), reads=[acc], writes=[cch])

        ub = Rot([c.sb([128, 8, LB], BF16, "ub%d" % i) for i in range(3)])
        xc = [c.sb([128, LB + 3], F32, "xc%d" % t) for t in range(2)]
        for t in range(2):
            c.op("pool", lambda e: e.memset(xc[t][:], 0.0), writes=[xc[t]])
        xr = Rot([c.sb([128, LB], F32, "xr%d" % i) for i in range(2)])
        xrb = Rot([c.sb([128, 2, LB], BF16, "xrb%d" % i) for i in range(2)])
        gt = Rot([c.sb([128, LB], F32, "gt%d" % i) for i in range(4)])
        xrs = Rot([c.sb([128, LB], F32, "xrs%d" % i) for i in range(4)])
        tmp = Rot([c.sb([128, LB], F32, "tmp%d" % i) for i in range(6)])
        hs = [Rot([c.sb([128, LB], F32, "hs%d_%d" % (t, i)) for i in range(2)]) for t in range(2)]
        yb = Rot([c.sb([128, 2, LB], BF16, "yb%d" % i) for i in range(2)])
        pp = Rot([c.ps([128, 512], F32, "pp%d" % i) for i in range(4)])
        pg_ = Rot([c.ps([128, 512], F32, "pq%d" % i) for i in range(4)])
        zz = c.sb([128, 2, PADF], BF16, "zz")
        c.op("pool", lambda e: e.memset(zz[:], 0.0), writes=[zz])
        y3 = yT[:].rearrange("(t p) n -> p t n", p=128)
        c.dma("sp", y3[:, :, 0:PADF], zz[:], reads=[zz], writes=[yT])
        u3 = uT[:].rearrange("(k p) n -> p k n", p=128)
        prev_hs = [None, None]
        P = lambda t, j: ps_[:, t * 8 + j:t * 8 + j + 1]
        n = LB
        for blk in range(LNB):
            t0 = PADF + blk * LB
            u = ub.get()
            c.dma("sp", u[:], u3[:, :, t0:t0 + n], writes=[u])
            gts, xrf = [], []
            xb_ = xrb.get()
            for t in range(2):
                pgt = pp.get()
                for k in range(8):
                    c.op("pe", lambda e: e.matmul(pgt[:, :n], lhsT=wgb[:, k, t * 128:(t + 1) * 128], rhs=u[:, k, :],
                                                  start=(k == 0), stop=(k == 7)), reads=[wgb, u], writes=[pgt], inc=(k == 7))
                pxt = pp.get()
                for k in range(8):
                    c.op("pe", lambda e: e.matmul(pxt[:, :n], lhsT=wxb[:, k, t * 128:(t + 1) * 128], rhs=u[:, k, :],
                                                  start=(k == 0), stop=(k == 7)), reads=[wxb, u], writes=[pxt], inc=(k == 7))
                s = tmp.get()
                c.op("act", lambda e: e.activation(out=s[:], in_=pgt[:, :n], func=AF.Square), reads=[pgt], writes=[s])
                c.op("dve", lambda e: e.tensor_scalar(out=s[:], in0=s[:], scalar1=0.044715, scalar2=1.0,
                                                      op0=ALU.mult, op1=ALU.add), reads=[s], writes=[s])
                c.op("dve", lambda e: e.tensor_tensor(out=s[:], in0=s[:], in1=pgt[:, :n], op=ALU.mult),
                     reads=[s, pgt], writes=[s])
                c.op("act", lambda e: e.activation(out=s[:], in_=s[:], func=AF.Sigmoid, scale=1.5957691216057308),
                     reads=[s], writes=[s])
                g_ = gt.get()
                c.op("dve", lambda e: e.tensor_tensor(out=g_[:], in0=s[:], in1=pgt[:, :n], op=ALU.mult),
                     reads=[s, pgt], writes=[g_])
                gts.append(g_)
                c.op("act", lambda e: e.activation(out=xc[t][:, 3:3 + n], in_=pxt[:, :n], func=AF.Copy),
                     reads=[pxt], writes=[xc[t]])
                x_ = xrs.get()
                c.op("dve", lambda e: e.tensor_scalar(out=x_[:], in0=xc[t][:, 0:n], scalar1=P(t, 0), scalar2=P(t, 4),
                                                      op0=ALU.mult, op1=ALU.add), reads=[xc[t], ps_], writes=[x_])
                for j in range(1, 4):
                    c.op("dve", lambda e: e.scalar_tensor_tensor(out=x_[:], in0=xc[t][:, j:j + n], scalar=P(t, j),
                                                                 in1=x_[:], op0=ALU.mult, op1=ALU.add),
                         reads=[xc[t], ps_, x_], writes=[x_])
                c.op("pool", lambda e: e.tensor_copy(out=xc[t][:, 0:3], in_=xc[t][:, n:n + 3]),
                     reads=[xc[t]], writes=[xc[t]])
                c.op("pool", lambda e: e.tensor_copy(out=xb_[:, t, :], in_=x_[:]), reads=[x_], writes=[xb_])
                xrf.append(x_)
            y_ = yb.get()
            for t in range(2):
                pr = pg_.get()
                for k in range(2):
                    c.op("pe", lambda e: e.matmul(pr[:, :n], lhsT=wrb[:, k, t * 128:(t + 1) * 128], rhs=xb_[:, k, :],
                                                  start=(k == 0), stop=(k == 1)), reads=[wrb, xb_], writes=[pr], inc=(k == 1))
                pi_ = pg_.get()
                for k in range(2):
                    c.op("pe", lambda e: e.matmul(pi_[:, :n], lhsT=wib[:, k, t * 128:(t + 1) * 128], rhs=xb_[:, k, :],
                                                  start=(k == 0), stop=(k == 1)), reads=[wib, xb_], writes=[pi_], inc=(k == 1))
                a_ = tmp.get()
                c.op("act", lambda e: e.activation(out=a_[:], in_=pr[:, :n], func=AF.Sigmoid, bias=P(t, 5)),
                     reads=[pr, ps_], writes=[a_])
                c.op("act", lambda e: e.activation(out=a_[:], in_=a_[:], func=AF.Exp, scale=cch[:, t:t + 1]),
                     reads=[a_, cch], writes=[a_])
                i_ = tmp.get()
                c.op("act", lambda e: e.activation(out=i_[:], in_=pi_[:, :n], func=AF.Sigmoid, bias=P(t, 6)),
                     reads=[pi_, ps_], writes=[i_])
                m_ = tmp.get()
                c.op("dve", lambda e: e.tensor_tensor(out=m_[:], in0=a_[:], in1=a_[:], op=ALU.mult), reads=[a_], writes=[m_])
                c.op("act", lambda e: e.activation(out=m_[:], in_=m_[:], func=AF.Sqrt, bias=one_t[:, 0:1], scale=-1.0),
                     reads=[m_, one_t], writes=[m_])
                c.op("dve", lambda e: e.tensor_tensor(out=i_[:], in0=i_[:], in1=xrf[t][:], op=ALU.mult),
                     reads=[i_, xrf[t]], writes=[i_])
                c.op("dve", lambda e: e.tensor_tensor(out=i_[:], in0=i_[:], in1=m_[:], op=ALU.mult),
                     reads=[i_, m_], writes=[i_])
                h_ = hs[t].get()
                if prev_hs[t] is None:
                    c.op("dve", lambda e: e.tensor_tensor_scan(out=h_[:], data0=a_[:], data1=i_[:], initial=0.0,
                                                               op0=ALU.mult, op1=ALU.add), reads=[a_, i_], writes=[h_])
                else:
                    ph = prev_hs[t]
                    c.op("dve", lambda e: e.tensor_tensor_scan(out=h_[:], data0=a_[:], data1=i_[:], initial=ph[:, n - 1:n],
                                                               op0=ALU.mult, op1=ALU.add), reads=[a_, i_, ph], writes=[h_])
                prev_hs[t] = h_
                c.op("pool", lambda e: e.tensor_tensor(out=y_[:, t, :], in0=h_[:], in1=gts[t][:], op=ALU.mult),
                     reads=[h_, gts[t]], writes=[y_])
            c.dma("sp", y3[:, :, t0:t0 + n], y_[:], reads=[y_], writes=[yT])


def lru_params(cw, cb, br, bi, lam):
    prm = np.zeros((128, 2, 8), np.float32)
    for t in range(2):
        sl = slice(t * 128, (t + 1) * 128)
        prm[:, t, 0:4] = cw[:, sl].T
        prm[:, t, 4] = cb[sl]
        prm[:, t, 5] = br[sl]
        prm[:, t, 6] = bi[sl]
        prm[:, t, 7] = lam[sl]
    return np.ascontiguousarray(prm.reshape(128, 16))


TD = T + 64
DN_BLOCKS = [(i * 512, 4) for i in range(16)] + [(8192, 1)]


def dn_consts():
    i = np.arange(128)
    same = (i[:, None] // 64) == (i[None, :] // 64)
    cst = np.zeros((128, 6, 128), np.float32)
    cst[:, 0, :] = np.eye(128)
    cst[:, 1, :] = (same & (i[:, None] <= i[None, :]))
    cst[:, 2, :] = (same & (i[:, None] > i[None, :]))
    cst[:, 3, :] = np.where(same & (i[:, None] >= i[None, :]), 0.0, -30000.0)
    cst[:, 4, :] = (same & (i[:, None] > i[None, :]))
    cst[:, 5, :] = -1.0
    return cst


def build_dn(blocks=None, TD=TD, dbg=99):
    blocks = blocks or DN_BLOCKS
    nc = bass.Bass("TRN2", target_bir_lowering=False)
    with ExitStack() as es:
        c = Ctx(nc, es)
        io = {}
        io["uT"] = c.dram("uT", [D, TD], BF16, "ExternalInput")
        io["wcat"] = c.dram("wcat", [D, 1028], F32, "ExternalInput")
        io["cst"] = c.dram("cst", [128, 6, 128], F32, "ExternalInput")
        io["prm"] = c.dram("prm", [128, 32], F32, "ExternalInput")
        io["oT"] = c.dram("oT", [256, TD], BF16, "ExternalOutput")
        make_consts(c)
        emit_dn(c, io, blocks, dbg)
        c.finish()
    return nc


def emit_dn(c, io, blocks=None, dbg=99):
    blocks = blocks or DN_BLOCKS
    if True:
        uT, wcat, cstd, prm, oT = (io[k] for k in ("uT", "wcat", "cst", "prm", "oT"))
        one_t = c.sb([128, 1], F32, "one_t")
        c.op("pool", lambda e: e.memset(one_t[:], 1.0), writes=[one_t])
        cst = c.sb([128, 6, 128], F32, "cst_s")
        c.dma("sp", cst[:], cstd[:], writes=[cst])
        ident = cst[:, 0, :]
        U2 = cst[:, 1, :]
        R2 = cst[:, 2, :]
        negmask = cst[:, 3, :]
        smask = cst[:, 4, :]
        negones = cst[:, 5, :]
        ps_ = c.sb([128, 32], F32, "prm_s")
        c.dma("sp", ps_[:], prm[:], writes=[ps_])
        nea = c.sb([128, 2], F32, "nea")
        c.op("act", lambda e: e.activation(out=nea[:], in_=ps_[:, 24:26], func=AF.Exp), reads=[ps_], writes=[nea])
        c.op("dve", lambda e: e.tensor_scalar(out=nea[:], in0=nea[:], scalar1=-1.0, scalar2=None, op0=ALU.mult),
             reads=[nea], writes=[nea])
        wb = c.sb([128, 8, 1028], BF16, "wb")
        w3 = wcat[:].rearrange("(k p) n -> p k n", p=128)
        for k0 in range(0, 8, 2):
            load_cast(c, wb, wb[:, k0:k0 + 2, :], w3[:, k0:k0 + 2, :])

        ub = Rot([c.sb([128, 8, 512], BF16, "ub%d" % i) for i in range(2)])
        xc = [c.sb([128, 515], F32, "xc%d" % j) for j in range(6)]
        for j in range(6):
            c.op("pool", lambda e: e.memset(xc[j][:], 0.0), writes=[xc[j]])
        ft = [Rot([c.sb([128, 512], F32, "ft%d_%d" % (j, i)) for i in range(2)]) for j in range(6)]
        szr = [Rot([c.sb([128, 512], F32, "sz%d_%d" % (h, i)) for i in range(2)]) for h in range(2)]
        sq4 = [c.sb([128, 512], F32, "sq%d" % i) for i in range(4)]
        rstd4 = [c.sb([128, 512], F32, "rstd%d" % i) for i in range(4)]
        obr = Rot([c.sb([128, 2, 512], BF16, "ob%d" % i) for i in range(2)])
        S = [c.sb([128, 128], F32, "S%d" % h) for h in range(2)]
        for h in range(2):
            c.op("pool", lambda e: e.memset(S[h][:], 0.0), writes=[S[h]])
        bt = Rot([c.sb([128, 4, 2], F32, "bt%d" % i) for i in range(2)])
        gg = Rot([c.sb([128, 4, 2], F32, "gg%d" % i) for i in range(2)])
        gtmp = Rot([c.sb([128, 4, 2], F32, "gtmp%d" % i) for i in range(6)])
        sm2 = Rot([c.sb([128, 2], F32, "sm2_%d" % i) for i in range(24)])
        sm1 = Rot([c.sb([128, 1], F32, "sm1_%d" % i) for i in range(8)])
        NSQ = 96
        sq128 = Rot([c.sb([128, 128], F32, "m%d" % i) for i in range(NSQ)])
        LL = {nm: [c.sb([128, 128], F32, "%s%d" % (nm, i)) for i in range(8)]
              for nm in ("dec", "egrow", "kd", "attT", "qg", "usb", "wTs", "otok")}
        pbig = Rot([c.ps([128, 512], F32, "pbig%d" % i) for i in range(2)])
        banks = [c.ps([128, 512], F32, "pbank%d" % i) for i in range(6)]
        psm = Rot([banks[i].view(banks[i].t[:, 0:128]) for i in range(6)])

        u3 = uT[:].rearrange("(k p) n -> p k n", p=128)
        o3 = oT[:].rearrange("(h p) n -> p h n", p=128)

        def mm(out_t, out_ap, lhsT, rhs, reads, start=True, stop=True):
            c.op("pe", lambda e: e.matmul(out_ap, lhsT=lhsT, rhs=rhs, start=start, stop=stop),
                 reads=reads, writes=[out_t], inc=stop)

        def evac(eng, dst_t, dst_ap, src_t, src_ap, scale=None, extra=()):
            if eng == "act":
                if scale is None:
                    c.op("act", lambda e: e.activation(out=dst_ap, in_=src_ap, func=AF.Copy),
                         reads=[src_t], writes=[dst_t])
                else:
                    c.op("act", lambda e: e.activation(out=dst_ap, in_=src_ap, func=AF.Copy, scale=scale),
                         reads=[src_t] + list(extra), writes=[dst_t])
            else:
                c.op("dve", lambda e: e.tensor_copy(out=dst_ap, in_=src_ap), reads=[src_t], writes=[dst_t])

        for (b0, ntl) in blocks:
            n = ntl * 128
            u = ub.get()
            c.dma("sp", u[:, :, :n], u3[:, :, b0:b0 + n], writes=[u])
            pba_t = psm.get()
            for tt in range(ntl):
                for k in range(8):
                    mm(pba_t, pba_t[:, tt * 4:(tt + 1) * 4], u[:, k, tt * 128:(tt + 1) * 128], wb[:, k, 1024:1028],
                       [u, wb], start=(k == 0), stop=(k == 7))
            pba = pba_t[:, 0:4 * ntl].rearrange("p (t f) -> p t f", f=4)
            btb = bt.get()
            ggb = gg.get()
            c.op("act", lambda e: e.activation(out=btb[:, :ntl, :], in_=pba[:, :, 0:2], func=AF.Sigmoid),
                 reads=[pba_t], writes=[btb])
            x_ = gtmp.get(); ax = gtmp.get(); rl = gtmp.get()
            for h in range(2):
                c.op("dve", lambda e: e.tensor_scalar(out=x_[:, :ntl, h:h + 1], in0=pba[:, :, 2 + h:3 + h],
                                                      scalar1=ps_[:, 26 + h:27 + h], scalar2=None, op0=ALU.add),
                     reads=[pba_t, ps_], writes=[x_])
            c.op("act", lambda e: e.activation(out=ax[:, :ntl, :], in_=x_[:, :ntl, :], func=AF.Abs),
                 reads=[x_], writes=[ax])
            c.op("act", lambda e: e.activation(out=ax[:, :ntl, :], in_=ax[:, :ntl, :], func=AF.Exp, scale=-1.0),
                 reads=[ax], writes=[ax])
            c.op("act", lambda e: e.activation(out=ax[:, :ntl, :], in_=ax[:, :ntl, :], func=AF.Ln, bias=one_t[:, 0:1]),
                 reads=[ax, one_t], writes=[ax])
            c.op("dve", lambda e: e.tensor_scalar(out=rl[:, :ntl, :], in0=x_[:, :ntl, :], scalar1=0.0, scalar2=None,
                                                  op0=ALU.max), reads=[x_], writes=[rl])
            c.op("dve", lambda e: e.tensor_tensor(out=rl[:, :ntl, :], in0=rl[:, :ntl, :], in1=ax[:, :ntl, :], op=ALU.add),
                 reads=[rl, ax], writes=[rl])
            for h in range(2):
                c.op("dve", lambda e: e.tensor_scalar(out=ggb[:, :ntl, h:h + 1], in0=rl[:, :ntl, h:h + 1],
                                                      scalar1=nea[:, h:h + 1], scalar2=None, op0=ALU.mult),
                     reads=[rl, nea], writes=[ggb])
            F = []
            for j in range(8):
                p = pbig.get()
                for k in range(8):
                    mm(p, p[:, :n], wb[:, k, j * 128:(j + 1) * 128], u[:, k, :n], [wb, u], start=(k == 0), stop=(k == 7))
                if j < 6:
                    c.op("act", lambda e: e.activation(out=xc[j][:, 3:3 + n], in_=p[:, :n], func=AF.Copy),
                         reads=[p], writes=[xc[j]])
                    f = ft[j].get()
                    c.op("dve", lambda e: e.tensor_scalar(out=f[:, :n], in0=xc[j][:, 0:n], scalar1=ps_[:, j * 4:j * 4 + 1],
                                                          scalar2=None, op0=ALU.mult), reads=[xc[j], ps_], writes=[f])
                    for tp in range(1, 4):
                        c.op("dve", lambda e: e.scalar_tensor_tensor(out=f[:, :n], in0=xc[j][:, tp:tp + n],
                                                                     scalar=ps_[:, j * 4 + tp:j * 4 + tp + 1], in1=f[:, :n],
                                                                     op0=ALU.mult, op1=ALU.add),
                             reads=[xc[j], ps_, f], writes=[f])
                    c.op("pool", lambda e: e.tensor_copy(out=xc[j][:, 0:3], in_=xc[j][:, n:n + 3]),
                         reads=[xc[j]], writes=[xc[j]])
                    c.op("act", lambda e: e.activation(out=f[:, :n], in_=f[:, :n], func=AF.Silu), reads=[f], writes=[f])
                    F.append(f)
                else:
                    z = szr[j - 6].get()
                    c.op("act", lambda e: e.activation(out=z[:, :n], in_=p[:, :n], func=AF.Silu), reads=[p], writes=[z])
                    F.append(z)
            pst4 = [pbig.get(), pbig.get(), banks[4], banks[5]]
            for j in range(4):
                c.op("act", lambda e: e.activation(out=sq4[j][:, :n], in_=F[j][:, :n], func=AF.Square),
                     reads=[F[j]], writes=[sq4[j]])
            for j in range(4):
                c.op("pe", lambda e: e.matmul(pst4[j][:, :n], lhsT=c.ones[:], rhs=sq4[j][:, :n], start=True, stop=True),
                     reads=[c.ones, sq4[j]], writes=[pst4[j]])
            for j in range(4):
                c.op("act", lambda e: e.activation(out=rstd4[j][:, :n], in_=pst4[j][:, :n], func=AF.Sqrt,
                                                   bias=c.eps_t[:, 0:1], scale=1.0),
                     reads=[pst4[j], c.eps_t], writes=[rstd4[j]])
            for j in range(4):
                c.op("dve", lambda e: e.reciprocal(out=rstd4[j][:, :n], in_=rstd4[j][:, :n]), reads=[rstd4[j]], writes=[rstd4[j]])
            for j in range(4):
                if j < 2:
                    c.op("dve", lambda e: e.scalar_tensor_tensor(out=F[j][:, :n], in0=F[j][:, :n], scalar=128.0 ** -0.5,
                                                                 in1=rstd4[j][:, :n], op0=ALU.mult, op1=ALU.mult),
                         reads=[F[j], rstd4[j]], writes=[F[j]])
                else:
                    c.op("dve", lambda e: e.tensor_tensor(out=F[j][:, :n], in0=F[j][:, :n], in1=rstd4[j][:, :n], op=ALU.mult),
                         reads=[F[j], rstd4[j]], writes=[F[j]])
            ob = obr.get()
            TT = list(range(ntl))
            CH = [(tt, h) for tt in TT for h in range(2)]
            ci = {ch: i for i, ch in enumerate(CH)}
            csl = {tt: slice(tt * 128, (tt + 1) * 128) for tt in TT}
            egc, ed, be = {}, {}, {}
            for tt in TT:
                pg1 = psm.get(); pg2 = psm.get()
                mm(pg1, pg1[:, 0:2], U2, ggb[:, tt, :], [cst, ggb])
                mm(pg2, pg2[:, 0:2], R2, ggb[:, tt, :], [cst, ggb])
                egc[tt] = sm2.get(); ed[tt] = sm2.get(); be[tt] = sm2.get()
                c.op("act", lambda e: e.activation(out=egc[tt][:], in_=pg1[:, 0:2], func=AF.Exp), reads=[pg1], writes=[egc[tt]])
                c.op("act", lambda e: e.activation(out=ed[tt][:], in_=pg2[:, 0:2], func=AF.Exp), reads=[pg2], writes=[ed[tt]])
                c.op("dve", lambda e: e.tensor_tensor(out=be[tt][:], in0=egc[tt][:], in1=btb[:, tt, :], op=ALU.mult),
                     reads=[egc[tt], btb], writes=[be[tt]])
            Ug, dec, decs, egrow, P, Q, Y = {}, {}, {}, {}, {}, {}, {}
            for ch in CH:
                tt, h = ch
                Ug[ch] = sq128.get()
                c.op("dve", lambda e: e.tensor_scalar(out=Ug[ch][:], in0=U2, scalar1=ggb[:, tt, h:h + 1], scalar2=None,
                                                      op0=ALU.mult), reads=[cst, ggb], writes=[Ug[ch]])
            for ch in CH:
                tt, h = ch
                pd = psm.get()
                mm(pd, pd[:], Ug[ch][:], c.ones[:], [Ug[ch], c.ones], start=True, stop=False)
                mm(pd, pd[:], negones, Ug[ch][:], [cst, Ug[ch]], start=False, stop=True)
                dec[ch] = LL["dec"][ci[ch]]
                c.op("dve", lambda e: e.tensor_tensor(out=dec[ch][:], in0=pd[:], in1=negmask, op=ALU.add),
                     reads=[pd, cst], writes=[dec[ch]])
                c.op("act", lambda e: e.activation(out=dec[ch][:], in_=dec[ch][:], func=AF.Exp),
                     reads=[dec[ch]], writes=[dec[ch]])
                decs[ch] = sq128.get()
                c.op("pool", lambda e: e.tensor_tensor(out=decs[ch][:], in0=dec[ch][:], in1=smask, op=ALU.mult),
                     reads=[dec[ch], cst], writes=[decs[ch]])
                pe_ = psm.get()
                mm(pe_, pe_[:], c.ones[:], Ug[ch][:], [c.ones, Ug[ch]])
                egrow[ch] = LL["egrow"][ci[ch]]
                c.op("act", lambda e: e.activation(out=egrow[ch][:], in_=pe_[:], func=AF.Exp),
                     reads=[pe_], writes=[egrow[ch]])
            for ch in CH:
                tt, h = ch
                kT = F[2 + h][:, csl[tt]]
                pk = psm.get()
                mm(pk, pk[:], kT, kT, [F[2 + h]])
                P[ch] = sq128.get()
                c.op("dve", lambda e: e.scalar_tensor_tensor(out=P[ch][:], in0=pk[:], scalar=btb[:, tt, h:h + 1],
                                                             in1=decs[ch][:], op0=ALU.mult, op1=ALU.mult),
                     reads=[pk, btb, decs[ch]], writes=[P[ch]])
            for ch in CH:
                pb = psm.get()
                mm(pb, pb[:], P[ch][:], ident, [P[ch], cst])
                Q[ch] = sq128.get(); Y[ch] = sq128.get()
                evac("act", Q[ch], Q[ch][:], pb, pb[:])
                c.op("pool", lambda e: e.tensor_tensor(out=Y[ch][:], in0=ident, in1=Q[ch][:], op=ALU.subtract),
                     reads=[Q[ch], cst], writes=[Y[ch]])
            for s in range(5):
                Pn, Qn = {}, {}
                for ch in CH:
                    pp_ = psm.get()
                    mm(pp_, pp_[:], Q[ch][:], P[ch][:], [Q[ch], P[ch]])
                    Pn[ch] = sq128.get()
                    evac("act", Pn[ch], Pn[ch][:], pp_, pp_[:])
                    if s < 4:
                        pq = psm.get()
                        mm(pq, pq[:], P[ch][:], Q[ch][:], [P[ch], Q[ch]])
                        Qn[ch] = sq128.get()
                        evac("dve", Qn[ch], Qn[ch][:], pq, pq[:])
                for ch in CH:
                    py_ = psm.get()
                    mm(py_, py_[:], Pn[ch][:], Y[ch][:], [Pn[ch], Y[ch]])
                    Yn = sq128.get()
                    c.op("dve", lambda e: e.tensor_tensor(out=Yn[:], in0=py_[:], in1=Y[ch][:], op=ALU.add),
                         reads=[py_, Y[ch]], writes=[Yn])
                    Y[ch] = Yn
                    P[ch] = Pn[ch]
                    if s < 4:
                        Q[ch] = Qn[ch]
            kbg, kd, vb, usb, wTs, attT, qg = {}, {}, {}, {}, {}, {}, {}
            for ch in CH:
                tt, h = ch
                kT = F[2 + h][:, csl[tt]]; vT = F[4 + h][:, csl[tt]]; qT = F[h][:, csl[tt]]
                pkt = psm.get()
                mm(pkt, pkt[:], kT, ident, [F[2 + h], cst])
                kbg[ch] = sq128.get(); kd[ch] = LL["kd"][ci[ch]]
                evac("act", kbg[ch], kbg[ch][:], pkt, pkt[:], scale=be[tt][:, h:h + 1], extra=[be[tt]])
                evac("act", kd[ch], kd[ch][:], pkt, pkt[:], scale=ed[tt][:, h:h + 1], extra=[ed[tt]])
                pvt = psm.get()
                mm(pvt, pvt[:], vT, ident, [F[4 + h], cst])
                vb[ch] = sq128.get()
                evac("act", vb[ch], vb[ch][:], pvt, pvt[:], scale=btb[:, tt, h:h + 1], extra=[btb])
                pqk = psm.get()
                mm(pqk, pqk[:], qT, kT, [F[h], F[2 + h]])
                att = sq128.get()
                c.op("dve", lambda e: e.tensor_tensor(out=att[:], in0=pqk[:], in1=dec[ch][:], op=ALU.mult),
                     reads=[pqk, dec[ch]], writes=[att])
                pat = psm.get()
                mm(pat, pat[:], att[:], ident, [att, cst])
                attT[ch] = LL["attT"][ci[ch]]
                evac("dve", attT[ch], attT[ch][:], pat, pat[:])
                qg[ch] = LL["qg"][ci[ch]]
                c.op("pool", lambda e: e.tensor_tensor(out=qg[ch][:], in0=qT, in1=egrow[ch][:], op=ALU.mult),
                     reads=[F[h], egrow[ch]], writes=[qg[ch]])
            for ch in CH:
                pu = psm.get()
                mm(pu, pu[:], Y[ch][:], vb[ch][:], [Y[ch], vb[ch]])
                usb[ch] = LL["usb"][ci[ch]]
                evac("act", usb[ch], usb[ch][:], pu, pu[:])
                pw = psm.get()
                mm(pw, pw[:], kbg[ch][:], Y[ch][:], [kbg[ch], Y[ch]])
                wTs[ch] = LL["wTs"][ci[ch]]
                evac("dve", wTs[ch], wTs[ch][:], pw, pw[:])
            otok = {ch: LL["otok"][ci[ch]] for ch in CH}
            for tt in TT:
                vnew = {h: sq128.get() for h in range(2)}
                for half in range(2):
                    r = slice(half * 64, half * 64 + 64)
                    for h in range(2):
                        ch = (tt, h)
                        pws = psm.get()
                        mm(pws, pws[:], wTs[ch][:], S[h][:], [wTs[ch], S[h]])
                        c.op("dve", lambda e: e.tensor_tensor(out=vnew[h][r, :], in0=usb[ch][r, :], in1=pws[r, :], op=ALU.subtract),
                             reads=[usb[ch], pws], writes=[vnew[h]])
                    for h in range(2):
                        ch = (tt, h)
                        po = psm.get()
                        mm(po, po[:], qg[ch][:], S[h][:], [qg[ch], S[h]], start=True, stop=False)
                        mm(po, po[:], attT[ch][r, :], vnew[h][r, :], [attT[ch], vnew[h]], start=False, stop=True)
                        evac("act", otok[ch], otok[ch][r, :], po, po[r, :])
                        pst = psm.get()
                        mm(pst, pst[:], kd[ch][r, :], vnew[h][r, :], [kd[ch], vnew[h]])
                        c.op("dve", lambda e: e.scalar_tensor_tensor(out=S[h][:], in0=S[h][:],
                                                                     scalar=egrow[ch][:, half * 64 + 63:half * 64 + 64],
                                                                     in1=pst[:], op0=ALU.mult, op1=ALU.add),
                             reads=[S[h], egrow[ch], pst], writes=[S[h]])
            ssd, ond, potd = {}, {}, {}
            for ch in CH:
                junk = sq128.get()
                ssd[ch] = sm1.get()
                c.op("act", lambda e: e.activation(out=junk[:], in_=otok[ch][:], func=AF.Square, accum_out=ssd[ch][:]),
                     reads=[otok[ch]], writes=[junk, ssd[ch]])
            for ch in CH:
                c.op("act", lambda e: e.activation(out=ssd[ch][:], in_=ssd[ch][:], func=AF.Sqrt, bias=c.eps_t[:, 0:1], scale=1.0 / 128),
                     reads=[ssd[ch], c.eps_t], writes=[ssd[ch]])
            for ch in CH:
                c.op("dve", lambda e: e.reciprocal(out=ssd[ch][:], in_=ssd[ch][:]), reads=[ssd[ch]], writes=[ssd[ch]])
            for ch in CH:
                ond[ch] = sq128.get()
                c.op("dve", lambda e: e.tensor_scalar(out=ond[ch][:], in0=otok[ch][:], scalar1=ssd[ch][:, 0:1], scalar2=None, op0=ALU.mult),
                     reads=[otok[ch], ssd[ch]], writes=[ond[ch]])
            for ch in CH:
                tt, h = ch
                pot = psm.get()
                mm(pot, pot[:], ond[ch][:], ident, [ond[ch], cst])
                c.op("dve", lambda e: e.scalar_tensor_tensor(out=ob[:, h, csl[tt]], in0=pot[:], scalar=ps_[:, 28:29],
                                                             in1=F[6 + h][:, csl[tt]], op0=ALU.mult, op1=ALU.mult),
                     reads=[pot, ps_, F[6 + h]], writes=[ob])
            c.dma("sp", o3[:, :, b0:b0 + n], ob[:, :, :n], reads=[ob], writes=[oT])


def dn_params(conv_w6, a_log2, dt_bias2, norm_w):
    prm = np.zeros((128, 32), np.float32)
    for j in range(6):
        prm[:, j * 4:(j + 1) * 4] = conv_w6[:, j * 128:(j + 1) * 128].T
    prm[:, 24:26] = a_log2[None, :]
    prm[:, 26:28] = dt_bias2[None, :]
    prm[:, 28] = norm_w
    return prm


DA_DS = (-128, 0, 128, 256, 384)
DA_NEGM = -240000.0
LAMBDA_INIT_L1 = 0.8 - 0.6 * math.exp(-0.3 * 1)


def _t5_bucket_np(rel):
    import jax
    import jax.numpy as jnp
    with jax.default_device(jax.devices("cpu")[0]):
        rel = jnp.asarray(rel, jnp.int32)
        nb = 16
        ret = jnp.where(rel > 0, nb, 0)
        n = jnp.abs(rel)
        max_exact = nb // 2
        nf = jnp.maximum(n, 1).astype(jnp.float32)
        large = max_exact + (jnp.log(nf / max_exact) / math.log(128 / max_exact)
                             * (nb - max_exact)).astype(jnp.int32)
        large = jnp.minimum(large, nb - 1)
        return np.asarray(ret + jnp.where(n < max_exact, n, large))


def da_consts():
    r = np.arange(-639, 513)
    bk = _t5_bucket_np(r)
    oh = np.zeros((32, 1152), np.float32)
    oh[bk, np.arange(1152)] = 1.0
    oh[15, :] -= 1.0
    kk = np.arange(128)[:, None]
    qq = np.arange(512)[None, :]
    md = np.zeros((128, 5, 512), np.float32)
    for i, d in enumerate(DA_DS):
        allowed = ((d + kk) // 64) <= (qq // 64)
        md[:, i, :] = np.where(allowed, 0.0, DA_NEGM)
    pm = np.zeros((128, 1), np.float32)
    pm[:112] = -30000.0
    return oh, md, pm


def build_da(lambda_init=LAMBDA_INIT_L1):
    NKT = TD // 128
    qtiles = [(0, 128)] + [(128 + 512 * i, 512) for i in range(16)]
    nc = bass.Bass("TRN2", target_bir_lowering=False)
    with ExitStack() as es:
        c = Ctx(nc, es)
        io = {}
        io["uT"] = c.dram("uT", [D, TD], BF16, "ExternalInput")
        io["wcat"] = c.dram("wcat", [D, 768], F32, "ExternalInput")
        io["oh"] = c.dram("oh", [32, 1152], F32, "ExternalInput")
        io["md"] = c.dram("md", [128, 5, 512], F32, "ExternalInput")
        io["pm"] = c.dram("pm", [128, 1], F32, "ExternalInput")
        io["rb"] = c.dram("rb", [32, 2], F32, "ExternalInput")
        io["rb15"] = c.dram("rb15", [128, 2], F32, "ExternalInput")
        io["lamv"] = c.dram("lamv", [128, 4, 64], F32, "ExternalInput")
        io["sw"] = c.dram("sw", [128, 1], F32, "ExternalInput")
        io["identf"] = c.dram("identf", [128, 128], F32, "ExternalInput")
        io["oT"] = c.dram("oT", [256, TD], BF16, "ExternalOutput")
        make_consts(c)
        emit_da(c, io, lambda_init, "")
        c.finish()
    return nc


def emit_da(c, io, lambda_init, tag):
    NKT = TD // 128
    qtiles = [(0, 128)] + [(128 + 512 * i, 512) for i in range(16)]
    if True:
        uT, wcat, ohd, mdd, pmd, rb, rb15, lamv, sw, idd, oT = (io[k] for k in (
            "uT", "wcat", "oh", "md", "pm", "rb", "rb15", "lamv", "sw", "identf", "oT"))
        tvd = c.dram("tvscr" + tag, [2, 1152], F32, "Internal")
        identb = c.sb([128, 128], BF16, "identb")
        onesb = c.sb([128, 128], BF16, "onesb")
        c.op("pool", lambda e: e.memset(onesb[:], 1.0), writes=[onesb])
        QT = [c.sb([128, TD], BF16, "QT%d" % h) for h in range(2)]
        KT = [c.sb([128, TD], BF16, "KT%d" % h) for h in range(2)]
        V = c.sb([128, NKT, 256], BF16, "V")
        BH = [[c.sb([128, 512], BF16, "BH%d_%d" % (h, i)) for i in range(5)] for h in range(2)]
        BL = [[c.sb([128, 512], BF16, "BL%d_%d" % (h, i)) for i in range(5)] for h in range(2)]
        biasc = c.sb([128, 2], F32, "biasc")
        bias0 = c.sb([128, 2], F32, "bias0")
        neglam = c.sb([128, 1], F32, "neglam")
        swp = c.sb([128, 1], F32, "swp")

        with ExitStack() as es1:
            c.dma("sp", biasc[:], rb15[:], writes=[biasc])
            pms = c.sb([128, 1], F32, "pms", es=es1)
            c.dma("sp", pms[:], pmd[:], writes=[pms])
            c.op("dve", lambda e: e.tensor_scalar(out=bias0[:], in0=biasc[:], scalar1=pms[:, 0:1], scalar2=None, op0=ALU.add),
                 reads=[biasc, pms], writes=[bias0])
            sws = c.sb([128, 1], F32, "sws", es=es1)
            c.dma("sp", sws[:], sw[:], writes=[sws])
            c.op("dve", lambda e: e.tensor_scalar(out=swp[:], in0=sws[:], scalar1=1.0 - lambda_init, scalar2=None, op0=ALU.mult),
                 reads=[sws], writes=[swp])
            lv = c.sb([128, 4, 64], F32, "lv", es=es1)
            c.dma("sp", lv[:], lamv[:], writes=[lv])
            pr = c.sb([128, 2, 64], F32, "lpr", es=es1)
            sm = c.sb([128, 2], F32, "lsm", es=es1)
            for i in range(2):
                c.op("dve", lambda e: e.tensor_tensor(out=pr[:, i, :], in0=lv[:, 2 * i, :], in1=lv[:, 2 * i + 1, :], op=ALU.mult),
                     reads=[lv], writes=[pr])
                c.op("dve", lambda e: e.reduce_sum(out=sm[:, i:i + 1], in_=pr[:, i, :], axis=AX.X), reads=[pr], writes=[sm])
            c.op("act", lambda e: e.activation(out=sm[:], in_=sm[:], func=AF.Exp), reads=[sm], writes=[sm])
            c.op("dve", lambda e: e.tensor_tensor(out=neglam[:], in0=sm[:, 1:2], in1=sm[:, 0:1], op=ALU.subtract),
                 reads=[sm], writes=[neglam])
            c.op("dve", lambda e: e.tensor_scalar(out=neglam[:], in0=neglam[:], scalar1=-lambda_init, scalar2=None, op0=ALU.add),
                 reads=[neglam], writes=[neglam])
            idf = c.sb([128, 128], F32, "idf", es=es1)
            c.dma("sp", idf[:], idd[:], writes=[idf])
            c.op("dve", lambda e: e.tensor_copy(out=identb[:], in_=idf[:]), reads=[idf], writes=[identb])
            ohs = c.sb([32, 1152], F32, "ohs", es=es1)
            c.dma("sp", ohs[:], ohd[:], writes=[ohs])
            rbs = c.sb([32, 2], F32, "rbs", es=es1)
            c.dma("sp", rbs[:], rb[:], writes=[rbs])
            tvs = c.sb([2, 1152], F32, "tvs", es=es1)
            ptv = c.ps([128, 512], F32, "ptv", es=es1)
            for j in range(3):
                c.op("pe", lambda e: e.matmul(ptv[0:2, 0:384], lhsT=rbs[:], rhs=ohs[:, j * 384:(j + 1) * 384], start=True, stop=True),
                     reads=[rbs, ohs], writes=[ptv])
                c.op("act", lambda e: e.activation(out=tvs[:, j * 384:(j + 1) * 384], in_=ptv[0:2, 0:384], func=AF.Copy),
                     reads=[ptv], writes=[tvs])
            c.dma("sp", tvd[:], tvs[:], reads=[tvs], writes=[tvd])
            mds = c.sb([128, 5, 512], F32, "mds", es=es1)
            c.dma("sp", mds[:], mdd[:], writes=[mds])
            G = Rot([c.sb([128, 512], F32, "G%d" % i, es=es1) for i in range(2)])
            Bt = Rot([c.sb([128, 512], F32, "Bt%d" % i, es=es1) for i in range(2)])
            for h in range(2):
                for i, d in enumerate(DA_DS):
                    g_ = G.get()
                    src = bass.AP(tensor=tvd.t.tensor, offset=h * 1152 + d + 128, ap=[[1, 128], [1, 512]])
                    c.dma("sp", g_[:], src, reads=[tvd], writes=[g_])
                    b_ = Bt.get()
                    c.op("dve", lambda e: e.scalar_tensor_tensor(out=b_[:], in0=g_[:, ::-1], scalar=8.0, in1=mds[:, i, :],
                                                                 op0=ALU.mult, op1=ALU.add), reads=[g_, mds], writes=[b_])
                    c.op("act", lambda e: e.activation(out=BH[h][i][:], in_=b_[:], func=AF.Copy), reads=[b_], writes=[BH[h][i]])
                    c.op("dve", lambda e: e.tensor_tensor(out=BL[h][i][:], in0=b_[:], in1=BH[h][i][:], op=ALU.subtract),
                         reads=[b_, BH[h][i]], writes=[BL[h][i]])

        c.barrier()
        with ExitStack() as es2:
            wb = c.sb([128, 8, 768], BF16, "wb", es=es2)
            w3 = wcat[:].rearrange("(k p) n -> p k n", p=128)
            for k0 in range(0, 8, 2):
                load_cast(c, wb, wb[:, k0:k0 + 2, :], w3[:, k0:k0 + 2, :])
            ub = Rot([c.sb([128, 8, 512], BF16, "ub%d" % i, es=es2) for i in range(2)])
            pbig = Rot([c.ps([128, 512], F32, "pb%d" % i, es=es2) for i in range(6)])
            u3 = uT[:].rearrange("(k p) n -> p k n", p=128)
            for (b0, ntl) in DN_BLOCKS:
                n = ntl * 128
                u = ub.get()
                c.dma("sp", u[:, :, :n], u3[:, :, b0:b0 + n], writes=[u])
                for j in range(4):
                    p = pbig.get()
                    for k in range(8):
                        c.op("pe", lambda e: e.matmul(p[:, :n], lhsT=wb[:, k, j * 128:(j + 1) * 128], rhs=u[:, k, :n],
                                                      start=(k == 0), stop=(k == 7)), reads=[wb, u], writes=[p], inc=(k == 7))
                    dst = (QT[j] if j < 2 else KT[j - 2])
                    if j % 2 == 0:
                        c.op("act", lambda e: e.activation(out=dst[:, b0:b0 + n], in_=p[:, :n], func=AF.Copy), reads=[p], writes=[dst])
                    else:
                        c.op("dve", lambda e: e.tensor_copy(out=dst[:, b0:b0 + n], in_=p[:, :n]), reads=[p], writes=[dst])
                for tt in range(ntl):
                    p = pbig.get()
                    for k in range(8):
                        c.op("pe", lambda e: e.matmul(p[:, 0:256], lhsT=u[:, k, tt * 128:(tt + 1) * 128], rhs=wb[:, k, 512:768],
                                                      start=(k == 0), stop=(k == 7)), reads=[wb, u], writes=[p], inc=(k == 7))
                    kt = b0 // 128 + tt
                    if tt % 2 == 0:
                        c.op("act", lambda e: e.activation(out=V[:, kt, :], in_=p[:, 0:256], func=AF.Copy), reads=[p], writes=[V])
                    else:
                        c.op("dve", lambda e: e.tensor_copy(out=V[:, kt, :], in_=p[:, 0:256]), reads=[p], writes=[V])

        c.barrier()
        with ExitStack() as es3:
            sps = Rot([c.ps([128, 512], F32, "sps%d" % i, es=es3) for i in range(4)])
            oacc = [c.ps([128, 512], F32, "oacc%d" % i, es=es3) for i in range(2)]
            dacc = [c.ps([128, 512], F32, "dacc%d" % i, es=es3) for i in range(2)]
            ptb = Rot([c.sb([128, 512], BF16, "pt%d" % i, es=es3) for i in range(4)])
            rr = [c.sb([128, 512], F32, "rr%d" % i, es=es3) for i in range(2)]
            aa = [c.sb([128, 512], F32, "aa%d" % i, es=es3) for i in range(2)]
            sqb = Rot([c.sb([128, 512], F32, "sq%d" % i, es=es3) for i in range(2)])
            rstd = c.sb([128, 512], F32, "rstd", es=es3)
            obr = Rot([c.sb([128, 512], BF16, "ob%d" % i, es=es3) for i in range(2)])
            dsr = [Rot([c.sb([128, 512], F32, "dsum%d_%d" % (cc, i), es=es3) for i in range(2)]) for cc in range(2)]
            for (q0, nq) in qtiles:
                ktmax = (q0 + nq) // 128 - 1
                for h in range(2):
                    units = [(kt, cc) for kt in range(ktmax + 1) for cc in range(2)]
                    dsum = [dsr[0].get(), dsr[1].get()]

                    def emit_s(kt, cc):
                        ps = sps.get()
                        d = kt * 128 - q0
                        near = d in DA_DS
                        rs = slice(cc * 64, cc * 64 + 64)
                        c.op("pe", lambda e: e.matmul(ps[:, :nq], lhsT=KT[h][rs, kt * 128:(kt + 1) * 128], rhs=QT[h][rs, q0:q0 + nq],
                                                      start=True, stop=not near), reads=[KT[h], QT[h]], writes=[ps], inc=not near)
                        if near:
                            i = DA_DS.index(d)
                            c.op("pe", lambda e: e.matmul(ps[:, :nq], lhsT=identb[:], rhs=BH[h][i][:, :nq], start=False, stop=False),
                                 reads=[identb, BH[h][i]], writes=[ps], inc=False)
                            c.op("pe", lambda e: e.matmul(ps[:, :nq], lhsT=identb[:], rhs=BL[h][i][:, :nq], start=False, stop=True),
                                 reads=[identb, BL[h][i]], writes=[ps])
                        return ps

                    pend = [emit_s(*units[0])]
                    if len(units) > 1:
                        pend.append(emit_s(*units[1]))
                    for ui, (kt, cc) in enumerate(units):
                        ps = pend[ui]
                        pt = ptb.get()
                        bsrc = bias0 if kt == 0 else biasc
                        c.op("act", lambda e: e.activation(out=pt[:, :nq], in_=ps[:, :nq], func=AF.Exp, bias=bsrc[:, h:h + 1], scale=0.125),
                             reads=[ps, bsrc], writes=[pt])
                        if ui + 2 < len(units):
                            pend.append(emit_s(*units[ui + 2]))
                        c.op("pe", lambda e: e.matmul(oacc[cc][:, :nq], lhsT=V[:, kt, h * 128:(h + 1) * 128], rhs=pt[:, :nq],
                                                      start=(kt == 0), stop=(kt == ktmax)), reads=[V, pt], writes=[oacc[cc]])
                        if kt == 0:
                            c.op("dve", lambda e: e.tensor_copy(out=dsum[cc][:, :nq], in_=pt[:, :nq]), reads=[pt], writes=[dsum[cc]])
                        else:
                            c.op("dve", lambda e: e.tensor_tensor(out=dsum[cc][:, :nq], in0=dsum[cc][:, :nq], in1=pt[:, :nq], op=ALU.add),
                                 reads=[pt, dsum[cc]], writes=[dsum[cc]])
                    for cc in range(2):
                        c.op("pe", lambda e: e.matmul(dacc[cc][:, :nq], lhsT=c.ones[:], rhs=dsum[cc][:, :nq], start=True, stop=True),
                             reads=[c.ones, dsum[cc]], writes=[dacc[cc]])
                    for cc in range(2):
                        if q0 == 0:
                            c.op("dve", lambda e: e.tensor_scalar(out=rr[cc][:, :nq], in0=dacc[cc][:, :nq], scalar1=1e-30, scalar2=None,
                                                                  op0=ALU.max), reads=[dacc[cc]], writes=[rr[cc]])
                            c.op("dve", lambda e: e.reciprocal(out=rr[cc][:, :nq], in_=rr[cc][:, :nq]), reads=[rr[cc]], writes=[rr[cc]])
                        else:
                            c.op("dve", lambda e: e.reciprocal(out=rr[cc][:, :nq], in_=dacc[cc][:, :nq]), reads=[dacc[cc]], writes=[rr[cc]])
                        c.op("dve", lambda e: e.tensor_tensor(out=aa[cc][:, :nq], in0=oacc[cc][:, :nq], in1=rr[cc][:, :nq], op=ALU.mult),
                             reads=[oacc[cc], rr[cc]], writes=[aa[cc]])
                    c.op("dve", lambda e: e.scalar_tensor_tensor(out=aa[0][:, :nq], in0=aa[1][:, :nq], scalar=neglam[:, 0:1],
                                                                 in1=aa[0][:, :nq], op0=ALU.mult, op1=ALU.add),
                         reads=[aa[0], aa[1], neglam], writes=[aa[0]])
                    rms_stats(c, c.ones, lambda k: (aa[0][:, :nq], aa[0]), 1, nq, sps.get(), sqb, rstd, 128.0)
                    ob = obr.get()
                    c.op("dve", lambda e: e.scalar_tensor_tensor(out=ob[:, :nq], in0=aa[0][:, :nq], scalar=swp[:, 0:1],
                                                                 in1=rstd[:, :nq], op0=ALU.mult, op1=ALU.mult),
                         reads=[aa[0], swp, rstd], writes=[ob])
                    if q0 == 0:
                        c.op("dve", lambda e: e.memset(ob[:, 0:112], 0.0), writes=[ob])
                    c.dma("sp", oT[h * 128:(h + 1) * 128, q0:q0 + nq], ob[:, :nq], reads=[ob], writes=[oT])


def build_pre():
    nc = bass.Bass("TRN2", target_bir_lowering=False)
    with ExitStack() as es:
        c = Ctx(nc, es)
        io = {}
        io["hT"] = c.dram("hT", [D, NTC], F32, "ExternalInput")
        io["nw"] = c.dram("nw", [128, 8], F32, "ExternalInput")
        io["uout"] = c.dram("uout", [D, NTC], BF16, "ExternalOutput")
        make_consts(c)
        emit_pre(c, io)
        c.finish()
    return nc


def emit_pre(c, io):
    if True:
        hT, nw, uout = io["hT"], io["nw"], io["uout"]
        nws = c.sb([128, 8], F32, "nws")
        c.dma("sp", nws[:], nw[:], writes=[nws])
        H = Rot([c.sb([128, 8, GS], F32, "H%d" % g) for g in range(2)])
        U = Rot([c.sb([128, 8, GS], BF16, "U%d" % g) for g in range(2)])
        sqb = Rot([c.sb([128, GS], F32, "sq%d" % i) for i in range(2)])
        rstd = c.sb([128, GS], F32, "rstd")
        pstat = c.ps([128, 512], F32, "pstat")
        hT3 = hT[:].rearrange("(k p) n -> p k n", p=128)
        uo3 = uout[:].rearrange("(k p) n -> p k n", p=128)
        for g in range(NG):
            h = H.get()
            c.dma("sp", h[:], hT3[:, :, g * GS:(g + 1) * GS], writes=[h])
            rms_stats(c, c.ones, lambda k: (h[:, k, :], h), 8, GS, pstat, sqb, rstd, float(D))
            u = U.get()
            for m in range(8):
                c.op("dve", lambda e: e.scalar_tensor_tensor(out=u[:, m, :], in0=h[:, m, :], scalar=nws[:, m:m + 1],
                                                             in1=rstd[:], op0=ALU.mult, op1=ALU.mult),
                     reads=[h, nws, rstd], writes=[u])
            c.dma("sp", uo3[:, :, g * GS:(g + 1) * GS], u[:], reads=[u], writes=[uout])


def _run(nc, in_maps):
    res = run_bass_kernel_spmd(nc, in_maps, core_ids=list(range(NCORES)))
    return res.results


def _nwcols(w):
    return np.ascontiguousarray(np.asarray(w, np.float32).reshape(8, 128).T)


def _tok_shards(fullT):
    out = []
    for b in range(B):
        for j in range(4):
            out.append(np.ascontiguousarray(fullT[b][:, j * NTC:(j + 1) * NTC]))
    return out


def _gather_tok(shards):
    return [np.concatenate([shards[4 * b + j] for j in range(4)], axis=1) for b in range(B)]


def kernel_unfused(x, meta_tokens, rel_bias, norm_mix_w, norm_mlp_w, final_norm_w,
           dn_w_in, dn_conv_w, dn_a_log, dn_dt_bias, dn_norm_w, dn_w_out,
           da_w_in, da_lam_q1, da_lam_k1, da_lam_q2, da_lam_k2, da_subln_w, da_w_out,
           lru_w_in, lru_conv_w, lru_conv_b, lru_w_rgate, lru_b_rgate, lru_w_igate,
           lru_b_igate, lru_lambda, lru_w_out, mlp_w1, mlp_w2):
    f32 = np.float32
    x = np.asarray(x, f32)
    meta = np.asarray(meta_tokens, f32)
    bf = ml_dtypes.bfloat16
    hT_full = []
    for b in range(B):
        seq = np.concatenate([np.zeros((PADF, D), f32), meta, x[b]], axis=0)
        hT_full.append(np.ascontiguousarray(seq.T))
    h_sh = _tok_shards(hT_full)
    nc = build_pre()
    r = _run(nc, [{"hT": h_sh[c], "nw": _nwcols(norm_mix_w[0])} for c in range(NCORES)])
    u_sh = [r[c]["uout"] for c in range(NCORES)]
    depth = 4
    zpad = np.zeros((D, 64), bf)
    for layer in range(depth):
        kind = layer % 3
        slot = layer // 3
        u_full = _gather_tok(u_sh)
        ims = []
        if kind == 0:
            w_in = np.asarray(dn_w_in[slot], f32)
            cw = np.asarray(dn_conv_w[slot], f32)
            cst = dn_consts()
            for c in range(NCORES):
                b, g = divmod(c, 4)
                h0 = 2 * g
                s = slice(h0 * 128, h0 * 128 + 256)
                wcat = np.concatenate([w_in[:, 0:1024][:, s], w_in[:, 1024:2048][:, s], w_in[:, 2048:3072][:, s],
                                       w_in[:, 3072:4096][:, s], w_in[:, 4096 + h0:4096 + h0 + 2],
                                       w_in[:, 4104 + h0:4104 + h0 + 2]], axis=1)
                cw6 = np.concatenate([cw[:, 0:1024][:, s], cw[:, 1024:2048][:, s], cw[:, 2048:3072][:, s]], axis=1)
                prm = dn_params(cw6, np.asarray(dn_a_log[slot], f32)[h0:h0 + 2],
                                np.asarray(dn_dt_bias[slot], f32)[h0:h0 + 2], np.asarray(dn_norm_w[slot], f32))
                ims.append({"uT": np.ascontiguousarray(np.concatenate([zpad, u_full[b]], axis=1)),
                            "wcat": np.ascontiguousarray(wcat), "cst": cst, "prm": prm})
            r = _run(build_dn(), ims)
            o_full = [np.concatenate([r[4 * b + g]["oT"][:, 64:] for g in range(4)], axis=0) for b in range(B)]
            wo = np.asarray(dn_w_out[slot], f32)
        elif kind == 1:
            w_in = np.asarray(da_w_in[slot], f32)
            oh, md, pm = da_consts()
            rbt = np.asarray(rel_bias, f32)
            lamv = np.stack([np.asarray(v[slot], f32) for v in (da_lam_q1, da_lam_k1, da_lam_q2, da_lam_k2)], axis=0)
            lamv = np.ascontiguousarray(np.broadcast_to(lamv[None], (128, 4, 64)))
            sw = np.ascontiguousarray(np.asarray(da_subln_w[slot], f32)[:, None])
            ident = np.eye(128, dtype=f32)
            for c in range(NCORES):
                b, g = divmod(c, 4)
                h0 = 2 * g
                s = slice(h0 * 128, h0 * 128 + 256)
                wcat = np.concatenate([w_in[:, 0:1024][:, s], w_in[:, 1024:2048][:, s], w_in[:, 2048:3072][:, s]], axis=1)
                ims.append({"uT": np.ascontiguousarray(np.concatenate([zpad, u_full[b]], axis=1)),
                            "wcat": np.ascontiguousarray(wcat), "oh": oh, "md": md, "pm": pm,
                            "rb": np.ascontiguousarray(rbt[:, h0:h0 + 2]),
                            "rb15": np.ascontiguousarray(np.broadcast_to(rbt[15:16, h0:h0 + 2], (128, 2))),
                            "lamv": lamv, "sw": sw, "identf": ident})
            lam_init = 0.8 - 0.6 * math.exp(-0.3 * layer)
            r = _run(build_da(lam_init), ims)
            o_full = [np.concatenate([r[4 * b + g]["oT"][:, 64:] for g in range(4)], axis=0) for b in range(B)]
            wo = np.asarray(da_w_out[slot], f32)
        else:
            w_in = np.asarray(lru_w_in[slot], f32)
            cw = np.asarray(lru_conv_w[slot], f32)
            for c in range(NCORES):
                b, g = divmod(c, 4)
                s = slice(g * 256, (g + 1) * 256)
                prm = lru_params(cw[:, s], np.asarray(lru_conv_b[slot], f32)[s], np.asarray(lru_b_rgate[slot], f32)[s],
                                 np.asarray(lru_b_igate[slot], f32)[s], np.asarray(lru_lambda[slot], f32)[s])
                ims.append({"uT": u_full[b], "wg": np.ascontiguousarray(w_in[:, 0:1024][:, s]),
                            "wx": np.ascontiguousarray(w_in[:, 1024:2048][:, s]),
                            "wr": np.ascontiguousarray(np.asarray(lru_w_rgate[slot], f32)[g]),
                            "wi": np.ascontiguousarray(np.asarray(lru_w_igate[slot], f32)[g]), "prm": prm})
            r = _run(build_lru(), ims)
            o_full = [np.concatenate([r[4 * b + g]["yT"] for g in range(4)], axis=0) for b in range(B)]
            wo = np.asarray(lru_w_out[slot], f32)
        o_sh = _tok_shards(o_full)
        final = (layer == depth - 1)
        nxt = final_norm_w if final else norm_mix_w[layer + 1]
        nw = np.ascontiguousarray(np.concatenate([_nwcols(norm_mlp_w[layer]), _nwcols(nxt)], axis=1))
        w1 = np.asarray(mlp_w1[layer], f32)
        w2 = np.asarray(mlp_w2[layer], f32)
        r = _run(build_post(final), [{"hT": h_sh[c], "oT": o_sh[c], "wo": wo, "w1": w1, "w2": w2, "nw": nw}
                                     for c in range(NCORES)])
        h_sh = [r[c]["hout"] for c in range(NCORES)]
        u_sh = [r[c]["uout"] for c in range(NCORES)]
    out_full = _gather_tok(u_sh)
    out = np.stack([np.ascontiguousarray(out_full[b][:, PADF + NMETA:].T) for b in range(B)], axis=0)
    return out.astype(f32)


DEPTH = 4


def _phase(c, fn):
    base = c.es
    with ExitStack() as pes:
        c.es = pes
        fn()
        c.barrier()
    c.es = base


def build_fused():
    nc = bass.Bass("TRN2", target_bir_lowering=False)
    with ExitStack() as es:
        c = Ctx(nc, es)
        h0T = c.dram("h0T", [D, T], F32, "ExternalInput")
        outT = c.dram("outT", [D, T], F32, "ExternalOutput")
        HT = c.dram("HT", [D, T], F32, "Internal")
        UT = c.dram("UT", [D, TD], BF16, "Internal")
        OT = c.dram("OT", [D, TD], BF16, "Internal")
        nw0 = c.dram("nw0", [128, 8], F32, "ExternalInput")
        W = {}
        for l in range(DEPTH):
            kind = l % 3
            p = "L%d_" % l
            if kind == 0:
                W[p + "wcat"] = c.dram(p + "wcat", [4, D, 1028], F32, "ExternalInput")
                W[p + "prm"] = c.dram(p + "prm", [4, 128, 32], F32, "ExternalInput")
            elif kind == 1:
                W[p + "wcat"] = c.dram(p + "wcat", [4, D, 768], F32, "ExternalInput")
                W[p + "rb"] = c.dram(p + "rb", [4, 32, 2], F32, "ExternalInput")
                W[p + "rb15"] = c.dram(p + "rb15", [4, 128, 2], F32, "ExternalInput")
                W[p + "lamv"] = c.dram(p + "lamv", [128, 4, 64], F32, "ExternalInput")
                W[p + "sw"] = c.dram(p + "sw", [128, 1], F32, "ExternalInput")
            else:
                W[p + "wg"] = c.dram(p + "wg", [4, D, 256], F32, "ExternalInput")
                W[p + "wx"] = c.dram(p + "wx", [4, D, 256], F32, "ExternalInput")
                W[p + "wr"] = c.dram(p + "wr", [4, 256, 256], F32, "ExternalInput")
                W[p + "wi"] = c.dram(p + "wi", [4, 256, 256], F32, "ExternalInput")
                W[p + "prm"] = c.dram(p + "prm", [4, 128, 16], F32, "ExternalInput")
            W[p + "wo"] = c.dram(p + "wo", [D, D], F32, "ExternalInput")
            W[p + "w1"] = c.dram(p + "w1", [D, DFF], F32, "ExternalInput")
            W[p + "w2"] = c.dram(p + "w2", [DFF, D], F32, "ExternalInput")
            W[p + "nw"] = c.dram(p + "nw", [128, 16], F32, "ExternalInput")
        dn_cst = c.dram("dn_cst", [128, 6, 128], F32, "ExternalInput")
        da_oh = c.dram("da_oh", [32, 1152], F32, "ExternalInput")
        da_md = c.dram("da_md", [128, 5, 512], F32, "ExternalInput")
        da_pm = c.dram("da_pm", [128, 1], F32, "ExternalInput")
        identf = c.dram("identf", [128, 128], F32, "ExternalInput")
        make_consts(c)

        def sh(t, j, off=0):
            return t.view(t.t[:, off + j * NTC:off + (j + 1) * NTC])

        def zero_front():
            z = c.sb([128, 8, 64], BF16, "zfront")
            c.op("pool", lambda e: e.memset(z[:], 0.0), writes=[z])
            c.dma("sp", UT[:].rearrange("(k p) n -> p k n", p=128)[:, :, 0:64], z[:], reads=[z], writes=[UT])
        _phase(c, zero_front)
        for j in range(4):
            _phase(c, lambda j=j: emit_pre(c, {"hT": sh(h0T, j), "nw": nw0, "uout": sh(UT, j, 64)}))
        for l in range(DEPTH):
            kind = l % 3
            p = "L%d_" % l
            for g in range(4):
                rows = slice(g * 256, (g + 1) * 256)
                if kind == 0:
                    io = {"uT": UT, "wcat": W[p + "wcat"].view(W[p + "wcat"].t[g]), "cst": dn_cst,
                          "prm": W[p + "prm"].view(W[p + "prm"].t[g]), "oT": OT.view(OT.t[rows, :])}
                    _phase(c, lambda io=io: emit_dn(c, io))
                elif kind == 1:
                    io = {"uT": UT, "wcat": W[p + "wcat"].view(W[p + "wcat"].t[g]), "oh": da_oh, "md": da_md, "pm": da_pm,
                          "rb": W[p + "rb"].view(W[p + "rb"].t[g]), "rb15": W[p + "rb15"].view(W[p + "rb15"].t[g]),
                          "lamv": W[p + "lamv"], "sw": W[p + "sw"], "identf": identf, "oT": OT.view(OT.t[rows, :])}
                    lam_init = 0.8 - 0.6 * math.exp(-0.3 * l)
                    _phase(c, lambda io=io, lam_init=lam_init, tag="_%d_%d" % (l, g): emit_da(c, io, lam_init, tag))
                else:
                    io = {"uT": UT.view(UT.t[:, 64:64 + T]), "wg": W[p + "wg"].view(W[p + "wg"].t[g]),
                          "wx": W[p + "wx"].view(W[p + "wx"].t[g]), "wr": W[p + "wr"].view(W[p + "wr"].t[g]),
                          "wi": W[p + "wi"].view(W[p + "wi"].t[g]), "prm": W[p + "prm"].view(W[p + "prm"].t[g]),
                          "yT": OT.view(OT.t[rows, 64:64 + T])}
                    _phase(c, lambda io=io: emit_lru(c, io))
            final = (l == DEPTH - 1)
            for j in range(4):
                io = {"hT": sh(h0T if l == 0 else HT, j), "oT": sh(OT, j, 64), "wo": W[p + "wo"], "w1": W[p + "w1"],
                      "w2": W[p + "w2"], "nw": W[p + "nw"], "hout": sh(HT, j),
                      "uout": sh(outT, j) if final else sh(UT, j, 64)}
                _phase(c, lambda io=io, final=final: emit_post(c, io, final))
            if l == 1:
                c.switch_sem("pe")
        c.finish()
    return nc


def fused_inputs(b, x, meta_tokens, rel_bias, norm_mix_w, norm_mlp_w, final_norm_w,
                 dn_w_in, dn_conv_w, dn_a_log, dn_dt_bias, dn_norm_w, dn_w_out,
                 da_w_in, da_lam_q1, da_lam_k1, da_lam_q2, da_lam_k2, da_subln_w, da_w_out,
                 lru_w_in, lru_conv_w, lru_conv_b, lru_w_rgate, lru_b_rgate, lru_w_igate,
                 lru_b_igate, lru_lambda, lru_w_out, mlp_w1, mlp_w2, shared=None):
    f32 = np.float32
    im = {}
    seq = np.concatenate([np.zeros((PADF, D), f32), np.asarray(meta_tokens, f32), np.asarray(x[b], f32)], axis=0)
    im["h0T"] = np.ascontiguousarray(seq.T)
    if shared is not None:
        im.update(shared)
        return im
    sh = {}
    sh["nw0"] = _nwcols(norm_mix_w[0])
    oh, md, pm = da_consts()
    sh["dn_cst"] = dn_consts()
    sh["da_oh"], sh["da_md"], sh["da_pm"] = oh, md, pm
    sh["identf"] = np.eye(128, dtype=f32)
    rbt = np.asarray(rel_bias, f32)
    for l in range(DEPTH):
        kind, slot = l % 3, l // 3
        p = "L%d_" % l
        if kind == 0:
            w_in = np.asarray(dn_w_in[slot], f32)
            cw = np.asarray(dn_conv_w[slot], f32)
            wc, pr = [], []
            for g in range(4):
                h0 = 2 * g
                s = slice(h0 * 128, h0 * 128 + 256)
                wc.append(np.concatenate([w_in[:, 0:1024][:, s], w_in[:, 1024:2048][:, s], w_in[:, 2048:3072][:, s],
                                          w_in[:, 3072:4096][:, s], w_in[:, 4096 + h0:4096 + h0 + 2],
                                          w_in[:, 4104 + h0:4104 + h0 + 2]], axis=1))
                cw6 = np.concatenate([cw[:, 0:1024][:, s], cw[:, 1024:2048][:, s], cw[:, 2048:3072][:, s]], axis=1)
                pr.append(dn_params(cw6, np.asarray(dn_a_log[slot], f32)[h0:h0 + 2],
                                    np.asarray(dn_dt_bias[slot], f32)[h0:h0 + 2], np.asarray(dn_norm_w[slot], f32)))
            sh[p + "wcat"] = np.ascontiguousarray(np.stack(wc))
            sh[p + "prm"] = np.ascontiguousarray(np.stack(pr))
            wo = dn_w_out[slot]
        elif kind == 1:
            w_in = np.asarray(da_w_in[slot], f32)
            wc, rb, rb15 = [], [], []
            for g in range(4):
                h0 = 2 * g
                s = slice(h0 * 128, h0 * 128 + 256)
                wc.append(np.concatenate([w_in[:, 0:1024][:, s], w_in[:, 1024:2048][:, s], w_in[:, 2048:3072][:, s]], axis=1))
                rb.append(rbt[:, h0:h0 + 2])
                rb15.append(np.broadcast_to(rbt[15:16, h0:h0 + 2], (128, 2)))
            sh[p + "wcat"] = np.ascontiguousarray(np.stack(wc))
            sh[p + "rb"] = np.ascontiguousarray(np.stack(rb))
            sh[p + "rb15"] = np.ascontiguousarray(np.stack(rb15))
            lamv = np.stack([np.asarray(v[slot], f32) for v in (da_lam_q1, da_lam_k1, da_lam_q2, da_lam_k2)], axis=0)
            sh[p + "lamv"] = np.ascontiguousarray(np.broadcast_to(lamv[None], (128, 4, 64)))
            sh[p + "sw"] = np.ascontiguousarray(np.asarray(da_subln_w[slot], f32)[:, None])
            wo = da_w_out[slot]
        else:
            w_in = np.asarray(lru_w_in[slot], f32)
            cw = np.asarray(lru_conv_w[slot], f32)
            wg, wx, pr = [], [], []
            for g in range(4):
                s = slice(g * 256, (g + 1) * 256)
                wg.append(w_in[:, 0:1024][:, s])
                wx.append(w_in[:, 1024:2048][:, s])
                pr.append(lru_params(cw[:, s], np.asarray(lru_conv_b[slot], f32)[s], np.asarray(lru_b_rgate[slot], f32)[s],
                                     np.asarray(lru_b_igate[slot], f32)[s], np.asarray(lru_lambda[slot], f32)[s]))
            sh[p + "wg"] = np.ascontiguousarray(np.stack(wg))
            sh[p + "wx"] = np.ascontiguousarray(np.stack(wx))
            sh[p + "wr"] = np.ascontiguousarray(np.asarray(lru_w_rgate[slot], f32))
            sh[p + "wi"] = np.ascontiguousarray(np.asarray(lru_w_igate[slot], f32))
            sh[p + "prm"] = np.ascontiguousarray(np.stack(pr))
            wo = lru_w_out[slot]
        final = (l == DEPTH - 1)
        nxt = final_norm_w if final else norm_mix_w[l + 1]
        sh[p + "wo"] = np.ascontiguousarray(np.asarray(wo, f32))
        sh[p + "w1"] = np.ascontiguousarray(np.asarray(mlp_w1[l], f32))
        sh[p + "w2"] = np.ascontiguousarray(np.asarray(mlp_w2[l], f32))
        sh[p + "nw"] = np.ascontiguousarray(np.concatenate([_nwcols(norm_mlp_w[l]), _nwcols(nxt)], axis=1))
    im.update(sh)
    im["_shared"] = sh
    return im


def kernel(**inputs):
    x = inputs["x"]
    im0 = fused_inputs(0, **inputs)
    shared = im0.pop("_shared")
    im1 = fused_inputs(1, **inputs, shared=shared)
    nc = build_fused()
    res = run_bass_kernel_spmd(nc, [im0, im1], core_ids=[0, 1])
    out = np.stack([np.ascontiguousarray(res.results[b]["outT"][:, PADF + NMETA:].T) for b in range(B)], axis=0)
    return out.astype(np.float32)
```

```python
import math
from contextlib import ExitStack

import numpy as np
import ml_dtypes
import concourse.bass as bass
import concourse.mybir as mybir
from concourse.bass_utils import run_bass_kernel_spmd

F32 = mybir.dt.float32
BF16 = mybir.dt.bfloat16
AF = mybir.ActivationFunctionType
ALU = mybir.AluOpType
AX = mybir.AxisListType

D = 1024
B = 2
SEQ = 8192
NMETA = 16
PADF = 48
T = PADF + NMETA + SEQ
NTC = T // 4
EPS = 1e-6
DFF = 4096
NCORES = 8


class Tl:
    def __init__(self, t, name, st=None):
        self.t = t
        self.name = name
        self.st = st if st is not None else [None, {}]

    @property
    def w(self):
        return self.st[0]

    @w.setter
    def w(self, v):
        self.st[0] = v

    @property
    def r(self):
        return self.st[1]

    @r.setter
    def r(self, v):
        self.st[1] = v

    def view(self, ap):
        return Tl(ap, self.name, self.st)

    def __getitem__(self, idx):
        return self.t[idx]


class Ctx:
    NDMA = 8

    def __init__(self, nc, es):
        self.nc = nc
        self.es = es
        self.eng = {"pe": nc.tensor, "act": nc.scalar, "dve": nc.vector,
                    "pool": nc.gpsimd, "sp": nc.sync}
        self.sem = {k: es.enter_context(nc.semaphore("s_" + k)) for k in self.eng}
        self.cnt = {k: 0 for k in self.eng}
        self.seen = {k: {} for k in self.eng}
        self.dsem = {}
        self.dcnt = {}
        for q in ("sp", "pool"):
            self.dsem[q] = [es.enter_context(nc.semaphore("d_%s%d" % (q, i)))
                            for i in range(self.NDMA)]
            self.dcnt[q] = 0
        self.ntile = 0

    def sb(self, shape, dt=F32, name=None, es=None):
        self.ntile += 1
        name = "%s_%d" % (name or "t", self.ntile)
        t = (es or self.es).enter_context(self.nc.sbuf_tensor(name, list(shape), dt))
        return Tl(t, name)

    def ps(self, shape, dt=F32, name=None, es=None):
        self.ntile += 1
        name = "%s_%d" % (name or "p", self.ntile)
        t = (es or self.es).enter_context(self.nc.psum_tensor(name, list(shape), dt))
        return Tl(t, name)

    def dram(self, name, shape, dt, kind):
        t = self.nc.dram_tensor(name, list(shape), dt, kind=kind)
        return Tl(t.ap(), name)

    def _wait(self, e, sem, val):
        key = id(sem)
        if self.seen[e].get(key, 0) >= val:
            return
        self.eng[e].wait_ge(sem, val)
        self.seen[e][key] = val

    def _deps(self, e, reads, writes):
        deps = {}

        def add(d):
            if d is None:
                return
            s, v = d
            if deps.get(id(s), (None, 0))[1] < v:
                deps[id(s)] = (s, v)
        for t in reads:
            add(t.w)
        for t in writes:
            add(t.w)
            for s_v in t.r.values():
                add(s_v)
        for s, v in deps.values():
            if e == "pe" and s is self.sem["pe"]:
                continue
            self._wait(e, s, v)

    def _mark(self, token, reads, writes):
        s, v = token
        for t in reads:
            t.r[id(s)] = (s, v)
        for t in writes:
            t.w = (s, v)
            t.r = {}

    def op(self, e, fn, reads=(), writes=(), inc=True):
        self._deps(e, reads, writes)
        ins = fn(self.eng[e])
        if inc:
            self.cnt[e] += 1
            ins.then_inc(self.sem[e], 1)
            tok = (self.sem[e], self.cnt[e])
        else:
            tok = (self.sem[e], self.cnt[e] + 1)
        self._mark(tok, reads, writes)
        return ins

    def dma(self, q, out, in_, reads=(), writes=(), **kw):
        self._deps(q, reads, writes)
        i = self.dcnt[q]
        s = self.dsem[q][i % self.NDMA]
        prev = 16 * (i // self.NDMA)
        if prev:
            self._wait(q, s, prev)
        ins = self.eng[q].dma_start(out=out, in_=in_, **kw)
        ins.then_inc(s, 16)
        self.dcnt[q] += 1
        self._mark((s, prev + 16), reads, writes)

    def switch_sem(self, e):
        self.nsw = getattr(self, "nsw", 0) + 1
        self.sem[e] = self.es.enter_context(self.nc.semaphore("s_%s_%d" % (e, self.nsw)))
        self.cnt[e] = 0

    def barrier(self):
        for e in self.eng:
            for e2 in self.eng:
                if e2 != e and self.cnt[e2] > 0:
                    self._wait(e, self.sem[e2], self.cnt[e2])
            for q in self.dsem:
                n = self.dcnt[q]
                for j, s in enumerate(self.dsem[q]):
                    k = (n - j + self.NDMA - 1) // self.NDMA
                    if k > 0:
                        self._wait(e, s, 16 * k)

    def finish(self):
        for q in self.dsem:
            n = self.dcnt[q]
            for j, s in enumerate(self.dsem[q]):
                k = (n - j + self.NDMA - 1) // self.NDMA
                if k > 0:
                    self._wait("sp", s, 16 * k)


class Rot:
    def __init__(self, tiles):
        self.tiles = tiles
        self.i = 0

    def get(self):
        t = self.tiles[self.i % len(self.tiles)]
        self.i += 1
        return t


def rms_stats(c, ones, src_tiles_fn, nk, n, pstat, sqrot, rstd, scale_div):
    for k in range(nk):
        src, src_t = src_tiles_fn(k)
        sq = sqrot.get()
        c.op("act", lambda e: e.activation(out=sq[:, :n], in_=src, func=AF.Square),
             reads=[src_t], writes=[sq])
        c.op("pe", lambda e: e.matmul(pstat[:, :n], lhsT=ones[:], rhs=sq[:, :n],
                                      start=(k == 0), stop=(k == nk - 1)),
             reads=[ones, sq], writes=[pstat])
    c.op("act", lambda e: e.activation(out=rstd[:, :n], in_=pstat[:, :n], func=AF.Sqrt,
                                       bias=c.eps_t[:, 0:1], scale=1.0 / scale_div),
         reads=[pstat, c.eps_t], writes=[rstd])
    c.op("dve", lambda e: e.reciprocal(out=rstd[:, :n], in_=rstd[:, :n]),
         reads=[rstd], writes=[rstd])


def make_consts(c):
    c.eps_t = c.sb([128, 1], F32, "eps_t")
    c.op("pool", lambda e: e.memset(c.eps_t[:], EPS), writes=[c.eps_t])
    c.ones = c.sb([128, 128], F32, "ones")
    c.op("pool", lambda e: e.memset(c.ones[:], 1.0), writes=[c.ones])


GS = 344
NG = NTC // GS


def build_post(final):
    nc = bass.Bass("TRN2", target_bir_lowering=False)
    with ExitStack() as es:
        c = Ctx(nc, es)
        io = {}
        io["hT"] = c.dram("hT", [D, NTC], F32, "ExternalInput")
        io["oT"] = c.dram("oT", [D, NTC], BF16, "ExternalInput")
        io["wo"] = c.dram("wo", [D, D], F32, "ExternalInput")
        io["w1"] = c.dram("w1", [D, DFF], F32, "ExternalInput")
        io["w2"] = c.dram("w2", [DFF, D], F32, "ExternalInput")
        io["nw"] = c.dram("nw", [128, 16], F32, "ExternalInput")
        io["hout"] = c.dram("hout", [D, NTC], F32, "ExternalOutput")
        io["uout"] = c.dram("uout", [D, NTC], F32 if final else BF16, "ExternalOutput")
        make_consts(c)
        emit_post(c, io, final)
        c.finish()
    return nc


def emit_post(c, io, final):
    if True:
        hT, oT, wo, w1, w2, nw, hout, uout = (io[k] for k in ("hT", "oT", "wo", "w1", "w2", "nw", "hout", "uout"))
        nws = c.sb([128, 16], F32, "nws")
        c.dma("sp", nws[:], nw[:], writes=[nws])

        H = [c.sb([128, 8, GS], F32, "H%d" % g) for g in range(NG)]
        XB = [c.sb([128, 8, GS], BF16, "XB%d" % g) for g in range(NG)]
        wbuf = Rot([c.sb([128, 8192], BF16, "wb%d" % i) for i in range(2)])
        abuf = Rot([c.sb([128, 4, GS], BF16, "ab%d" % i) for i in range(2)])
        rbuf = Rot([c.sb([128, GS], F32, "rb%d" % i) for i in range(3)])
        sqb = Rot([c.sb([128, GS], F32, "sq%d" % i) for i in range(2)])
        rstd = c.sb([128, GS], F32, "rstd")
        uo = Rot([c.sb([128, 8, GS], F32 if final else BF16, "uo%d" % i) for i in range(2)])
        pa = Rot([c.ps([128, 512], F32, "pa%d" % i) for i in range(4)])
        py = Rot([c.ps([128, 512], F32, "py%d" % i) for i in range(3)])
        pstat = c.ps([128, 512], F32, "pstat")

        hT3 = hT[:].rearrange("(k p) n -> p k n", p=128)
        oT3 = oT[:].rearrange("(k p) n -> p k n", p=128)
        ho3 = hout[:].rearrange("(k p) n -> p k n", p=128)
        uo3 = uout[:].rearrange("(k p) n -> p k n", p=128)

        wob = wbuf.get()
        wo3 = wo[:].rearrange("(k p) n -> p k n", p=128)
        for k0 in range(0, 8, 2):
            c.dma("pool", wob[:, k0 * 1024:(k0 + 2) * 1024].rearrange("p (k n) -> p k n", k=2), wo3[:, k0:k0 + 2, :], writes=[wob])
        for g in range(NG):
            c.dma("sp", XB[g][:], oT3[:, :, g * GS:(g + 1) * GS], writes=[XB[g]])
            c.dma("sp", H[g][:], hT3[:, :, g * GS:(g + 1) * GS], writes=[H[g]])

        def load_eighth(e8):
            wb = wbuf.get()
            w13 = w1[:].rearrange("(k p) n -> p k n", p=128)
            for k0 in range(0, 8, 4):
                c.dma("pool", wb[:, k0 * 512:(k0 + 4) * 512].rearrange("p (k n) -> p k n", k=4),
                      w13[:, k0:k0 + 4, e8 * 512:(e8 + 1) * 512], writes=[wb])
            w23 = w2[:].rearrange("(f p) n -> p f n", p=128)
            for f0 in range(0, 4, 2):
                c.dma("pool", wb[:, 4096 + f0 * 1024:4096 + (f0 + 2) * 1024].rearrange("p (f n) -> p f n", f=2),
                      w23[:, e8 * 4 + f0:e8 * 4 + f0 + 2, :], writes=[wb])
            return wb

        def norm_to(g, col, dst, dst_is_f32):
            rms_stats(c, c.ones, lambda k: (H[g][:, k, :], H[g]), 8, GS, pstat, sqb, rstd, float(D))
            for m in range(8):
                c.op("dve", lambda e: e.scalar_tensor_tensor(
                    out=dst[:, m, :], in0=H[g][:, m, :], scalar=nws[:, col + m:col + m + 1],
                    in1=rstd[:], op0=ALU.mult, op1=ALU.mult),
                    reads=[H[g], nws, rstd], writes=[dst])

        wnext = load_eighth(0)
        for g in range(NG):
            for m in range(8):
                p = py.get()
                for k in range(8):
                    c.op("pe", lambda e: e.matmul(p[:, :GS], lhsT=wob[:, k * 1024 + m * 128:k * 1024 + (m + 1) * 128],
                                                  rhs=XB[g][:, k, :], start=(k == 0), stop=(k == 7)),
                         reads=[wob, XB[g]], writes=[p], inc=(k == 7))
                c.op("dve", lambda e: e.tensor_tensor(out=H[g][:, m, :], in0=p[:, :GS], in1=H[g][:, m, :], op=ALU.add),
                     reads=[p, H[g]], writes=[H[g]])
            norm_to(g, 0, XB[g], False)
        for e8 in range(8):
            wb = wnext
            if e8 + 1 < 8:
                wnext = load_eighth(e8 + 1)
            for g in range(NG):
                ab = abuf.get()
                for f in range(4):
                    p = pa.get()
                    for k in range(8):
                        c.op("pe", lambda e: e.matmul(p[:, :GS], lhsT=wb[:, k * 512 + f * 128:k * 512 + (f + 1) * 128],
                                                      rhs=XB[g][:, k, :], start=(k == 0), stop=(k == 7)),
                             reads=[wb, XB[g]], writes=[p], inc=(k == 7))
                    r = rbuf.get()
                    c.op("act", lambda e: e.activation(out=r[:], in_=p[:, :GS], func=AF.Relu),
                         reads=[p], writes=[r])
                    c.op("dve", lambda e: e.tensor_tensor(out=ab[:, f, :], in0=r[:], in1=r[:], op=ALU.mult),
                         reads=[r], writes=[ab])
                for m in range(8):
                    p = py.get()
                    for f in range(4):
                        c.op("pe", lambda e: e.matmul(p[:, :GS], lhsT=wb[:, 4096 + f * 1024 + m * 128:4096 + f * 1024 + (m + 1) * 128],
                                                      rhs=ab[:, f, :], start=(f == 0), stop=(f == 3)),
                             reads=[wb, ab], writes=[p], inc=(f == 3))
                    c.op("dve", lambda e: e.tensor_tensor(out=H[g][:, m, :], in0=p[:, :GS], in1=H[g][:, m, :], op=ALU.add),
                         reads=[p, H[g]], writes=[H[g]])
                if e8 == 7:
                    c.dma("sp", ho3[:, :, g * GS:(g + 1) * GS], H[g][:], reads=[H[g]], writes=[hout])
                    u = uo.get()
                    norm_to(g, 8, u, final)
                    c.dma("sp", uo3[:, :, g * GS:(g + 1) * GS], u[:], reads=[u], writes=[uout])


def load_cast(c, dst_t, dst_ap, src_ap):
    c.dma("pool", dst_ap, src_ap, writes=[dst_t])


LB = 342
LNB = (T - PADF) // LB


def build_lru():
    nc = bass.Bass("TRN2", target_bir_lowering=False)
    with ExitStack() as es:
        c = Ctx(nc, es)
        io = {}
        io["uT"] = c.dram("uT", [D, T], BF16, "ExternalInput")
        io["wg"] = c.dram("wg", [D, 256], F32, "ExternalInput")
        io["wx"] = c.dram("wx", [D, 256], F32, "ExternalInput")
        io["wr"] = c.dram("wr", [256, 256], F32, "ExternalInput")
        io["wi"] = c.dram("wi", [256, 256], F32, "ExternalInput")
        io["prm"] = c.dram("prm", [128, 16], F32, "ExternalInput")
        io["yT"] = c.dram("yT", [256, T], BF16, "ExternalOutput")
        make_consts(c)
        emit_lru(c, io)
        c.finish()
    return nc


def emit_lru(c, io):
    if True:
        uT, wg, wx, wr, wi, prm, yT = (io[k] for k in ("uT", "wg", "wx", "wr", "wi", "prm", "yT"))
        one_t = c.sb([128, 1], F32, "one_t")
        c.op("pool", lambda e: e.memset(one_t[:], 1.0), writes=[one_t])
        ps_ = c.sb([128, 16], F32, "prm_s")
        c.dma("sp", ps_[:], prm[:], writes=[ps_])
        wgb = c.sb([128, 8, 256], BF16, "wgb")
        wxb = c.sb([128, 8, 256], BF16, "wxb")
        wrb = c.sb([128, 2, 256], BF16, "wrb")
        wib = c.sb([128, 2, 256], BF16, "wib")
        for (dst, src, nk) in ((wgb, wg, 8), (wxb, wx, 8), (wrb, wr, 2), (wib, wi, 2)):
            load_cast(c, dst, dst[:], src[:].rearrange("(k p) n -> p k n", p=128))
        cch = c.sb([128, 2], F32, "cch")
        ee = c.sb([128, 2], F32, "ee")
        acc = c.sb([128, 2], F32, "lacc")
        lam_ap = lambda: ps_[:, 7:16:8]
        c.op("act", lambda e: e.activation(out=ee[:], in_=lam_ap(), func=AF.Exp, scale=-1.0),
             reads=[ps_], writes=[ee])
        c.op("dve", lambda e: e.tensor_scalar(out=acc[:], in0=ee[:], scalar1=-1.0 / 6, scalar2=1.0 / 5,
                                              op0=ALU.mult, op1=ALU.add), reads=[ee], writes=[acc])
        for coef in (1.0 / 4, 1.0 / 3, 1.0 / 2, 1.0):
            c.op("dve", lambda e: e.tensor_tensor(out=acc[:], in0=acc[:], in1=ee[:], op=ALU.mult),
                 reads=[acc, ee], writes=[acc])
            c.op("dve", lambda e: e.tensor_scalar(out=acc[:], in0=acc[:], scalar1=-1.0, scalar2=coef,
                                                  op0=ALU.mult, op1=ALU.add), reads=[acc], writes=[acc])
        c.op("dve", lambda e: e.tensor_tensor(out=acc[:], in0=acc[:], in1=ee[:], op=ALU.mult),
             reads=[acc, ee], writes=[acc])
        c.op("dve", lambda e: e.tensor_scalar(out=cch[:], in0=acc[:], scalar1=-8.0, scalar2=None,
                                              op0=ALU.mult), reads=[acc], writes=[cch])

        ub = Rot([c.sb([128, 8, LB], BF16, "ub%d" % i) for i in range(3)])
        xc = [c.sb([128, LB + 3], F32, "xc%d" % t) for t in range(2)]
        for t in range(2):
            c.op("pool", lambda e: e.memset(xc[t][:], 0.0), writes=[xc[t]])
        xr = Rot([c.sb([128, LB], F32, "xr%d" % i) for i in range(2)])
        xrb = Rot([c.sb([128, 2, LB], BF16, "xrb%d" % i) for i in range(2)])
        gt = Rot([c.sb([128, LB], F32, "gt%d" % i) for i in range(4)])
        xrs = Rot([c.sb([128, LB], F32, "xrs%d" % i) for i in range(4)])
        tmp = Rot([c.sb([128, LB], F32, "tmp%d" % i) for i in range(6)])
        hs = [Rot([c.sb([128, LB], F32, "hs%d_%d" % (t, i)) for i in range(2)]) for t in range(2)]
        yb = Rot([c.sb([128, 2, LB], BF16, "yb%d" % i) for i in range(2)])
        pp = Rot([c.ps([128, 512], F32, "pp%d" % i) for i in range(4)])
        pg_ = Rot([c.ps([128, 512], F32, "pq%d" % i) for i in range(4)])
        zz = c.sb([128, 2, PADF], BF16, "zz")
        c.op("pool", lambda e: e.memset(zz[:], 0.0), writes=[zz])
        y3 = yT[:].rearrange("(t p) n -> p t n", p=128)
        c.dma("sp", y3[:, :, 0:PADF], zz[:], reads=[zz], writes=[yT])
        u3 = uT[:].rearrange("(k p) n -> p k n", p=128)
        prev_hs = [None, None]
        P = lambda t, j: ps_[:, t * 8 + j:t * 8 + j + 1]
        n = LB
        for blk in range(LNB):
            t0 = PADF + blk * LB
            u = ub.get()
            c.dma("sp", u[:], u3[:, :, t0:t0 + n], writes=[u])
            gts, xrf = [], []
            xb_ = xrb.get()
            for t in range(2):
                pgt = pp.get()
                for k in range(8):
                    c.op("pe", lambda e: e.matmul(pgt[:, :n], lhsT=wgb[:, k, t * 128:(t + 1) * 128], rhs=u[:, k, :],
                                                  start=(k == 0), stop=(k == 7)), reads=[wgb, u], writes=[pgt], inc=(k == 7))
                pxt = pp.get()
                for k in range(8):
                    c.op("pe", lambda e: e.matmul(pxt[:, :n], lhsT=wxb[:, k, t * 128:(t + 1) * 128], rhs=u[:, k, :],
                                                  start=(k == 0), stop=(k == 7)), reads=[wxb, u], writes=[pxt], inc=(k == 7))
                s = tmp.get()
                c.op("act", lambda e: e.activation(out=s[:], in_=pgt[:, :n], func=AF.Square), reads=[pgt], writes=[s])
                c.op("dve", lambda e: e.tensor_scalar(out=s[:], in0=s[:], scalar1=0.044715, scalar2=1.0,
                                                      op0=ALU.mult, op1=ALU.add), reads=[s], writes=[s])
                c.op("dve", lambda e: e.tensor_tensor(out=s[:], in0=s[:], in1=pgt[:, :n], op=ALU.mult),
                     reads=[s, pgt], writes=[s])
                c.op("act", lambda e: e.activation(out=s[:], in_=s[:], func=AF.Sigmoid, scale=1.5957691216057308),
                     reads=[s], writes=[s])
                g_ = gt.get()
                c.op("dve", lambda e: e.tensor_tensor(out=g_[:], in0=s[:], in1=pgt[:, :n], op=ALU.mult),
                     reads=[s, pgt], writes=[g_])
                gts.append(g_)
                c.op("act", lambda e: e.activation(out=xc[t][:, 3:3 + n], in_=pxt[:, :n], func=AF.Copy),
                     reads=[pxt], writes=[xc[t]])
                x_ = xrs.get()
                c.op("dve", lambda e: e.tensor_scalar(out=x_[:], in0=xc[t][:, 0:n], scalar1=P(t, 0), scalar2=P(t, 4),
                                                      op0=ALU.mult, op1=ALU.add), reads=[xc[t], ps_], writes=[x_])
                for j in range(1, 4):
                    c.op("dve", lambda e: e.scalar_tensor_tensor(out=x_[:], in0=xc[t][:, j:j + n], scalar=P(t, j),
                                                                 in1=x_[:], op0=ALU.mult, op1=ALU.add),
                         reads=[xc[t], ps_, x_], writes=[x_])
                c.op("pool", lambda e: e.tensor_copy(out=xc[t][:, 0:3], in_=xc[t][:, n:n + 3]),
                     reads=[xc[t]], writes=[xc[t]])
                c.op("pool", lambda e: e.tensor_copy(out=xb_[:, t, :], in_=x_[:]), reads=[x_], writes=[xb_])
                xrf.append(x_)
            y_ = yb.get()
            for t in range(2):
                pr = pg_.get()
                for k in range(2):
                    c.op("pe", lambda e: e.matmul(pr[:, :n], lhsT=wrb[:, k, t * 128:(t + 1) * 128], rhs=xb_[:, k, :],
                                                  start=(k == 0), stop=(k == 1)), reads=[wrb, xb_], writes=[pr], inc=(k == 1))
                pi_ = pg_.get()
                for k in range(2):
                    c.op("pe", lambda e: e.matmul(pi_[:, :n], lhsT=wib[:, k, t * 128:(t + 1) * 128], rhs=xb_[:, k, :],
                                                  start=(k == 0), stop=(k == 1)), reads=[wib, xb_], writes=[pi_], inc=(k == 1))
                a_ = tmp.get()
                c.op("act", lambda e: e.activation(out=a_[:], in_=pr[:, :n], func=AF.Sigmoid, bias=P(t, 5)),
                     reads=[pr, ps_], writes=[a_])
                c.op("act", lambda e: e.activation(out=a_[:], in_=a_[:], func=AF.Exp, scale=cch[:, t:t + 1]),
                     reads=[a_, cch], writes=[a_])
                i_ = tmp.get()
                c.op("act", lambda e: e.activation(out=i_[:], in_=pi_[:, :n], func=AF.Sigmoid, bias=P(t, 6)),
                     reads=[pi_, ps_], writes=[i_])
                m_ = tmp.get()
                c.op("dve", lambda e: e.tensor_tensor(out=m_[:], in0=a_[:], in1=a_[:], op=ALU.mult), reads=[a_], writes=[m_])
                c.op("act", lambda e: e.activation(out=m_[:], in_=m_[:], func=AF.Sqrt, bias=one_t[:, 0:1], scale=-1.0),
                     reads=[m_, one_t], writes=[m_])
                c.op("dve", lambda e: e.tensor_tensor(out=i_[:], in0=i_[:], in1=xrf[t][:], op=ALU.mult),
                     reads=[i_, xrf[t]], writes=[i_])
                c.op("dve", lambda e: e.tensor_tensor(out=i_[:], in0=i_[:], in1=m_[:], op=ALU.mult),
                     reads=[i_, m_], writes=[i_])
                h_ = hs[t].get()
                if prev_hs[t] is None:
                    c.op("dve", lambda e: e.tensor_tensor_scan(out=h_[:], data0=a_[:], data1=i_[:], initial=0.0,
                                                               op0=ALU.mult, op1=ALU.add), reads=[a_, i_], writes=[h_])
                else:
                    ph = prev_hs[t]
                    c.op("dve", lambda e: e.tensor_tensor_scan(out=h_[:], data0=a_[:], data1=i_[:], initial=ph[:, n - 1:n],
                                                               op0=ALU.mult, op1=ALU.add), reads=[a_, i_, ph], writes=[h_])
                prev_hs[t] = h_
                c.op("pool", lambda e: e.tensor_tensor(out=y_[:, t, :], in0=h_[:], in1=gts[t][:], op=ALU.mult),
                     reads=[h_, gts[t]], writes=[y_])
            c.dma("sp", y3[:, :, t0:t0 + n], y_[:], reads=[y_], writes=[yT])


def lru_params(cw, cb, br, bi, lam):
    prm = np.zeros((128, 2, 8), np.float32)
    for t in range(2):
        sl = slice(t * 128, (t + 1) * 128)
        prm[:, t, 0:4] = cw[:, sl].T
        prm[:, t, 4] = cb[sl]
        prm[:, t, 5] = br[sl]
        prm[:, t, 6] = bi[sl]
        prm[:, t, 7] = lam[sl]
    return np.ascontiguousarray(prm.reshape(128, 16))


TD = T + 64
DN_BLOCKS = [(i * 512, 4) for i in range(16)] + [(8192, 1)]


def dn_consts():
    i = np.arange(128)
    same = (i[:, None] // 64) == (i[None, :] // 64)
    cst = np.zeros((128, 6, 128), np.float32)
    cst[:, 0, :] = np.eye(128)
    cst[:, 1, :] = (same & (i[:, None] <= i[None, :]))
    cst[:, 2, :] = (same & (i[:, None] > i[None, :]))
    cst[:, 3, :] = np.where(same & (i[:, None] >= i[None, :]), 0.0, -30000.0)
    cst[:, 4, :] = (same & (i[:, None] > i[None, :]))
    cst[:, 5, :] = -1.0
    return cst


def build_dn(blocks=None, TD=TD, dbg=99):
    blocks = blocks or DN_BLOCKS
    nc = bass.Bass("TRN2", target_bir_lowering=False)
    with ExitStack() as es:
        c = Ctx(nc, es)
        io = {}
        io["uT"] = c.dram("uT", [D, TD], BF16, "ExternalInput")
        io["wcat"] = c.dram("wcat", [D, 1028], F32, "ExternalInput")
        io["cst"] = c.dram("cst", [128, 6, 128], F32, "ExternalInput")
        io["prm"] = c.dram("prm", [128, 32], F32, "ExternalInput")
        io["oT"] = c.dram("oT", [256, TD], BF16, "ExternalOutput")
        make_consts(c)
        emit_dn(c, io, blocks, dbg)
        c.finish()
    return nc


def emit_dn(c, io, blocks=None, dbg=99):
    blocks = blocks or DN_BLOCKS
    if True:
        uT, wcat, cstd, prm, oT = (io[k] for k in ("uT", "wcat", "cst", "prm", "oT"))
        one_t = c.sb([128, 1], F32, "one_t")
        c.op("pool", lambda e: e.memset(one_t[:], 1.0), writes=[one_t])
        cst = c.sb([128, 6, 128], F32, "cst_s")
        c.dma("sp", cst[:], cstd[:], writes=[cst])
        ident = cst[:, 0, :]
        U2 = cst[:, 1, :]
        R2 = cst[:, 2, :]
        negmask = cst[:, 3, :]
        smask = cst[:, 4, :]
        negones = cst[:, 5, :]
        ps_ = c.sb([128, 32], F32, "prm_s")
        c.dma("sp", ps_[:], prm[:], writes=[ps_])
        nea = c.sb([128, 2], F32, "nea")
        c.op("act", lambda e: e.activation(out=nea[:], in_=ps_[:, 24:26], func=AF.Exp), reads=[ps_], writes=[nea])
        c.op("dve", lambda e: e.tensor_scalar(out=nea[:], in0=nea[:], scalar1=-1.0, scalar2=None, op0=ALU.mult),
             reads=[nea], writes=[nea])
        wb = c.sb([128, 8, 1028], BF16, "wb")
        w3 = wcat[:].rearrange("(k p) n -> p k n", p=128)
        for k0 in range(0, 8, 2):
            load_cast(c, wb, wb[:, k0:k0 + 2, :], w3[:, k0:k0 + 2, :])

        ub = Rot([c.sb([128, 8, 512], BF16, "ub%d" % i) for i in range(2)])
        xc = [c.sb([128, 515], F32, "xc%d" % j) for j in range(6)]
        for j in range(6):
            c.op("pool", lambda e: e.memset(xc[j][:], 0.0), writes=[xc[j]])
        ft = [Rot([c.sb([128, 512], F32, "ft%d_%d" % (j, i)) for i in range(2)]) for j in range(6)]
        szr = [Rot([c.sb([128, 512], F32, "sz%d_%d" % (h, i)) for i in range(2)]) for h in range(2)]
        sq4 = [c.sb([128, 512], F32, "sq%d" % i) for i in range(4)]
        rstd4 = [c.sb([128, 512], F32, "rstd%d" % i) for i in range(4)]
        obr = Rot([c.sb([128, 2, 512], BF16, "ob%d" % i) for i in range(2)])
        S = [c.sb([128, 128], F32, "S%d" % h) for h in range(2)]
        for h in range(2):
            c.op("pool", lambda e: e.memset(S[h][:], 0.0), writes=[S[h]])
        bt = Rot([c.sb([128, 4, 2], F32, "bt%d" % i) for i in range(2)])
        gg = Rot([c.sb([128, 4, 2], F32, "gg%d" % i) for i in range(2)])
        gtmp = Rot([c.sb([128, 4, 2], F32, "gtmp%d" % i) for i in range(6)])
        sm2 = Rot([c.sb([128, 2], F32, "sm2_%d" % i) for i in range(24)])
        sm1 = Rot([c.sb([128, 1], F32, "sm1_%d" % i) for i in range(8)])
        NSQ = 96
        sq128 = Rot([c.sb([128, 128], F32, "m%d" % i) for i in range(NSQ)])
        LL = {nm: [c.sb([128, 128], F32, "%s%d" % (nm, i)) for i in range(8)]
              for nm in ("dec", "egrow", "kd", "attT", "qg", "usb", "wTs", "otok")}
        pbig = Rot([c.ps([128, 512], F32, "pbig%d" % i) for i in range(2)])
        banks = [c.ps([128, 512], F32, "pbank%d" % i) for i in range(6)]
        psm = Rot([banks[i].view(banks[i].t[:, 0:128]) for i in range(6)])

        u3 = uT[:].rearrange("(k p) n -> p k n", p=128)
        o3 = oT[:].rearrange("(h p) n -> p h n", p=128)

        def mm(out_t, out_ap, lhsT, rhs, reads, start=True, stop=True):
            c.op("pe", lambda e: e.matmul(out_ap, lhsT=lhsT, rhs=rhs, start=start, stop=stop),
                 reads=reads, writes=[out_t], inc=stop)

        def evac(eng, dst_t, dst_ap, src_t, src_ap, scale=None, extra=()):
            if eng == "act":
                if scale is None:
                    c.op("act", lambda e: e.activation(out=dst_ap, in_=src_ap, func=AF.Copy),
                         reads=[src_t], writes=[dst_t])
                else:
                    c.op("act", lambda e: e.activation(out=dst_ap, in_=src_ap, func=AF.Copy, scale=scale),
                         reads=[src_t] + list(extra), writes=[dst_t])
            else:
                c.op("dve", lambda e: e.tensor_copy(out=dst_ap, in_=src_ap), reads=[src_t], writes=[dst_t])

        for (b0, ntl) in blocks:
            n = ntl * 128
            u = ub.get()
            c.dma("sp", u[:, :, :n], u3[:, :, b0:b0 + n], writes=[u])
            pba_t = psm.get()
            for tt in range(ntl):
                for k in range(8):
                    mm(pba_t, pba_t[:, tt * 4:(tt + 1) * 4], u[:, k, tt * 128:(tt + 1) * 128], wb[:, k, 1024:1028],
                       [u, wb], start=(k == 0), stop=(k == 7))
            pba = pba_t[:, 0:4 * ntl].rearrange("p (t f) -> p t f", f=4)
            btb = bt.get()
            ggb = gg.get()
            c.op("act", lambda e: e.activation(out=btb[:, :ntl, :], in_=pba[:, :, 0:2], func=AF.Sigmoid),
                 reads=[pba_t], writes=[btb])
            x_ = gtmp.get(); ax = gtmp.get(); rl = gtmp.get()
            for h in range(2):
                c.op("dve", lambda e: e.tensor_scalar(out=x_[:, :ntl, h:h + 1], in0=pba[:, :, 2 + h:3 + h],
                                                      scalar1=ps_[:, 26 + h:27 + h], scalar2=None, op0=ALU.add),
                     reads=[pba_t, ps_], writes=[x_])
            c.op("act", lambda e: e.activation(out=ax[:, :ntl, :], in_=x_[:, :ntl, :], func=AF.Abs),
                 reads=[x_], writes=[ax])
            c.op("act", lambda e: e.activation(out=ax[:, :ntl, :], in_=ax[:, :ntl, :], func=AF.Exp, scale=-1.0),
                 reads=[ax], writes=[ax])
            c.op("act", lambda e: e.activation(out=ax[:, :ntl, :], in_=ax[:, :ntl, :], func=AF.Ln, bias=one_t[:, 0:1]),
                 reads=[ax, one_t], writes=[ax])
            c.op("dve", lambda e: e.tensor_scalar(out=rl[:, :ntl, :], in0=x_[:, :ntl, :], scalar1=0.0, scalar2=None,
                                                  op0=ALU.max), reads=[x_], writes=[rl])
            c.op("dve", lambda e: e.tensor_tensor(out=rl[:, :ntl, :], in0=rl[:, :ntl, :], in1=ax[:, :ntl, :], op=ALU.add),
                 reads=[rl, ax], writes=[rl])
            for h in range(2):
                c.op("dve", lambda e: e.tensor_scalar(out=ggb[:, :ntl, h:h + 1], in0=rl[:, :ntl, h:h + 1],
                                                      scalar1=nea[:, h:h + 1], scalar2=None, op0=ALU.mult),
                     reads=[rl, nea], writes=[ggb])
            F = []
            for j in range(8):
                p = pbig.get()
                for k in range(8):
                    mm(p, p[:, :n], wb[:, k, j * 128:(j + 1) * 128], u[:, k, :n], [wb, u], start=(k == 0), stop=(k == 7))
                if j < 6:
                    c.op("act", lambda e: e.activation(out=xc[j][:, 3:3 + n], in_=p[:, :n], func=AF.Copy),
                         reads=[p], writes=[xc[j]])
                    f = ft[j].get()
                    c.op("dve", lambda e: e.tensor_scalar(out=f[:, :n], in0=xc[j][:, 0:n], scalar1=ps_[:, j * 4:j * 4 + 1],
                                                          scalar2=None, op0=ALU.mult), reads=[xc[j], ps_], writes=[f])
                    for tp in range(1, 4):
                        c.op("dve", lambda e: e.scalar_tensor_tensor(out=f[:, :n], in0=xc[j][:, tp:tp + n],
                                                                     scalar=ps_[:, j * 4 + tp:j * 4 + tp + 1], in1=f[:, :n],
                                                                     op0=ALU.mult, op1=ALU.add),
                             reads=[xc[j], ps_, f], writes=[f])
                    c.op("pool", lambda e: e.tensor_copy(out=xc[j][:, 0:3], in_=xc[j][:, n:n + 3]),
                         reads=[xc[j]], writes=[xc[j]])
                    c.op("act", lambda e: e.activation(out=f[:, :n], in_=f[:, :n], func=AF.Silu), reads=[f], writes=[f])
                    F.append(f)
                else:
                    z = szr[j - 6].get()
                    c.op("act", lambda e: e.activation(out=z[:, :n], in_=p[:, :n], func=AF.Silu), reads=[p], writes=[z])
                    F.append(z)
            pst4 = [pbig.get(), pbig.get(), banks[4], banks[5]]
            for j in range(4):
                c.op("act", lambda e: e.activation(out=sq4[j][:, :n], in_=F[j][:, :n], func=AF.Square),
                     reads=[F[j]], writes=[sq4[j]])
            for j in range(4):
                c.op("pe", lambda e: e.matmul(pst4[j][:, :n], lhsT=c.ones[:], rhs=sq4[j][:, :n], start=True, stop=True),
                     reads=[c.ones, sq4[j]], writes=[pst4[j]])
            for j in range(4):
                c.op("act", lambda e: e.activation(out=rstd4[j][:, :n], in_=pst4[j][:, :n], func=AF.Sqrt,
                                                   bias=c.eps_t[:, 0:1], scale=1.0),
                     reads=[pst4[j], c.eps_t], writes=[rstd4[j]])
            for j in range(4):
                c.op("dve", lambda e: e.reciprocal(out=rstd4[j][:, :n], in_=rstd4[j][:, :n]), reads=[rstd4[j]], writes=[rstd4[j]])
            for j in range(4):
                if j < 2:
                    c.op("dve", lambda e: e.scalar_tensor_tensor(out=F[j][:, :n], in0=F[j][:, :n], scalar=128.0 ** -0.5,
                                                                 in1=rstd4[j][:, :n], op0=ALU.mult, op1=ALU.mult),
                         reads=[F[j], rstd4[j]], writes=[F[j]])
                else:
                    c.op("dve", lambda e: e.tensor_tensor(out=F[j][:, :n], in0=F[j][:, :n], in1=rstd4[j][:, :n], op=ALU.mult),
                         reads=[F[j], rstd4[j]], writes=[F[j]])
            ob = obr.get()
            TT = list(range(ntl))
            CH = [(tt, h) for tt in TT for h in range(2)]
            ci = {ch: i for i, ch in enumerate(CH)}
            csl = {tt: slice(tt * 128, (tt + 1) * 128) for tt in TT}
            egc, ed, be = {}, {}, {}
            for tt in TT:
                pg1 = psm.get(); pg2 = psm.get()
                mm(pg1, pg1[:, 0:2], U2, ggb[:, tt, :], [cst, ggb])
                mm(pg2, pg2[:, 0:2], R2, ggb[:, tt, :], [cst, ggb])
                egc[tt] = sm2.get(); ed[tt] = sm2.get(); be[tt] = sm2.get()
                c.op("act", lambda e: e.activation(out=egc[tt][:], in_=pg1[:, 0:2], func=AF.Exp), reads=[pg1], writes=[egc[tt]])
                c.op("act", lambda e: e.activation(out=ed[tt][:], in_=pg2[:, 0:2], func=AF.Exp), reads=[pg2], writes=[ed[tt]])
                c.op("dve", lambda e: e.tensor_tensor(out=be[tt][:], in0=egc[tt][:], in1=btb[:, tt, :], op=ALU.mult),
                     reads=[egc[tt], btb], writes=[be[tt]])
            Ug, dec, decs, egrow, P, Q, Y = {}, {}, {}, {}, {}, {}, {}
            for ch in CH:
                tt, h = ch
                Ug[ch] = sq128.get()
                c.op("dve", lambda e: e.tensor_scalar(out=Ug[ch][:], in0=U2, scalar1=ggb[:, tt, h:h + 1], scalar2=None,
                                                      op0=ALU.mult), reads=[cst, ggb], writes=[Ug[ch]])
            for ch in CH:
                tt, h = ch
                pd = psm.get()
                mm(pd, pd[:], Ug[ch][:], c.ones[:], [Ug[ch], c.ones], start=True, stop=False)
                mm(pd, pd[:], negones, Ug[ch][:], [cst, Ug[ch]], start=False, stop=True)
                dec[ch] = LL["dec"][ci[ch]]
                c.op("dve", lambda e: e.tensor_tensor(out=dec[ch][:], in0=pd[:], in1=negmask, op=ALU.add),
                     reads=[pd, cst], writes=[dec[ch]])
                c.op("act", lambda e: e.activation(out=dec[ch][:], in_=dec[ch][:], func=AF.Exp),
                     reads=[dec[ch]], writes=[dec[ch]])
                decs[ch] = sq128.get()
                c.op("pool", lambda e: e.tensor_tensor(out=decs[ch][:], in0=dec[ch][:], in1=smask, op=ALU.mult),
                     reads=[dec[ch], cst], writes=[decs[ch]])
                pe_ = psm.get()
                mm(pe_, pe_[:], c.ones[:], Ug[ch][:], [c.ones, Ug[ch]])
                egrow[ch] = LL["egrow"][ci[ch]]
                c.op("act", lambda e: e.activation(out=egrow[ch][:], in_=pe_[:], func=AF.Exp),
                     reads=[pe_], writes=[egrow[ch]])
            for ch in CH:
                tt, h = ch
                kT = F[2 + h][:, csl[tt]]
                pk = psm.get()
                mm(pk, pk[:], kT, kT, [F[2 + h]])
                P[ch] = sq128.get()
                c.op("dve", lambda e: e.scalar_tensor_tensor(out=P[ch][:], in0=pk[:], scalar=btb[:, tt, h:h + 1],
                                                             in1=decs[ch][:], op0=ALU.mult, op1=ALU.mult),
                     reads=[pk, btb, decs[ch]], writes=[P[ch]])
            for ch in CH:
                pb = psm.get()
                mm(pb, pb[:], P[ch][:], ident, [P[ch], cst])
                Q[ch] = sq128.get(); Y[ch] = sq128.get()
                evac("act", Q[ch], Q[ch][:], pb, pb[:])
                c.op("pool", lambda e: e.tensor_tensor(out=Y[ch][:], in0=ident, in1=Q[ch][:], op=ALU.subtract),
                     reads=[Q[ch], cst], writes=[Y[ch]])
            for s in range(5):
                Pn, Qn = {}, {}
                for ch in CH:
                    pp_ = psm.get()
                    mm(pp_, pp_[:], Q[ch][:], P[ch][:], [Q[ch], P[ch]])
                    Pn[ch] = sq128.get()
                    evac("act", Pn[ch], Pn[ch][:], pp_, pp_[:])
                    if s < 4:
                        pq = psm.get()
                        mm(pq, pq[:], P[ch][:], Q[ch][:], [P[ch], Q[ch]])
                        Qn[ch] = sq128.get()
                        evac("dve", Qn[ch], Qn[ch][:], pq, pq[:])
                for ch in CH:
                    py_ = psm.get()
                    mm(py_, py_[:], Pn[ch][:], Y[ch][:], [Pn[ch], Y[ch]])
                    Yn = sq128.get()
                    c.op("dve", lambda e: e.tensor_tensor(out=Yn[:], in0=py_[:], in1=Y[ch][:], op=ALU.add),
                         reads=[py_, Y[ch]], writes=[Yn])
                    Y[ch] = Yn
                    P[ch] = Pn[ch]
                    if s < 4:
                        Q[ch] = Qn[ch]
            kbg, kd, vb, usb, wTs, attT, qg = {}, {}, {}, {}, {}, {}, {}
            for ch in CH:
                tt, h = ch
                kT = F[2 + h][:, csl[tt]]; vT = F[4 + h][:, csl[tt]]; qT = F[h][:, csl[tt]]
                pkt = psm.get()
                mm(pkt, pkt[:], kT, ident, [F[2 + h], cst])
                kbg[ch] = sq128.get(); kd[ch] = LL["kd"][ci[ch]]
                evac("act", kbg[ch], kbg[ch][:], pkt, pkt[:], scale=be[tt][:, h:h + 1], extra=[be[tt]])
                evac("act", kd[ch], kd[ch][:], pkt, pkt[:], scale=ed[tt][:, h:h + 1], extra=[ed[tt]])
                pvt = psm.get()
                mm(pvt, pvt[:], vT, ident, [F[4 + h], cst])
                vb[ch] = sq128.get()
                evac("act", vb[ch], vb[ch][:], pvt, pvt[:], scale=btb[:, tt, h:h + 1], extra=[btb])
                pqk = psm.get()
                mm(pqk, pqk[:], qT, kT, [F[h], F[2 + h]])
                att = sq128.get()
                c.op("dve", lambda e: e.tensor_tensor(out=att[:], in0=pqk[:], in1=dec[ch][:], op=ALU.mult),
                     reads=[pqk, dec[ch]], writes=[att])
                pat = psm.get()
                mm(pat, pat[:], att[:], ident, [att, cst])
                attT[ch] = LL["attT"][ci[ch]]
                evac("dve", attT[ch], attT[ch][:], pat, pat[:])
                qg[ch] = LL["qg"][ci[ch]]
                c.op("pool", lambda e: e.tensor_tensor(out=qg[ch][:], in0=qT, in1=egrow[ch][:], op=ALU.mult),
                     reads=[F[h], egrow[ch]], writes=[qg[ch]])
            for ch in CH:
                pu = psm.get()
                mm(pu, pu[:], Y[ch][:], vb[ch][:], [Y[ch], vb[ch]])
                usb[ch] = LL["usb"][ci[ch]]
                evac("act", usb[ch], usb[ch][:], pu, pu[:])
                pw = psm.get()
                mm(pw, pw[:], kbg[ch][:], Y[ch][:], [kbg[ch], Y[ch]])
                wTs[ch] = LL["wTs"][ci[ch]]
                evac("dve", wTs[ch], wTs[ch][:], pw, pw[:])
            otok = {ch: LL["otok"][ci[ch]] for ch in CH}
            for tt in TT:
                vnew = {h: sq128.get() for h in range(2)}
                for half in range(2):
                    r = slice(half * 64, half * 64 + 64)
                    for h in range(2):
                        ch = (tt, h)
                        pws = psm.get()
                        mm(pws, pws[:], wTs[ch][:], S[h][:], [wTs[ch], S[h]])
                        c.op("dve", lambda e: e.tensor_tensor(out=vnew[h][r, :], in0=usb[ch][r, :], in1=pws[r, :], op=ALU.subtract),
                             reads=[usb[ch], pws], writes=[vnew[h]])
                    for h in range(2):
                        ch = (tt, h)
                        po = psm.get()
                        mm(po, po[:], qg[ch][:], S[h][:], [qg[ch], S[h]], start=True, stop=False)
                        mm(po, po[:], attT[ch][r, :], vnew[h][r, :], [attT[ch], vnew[h]], start=False, stop=True)
                        evac("act", otok[ch], otok[ch][r, :], po, po[r, :])
                        pst = psm.get()
                        mm(pst, pst[:], kd[ch][r, :], vnew[h][r, :], [kd[ch], vnew[h]])
                        c.op("dve", lambda e: e.scalar_tensor_tensor(out=S[h][:], in0=S[h][:],
                                                                     scalar=egrow[ch][:, half * 64 + 63:half * 64 + 64],
                                                                     in1=pst[:], op0=ALU.mult, op1=ALU.add),
                             reads=[S[h], egrow[ch], pst], writes=[S[h]])
            ssd, ond, potd = {}, {}, {}
            for ch in CH:
                junk = sq128.get()
                ssd[ch] = sm1.get()
                c.op("act", lambda e: e.activation(out=junk[:], in_=otok[ch][:], func=AF.Square, accum_out=ssd[ch][:]),
                     reads=[otok[ch]], writes=[junk, ssd[ch]])
            for ch in CH:
                c.op("act", lambda e: e.activation(out=ssd[ch][:], in_=ssd[ch][:], func=AF.Sqrt, bias=c.eps_t[:, 0:1], scale=1.0 / 128),
                     reads=[ssd[ch], c.eps_t], writes=[ssd[ch]])
            for ch in CH:
                c.op("dve", lambda e: e.reciprocal(out=ssd[ch][:], in_=ssd[ch][:]), reads=[ssd[ch]], writes=[ssd[ch]])
            for ch in CH:
                ond[ch] = sq128.get()
                c.op("dve", lambda e: e.tensor_scalar(out=ond[ch][:], in0=otok[ch][:], scalar1=ssd[ch][:, 0:1], scalar2=None, op0=ALU.mult),
                     reads=[otok[ch], ssd[ch]], writes=[ond[ch]])
            for ch in CH:
                tt, h = ch
                pot = psm.get()
                mm(pot, pot[:], ond[ch][:], ident, [ond[ch], cst])
                c.op("dve", lambda e: e.scalar_tensor_tensor(out=ob[:, h, csl[tt]], in0=pot[:], scalar=ps_[:, 28:29],
                                                             in1=F[6 + h][:, csl[tt]], op0=ALU.mult, op1=ALU.mult),
                     reads=[pot, ps_, F[6 + h]], writes=[ob])
            c.dma("sp", o3[:, :, b0:b0 + n], ob[:, :, :n], reads=[ob], writes=[oT])


def dn_params(conv_w6, a_log2, dt_bias2, norm_w):
    prm = np.zeros((128, 32), np.float32)
    for j in range(6):
        prm[:, j * 4:(j + 1) * 4] = conv_w6[:, j * 128:(j + 1) * 128].T
    prm[:, 24:26] = a_log2[None, :]
    prm[:, 26:28] = dt_bias2[None, :]
    prm[:, 28] = norm_w
    return prm


DA_DS = (-128, 0, 128, 256, 384)
DA_NEGM = -240000.0
LAMBDA_INIT_L1 = 0.8 - 0.6 * math.exp(-0.3 * 1)


def _t5_bucket_np(rel):
    import jax
    import jax.numpy as jnp
    with jax.default_device(jax.devices("cpu")[0]):
        rel = jnp.asarray(rel, jnp.int32)
        nb = 16
        ret = jnp.where(rel > 0, nb, 0)
        n = jnp.abs(rel)
        max_exact = nb // 2
        nf = jnp.maximum(n, 1).astype(jnp.float32)
        large = max_exact + (jnp.log(nf / max_exact) / math.log(128 / max_exact)
                             * (nb - max_exact)).astype(jnp.int32)
        large = jnp.minimum(large, nb - 1)
        return np.asarray(ret + jnp.where(n < max_exact, n, large))


def da_consts():
    r = np.arange(-639, 513)
    bk = _t5_bucket_np(r)
    oh = np.zeros((32, 1152), np.float32)
    oh[bk, np.arange(1152)] = 1.0
    oh[15, :] -= 1.0
    kk = np.arange(128)[:, None]
    qq = np.arange(512)[None, :]
    md = np.zeros((128, 5, 512), np.float32)
    for i, d in enumerate(DA_DS):
        allowed = ((d + kk) // 64) <= (qq // 64)
        md[:, i, :] = np.where(allowed, 0.0, DA_NEGM)
    pm = np.zeros((128, 1), np.float32)
    pm[:112] = -30000.0
    return oh, md, pm


def build_da(lambda_init=LAMBDA_INIT_L1):
    NKT = TD // 128
    qtiles = [(0, 128)] + [(128 + 512 * i, 512) for i in range(16)]
    nc = bass.Bass("TRN2", target_bir_lowering=False)
    with ExitStack() as es:
        c = Ctx(nc, es)
        io = {}
        io["uT"] = c.dram("uT", [D, TD], BF16, "ExternalInput")
        io["wcat"] = c.dram("wcat", [D, 768], F32, "ExternalInput")
        io["oh"] = c.dram("oh", [32, 1152], F32, "ExternalInput")
        io["md"] = c.dram("md", [128, 5, 512], F32, "ExternalInput")
        io["pm"] = c.dram("pm", [128, 1], F32, "ExternalInput")
        io["rb"] = c.dram("rb", [32, 2], F32, "ExternalInput")
        io["rb15"] = c.dram("rb15", [128, 2], F32, "ExternalInput")
        io["lamv"] = c.dram("lamv", [128, 4, 64], F32, "ExternalInput")
        io["sw"] = c.dram("sw", [128, 1], F32, "ExternalInput")
        io["identf"] = c.dram("identf", [128, 128], F32, "ExternalInput")
        io["oT"] = c.dram("oT", [256, TD], BF16, "ExternalOutput")
        make_consts(c)
        emit_da(c, io, lambda_init, "")
        c.finish()
    return nc


def emit_da(c, io, lambda_init, tag):
    NKT = TD // 128
    qtiles = [(0, 128)] + [(128 + 512 * i, 512) for i in range(16)]
    if True:
        uT, wcat, ohd, mdd, pmd, rb, rb15, lamv, sw, idd, oT = (io[k] for k in (
            "uT", "wcat", "oh", "md", "pm", "rb", "rb15", "lamv", "sw", "identf", "oT"))
        tvd = c.dram("tvscr" + tag, [2, 1152], F32, "Internal")
        identb = c.sb([128, 128], BF16, "identb")
        onesb = c.sb([128, 128], BF16, "onesb")
        c.op("pool", lambda e: e.memset(onesb[:], 1.0), writes=[onesb])
        QT = [c.sb([128, TD], BF16, "QT%d" % h) for h in range(2)]
        KT = [c.sb([128, TD], BF16, "KT%d" % h) for h in range(2)]
        V = c.sb([128, NKT, 256], BF16, "V")
        BH = [[c.sb([128, 512], BF16, "BH%d_%d" % (h, i)) for i in range(5)] for h in range(2)]
        BL = [[c.sb([128, 512], BF16, "BL%d_%d" % (h, i)) for i in range(5)] for h in range(2)]
        biasc = c.sb([128, 2], F32, "biasc")
        bias0 = c.sb([128, 2], F32, "bias0")
        neglam = c.sb([128, 1], F32, "neglam")
        swp = c.sb([128, 1], F32, "swp")

        with ExitStack() as es1:
            c.dma("sp", biasc[:], rb15[:], writes=[biasc])
            pms = c.sb([128, 1], F32, "pms", es=es1)
            c.dma("sp", pms[:], pmd[:], writes=[pms])
            c.op("dve", lambda e: e.tensor_scalar(out=bias0[:], in0=biasc[:], scalar1=pms[:, 0:1], scalar2=None, op0=ALU.add),
                 reads=[biasc, pms], writes=[bias0])
            sws = c.sb([128, 1], F32, "sws", es=es1)
            c.dma("sp", sws[:], sw[:], writes=[sws])
            c.op("dve", lambda e: e.tensor_scalar(out=swp[:], in0=sws[:], scalar1=1.0 - lambda_init, scalar2=None, op0=ALU.mult),
                 reads=[sws], writes=[swp])
            lv = c.sb([128, 4, 64], F32, "lv", es=es1)
            c.dma("sp", lv[:], lamv[:], writes=[lv])
            pr = c.sb([128, 2, 64], F32, "lpr", es=es1)
            sm = c.sb([128, 2], F32, "lsm", es=es1)
            for i in range(2):
                c.op("dve", lambda e: e.tensor_tensor(out=pr[:, i, :], in0=lv[:, 2 * i, :], in1=lv[:, 2 * i + 1, :], op=ALU.mult),
                     reads=[lv], writes=[pr])
                c.op("dve", lambda e: e.reduce_sum(out=sm[:, i:i + 1], in_=pr[:, i, :], axis=AX.X), reads=[pr], writes=[sm])
            c.op("act", lambda e: e.activation(out=sm[:], in_=sm[:], func=AF.Exp), reads=[sm], writes=[sm])
            c.op("dve", lambda e: e.tensor_tensor(out=neglam[:], in0=sm[:, 1:2], in1=sm[:, 0:1], op=ALU.subtract),
                 reads=[sm], writes=[neglam])
            c.op("dve", lambda e: e.tensor_scalar(out=neglam[:], in0=neglam[:], scalar1=-lambda_init, scalar2=None, op0=ALU.add),
                 reads=[neglam], writes=[neglam])
            idf = c.sb([128, 128], F32, "idf", es=es1)
            c.dma("sp", idf[:], idd[:], writes=[idf])
            c.op("dve", lambda e: e.tensor_copy(out=identb[:], in_=idf[:]), reads=[idf], writes=[identb])
            ohs = c.sb([32, 1152], F32, "ohs", es=es1)
            c.dma("sp", ohs[:], ohd[:], writes=[ohs])
            rbs = c.sb([32, 2], F32, "rbs", es=es1)
            c.dma("sp", rbs[:], rb[:], writes=[rbs])
            tvs = c.sb([2, 1152], F32, "tvs", es=es1)
            ptv = c.ps([128, 512], F32, "ptv", es=es1)
            for j in range(3):
                c.op("pe", lambda e: e.matmul(ptv[0:2, 0:384], lhsT=rbs[:], rhs=ohs[:, j * 384:(j + 1) * 384], start=True, stop=True),
                     reads=[rbs, ohs], writes=[ptv])
                c.op("act", lambda e: e.activation(out=tvs[:, j * 384:(j + 1) * 384], in_=ptv[0:2, 0:384], func=AF.Copy),
                     reads=[ptv], writes=[tvs])
            c.dma("sp", tvd[:], tvs[:], reads=[tvs], writes=[tvd])
            mds = c.sb([128, 5, 512], F32, "mds", es=es1)
            c.dma("sp", mds[:], mdd[:], writes=[mds])
            G = Rot([c.sb([128, 512], F32, "G%d" % i, es=es1) for i in range(2)])
            Bt = Rot([c.sb([128, 512], F32, "Bt%d" % i, es=es1) for i in range(2)])
            for h in range(2):
                for i, d in enumerate(DA_DS):
                    g_ = G.get()
                    src = bass.AP(tensor=tvd.t.tensor, offset=h * 1152 + d + 128, ap=[[1, 128], [1, 512]])
                    c.dma("sp", g_[:], src, reads=[tvd], writes=[g_])
                    b_ = Bt.get()
                    c.op("dve", lambda e: e.scalar_tensor_tensor(out=b_[:], in0=g_[:, ::-1], scalar=8.0, in1=mds[:, i, :],
                                                                 op0=ALU.mult, op1=ALU.add), reads=[g_, mds], writes=[b_])
                    c.op("act", lambda e: e.activation(out=BH[h][i][:], in_=b_[:], func=AF.Copy), reads=[b_], writes=[BH[h][i]])
                    c.op("dve", lambda e: e.tensor_tensor(out=BL[h][i][:], in0=b_[:], in1=BH[h][i][:], op=ALU.subtract),
                         reads=[b_, BH[h][i]], writes=[BL[h][i]])

        c.barrier()
        with ExitStack() as es2:
            wb = c.sb([128, 8, 768], BF16, "wb", es=es2)
            w3 = wcat[:].rearrange("(k p) n -> p k n", p=128)
            for k0 in range(0, 8, 2):
                load_cast(c, wb, wb[:, k0:k0 + 2, :], w3[:, k0:k0 + 2, :])
            ub = Rot([c.sb([128, 8, 512], BF16, "ub%d" % i, es=es2) for i in range(2)])
            pbig = Rot([c.ps([128, 512], F32, "pb%d" % i, es=es2) for i in range(6)])
            u3 = uT[:].rearrange("(k p) n -> p k n", p=128)
            for (b0, ntl) in DN_BLOCKS:
                n = ntl * 128
                u = ub.get()
                c.dma("sp", u[:, :, :n], u3[:, :, b0:b0 + n], writes=[u])
                for j in range(4):
                    p = pbig.get()
                    for k in range(8):
                        c.op("pe", lambda e: e.matmul(p[:, :n], lhsT=wb[:, k, j * 128:(j + 1) * 128], rhs=u[:, k, :n],
                                                      start=(k == 0), stop=(k == 7)), reads=[wb, u], writes=[p], inc=(k == 7))
                    dst = (QT[j] if j < 2 else KT[j - 2])
                    if j % 2 == 0:
                        c.op("act", lambda e: e.activation(out=dst[:, b0:b0 + n], in_=p[:, :n], func=AF.Copy), reads=[p], writes=[dst])
                    else:
                        c.op("dve", lambda e: e.tensor_copy(out=dst[:, b0:b0 + n], in_=p[:, :n]), reads=[p], writes=[dst])
                for tt in range(ntl):
                    p = pbig.get()
                    for k in range(8):
                        c.op("pe", lambda e: e.matmul(p[:, 0:256], lhsT=u[:, k, tt * 128:(tt + 1) * 128], rhs=wb[:, k, 512:768],
                                                      start=(k == 0), stop=(k == 7)), reads=[wb, u], writes=[p], inc=(k == 7))
                    kt = b0 // 128 + tt
                    if tt % 2 == 0:
                        c.op("act", lambda e: e.activation(out=V[:, kt, :], in_=p[:, 0:256], func=AF.Copy), reads=[p], writes=[V])
                    else:
                        c.op("dve", lambda e: e.tensor_copy(out=V[:, kt, :], in_=p[:, 0:256]), reads=[p], writes=[V])

        c.barrier()
        with ExitStack() as es3:
            sps2 = Rot([c.ps([128, 1024], F32, "sps%d" % i, es=es3) for i in range(2)])
            oacc = [c.ps([128, 512], F32, "oacc%d" % i, es=es3) for i in range(2)]
            dacc = [c.ps([128, 512], F32, "dacc%d" % i, es=es3) for i in range(2)]
            ptb2 = Rot([c.sb([128, 1024], BF16, "pt%d" % i, es=es3) for i in range(3)])
            rr = [c.sb([128, 512], F32, "rr%d" % i, es=es3) for i in range(2)]
            aa = [c.sb([128, 512], F32, "aa%d" % i, es=es3) for i in range(2)]
            sqb = Rot([c.sb([128, 512], F32, "sq%d" % i, es=es3) for i in range(2)])
            rstd = c.sb([128, 512], F32, "rstd", es=es3)
            obr = Rot([c.sb([128, 512], BF16, "ob%d" % i, es=es3) for i in range(2)])
            dsr2 = Rot([c.sb([128, 1024], F32, "dsum%d" % i, es=es3) for i in range(2)])

            def both(t, nq):
                return t[:, :].rearrange("p (c n) -> p c n", c=2)[:, :, :nq]

            for (q0, nq) in qtiles:
                ktmax = (q0 + nq) // 128 - 1
                for h in range(2):
                    dsum = dsr2.get()

                    def emit_s(kt):
                        ps = sps2.get()
                        d = kt * 128 - q0
                        near = d in DA_DS
                        for cc in range(2):
                            rs = slice(cc * 64, cc * 64 + 64)
                            o_ = ps[:, cc * 512:cc * 512 + nq]
                            c.op("pe", lambda e: e.matmul(o_, lhsT=KT[h][rs, kt * 128:(kt + 1) * 128], rhs=QT[h][rs, q0:q0 + nq],
                                                          start=True, stop=not near), reads=[KT[h], QT[h]], writes=[ps], inc=not near)
                            if near:
                                i = DA_DS.index(d)
                                c.op("pe", lambda e: e.matmul(o_, lhsT=identb[:], rhs=BH[h][i][:, :nq], start=False, stop=False),
                                     reads=[identb, BH[h][i]], writes=[ps], inc=False)
                                c.op("pe", lambda e: e.matmul(o_, lhsT=identb[:], rhs=BL[h][i][:, :nq], start=False, stop=True),
                                     reads=[identb, BL[h][i]], writes=[ps])
                        return ps

                    ps_cur = emit_s(0)
                    for kt in range(ktmax + 1):
                        ps_next = emit_s(kt + 1) if kt < ktmax else None
                        pt = ptb2.get()
                        bsrc = bias0 if kt == 0 else biasc
                        c.op("act", lambda e: e.activation(out=both(pt, nq), in_=both(ps_cur, nq), func=AF.Exp,
                                                           bias=bsrc[:, h:h + 1], scale=0.125),
                             reads=[ps_cur, bsrc], writes=[pt])
                        for cc in range(2):
                            c.op("pe", lambda e: e.matmul(oacc[cc][:, :nq], lhsT=V[:, kt, h * 128:(h + 1) * 128],
                                                          rhs=pt[:, cc * 512:cc * 512 + nq],
                                                          start=(kt == 0), stop=(kt == ktmax)), reads=[V, pt], writes=[oacc[cc]])
                        if kt == 0:
                            c.op("dve", lambda e: e.tensor_copy(out=both(dsum, nq), in_=both(pt, nq)), reads=[pt], writes=[dsum])
                        else:
                            c.op("dve", lambda e: e.tensor_tensor(out=both(dsum, nq), in0=both(dsum, nq), in1=both(pt, nq), op=ALU.add),
                                 reads=[pt, dsum], writes=[dsum])
                        ps_cur = ps_next
                    for cc in range(2):
                        c.op("pe", lambda e: e.matmul(dacc[cc][:, :nq], lhsT=c.ones[:], rhs=dsum[:, cc * 512:cc * 512 + nq], start=True, stop=True),
                             reads=[c.ones, dsum], writes=[dacc[cc]])
                    for cc in range(2):
                        if q0 == 0:
                            c.op("dve", lambda e: e.tensor_scalar(out=rr[cc][:, :nq], in0=dacc[cc][:, :nq], scalar1=1e-30, scalar2=None,
                                                                  op0=ALU.max), reads=[dacc[cc]], writes=[rr[cc]])
                            c.op("dve", lambda e: e.reciprocal(out=rr[cc][:, :nq], in_=rr[cc][:, :nq]), reads=[rr[cc]], writes=[rr[cc]])
                        else:
                            c.op("dve", lambda e: e.reciprocal(out=rr[cc][:, :nq], in_=dacc[cc][:, :nq]), reads=[dacc[cc]], writes=[rr[cc]])
                        c.op("dve", lambda e: e.tensor_tensor(out=aa[cc][:, :nq], in0=oacc[cc][:, :nq], in1=rr[cc][:, :nq], op=ALU.mult),
                             reads=[oacc[cc], rr[cc]], writes=[aa[cc]])
                    c.op("dve", lambda e: e.scalar_tensor_tensor(out=aa[0][:, :nq], in0=aa[1][:, :nq], scalar=neglam[:, 0:1],
                                                                 in1=aa[0][:, :nq], op0=ALU.mult, op1=ALU.add),
                         reads=[aa[0], aa[1], neglam], writes=[aa[0]])
                    pstat_ = sps2.get()
                    rms_stats(c, c.ones, lambda k: (aa[0][:, :nq], aa[0]), 1, nq, pstat_.view(pstat_.t[:, 0:512]), sqb, rstd, 128.0)
                    ob = obr.get()
                    c.op("dve", lambda e: e.scalar_tensor_tensor(out=ob[:, :nq], in0=aa[0][:, :nq], scalar=swp[:, 0:1],
                                                                 in1=rstd[:, :nq], op0=ALU.mult, op1=ALU.mult),
                         reads=[aa[0], swp, rstd], writes=[ob])
                    if q0 == 0:
                        c.op("dve", lambda e: e.memset(ob[:, 0:112], 0.0), writes=[ob])
                    c.dma("sp", oT[h * 128:(h + 1) * 128, q0:q0 + nq], ob[:, :nq], reads=[ob], writes=[oT])


def build_pre():
    nc = bass.Bass("TRN2", target_bir_lowering=False)
    with ExitStack() as es:
        c = Ctx(nc, es)
        io = {}
        io["hT"] = c.dram("hT", [D, NTC], F32, "ExternalInput")
        io["nw"] = c.dram("nw", [128, 8], F32, "ExternalInput")
        io["uout"] = c.dram("uout", [D, NTC], BF16, "ExternalOutput")
        make_consts(c)
        emit_pre(c, io)
        c.finish()
    return nc


def emit_pre(c, io):
    if True:
        hT, nw, uout = io["hT"], io["nw"], io["uout"]
        nws = c.sb([128, 8], F32, "nws")
        c.dma("sp", nws[:], nw[:], writes=[nws])
        H = Rot([c.sb([128, 8, GS], F32, "H%d" % g) for g in range(2)])
        U = Rot([c.sb([128, 8, GS], BF16, "U%d" % g) for g in range(2)])
        sqb = Rot([c.sb([128, GS], F32, "sq%d" % i) for i in range(2)])
        rstd = c.sb([128, GS], F32, "rstd")
        pstat = c.ps([128, 512], F32, "pstat")
        hT3 = hT[:].rearrange("(k p) n -> p k n", p=128)
        uo3 = uout[:].rearrange("(k p) n -> p k n", p=128)
        for g in range(NG):
            h = H.get()
            c.dma("sp", h[:], hT3[:, :, g * GS:(g + 1) * GS], writes=[h])
            rms_stats(c, c.ones, lambda k: (h[:, k, :], h), 8, GS, pstat, sqb, rstd, float(D))
            u = U.get()
            for m in range(8):
                c.op("dve", lambda e: e.scalar_tensor_tensor(out=u[:, m, :], in0=h[:, m, :], scalar=nws[:, m:m + 1],
                                                             in1=rstd[:], op0=ALU.mult, op1=ALU.mult),
                     reads=[h, nws, rstd], writes=[u])
            c.dma("sp", uo3[:, :, g * GS:(g + 1) * GS], u[:], reads=[u], writes=[uout])


def _run(nc, in_maps):
    res = run_bass_kernel_spmd(nc, in_maps, core_ids=list(range(NCORES)))
    return res.results


def _nwcols(w):
    return np.ascontiguousarray(np.asarray(w, np.float32).reshape(8, 128).T)


def _tok_shards(fullT):
    out = []
    for b in range(B):
        for j in range(4):
            out.append(np.ascontiguousarray(fullT[b][:, j * NTC:(j + 1) * NTC]))
    return out


def _gather_tok(shards):
    return [np.concatenate([shards[4 * b + j] for j in range(4)], axis=1) for b in range(B)]


def kernel_unfused(x, meta_tokens, rel_bias, norm_mix_w, norm_mlp_w, final_norm_w,
           dn_w_in, dn_conv_w, dn_a_log, dn_dt_bias, dn_norm_w, dn_w_out,
           da_w_in, da_lam_q1, da_lam_k1, da_lam_q2, da_lam_k2, da_subln_w, da_w_out,
           lru_w_in, lru_conv_w, lru_conv_b, lru_w_rgate, lru_b_rgate, lru_w_igate,
           lru_b_igate, lru_lambda, lru_w_out, mlp_w1, mlp_w2):
    f32 = np.float32
    x = np.asarray(x, f32)
    meta = np.asarray(meta_tokens, f32)
    bf = ml_dtypes.bfloat16
    hT_full = []
    for b in range(B):
        seq = np.concatenate([np.zeros((PADF, D), f32), meta, x[b]], axis=0)
        hT_full.append(np.ascontiguousarray(seq.T))
    h_sh = _tok_shards(hT_full)
    nc = build_pre()
    r = _run(nc, [{"hT": h_sh[c], "nw": _nwcols(norm_mix_w[0])} for c in range(NCORES)])
    u_sh = [r[c]["uout"] for c in range(NCORES)]
    depth = 4
    zpad = np.zeros((D, 64), bf)
    for layer in range(depth):
        kind = layer % 3
        slot = layer // 3
        u_full = _gather_tok(u_sh)
        ims = []
        if kind == 0:
            w_in = np.asarray(dn_w_in[slot], f32)
            cw = np.asarray(dn_conv_w[slot], f32)
            cst = dn_consts()
            for c in range(NCORES):
                b, g = divmod(c, 4)
                h0 = 2 * g
                s = slice(h0 * 128, h0 * 128 + 256)
                wcat = np.concatenate([w_in[:, 0:1024][:, s], w_in[:, 1024:2048][:, s], w_in[:, 2048:3072][:, s],
                                       w_in[:, 3072:4096][:, s], w_in[:, 4096 + h0:4096 + h0 + 2],
                                       w_in[:, 4104 + h0:4104 + h0 + 2]], axis=1)
                cw6 = np.concatenate([cw[:, 0:1024][:, s], cw[:, 1024:2048][:, s], cw[:, 2048:3072][:, s]], axis=1)
                prm = dn_params(cw6, np.asarray(dn_a_log[slot], f32)[h0:h0 + 2],
                                np.asarray(dn_dt_bias[slot], f32)[h0:h0 + 2], np.asarray(dn_norm_w[slot], f32))
                ims.append({"uT": np.ascontiguousarray(np.concatenate([zpad, u_full[b]], axis=1)),
                            "wcat": np.ascontiguousarray(wcat), "cst": cst, "prm": prm})
            r = _run(build_dn(), ims)
            o_full = [np.concatenate([r[4 * b + g]["oT"][:, 64:] for g in range(4)], axis=0) for b in range(B)]
            wo = np.asarray(dn_w_out[slot], f32)
        elif kind == 1:
            w_in = np.asarray(da_w_in[slot], f32)
            oh, md, pm = da_consts()
            rbt = np.asarray(rel_bias, f32)
            lamv = np.stack([np.asarray(v[slot], f32) for v in (da_lam_q1, da_lam_k1, da_lam_q2, da_lam_k2)], axis=0)
            lamv = np.ascontiguousarray(np.broadcast_to(lamv[None], (128, 4, 64)))
            sw = np.ascontiguousarray(np.asarray(da_subln_w[slot], f32)[:, None])
            ident = np.eye(128, dtype=f32)
            for c in range(NCORES):
                b, g = divmod(c, 4)
                h0 = 2 * g
                s = slice(h0 * 128, h0 * 128 + 256)
                wcat = np.concatenate([w_in[:, 0:1024][:, s], w_in[:, 1024:2048][:, s], w_in[:, 2048:3072][:, s]], axis=1)
                ims.append({"uT": np.ascontiguousarray(np.concatenate([zpad, u_full[b]], axis=1)),
                            "wcat": np.ascontiguousarray(wcat), "oh": oh, "md": md, "pm": pm,
                            "rb": np.ascontiguousarray(rbt[:, h0:h0 + 2]),
                            "rb15": np.ascontiguousarray(np.broadcast_to(rbt[15:16, h0:h0 + 2], (128, 2))),
                            "lamv": lamv, "sw": sw, "identf": ident})
            lam_init = 0.8 - 0.6 * math.exp(-0.3 * layer)
            r = _run(build_da(lam_init), ims)
            o_full = [np.concatenate([r[4 * b + g]["oT"][:, 64:] for g in range(4)], axis=0) for b in range(B)]
            wo = np.asarray(da_w_out[slot], f32)
        else:
            w_in = np.asarray(lru_w_in[slot], f32)
            cw = np.asarray(lru_conv_w[slot], f32)
            for c in range(NCORES):
                b, g = divmod(c, 4)
                s = slice(g * 256, (g + 1) * 256)
                prm = lru_params(cw[:, s], np.asarray(lru_conv_b[slot], f32)[s], np.asarray(lru_b_rgate[slot], f32)[s],
                                 np.asarray(lru_b_igate[slot], f32)[s], np.asarray(lru_lambda[slot], f32)[s])
                ims.append({"uT": u_full[b], "wg": np.ascontiguousarray(w_in[:, 0:1024][:, s]),
                            "wx": np.ascontiguousarray(w_in[:, 1024:2048][:, s]),
                            "wr": np.ascontiguousarray(np.asarray(lru_w_rgate[slot], f32)[g]),
                            "wi": np.ascontiguousarray(np.asarray(lru_w_igate[slot], f32)[g]), "prm": prm})
            r = _run(build_lru(), ims)
            o_full = [np.concatenate([r[4 * b + g]["yT"] for g in range(4)], axis=0) for b in range(B)]
            wo = np.asarray(lru_w_out[slot], f32)
        o_sh = _tok_shards(o_full)
        final = (layer == depth - 1)
        nxt = final_norm_w if final else norm_mix_w[layer + 1]
        nw = np.ascontiguousarray(np.concatenate([_nwcols(norm_mlp_w[layer]), _nwcols(nxt)], axis=1))
        w1 = np.asarray(mlp_w1[layer], f32)
        w2 = np.asarray(mlp_w2[layer], f32)
        r = _run(build_post(final), [{"hT": h_sh[c], "oT": o_sh[c], "wo": wo, "w1": w1, "w2": w2, "nw": nw}
                                     for c in range(NCORES)])
        h_sh = [r[c]["hout"] for c in range(NCORES)]
        u_sh = [r[c]["uout"] for c in range(NCORES)]
    out_full = _gather_tok(u_sh)
    out = np.stack([np.ascontiguousarray(out_full[b][:, PADF + NMETA:].T) for b in range(B)], axis=0)
    return out.astype(f32)


DEPTH = 4


def _phase(c, fn):
    base = c.es
    with ExitStack() as pes:
        c.es = pes
        fn()
        c.barrier()
    c.es = base


def build_fused():
    nc = bass.Bass("TRN2", target_bir_lowering=False)
    with ExitStack() as es:
        c = Ctx(nc, es)
        h0T = c.dram("h0T", [D, T], F32, "ExternalInput")
        outT = c.dram("outT", [D, T], F32, "ExternalOutput")
        HT = c.dram("HT", [D, T], F32, "Internal")
        UT = c.dram("UT", [D, TD], BF16, "Internal")
        OT = c.dram("OT", [D, TD], BF16, "Internal")
        nw0 = c.dram("nw0", [128, 8], F32, "ExternalInput")
        W = {}
        for l in range(DEPTH):
            kind = l % 3
            p = "L%d_" % l
            if kind == 0:
                W[p + "wcat"] = c.dram(p + "wcat", [4, D, 1028], F32, "ExternalInput")
                W[p + "prm"] = c.dram(p + "prm", [4, 128, 32], F32, "ExternalInput")
            elif kind == 1:
                W[p + "wcat"] = c.dram(p + "wcat", [4, D, 768], F32, "ExternalInput")
                W[p + "rb"] = c.dram(p + "rb", [4, 32, 2], F32, "ExternalInput")
                W[p + "rb15"] = c.dram(p + "rb15", [4, 128, 2], F32, "ExternalInput")
                W[p + "lamv"] = c.dram(p + "lamv", [128, 4, 64], F32, "ExternalInput")
                W[p + "sw"] = c.dram(p + "sw", [128, 1], F32, "ExternalInput")
            else:
                W[p + "wg"] = c.dram(p + "wg", [4, D, 256], F32, "ExternalInput")
                W[p + "wx"] = c.dram(p + "wx", [4, D, 256], F32, "ExternalInput")
                W[p + "wr"] = c.dram(p + "wr", [4, 256, 256], F32, "ExternalInput")
                W[p + "wi"] = c.dram(p + "wi", [4, 256, 256], F32, "ExternalInput")
                W[p + "prm"] = c.dram(p + "prm", [4, 128, 16], F32, "ExternalInput")
            W[p + "wo"] = c.dram(p + "wo", [D, D], F32, "ExternalInput")
            W[p + "w1"] = c.dram(p + "w1", [D, DFF], F32, "ExternalInput")
            W[p + "w2"] = c.dram(p + "w2", [DFF, D], F32, "ExternalInput")
            W[p + "nw"] = c.dram(p + "nw", [128, 16], F32, "ExternalInput")
        dn_cst = c.dram("dn_cst", [128, 6, 128], F32, "ExternalInput")
        da_oh = c.dram("da_oh", [32, 1152], F32, "ExternalInput")
        da_md = c.dram("da_md", [128, 5, 512], F32, "ExternalInput")
        da_pm = c.dram("da_pm", [128, 1], F32, "ExternalInput")
        identf = c.dram("identf", [128, 128], F32, "ExternalInput")
        make_consts(c)

        def sh(t, j, off=0):
            return t.view(t.t[:, off + j * NTC:off + (j + 1) * NTC])

        def zero_front():
            z = c.sb([128, 8, 64], BF16, "zfront")
            c.op("pool", lambda e: e.memset(z[:], 0.0), writes=[z])
            c.dma("sp", UT[:].rearrange("(k p) n -> p k n", p=128)[:, :, 0:64], z[:], reads=[z], writes=[UT])
        _phase(c, zero_front)
        for j in range(4):
            _phase(c, lambda j=j: emit_pre(c, {"hT": sh(h0T, j), "nw": nw0, "uout": sh(UT, j, 64)}))
        for l in range(DEPTH):
            kind = l % 3
            p = "L%d_" % l
            for g in range(4):
                rows = slice(g * 256, (g + 1) * 256)
                if kind == 0:
                    io = {"uT": UT, "wcat": W[p + "wcat"].view(W[p + "wcat"].t[g]), "cst": dn_cst,
                          "prm": W[p + "prm"].view(W[p + "prm"].t[g]), "oT": OT.view(OT.t[rows, :])}
                    _phase(c, lambda io=io: emit_dn(c, io))
                elif kind == 1:
                    io = {"uT": UT, "wcat": W[p + "wcat"].view(W[p + "wcat"].t[g]), "oh": da_oh, "md": da_md, "pm": da_pm,
                          "rb": W[p + "rb"].view(W[p + "rb"].t[g]), "rb15": W[p + "rb15"].view(W[p + "rb15"].t[g]),
                          "lamv": W[p + "lamv"], "sw": W[p + "sw"], "identf": identf, "oT": OT.view(OT.t[rows, :])}
                    lam_init = 0.8 - 0.6 * math.exp(-0.3 * l)
                    _phase(c, lambda io=io, lam_init=lam_init, tag="_%d_%d" % (l, g): emit_da(c, io, lam_init, tag))
                else:
                    io = {"uT": UT.view(UT.t[:, 64:64 + T]), "wg": W[p + "wg"].view(W[p + "wg"].t[g]),
                          "wx": W[p + "wx"].view(W[p + "wx"].t[g]), "wr": W[p + "wr"].view(W[p + "wr"].t[g]),
                          "wi": W[p + "wi"].view(W[p + "wi"].t[g]), "prm": W[p + "prm"].view(W[p + "prm"].t[g]),
                          "yT": OT.view(OT.t[rows, 64:64 + T])}
                    _phase(c, lambda io=io: emit_lru(c, io))
            final = (l == DEPTH - 1)
            for j in range(4):
                io = {"hT": sh(h0T if l == 0 else HT, j), "oT": sh(OT, j, 64), "wo": W[p + "wo"], "w1": W[p + "w1"],
                      "w2": W[p + "w2"], "nw": W[p + "nw"], "hout": sh(HT, j),
                      "uout": sh(outT, j) if final else sh(UT, j, 64)}
                _phase(c, lambda io=io, final=final: emit_post(c, io, final))
            if l == 1:
                c.switch_sem("pe")
        c.finish()
    return nc


def fused_inputs(b, x, meta_tokens, rel_bias, norm_mix_w, norm_mlp_w, final_norm_w,
                 dn_w_in, dn_conv_w, dn_a_log, dn_dt_bias, dn_norm_w, dn_w_out,
                 da_w_in, da_lam_q1, da_lam_k1, da_lam_q2, da_lam_k2, da_subln_w, da_w_out,
                 lru_w_in, lru_conv_w, lru_conv_b, lru_w_rgate, lru_b_rgate, lru_w_igate,
                 lru_b_igate, lru_lambda, lru_w_out, mlp_w1, mlp_w2, shared=None):
    f32 = np.float32
    im = {}
    seq = np.concatenate([np.zeros((PADF, D), f32), np.asarray(meta_tokens, f32), np.asarray(x[b], f32)], axis=0)
    im["h0T"] = np.ascontiguousarray(seq.T)
    if shared is not None:
        im.update(shared)
        return im
    sh = {}
    sh["nw0"] = _nwcols(norm_mix_w[0])
    oh, md, pm = da_consts()
    sh["dn_cst"] = dn_consts()
    sh["da_oh"], sh["da_md"], sh["da_pm"] = oh, md, pm
    sh["identf"] = np.eye(128, dtype=f32)
    rbt = np.asarray(rel_bias, f32)
    for l in range(DEPTH):
        kind, slot = l % 3, l // 3
        p = "L%d_" % l
        if kind == 0:
            w_in = np.asarray(dn_w_in[slot], f32)
            cw = np.asarray(dn_conv_w[slot], f32)
            wc, pr = [], []
            for g in range(4):
                h0 = 2 * g
                s = slice(h0 * 128, h0 * 128 + 256)
                wc.append(np.concatenate([w_in[:, 0:1024][:, s], w_in[:, 1024:2048][:, s], w_in[:, 2048:3072][:, s],
                                          w_in[:, 3072:4096][:, s], w_in[:, 4096 + h0:4096 + h0 + 2],
                                          w_in[:, 4104 + h0:4104 + h0 + 2]], axis=1))
                cw6 = np.concatenate([cw[:, 0:1024][:, s], cw[:, 1024:2048][:, s], cw[:, 2048:3072][:, s]], axis=1)
                pr.append(dn_params(cw6, np.asarray(dn_a_log[slot], f32)[h0:h0 + 2],
                                    np.asarray(dn_dt_bias[slot], f32)[h0:h0 + 2], np.asarray(dn_norm_w[slot], f32)))
            sh[p + "wcat"] = np.ascontiguousarray(np.stack(wc))
            sh[p + "prm"] = np.ascontiguousarray(np.stack(pr))
            wo = dn_w_out[slot]
        elif kind == 1:
            w_in = np.asarray(da_w_in[slot], f32)
            wc, rb, rb15 = [], [], []
            for g in range(4):
                h0 = 2 * g
                s = slice(h0 * 128, h0 * 128 + 256)
                wc.append(np.concatenate([w_in[:, 0:1024][:, s], w_in[:, 1024:2048][:, s], w_in[:, 2048:3072][:, s]], axis=1))
                rb.append(rbt[:, h0:h0 + 2])
                rb15.append(np.broadcast_to(rbt[15:16, h0:h0 + 2], (128, 2)))
            sh[p + "wcat"] = np.ascontiguousarray(np.stack(wc))
            sh[p + "rb"] = np.ascontiguousarray(np.stack(rb))
            sh[p + "rb15"] = np.ascontiguousarray(np.stack(rb15))
            lamv = np.stack([np.asarray(v[slot], f32) for v in (da_lam_q1, da_lam_k1, da_lam_q2, da_lam_k2)], axis=0)
            sh[p + "lamv"] = np.ascontiguousarray(np.broadcast_to(lamv[None], (128, 4, 64)))
            sh[p + "sw"] = np.ascontiguousarray(np.asarray(da_subln_w[slot], f32)[:, None])
            wo = da_w_out[slot]
        else:
            w_in = np.asarray(lru_w_in[slot], f32)
            cw = np.asarray(lru_conv_w[slot], f32)
            wg, wx, pr = [], [], []
            for g in range(4):
                s = slice(g * 256, (g + 1) * 256)
                wg.append(w_in[:, 0:1024][:, s])
                wx.append(w_in[:, 1024:2048][:, s])
                pr.append(lru_params(cw[:, s], np.asarray(lru_conv_b[slot], f32)[s], np.asarray(lru_b_rgate[slot], f32)[s],
                                     np.asarray(lru_b_igate[slot], f32)[s], np.asarray(lru_lambda[slot], f32)[s]))
            sh[p + "wg"] = np.ascontiguousarray(np.stack(wg))
            sh[p + "wx"] = np.ascontiguousarray(np.stack(wx))
            sh[p + "wr"] = np.ascontiguousarray(np.asarray(lru_w_rgate[slot], f32))
            sh[p + "wi"] = np.ascontiguousarray(np.asarray(lru_w_igate[slot], f32))
            sh[p + "prm"] = np.ascontiguousarray(np.stack(pr))
            wo = lru_w_out[slot]
        final = (l == DEPTH - 1)
        nxt = final_norm_w if final else norm_mix_w[l + 1]
        sh[p + "wo"] = np.ascontiguousarray(np.asarray(wo, f32))
        sh[p + "w1"] = np.ascontiguousarray(np.asarray(mlp_w1[l], f32))
        sh[p + "w2"] = np.ascontiguousarray(np.asarray(mlp_w2[l], f32))
        sh[p + "nw"] = np.ascontiguousarray(np.concatenate([_nwcols(norm_mlp_w[l]), _nwcols(nxt)], axis=1))
    im.update(sh)
    im["_shared"] = sh
    return im


def kernel(**inputs):
    x = inputs["x"]
    im0 = fused_inputs(0, **inputs)
    shared = im0.pop("_shared")
    im1 = fused_inputs(1, **inputs, shared=shared)
    nc = build_fused()
    res = run_bass_kernel_spmd(nc, [im0, im1], core_ids=[0, 1])
    out = np.stack([np.ascontiguousarray(res.results[b]["outT"][:, PADF + NMETA:].T) for b in range(B)], axis=0)
    return out.astype(np.float32)
```

```python
import math
from contextlib import ExitStack

import numpy as np
import ml_dtypes
import concourse.bass as bass
import concourse.mybir as mybir
from concourse.bass_utils import run_bass_kernel_spmd

F32 = mybir.dt.float32
BF16 = mybir.dt.bfloat16
AF = mybir.ActivationFunctionType
ALU = mybir.AluOpType
AX = mybir.AxisListType

D = 1024
B = 2
SEQ = 8192
NMETA = 16
PADF = 48
T = PADF + NMETA + SEQ
NTC = T // 4
EPS = 1e-6
DFF = 4096
NCORES = 8


class Tl:
    def __init__(self, t, name, st=None):
        self.t = t
        self.name = name
        self.st = st if st is not None else [None, {}]

    @property
    def w(self):
        return self.st[0]

    @w.setter
    def w(self, v):
        self.st[0] = v

    @property
    def r(self):
        return self.st[1]

    @r.setter
    def r(self, v):
        self.st[1] = v

    def view(self, ap):
        return Tl(ap, self.name, self.st)

    def __getitem__(self, idx):
        return self.t[idx]


class Ctx:
    NDMA = 8

    def __init__(self, nc, es):
        self.nc = nc
        self.es = es
        self.eng = {"pe": nc.tensor, "act": nc.scalar, "dve": nc.vector,
                    "pool": nc.gpsimd, "sp": nc.sync}
        self.sem = {k: es.enter_context(nc.semaphore("s_" + k)) for k in self.eng}
        self.cnt = {k: 0 for k in self.eng}
        self.seen = {k: {} for k in self.eng}
        self.dsem = {}
        self.dcnt = {}
        for q in ("sp", "pool"):
            self.dsem[q] = [es.enter_context(nc.semaphore("d_%s%d" % (q, i)))
                            for i in range(self.NDMA)]
            self.dcnt[q] = 0
        self.ntile = 0

    def sb(self, shape, dt=F32, name=None, es=None):
        self.ntile += 1
        name = "%s_%d" % (name or "t", self.ntile)
        t = (es or self.es).enter_context(self.nc.sbuf_tensor(name, list(shape), dt))
        return Tl(t, name)

    def ps(self, shape, dt=F32, name=None, es=None):
        self.ntile += 1
        name = "%s_%d" % (name or "p", self.ntile)
        t = (es or self.es).enter_context(self.nc.psum_tensor(name, list(shape), dt))
        return Tl(t, name)

    def dram(self, name, shape, dt, kind):
        t = self.nc.dram_tensor(name, list(shape), dt, kind=kind)
        return Tl(t.ap(), name)

    def _wait(self, e, sem, val):
        key = id(sem)
        if self.seen[e].get(key, 0) >= val:
            return
        self.eng[e].wait_ge(sem, val)
        self.seen[e][key] = val

    def _deps(self, e, reads, writes):
        deps = {}

        def add(d):
            if d is None:
                return
            s, v = d
            if deps.get(id(s), (None, 0))[1] < v:
                deps[id(s)] = (s, v)
        for t in reads:
            add(t.w)
        for t in writes:
            add(t.w)
            for s_v in t.r.values():
                add(s_v)
        for s, v in deps.values():
            if e == "pe" and s is self.sem["pe"]:
                continue
            self._wait(e, s, v)

    def _mark(self, token, reads, writes):
        s, v = token
        for t in reads:
            t.r[id(s)] = (s, v)
        for t in writes:
            t.w = (s, v)
            t.r = {}

    def op(self, e, fn, reads=(), writes=(), inc=True):
        self._deps(e, reads, writes)
        ins = fn(self.eng[e])
        if inc:
            self.cnt[e] += 1
            ins.then_inc(self.sem[e], 1)
            tok = (self.sem[e], self.cnt[e])
        else:
            tok = (self.sem[e], self.cnt[e] + 1)
        self._mark(tok, reads, writes)
        return ins

    def dma(self, q, out, in_, reads=(), writes=(), **kw):
        self._deps(q, reads, writes)
        i = self.dcnt[q]
        s = self.dsem[q][i % self.NDMA]
        prev = 16 * (i // self.NDMA)
        if prev:
            self._wait(q, s, prev)
        ins = self.eng[q].dma_start(out=out, in_=in_, **kw)
        ins.then_inc(s, 16)
        self.dcnt[q] += 1
        self._mark((s, prev + 16), reads, writes)

    def switch_sem(self, e):
        self.nsw = getattr(self, "nsw", 0) + 1
        self.sem[e] = self.es.enter_context(self.nc.semaphore("s_%s_%d" % (e, self.nsw)))
        self.cnt[e] = 0

    def barrier(self):
        for e in self.eng:
            for e2 in self.eng:
                if e2 != e and self.cnt[e2] > 0:
                    self._wait(e, self.sem[e2], self.cnt[e2])
            for q in self.dsem:
                n = self.dcnt[q]
                for j, s in enumerate(self.dsem[q]):
                    k = (n - j + self.NDMA - 1) // self.NDMA
                    if k > 0:
                        self._wait(e, s, 16 * k)

    def finish(self):
        for q in self.dsem:
            n = self.dcnt[q]
            for j, s in enumerate(self.dsem[q]):
                k = (n - j + self.NDMA - 1) // self.NDMA
                if k > 0:
                    self._wait("sp", s, 16 * k)


class Rot:
    def __init__(self, tiles):
        self.tiles = tiles
        self.i = 0

    def get(self):
        t = self.tiles[self.i % len(self.tiles)]
        self.i += 1
        return t


def rms_stats(c, ones, src_tiles_fn, nk, n, pstat, sqrot, rstd, scale_div):
    for k in range(nk):
        src, src_t = src_tiles_fn(k)
        sq = sqrot.get()
        c.op("act", lambda e: e.activation(out=sq[:, :n], in_=src, func=AF.Square),
             reads=[src_t], writes=[sq])
        c.op("pe", lambda e: e.matmul(pstat[:, :n], lhsT=ones[:], rhs=sq[:, :n],
                                      start=(k == 0), stop=(k == nk - 1)),
             reads=[ones, sq], writes=[pstat])
    c.op("act", lambda e: e.activation(out=rstd[:, :n], in_=pstat[:, :n], func=AF.Sqrt,
                                       bias=c.eps_t[:, 0:1], scale=1.0 / scale_div),
         reads=[pstat, c.eps_t], writes=[rstd])
    c.op("dve", lambda e: e.reciprocal(out=rstd[:, :n], in_=rstd[:, :n]),
         reads=[rstd], writes=[rstd])


def make_consts(c):
    c.eps_t = c.sb([128, 1], F32, "eps_t")
    c.op("pool", lambda e: e.memset(c.eps_t[:], EPS), writes=[c.eps_t])
    c.ones = c.sb([128, 128], F32, "ones")
    c.op("pool", lambda e: e.memset(c.ones[:], 1.0), writes=[c.ones])


GS = 344
NG = NTC // GS


def build_post(final):
    nc = bass.Bass("TRN2", target_bir_lowering=False)
    with ExitStack() as es:
        c = Ctx(nc, es)
        io = {}
        io["hT"] = c.dram("hT", [D, NTC], F32, "ExternalInput")
        io["oT"] = c.dram("oT", [D, NTC], BF16, "ExternalInput")
        io["wo"] = c.dram("wo", [D, D], F32, "ExternalInput")
        io["w1"] = c.dram("w1", [D, DFF], F32, "ExternalInput")
        io["w2"] = c.dram("w2", [DFF, D], F32, "ExternalInput")
        io["nw"] = c.dram("nw", [128, 16], F32, "ExternalInput")
        io["hout"] = c.dram("hout", [D, NTC], F32, "ExternalOutput")
        io["uout"] = c.dram("uout", [D, NTC], F32 if final else BF16, "ExternalOutput")
        make_consts(c)
        emit_post(c, io, final)
        c.finish()
    return nc


def emit_post(c, io, final):
    if True:
        hT, oT, wo, w1, w2, nw, hout, uout = (io[k] for k in ("hT", "oT", "wo", "w1", "w2", "nw", "hout", "uout"))
        nws = c.sb([128, 16], F32, "nws")
        c.dma("sp", nws[:], nw[:], writes=[nws])

        H = [c.sb([128, 8, GS], F32, "H%d" % g) for g in range(NG)]
        XB = [c.sb([128, 8, GS], BF16, "XB%d" % g) for g in range(NG)]
        wbuf = Rot([c.sb([128, 8192], BF16, "wb%d" % i) for i in range(2)])
        abuf = Rot([c.sb([128, 4, GS], BF16, "ab%d" % i) for i in range(2)])
        rbuf = Rot([c.sb([128, GS], F32, "rb%d" % i) for i in range(3)])
        sqb = Rot([c.sb([128, GS], F32, "sq%d" % i) for i in range(2)])
        rstd = c.sb([128, GS], F32, "rstd")
        uo = Rot([c.sb([128, 8, GS], F32 if final else BF16, "uo%d" % i) for i in range(2)])
        pa = Rot([c.ps([128, 512], F32, "pa%d" % i) for i in range(4)])
        py = Rot([c.ps([128, 512], F32, "py%d" % i) for i in range(3)])
        pstat = c.ps([128, 512], F32, "pstat")

        hT3 = hT[:].rearrange("(k p) n -> p k n", p=128)
        oT3 = oT[:].rearrange("(k p) n -> p k n", p=128)
        ho3 = hout[:].rearrange("(k p) n -> p k n", p=128)
        uo3 = uout[:].rearrange("(k p) n -> p k n", p=128)

        wob = wbuf.get()
        wo3 = wo[:].rearrange("(k p) n -> p k n", p=128)
        for k0 in range(0, 8, 2):
            c.dma("pool", wob[:, k0 * 1024:(k0 + 2) * 1024].rearrange("p (k n) -> p k n", k=2), wo3[:, k0:k0 + 2, :], writes=[wob])
        for g in range(NG):
            c.dma("sp", XB[g][:], oT3[:, :, g * GS:(g + 1) * GS], writes=[XB[g]])
            c.dma("sp", H[g][:], hT3[:, :, g * GS:(g + 1) * GS], writes=[H[g]])

        def load_eighth(e8):
            wb = wbuf.get()
            w13 = w1[:].rearrange("(k p) n -> p k n", p=128)
            for k0 in range(0, 8, 4):
                c.dma("pool", wb[:, k0 * 512:(k0 + 4) * 512].rearrange("p (k n) -> p k n", k=4),
                      w13[:, k0:k0 + 4, e8 * 512:(e8 + 1) * 512], writes=[wb])
            w23 = w2[:].rearrange("(f p) n -> p f n", p=128)
            for f0 in range(0, 4, 2):
                c.dma("pool", wb[:, 4096 + f0 * 1024:4096 + (f0 + 2) * 1024].rearrange("p (f n) -> p f n", f=2),
                      w23[:, e8 * 4 + f0:e8 * 4 + f0 + 2, :], writes=[wb])
            return wb

        def norm_to(g, col, dst, dst_is_f32):
            rms_stats(c, c.ones, lambda k: (H[g][:, k, :], H[g]), 8, GS, pstat, sqb, rstd, float(D))
            for m in range(8):
                c.op("dve", lambda e: e.scalar_tensor_tensor(
                    out=dst[:, m, :], in0=H[g][:, m, :], scalar=nws[:, col + m:col + m + 1],
                    in1=rstd[:], op0=ALU.mult, op1=ALU.mult),
                    reads=[H[g], nws, rstd], writes=[dst])

        wnext = load_eighth(0)
        for g in range(NG):
            for m in range(8):
                p = py.get()
                for k in range(8):
                    c.op("pe", lambda e: e.matmul(p[:, :GS], lhsT=wob[:, k * 1024 + m * 128:k * 1024 + (m + 1) * 128],
                                                  rhs=XB[g][:, k, :], start=(k == 0), stop=(k == 7)),
                         reads=[wob, XB[g]], writes=[p], inc=(k == 7))
                c.op("dve", lambda e: e.tensor_tensor(out=H[g][:, m, :], in0=p[:, :GS], in1=H[g][:, m, :], op=ALU.add),
                     reads=[p, H[g]], writes=[H[g]])
            norm_to(g, 0, XB[g], False)
        def mlp_up(e8, g):
            wb = wbs[e8]
            ab = abuf.get()
            for f in range(4):
                p = pa.get()
                for k in range(8):
                    c.op("pe", lambda e: e.matmul(p[:, :GS], lhsT=wb[:, k * 512 + f * 128:k * 512 + (f + 1) * 128],
                                                  rhs=XB[g][:, k, :], start=(k == 0), stop=(k == 7)),
                         reads=[wb, XB[g]], writes=[p], inc=(k == 7))
                r = rbuf.get()
                c.op("act", lambda e: e.activation(out=r[:], in_=p[:, :GS], func=AF.Relu),
                     reads=[p], writes=[r])
                c.op("dve", lambda e: e.tensor_tensor(out=ab[:, f, :], in0=r[:], in1=r[:], op=ALU.mult),
                     reads=[r], writes=[ab])
            return ab

        def mlp_down(e8, g, ab):
            wb = wbs[e8]
            for m in range(8):
                p = py.get()
                for f in range(4):
                    c.op("pe", lambda e: e.matmul(p[:, :GS], lhsT=wb[:, 4096 + f * 1024 + m * 128:4096 + f * 1024 + (m + 1) * 128],
                                                  rhs=ab[:, f, :], start=(f == 0), stop=(f == 3)),
                         reads=[wb, ab], writes=[p], inc=(f == 3))
                c.op("dve", lambda e: e.tensor_tensor(out=H[g][:, m, :], in0=p[:, :GS], in1=H[g][:, m, :], op=ALU.add),
                     reads=[p, H[g]], writes=[H[g]])

        wbs = {0: wnext}
        wbs[1] = load_eighth(1)
        steps = [(e8, g) for e8 in range(8) for g in range(NG)]
        ab_cur = mlp_up(0, 0)
        for i, (e8, g) in enumerate(steps):
            ab_next = mlp_up(*steps[i + 1]) if i + 1 < len(steps) else None
            mlp_down(e8, g, ab_cur)
            ab_cur = ab_next
            if g == NG - 1 and e8 + 2 < 8:
                wbs[e8 + 2] = load_eighth(e8 + 2)
            if True:
                if e8 == 7:
                    c.dma("sp", ho3[:, :, g * GS:(g + 1) * GS], H[g][:], reads=[H[g]], writes=[hout])
                    u = uo.get()
                    norm_to(g, 8, u, final)
                    c.dma("sp", uo3[:, :, g * GS:(g + 1) * GS], u[:], reads=[u], writes=[uout])


def load_cast(c, dst_t, dst_ap, src_ap):
    c.dma("pool", dst_ap, src_ap, writes=[dst_t])


LB = 342
LNB = (T - PADF) // LB


def build_lru():
    nc = bass.Bass("TRN2", target_bir_lowering=False)
    with ExitStack() as es:
        c = Ctx(nc, es)
        io = {}
        io["uT"] = c.dram("uT", [D, T], BF16, "ExternalInput")
        io["wg"] = c.dram("wg", [D, 256], F32, "ExternalInput")
        io["wx"] = c.dram("wx", [D, 256], F32, "ExternalInput")
        io["wr"] = c.dram("wr", [256, 256], F32, "ExternalInput")
        io["wi"] = c.dram("wi", [256, 256], F32, "ExternalInput")
        io["prm"] = c.dram("prm", [128, 16], F32, "ExternalInput")
        io["yT"] = c.dram("yT", [256, T], BF16, "ExternalOutput")
        make_consts(c)
        emit_lru(c, io)
        c.finish()
    return nc


def emit_lru(c, io):
    if True:
        uT, wg, wx, wr, wi, prm, yT = (io[k] for k in ("uT", "wg", "wx", "wr", "wi", "prm", "yT"))
        one_t = c.sb([128, 1], F32, "one_t")
        c.op("pool", lambda e: e.memset(one_t[:], 1.0), writes=[one_t])
        ps_ = c.sb([128, 16], F32, "prm_s")
        c.dma("sp", ps_[:], prm[:], writes=[ps_])
        wgb = c.sb([128, 8, 256], BF16, "wgb")
        wxb = c.sb([128, 8, 256], BF16, "wxb")
        wrb = c.sb([128, 2, 256], BF16, "wrb")
        wib = c.sb([128, 2, 256], BF16, "wib")
        for (dst, src, nk) in ((wgb, wg, 8), (wxb, wx, 8), (wrb, wr, 2), (wib, wi, 2)):
            load_cast(c, dst, dst[:], src[:].rearrange("(k p) n -> p k n", p=128))
        cch = c.sb([128, 2], F32, "cch")
        ee = c.sb([128, 2], F32, "ee")
        acc = c.sb([128, 2], F32, "lacc")
        lam_ap = lambda: ps_[:, 7:16:8]
        c.op("act", lambda e: e.activation(out=ee[:], in_=lam_ap(), func=AF.Exp, scale=-1.0),
             reads=[ps_], writes=[ee])
        c.op("dve", lambda e: e.tensor_scalar(out=acc[:], in0=ee[:], scalar1=-1.0 / 6, scalar2=1.0 / 5,
                                              op0=ALU.mult, op1=ALU.add), reads=[ee], writes=[acc])
        for coef in (1.0 / 4, 1.0 / 3, 1.0 / 2, 1.0):
            c.op("dve", lambda e: e.tensor_tensor(out=acc[:], in0=acc[:], in1=ee[:], op=ALU.mult),
                 reads=[acc, ee], writes=[acc])
            c.op("dve", lambda e: e.tensor_scalar(out=acc[:], in0=acc[:], scalar1=-1.0, scalar2=coef,
                                                  op0=ALU.mult, op1=ALU.add), reads=[acc], writes=[acc])
        c.op("dve", lambda e: e.tensor_tensor(out=acc[:], in0=acc[:], in1=ee[:], op=ALU.mult),
             reads=[acc, ee], writes=[acc])
        c.op("dve", lambda e: e.tensor_scalar(out=cch[:], in0=acc[:], scalar1=-8.0, scalar2=None,
                                              op0=ALU.mult), reads=[acc], writes=[cch])

        ub = Rot([c.sb([128, 8, LB], BF16, "ub%d" % i) for i in range(3)])
        xc = [c.sb([128, LB + 3], F32, "xc%d" % t) for t in range(2)]
        for t in range(2):
            c.op("pool", lambda e: e.memset(xc[t][:], 0.0), writes=[xc[t]])
        xr = Rot([c.sb([128, LB], F32, "xr%d" % i) for i in range(2)])
        xrb = Rot([c.sb([128, 2, LB], BF16, "xrb%d" % i) for i in range(2)])
        gt = Rot([c.sb([128, LB], F32, "gt%d" % i) for i in range(4)])
        xrs = Rot([c.sb([128, LB], F32, "xrs%d" % i) for i in range(4)])
        tmp = Rot([c.sb([128, LB], F32, "tmp%d" % i) for i in range(6)])
        hs = [Rot([c.sb([128, LB], F32, "hs%d_%d" % (t, i)) for i in range(2)]) for t in range(2)]
        yb = Rot([c.sb([128, 2, LB], BF16, "yb%d" % i) for i in range(2)])
        pp = Rot([c.ps([128, 512], F32, "pp%d" % i) for i in range(4)])
        pg_ = Rot([c.ps([128, 512], F32, "pq%d" % i) for i in range(4)])
        zz = c.sb([128, 2, PADF], BF16, "zz")
        c.op("pool", lambda e: e.memset(zz[:], 0.0), writes=[zz])
        y3 = yT[:].rearrange("(t p) n -> p t n", p=128)
        c.dma("sp", y3[:, :, 0:PADF], zz[:], reads=[zz], writes=[yT])
        u3 = uT[:].rearrange("(k p) n -> p k n", p=128)
        prev_hs = [None, None]
        P = lambda t, j: ps_[:, t * 8 + j:t * 8 + j + 1]
        n = LB
        for blk in range(LNB):
            t0 = PADF + blk * LB
            u = ub.get()
            c.dma("sp", u[:], u3[:, :, t0:t0 + n], writes=[u])
            gts, xrf = [], []
            xb_ = xrb.get()
            for t in range(2):
                pgt = pp.get()
                for k in range(8):
                    c.op("pe", lambda e: e.matmul(pgt[:, :n], lhsT=wgb[:, k, t * 128:(t + 1) * 128], rhs=u[:, k, :],
                                                  start=(k == 0), stop=(k == 7)), reads=[wgb, u], writes=[pgt], inc=(k == 7))
                pxt = pp.get()
                for k in range(8):
                    c.op("pe", lambda e: e.matmul(pxt[:, :n], lhsT=wxb[:, k, t * 128:(t + 1) * 128], rhs=u[:, k, :],
                                                  start=(k == 0), stop=(k == 7)), reads=[wxb, u], writes=[pxt], inc=(k == 7))
                s = tmp.get()
                c.op("act", lambda e: e.activation(out=s[:], in_=pgt[:, :n], func=AF.Square), reads=[pgt], writes=[s])
                c.op("dve", lambda e: e.tensor_scalar(out=s[:], in0=s[:], scalar1=0.044715, scalar2=1.0,
                                                      op0=ALU.mult, op1=ALU.add), reads=[s], writes=[s])
                c.op("dve", lambda e: e.tensor_tensor(out=s[:], in0=s[:], in1=pgt[:, :n], op=ALU.mult),
                     reads=[s, pgt], writes=[s])
                c.op("act", lambda e: e.activation(out=s[:], in_=s[:], func=AF.Sigmoid, scale=1.5957691216057308),
                     reads=[s], writes=[s])
                g_ = gt.get()
                c.op("dve", lambda e: e.tensor_tensor(out=g_[:], in0=s[:], in1=pgt[:, :n], op=ALU.mult),
                     reads=[s, pgt], writes=[g_])
                gts.append(g_)
                c.op("act", lambda e: e.activation(out=xc[t][:, 3:3 + n], in_=pxt[:, :n], func=AF.Copy),
                     reads=[pxt], writes=[xc[t]])
                x_ = xrs.get()
                c.op("dve", lambda e: e.tensor_scalar(out=x_[:], in0=xc[t][:, 0:n], scalar1=P(t, 0), scalar2=P(t, 4),
                                                      op0=ALU.mult, op1=ALU.add), reads=[xc[t], ps_], writes=[x_])
                for j in range(1, 4):
                    c.op("dve", lambda e: e.scalar_tensor_tensor(out=x_[:], in0=xc[t][:, j:j + n], scalar=P(t, j),
                                                                 in1=x_[:], op0=ALU.mult, op1=ALU.add),
                         reads=[xc[t], ps_, x_], writes=[x_])
                c.op("pool", lambda e: e.tensor_copy(out=xc[t][:, 0:3], in_=xc[t][:, n:n + 3]),
                     reads=[xc[t]], writes=[xc[t]])
                c.op("pool", lambda e: e.tensor_copy(out=xb_[:, t, :], in_=x_[:]), reads=[x_], writes=[xb_])
                xrf.append(x_)
            y_ = yb.get()
            for t in range(2):
                pr = pg_.get()
                for k in range(2):
                    c.op("pe", lambda e: e.matmul(pr[:, :n], lhsT=wrb[:, k, t * 128:(t + 1) * 128], rhs=xb_[:, k, :],
                                                  start=(k == 0), stop=(k == 1)), reads=[wrb, xb_], writes=[pr], inc=(k == 1))
                pi_ = pg_.get()
                for k in range(2):
                    c.op("pe", lambda e: e.matmul(pi_[:, :n], lhsT=wib[:, k, t * 128:(t + 1) * 128], rhs=xb_[:, k, :],
                                                  start=(k == 0), stop=(k == 1)), reads=[wib, xb_], writes=[pi_], inc=(k == 1))
                a_ = tmp.get()
                c.op("act", lambda e: e.activation(out=a_[:], in_=pr[:, :n], func=AF.Sigmoid, bias=P(t, 5)),
                     reads=[pr, ps_], writes=[a_])
                c.op("act", lambda e: e.activation(out=a_[:], in_=a_[:], func=AF.Exp, scale=cch[:, t:t + 1]),
                     reads=[a_, cch], writes=[a_])
                i_ = tmp.get()
                c.op("act", lambda e: e.activation(out=i_[:], in_=pi_[:, :n], func=AF.Sigmoid, bias=P(t, 6)),
                     reads=[pi_, ps_], writes=[i_])
                m_ = tmp.get()
                c.op("dve", lambda e: e.tensor_tensor(out=m_[:], in0=a_[:], in1=a_[:], op=ALU.mult), reads=[a_], writes=[m_])
                c.op("act", lambda e: e.activation(out=m_[:], in_=m_[:], func=AF.Sqrt, bias=one_t[:, 0:1], scale=-1.0),
                     reads=[m_, one_t], writes=[m_])
                c.op("dve", lambda e: e.tensor_tensor(out=i_[:], in0=i_[:], in1=xrf[t][:], op=ALU.mult),
                     reads=[i_, xrf[t]], writes=[i_])
                c.op("dve", lambda e: e.tensor_tensor(out=i_[:], in0=i_[:], in1=m_[:], op=ALU.mult),
                     reads=[i_, m_], writes=[i_])
                h_ = hs[t].get()
                if prev_hs[t] is None:
                    c.op("dve", lambda e: e.tensor_tensor_scan(out=h_[:], data0=a_[:], data1=i_[:], initial=0.0,
                                                               op0=ALU.mult, op1=ALU.add), reads=[a_, i_], writes=[h_])
                else:
                    ph = prev_hs[t]
                    c.op("dve", lambda e: e.tensor_tensor_scan(out=h_[:], data0=a_[:], data1=i_[:], initial=ph[:, n - 1:n],
                                                               op0=ALU.mult, op1=ALU.add), reads=[a_, i_, ph], writes=[h_])
                prev_hs[t] = h_
                c.op("pool", lambda e: e.tensor_tensor(out=y_[:, t, :], in0=h_[:], in1=gts[t][:], op=ALU.mult),
                     reads=[h_, gts[t]], writes=[y_])
            c.dma("sp", y3[:, :, t0:t0 + n], y_[:], reads=[y_], writes=[yT])


def lru_params(cw, cb, br, bi, lam):
    prm = np.zeros((128, 2, 8), np.float32)
    for t in range(2):
        sl = slice(t * 128, (t + 1) * 128)
        prm[:, t, 0:4] = cw[:, sl].T
        prm[:, t, 4] = cb[sl]
        prm[:, t, 5] = br[sl]
        prm[:, t, 6] = bi[sl]
        prm[:, t, 7] = lam[sl]
    return np.ascontiguousarray(prm.reshape(128, 16))


TD = T + 64
DN_BLOCKS = [(i * 512, 4) for i in range(16)] + [(8192, 1)]


def dn_consts():
    i = np.arange(128)
    same = (i[:, None] // 64) == (i[None, :] // 64)
    cst = np.zeros((128, 6, 128), np.float32)
    cst[:, 0, :] = np.eye(128)
    cst[:, 1, :] = (same & (i[:, None] <= i[None, :]))
    cst[:, 2, :] = (same & (i[:, None] > i[None, :]))
    cst[:, 3, :] = np.where(same & (i[:, None] >= i[None, :]), 0.0, -30000.0)
    cst[:, 4, :] = (same & (i[:, None] > i[None, :]))
    cst[:, 5, :] = -1.0
    return cst


def build_dn(blocks=None, TD=TD, dbg=99):
    blocks = blocks or DN_BLOCKS
    nc = bass.Bass("TRN2", target_bir_lowering=False)
    with ExitStack() as es:
        c = Ctx(nc, es)
        io = {}
        io["uT"] = c.dram("uT", [D, TD], BF16, "ExternalInput")
        io["wcat"] = c.dram("wcat", [D, 1028], F32, "ExternalInput")
        io["cst"] = c.dram("cst", [128, 6, 128], F32, "ExternalInput")
        io["prm"] = c.dram("prm", [128, 32], F32, "ExternalInput")
        io["oT"] = c.dram("oT", [256, TD], BF16, "ExternalOutput")
        make_consts(c)
        emit_dn(c, io, blocks, dbg)
        c.finish()
    return nc


def emit_dn(c, io, blocks=None, dbg=99):
    blocks = blocks or DN_BLOCKS
    if True:
        uT, wcat, cstd, prm, oT = (io[k] for k in ("uT", "wcat", "cst", "prm", "oT"))
        one_t = c.sb([128, 1], F32, "one_t")
        c.op("pool", lambda e: e.memset(one_t[:], 1.0), writes=[one_t])
        cst = c.sb([128, 6, 128], F32, "cst_s")
        c.dma("sp", cst[:], cstd[:], writes=[cst])
        ident = cst[:, 0, :]
        U2 = cst[:, 1, :]
        R2 = cst[:, 2, :]
        negmask = cst[:, 3, :]
        smask = cst[:, 4, :]
        negones = cst[:, 5, :]
        ps_ = c.sb([128, 32], F32, "prm_s")
        c.dma("sp", ps_[:], prm[:], writes=[ps_])
        nea = c.sb([128, 2], F32, "nea")
        c.op("act", lambda e: e.activation(out=nea[:], in_=ps_[:, 24:26], func=AF.Exp), reads=[ps_], writes=[nea])
        c.op("dve", lambda e: e.tensor_scalar(out=nea[:], in0=nea[:], scalar1=-1.0, scalar2=None, op0=ALU.mult),
             reads=[nea], writes=[nea])
        wb = c.sb([128, 8, 1028], BF16, "wb")
        w3 = wcat[:].rearrange("(k p) n -> p k n", p=128)
        for k0 in range(0, 8, 2):
            load_cast(c, wb, wb[:, k0:k0 + 2, :], w3[:, k0:k0 + 2, :])

        ub = Rot([c.sb([128, 8, 512], BF16, "ub%d" % i) for i in range(2)])
        xc = [c.sb([128, 515], F32, "xc%d" % j) for j in range(6)]
        for j in range(6):
            c.op("pool", lambda e: e.memset(xc[j][:], 0.0), writes=[xc[j]])
        ft = [Rot([c.sb([128, 512], F32, "ft%d_%d" % (j, i)) for i in range(2)]) for j in range(6)]
        szr = [Rot([c.sb([128, 512], F32, "sz%d_%d" % (h, i)) for i in range(2)]) for h in range(2)]
        sq4 = [c.sb([128, 512], F32, "sq%d" % i) for i in range(4)]
        rstd4 = [c.sb([128, 512], F32, "rstd%d" % i) for i in range(4)]
        obr = Rot([c.sb([128, 2, 512], BF16, "ob%d" % i) for i in range(2)])
        S = [c.sb([128, 128], F32, "S%d" % h) for h in range(2)]
        for h in range(2):
            c.op("pool", lambda e: e.memset(S[h][:], 0.0), writes=[S[h]])
        bt = Rot([c.sb([128, 4, 2], F32, "bt%d" % i) for i in range(2)])
        gg = Rot([c.sb([128, 4, 2], F32, "gg%d" % i) for i in range(2)])
        gtmp = Rot([c.sb([128, 4, 2], F32, "gtmp%d" % i) for i in range(6)])
        sm2 = Rot([c.sb([128, 2], F32, "sm2_%d" % i) for i in range(24)])
        sm1 = Rot([c.sb([128, 1], F32, "sm1_%d" % i) for i in range(8)])
        NSQ = 96
        sq128 = Rot([c.sb([128, 128], F32, "m%d" % i) for i in range(NSQ)])
        LL = {nm: [c.sb([128, 128], F32, "%s%d" % (nm, i)) for i in range(8)]
              for nm in ("dec", "egrow", "kd", "attT", "qg", "usb", "wTs", "otok")}
        pbig = Rot([c.ps([128, 512], F32, "pbig%d" % i) for i in range(2)])
        banks = [c.ps([128, 512], F32, "pbank%d" % i) for i in range(6)]
        psm = Rot([banks[i].view(banks[i].t[:, 0:128]) for i in range(6)])

        u3 = uT[:].rearrange("(k p) n -> p k n", p=128)
        o3 = oT[:].rearrange("(h p) n -> p h n", p=128)

        def mm(out_t, out_ap, lhsT, rhs, reads, start=True, stop=True):
            c.op("pe", lambda e: e.matmul(out_ap, lhsT=lhsT, rhs=rhs, start=start, stop=stop),
                 reads=reads, writes=[out_t], inc=stop)

        def evac(eng, dst_t, dst_ap, src_t, src_ap, scale=None, extra=()):
            if eng == "act":
                if scale is None:
                    c.op("act", lambda e: e.activation(out=dst_ap, in_=src_ap, func=AF.Copy),
                         reads=[src_t], writes=[dst_t])
                else:
                    c.op("act", lambda e: e.activation(out=dst_ap, in_=src_ap, func=AF.Copy, scale=scale),
                         reads=[src_t] + list(extra), writes=[dst_t])
            else:
                c.op("dve", lambda e: e.tensor_copy(out=dst_ap, in_=src_ap), reads=[src_t], writes=[dst_t])

        for (b0, ntl) in blocks:
            n = ntl * 128
            u = ub.get()
            c.dma("sp", u[:, :, :n], u3[:, :, b0:b0 + n], writes=[u])
            pba_t = psm.get()
            for tt in range(ntl):
                for k in range(8):
                    mm(pba_t, pba_t[:, tt * 4:(tt + 1) * 4], u[:, k, tt * 128:(tt + 1) * 128], wb[:, k, 1024:1028],
                       [u, wb], start=(k == 0), stop=(k == 7))
            pba = pba_t[:, 0:4 * ntl].rearrange("p (t f) -> p t f", f=4)
            btb = bt.get()
            ggb = gg.get()
            c.op("act", lambda e: e.activation(out=btb[:, :ntl, :], in_=pba[:, :, 0:2], func=AF.Sigmoid),
                 reads=[pba_t], writes=[btb])
            x_ = gtmp.get(); ax = gtmp.get(); rl = gtmp.get()
            for h in range(2):
                c.op("dve", lambda e: e.tensor_scalar(out=x_[:, :ntl, h:h + 1], in0=pba[:, :, 2 + h:3 + h],
                                                      scalar1=ps_[:, 26 + h:27 + h], scalar2=None, op0=ALU.add),
                     reads=[pba_t, ps_], writes=[x_])
            c.op("act", lambda e: e.activation(out=ax[:, :ntl, :], in_=x_[:, :ntl, :], func=AF.Abs),
                 reads=[x_], writes=[ax])
            c.op("act", lambda e: e.activation(out=ax[:, :ntl, :], in_=ax[:, :ntl, :], func=AF.Exp, scale=-1.0),
                 reads=[ax], writes=[ax])
            c.op("act", lambda e: e.activation(out=ax[:, :ntl, :], in_=ax[:, :ntl, :], func=AF.Ln, bias=one_t[:, 0:1]),
                 reads=[ax, one_t], writes=[ax])
            c.op("dve", lambda e: e.tensor_scalar(out=rl[:, :ntl, :], in0=x_[:, :ntl, :], scalar1=0.0, scalar2=None,
                                                  op0=ALU.max), reads=[x_], writes=[rl])
            c.op("dve", lambda e: e.tensor_tensor(out=rl[:, :ntl, :], in0=rl[:, :ntl, :], in1=ax[:, :ntl, :], op=ALU.add),
                 reads=[rl, ax], writes=[rl])
            for h in range(2):
                c.op("dve", lambda e: e.tensor_scalar(out=ggb[:, :ntl, h:h + 1], in0=rl[:, :ntl, h:h + 1],
                                                      scalar1=nea[:, h:h + 1], scalar2=None, op0=ALU.mult),
                     reads=[rl, nea], writes=[ggb])
            F = []
            for j in range(8):
                p = pbig.get()
                for k in range(8):
                    mm(p, p[:, :n], wb[:, k, j * 128:(j + 1) * 128], u[:, k, :n], [wb, u], start=(k == 0), stop=(k == 7))
                if j < 6:
                    c.op("act", lambda e: e.activation(out=xc[j][:, 3:3 + n], in_=p[:, :n], func=AF.Copy),
                         reads=[p], writes=[xc[j]])
                    f = ft[j].get()
                    c.op("dve", lambda e: e.tensor_scalar(out=f[:, :n], in0=xc[j][:, 0:n], scalar1=ps_[:, j * 4:j * 4 + 1],
                                                          scalar2=None, op0=ALU.mult), reads=[xc[j], ps_], writes=[f])
                    for tp in range(1, 4):
                        c.op("dve", lambda e: e.scalar_tensor_tensor(out=f[:, :n], in0=xc[j][:, tp:tp + n],
                                                                     scalar=ps_[:, j * 4 + tp:j * 4 + tp + 1], in1=f[:, :n],
                                                                     op0=ALU.mult, op1=ALU.add),
                             reads=[xc[j], ps_, f], writes=[f])
                    c.op("pool", lambda e: e.tensor_copy(out=xc[j][:, 0:3], in_=xc[j][:, n:n + 3]),
                         reads=[xc[j]], writes=[xc[j]])
                    c.op("act", lambda e: e.activation(out=f[:, :n], in_=f[:, :n], func=AF.Silu), reads=[f], writes=[f])
                    F.append(f)
                else:
                    z = szr[j - 6].get()
                    c.op("act", lambda e: e.activation(out=z[:, :n], in_=p[:, :n], func=AF.Silu), reads=[p], writes=[z])
                    F.append(z)
            pst4 = [pbig.get(), pbig.get(), banks[4], banks[5]]
            for j in range(4):
                c.op("act", lambda e: e.activation(out=sq4[j][:, :n], in_=F[j][:, :n], func=AF.Square),
                     reads=[F[j]], writes=[sq4[j]])
            for j in range(4):
                c.op("pe", lambda e: e.matmul(pst4[j][:, :n], lhsT=c.ones[:], rhs=sq4[j][:, :n], start=True, stop=True),
                     reads=[c.ones, sq4[j]], writes=[pst4[j]])
            for j in range(4):
                c.op("act", lambda e: e.activation(out=rstd4[j][:, :n], in_=pst4[j][:, :n], func=AF.Sqrt,
                                                   bias=c.eps_t[:, 0:1], scale=1.0),
                     reads=[pst4[j], c.eps_t], writes=[rstd4[j]])
            for j in range(4):
                c.op("dve", lambda e: e.reciprocal(out=rstd4[j][:, :n], in_=rstd4[j][:, :n]), reads=[rstd4[j]], writes=[rstd4[j]])
            for j in range(4):
                if j < 2:
                    c.op("dve", lambda e: e.scalar_tensor_tensor(out=F[j][:, :n], in0=F[j][:, :n], scalar=128.0 ** -0.5,
                                                                 in1=rstd4[j][:, :n], op0=ALU.mult, op1=ALU.mult),
                         reads=[F[j], rstd4[j]], writes=[F[j]])
                else:
                    c.op("dve", lambda e: e.tensor_tensor(out=F[j][:, :n], in0=F[j][:, :n], in1=rstd4[j][:, :n], op=ALU.mult),
                         reads=[F[j], rstd4[j]], writes=[F[j]])
            ob = obr.get()
            TT = list(range(ntl))
            CH = [(tt, h) for tt in TT for h in range(2)]
            ci = {ch: i for i, ch in enumerate(CH)}
            csl = {tt: slice(tt * 128, (tt + 1) * 128) for tt in TT}
            egc, ed, be = {}, {}, {}
            for tt in TT:
                pg1 = psm.get(); pg2 = psm.get()
                mm(pg1, pg1[:, 0:2], U2, ggb[:, tt, :], [cst, ggb])
                mm(pg2, pg2[:, 0:2], R2, ggb[:, tt, :], [cst, ggb])
                egc[tt] = sm2.get(); ed[tt] = sm2.get(); be[tt] = sm2.get()
                c.op("act", lambda e: e.activation(out=egc[tt][:], in_=pg1[:, 0:2], func=AF.Exp), reads=[pg1], writes=[egc[tt]])
                c.op("act", lambda e: e.activation(out=ed[tt][:], in_=pg2[:, 0:2], func=AF.Exp), reads=[pg2], writes=[ed[tt]])
                c.op("dve", lambda e: e.tensor_tensor(out=be[tt][:], in0=egc[tt][:], in1=btb[:, tt, :], op=ALU.mult),
                     reads=[egc[tt], btb], writes=[be[tt]])
            Ug, dec, decs, egrow, P, Q, Y = {}, {}, {}, {}, {}, {}, {}
            for ch in CH:
                tt, h = ch
                Ug[ch] = sq128.get()
                c.op("dve", lambda e: e.tensor_scalar(out=Ug[ch][:], in0=U2, scalar1=ggb[:, tt, h:h + 1], scalar2=None,
                                                      op0=ALU.mult), reads=[cst, ggb], writes=[Ug[ch]])
            for ch in CH:
                tt, h = ch
                pd = psm.get()
                mm(pd, pd[:], Ug[ch][:], c.ones[:], [Ug[ch], c.ones], start=True, stop=False)
                mm(pd, pd[:], negones, Ug[ch][:], [cst, Ug[ch]], start=False, stop=True)
                dec[ch] = LL["dec"][ci[ch]]
                c.op("dve", lambda e: e.tensor_tensor(out=dec[ch][:], in0=pd[:], in1=negmask, op=ALU.add),
                     reads=[pd, cst], writes=[dec[ch]])
                c.op("act", lambda e: e.activation(out=dec[ch][:], in_=dec[ch][:], func=AF.Exp),
                     reads=[dec[ch]], writes=[dec[ch]])
                decs[ch] = sq128.get()
                c.op("pool", lambda e: e.tensor_tensor(out=decs[ch][:], in0=dec[ch][:], in1=smask, op=ALU.mult),
                     reads=[dec[ch], cst], writes=[decs[ch]])
                pe_ = psm.get()
                mm(pe_, pe_[:], c.ones[:], Ug[ch][:], [c.ones, Ug[ch]])
                egrow[ch] = LL["egrow"][ci[ch]]
                c.op("act", lambda e: e.activation(out=egrow[ch][:], in_=pe_[:], func=AF.Exp),
                     reads=[pe_], writes=[egrow[ch]])
            for ch in CH:
                tt, h = ch
                kT = F[2 + h][:, csl[tt]]
                pk = psm.get()
                mm(pk, pk[:], kT, kT, [F[2 + h]])
                P[ch] = sq128.get()
                c.op("dve", lambda e: e.scalar_tensor_tensor(out=P[ch][:], in0=pk[:], scalar=btb[:, tt, h:h + 1],
                                                             in1=decs[ch][:], op0=ALU.mult, op1=ALU.mult),
                     reads=[pk, btb, decs[ch]], writes=[P[ch]])
            for ch in CH:
                pb = psm.get()
                mm(pb, pb[:], P[ch][:], ident, [P[ch], cst])
                Q[ch] = sq128.get(); Y[ch] = sq128.get()
                evac("act", Q[ch], Q[ch][:], pb, pb[:])
                c.op("pool", lambda e: e.tensor_tensor(out=Y[ch][:], in0=ident, in1=Q[ch][:], op=ALU.subtract),
                     reads=[Q[ch], cst], writes=[Y[ch]])
            for s in range(5):
                Pn, Qn = {}, {}
                for ch in CH:
                    pp_ = psm.get()
                    mm(pp_, pp_[:], Q[ch][:], P[ch][:], [Q[ch], P[ch]])
                    Pn[ch] = sq128.get()
                    evac("act", Pn[ch], Pn[ch][:], pp_, pp_[:])
                    if s < 4:
                        pq = psm.get()
                        mm(pq, pq[:], P[ch][:], Q[ch][:], [P[ch], Q[ch]])
                        Qn[ch] = sq128.get()
                        evac("dve", Qn[ch], Qn[ch][:], pq, pq[:])
                for ch in CH:
                    py_ = psm.get()
                    mm(py_, py_[:], Pn[ch][:], Y[ch][:], [Pn[ch], Y[ch]])
                    Yn = sq128.get()
                    c.op("dve", lambda e: e.tensor_tensor(out=Yn[:], in0=py_[:], in1=Y[ch][:], op=ALU.add),
                         reads=[py_, Y[ch]], writes=[Yn])
                    Y[ch] = Yn
                    P[ch] = Pn[ch]
                    if s < 4:
                        Q[ch] = Qn[ch]
            kbg, kd, vb, usb, wTs, attT, qg = {}, {}, {}, {}, {}, {}, {}
            for ch in CH:
                tt, h = ch
                kT = F[2 + h][:, csl[tt]]; vT = F[4 + h][:, csl[tt]]; qT = F[h][:, csl[tt]]
                pkt = psm.get()
                mm(pkt, pkt[:], kT, ident, [F[2 + h], cst])
                kbg[ch] = sq128.get(); kd[ch] = LL["kd"][ci[ch]]
                evac("act", kbg[ch], kbg[ch][:], pkt, pkt[:], scale=be[tt][:, h:h + 1], extra=[be[tt]])
                evac("act", kd[ch], kd[ch][:], pkt, pkt[:], scale=ed[tt][:, h:h + 1], extra=[ed[tt]])
                pvt = psm.get()
                mm(pvt, pvt[:], vT, ident, [F[4 + h], cst])
                vb[ch] = sq128.get()
                evac("act", vb[ch], vb[ch][:], pvt, pvt[:], scale=btb[:, tt, h:h + 1], extra=[btb])
                pqk = psm.get()
                mm(pqk, pqk[:], qT, kT, [F[h], F[2 + h]])
                att = sq128.get()
                c.op("dve", lambda e: e.tensor_tensor(out=att[:], in0=pqk[:], in1=dec[ch][:], op=ALU.mult),
                     reads=[pqk, dec[ch]], writes=[att])
                pat = psm.get()
                mm(pat, pat[:], att[:], ident, [att, cst])
                attT[ch] = LL["attT"][ci[ch]]
                evac("dve", attT[ch], attT[ch][:], pat, pat[:])
                qg[ch] = LL["qg"][ci[ch]]
                c.op("pool", lambda e: e.tensor_tensor(out=qg[ch][:], in0=qT, in1=egrow[ch][:], op=ALU.mult),
                     reads=[F[h], egrow[ch]], writes=[qg[ch]])
            for ch in CH:
                pu = psm.get()
                mm(pu, pu[:], Y[ch][:], vb[ch][:], [Y[ch], vb[ch]])
                usb[ch] = LL["usb"][ci[ch]]
                evac("act", usb[ch], usb[ch][:], pu, pu[:])
                pw = psm.get()
                mm(pw, pw[:], kbg[ch][:], Y[ch][:], [kbg[ch], Y[ch]])
                wTs[ch] = LL["wTs"][ci[ch]]
                evac("dve", wTs[ch], wTs[ch][:], pw, pw[:])
            otok = {ch: LL["otok"][ci[ch]] for ch in CH}
            for tt in TT:
                vnew = {h: sq128.get() for h in range(2)}
                for half in range(2):
                    r = slice(half * 64, half * 64 + 64)
                    for h in range(2):
                        ch = (tt, h)
                        pws = psm.get()
                        mm(pws, pws[:], wTs[ch][:], S[h][:], [wTs[ch], S[h]])
                        c.op("dve", lambda e: e.tensor_tensor(out=vnew[h][r, :], in0=usb[ch][r, :], in1=pws[r, :], op=ALU.subtract),
                             reads=[usb[ch], pws], writes=[vnew[h]])
                    for h in range(2):
                        ch = (tt, h)
                        po = psm.get()
                        mm(po, po[:], qg[ch][:], S[h][:], [qg[ch], S[h]], start=True, stop=False)
                        mm(po, po[:], attT[ch][r, :], vnew[h][r, :], [attT[ch], vnew[h]], start=False, stop=True)
                        evac("act", otok[ch], otok[ch][r, :], po, po[r, :])
                        pst = psm.get()
                        mm(pst, pst[:], kd[ch][r, :], vnew[h][r, :], [kd[ch], vnew[h]])
                        c.op("dve", lambda e: e.scalar_tensor_tensor(out=S[h][:], in0=S[h][:],
                                                                     scalar=egrow[ch][:, half * 64 + 63:half * 64 + 64],
                                                                     in1=pst[:], op0=ALU.mult, op1=ALU.add),
                             reads=[S[h], egrow[ch], pst], writes=[S[h]])
            ssd, ond, potd = {}, {}, {}
            for ch in CH:
                junk = sq128.get()
                ssd[ch] = sm1.get()
                c.op("act", lambda e: e.activation(out=junk[:], in_=otok[ch][:], func=AF.Square, accum_out=ssd[ch][:]),
                     reads=[otok[ch]], writes=[junk, ssd[ch]])
            for ch in CH:
                c.op("act", lambda e: e.activation(out=ssd[ch][:], in_=ssd[ch][:], func=AF.Sqrt, bias=c.eps_t[:, 0:1], scale=1.0 / 128),
                     reads=[ssd[ch], c.eps_t], writes=[ssd[ch]])
            for ch in CH:
                c.op("dve", lambda e: e.reciprocal(out=ssd[ch][:], in_=ssd[ch][:]), reads=[ssd[ch]], writes=[ssd[ch]])
            for ch in CH:
                ond[ch] = sq128.get()
                c.op("dve", lambda e: e.tensor_scalar(out=ond[ch][:], in0=otok[ch][:], scalar1=ssd[ch][:, 0:1], scalar2=None, op0=ALU.mult),
                     reads=[otok[ch], ssd[ch]], writes=[ond[ch]])
            for ch in CH:
                tt, h = ch
                pot = psm.get()
                mm(pot, pot[:], ond[ch][:], ident, [ond[ch], cst])
                c.op("dve", lambda e: e.scalar_tensor_tensor(out=ob[:, h, csl[tt]], in0=pot[:], scalar=ps_[:, 28:29],
                                                             in1=F[6 + h][:, csl[tt]], op0=ALU.mult, op1=ALU.mult),
                     reads=[pot, ps_, F[6 + h]], writes=[ob])
            c.dma("sp", o3[:, :, b0:b0 + n], ob[:, :, :n], reads=[ob], writes=[oT])


def dn_params(conv_w6, a_log2, dt_bias2, norm_w):
    prm = np.zeros((128, 32), np.float32)
    for j in range(6):
        prm[:, j * 4:(j + 1) * 4] = conv_w6[:, j * 128:(j + 1) * 128].T
    prm[:, 24:26] = a_log2[None, :]
    prm[:, 26:28] = dt_bias2[None, :]
    prm[:, 28] = norm_w
    return prm


DA_DS = (-128, 0, 128, 256, 384)
DA_NEGM = -240000.0
LAMBDA_INIT_L1 = 0.8 - 0.6 * math.exp(-0.3 * 1)


def _t5_bucket_np(rel):
    import jax
    import jax.numpy as jnp
    with jax.default_device(jax.devices("cpu")[0]):
        rel = jnp.asarray(rel, jnp.int32)
        nb = 16
        ret = jnp.where(rel > 0, nb, 0)
        n = jnp.abs(rel)
        max_exact = nb // 2
        nf = jnp.maximum(n, 1).astype(jnp.float32)
        large = max_exact + (jnp.log(nf / max_exact) / math.log(128 / max_exact)
                             * (nb - max_exact)).astype(jnp.int32)
        large = jnp.minimum(large, nb - 1)
        return np.asarray(ret + jnp.where(n < max_exact, n, large))


def da_consts():
    r = np.arange(-639, 513)
    bk = _t5_bucket_np(r)
    oh = np.zeros((32, 1152), np.float32)
    oh[bk, np.arange(1152)] = 1.0
    oh[15, :] -= 1.0
    kk = np.arange(128)[:, None]
    qq = np.arange(512)[None, :]
    md = np.zeros((128, 5, 512), np.float32)
    for i, d in enumerate(DA_DS):
        allowed = ((d + kk) // 64) <= (qq // 64)
        md[:, i, :] = np.where(allowed, 0.0, DA_NEGM)
    pm = np.zeros((128, 1), np.float32)
    pm[:112] = -30000.0
    return oh, md, pm


def build_da(lambda_init=LAMBDA_INIT_L1):
    NKT = TD // 128
    qtiles = [(0, 128)] + [(128 + 512 * i, 512) for i in range(16)]
    nc = bass.Bass("TRN2", target_bir_lowering=False)
    with ExitStack() as es:
        c = Ctx(nc, es)
        io = {}
        io["uT"] = c.dram("uT", [D, TD], BF16, "ExternalInput")
        io["wcat"] = c.dram("wcat", [D, 768], F32, "ExternalInput")
        io["oh"] = c.dram("oh", [32, 1152], F32, "ExternalInput")
        io["md"] = c.dram("md", [128, 5, 512], F32, "ExternalInput")
        io["pm"] = c.dram("pm", [128, 1], F32, "ExternalInput")
        io["rb"] = c.dram("rb", [32, 2], F32, "ExternalInput")
        io["rb15"] = c.dram("rb15", [128, 2], F32, "ExternalInput")
        io["lamv"] = c.dram("lamv", [128, 4, 64], F32, "ExternalInput")
        io["sw"] = c.dram("sw", [128, 1], F32, "ExternalInput")
        io["identf"] = c.dram("identf", [128, 128], F32, "ExternalInput")
        io["oT"] = c.dram("oT", [256, TD], BF16, "ExternalOutput")
        make_consts(c)
        emit_da(c, io, lambda_init, "")
        c.finish()
    return nc


def emit_da(c, io, lambda_init, tag):
    NKT = TD // 128
    qtiles = [(0, 128)] + [(128 + 512 * i, 512) for i in range(16)]
    if True:
        uT, wcat, ohd, mdd, pmd, rb, rb15, lamv, sw, idd, oT = (io[k] for k in (
            "uT", "wcat", "oh", "md", "pm", "rb", "rb15", "lamv", "sw", "identf", "oT"))
        tvd = c.dram("tvscr" + tag, [2, 1152], F32, "Internal")
        identb = c.sb([128, 128], BF16, "identb")
        onesb = c.sb([128, 128], BF16, "onesb")
        c.op("pool", lambda e: e.memset(onesb[:], 1.0), writes=[onesb])
        QT = [c.sb([128, TD], BF16, "QT%d" % h) for h in range(2)]
        KT = [c.sb([128, TD], BF16, "KT%d" % h) for h in range(2)]
        V = c.sb([128, NKT, 256], BF16, "V")
        BH = [[c.sb([128, 512], BF16, "BH%d_%d" % (h, i)) for i in range(5)] for h in range(2)]
        BL = [[c.sb([128, 512], BF16, "BL%d_%d" % (h, i)) for i in range(5)] for h in range(2)]
        biasc = c.sb([128, 2], F32, "biasc")
        bias0 = c.sb([128, 2], F32, "bias0")
        neglam = c.sb([128, 1], F32, "neglam")
        swp = c.sb([128, 1], F32, "swp")

        with ExitStack() as es1:
            c.dma("sp", biasc[:], rb15[:], writes=[biasc])
            pms = c.sb([128, 1], F32, "pms", es=es1)
            c.dma("sp", pms[:], pmd[:], writes=[pms])
            c.op("dve", lambda e: e.tensor_scalar(out=bias0[:], in0=biasc[:], scalar1=pms[:, 0:1], scalar2=None, op0=ALU.add),
                 reads=[biasc, pms], writes=[bias0])
            sws = c.sb([128, 1], F32, "sws", es=es1)
            c.dma("sp", sws[:], sw[:], writes=[sws])
            c.op("dve", lambda e: e.tensor_scalar(out=swp[:], in0=sws[:], scalar1=1.0 - lambda_init, scalar2=None, op0=ALU.mult),
                 reads=[sws], writes=[swp])
            lv = c.sb([128, 4, 64], F32, "lv", es=es1)
            c.dma("sp", lv[:], lamv[:], writes=[lv])
            pr = c.sb([128, 2, 64], F32, "lpr", es=es1)
            sm = c.sb([128, 2], F32, "lsm", es=es1)
            for i in range(2):
                c.op("dve", lambda e: e.tensor_tensor(out=pr[:, i, :], in0=lv[:, 2 * i, :], in1=lv[:, 2 * i + 1, :], op=ALU.mult),
                     reads=[lv], writes=[pr])
                c.op("dve", lambda e: e.reduce_sum(out=sm[:, i:i + 1], in_=pr[:, i, :], axis=AX.X), reads=[pr], writes=[sm])
            c.op("act", lambda e: e.activation(out=sm[:], in_=sm[:], func=AF.Exp), reads=[sm], writes=[sm])
            c.op("dve", lambda e: e.tensor_tensor(out=neglam[:], in0=sm[:, 1:2], in1=sm[:, 0:1], op=ALU.subtract),
                 reads=[sm], writes=[neglam])
            c.op("dve", lambda e: e.tensor_scalar(out=neglam[:], in0=neglam[:], scalar1=-lambda_init, scalar2=None, op0=ALU.add),
                 reads=[neglam], writes=[neglam])
            idf = c.sb([128, 128], F32, "idf", es=es1)
            c.dma("sp", idf[:], idd[:], writes=[idf])
            c.op("dve", lambda e: e.tensor_copy(out=identb[:], in_=idf[:]), reads=[idf], writes=[identb])
            ohs = c.sb([32, 1152], F32, "ohs", es=es1)
            c.dma("sp", ohs[:], ohd[:], writes=[ohs])
            rbs = c.sb([32, 2], F32, "rbs", es=es1)
            c.dma("sp", rbs[:], rb[:], writes=[rbs])
            tvs = c.sb([2, 1152], F32, "tvs", es=es1)
            ptv = c.ps([128, 512], F32, "ptv", es=es1)
            for j in range(3):
                c.op("pe", lambda e: e.matmul(ptv[0:2, 0:384], lhsT=rbs[:], rhs=ohs[:, j * 384:(j + 1) * 384], start=True, stop=True),
                     reads=[rbs, ohs], writes=[ptv])
                c.op("act", lambda e: e.activation(out=tvs[:, j * 384:(j + 1) * 384], in_=ptv[0:2, 0:384], func=AF.Copy),
                     reads=[ptv], writes=[tvs])
            c.dma("sp", tvd[:], tvs[:], reads=[tvs], writes=[tvd])
            mds = c.sb([128, 5, 512], F32, "mds", es=es1)
            c.dma("sp", mds[:], mdd[:], writes=[mds])
            G = Rot([c.sb([128, 512], F32, "G%d" % i, es=es1) for i in range(2)])
            Bt = Rot([c.sb([128, 512], F32, "Bt%d" % i, es=es1) for i in range(2)])
            for h in range(2):
                for i, d in enumerate(DA_DS):
                    g_ = G.get()
                    src = bass.AP(tensor=tvd.t.tensor, offset=h * 1152 + d + 128, ap=[[1, 128], [1, 512]])
                    c.dma("sp", g_[:], src, reads=[tvd], writes=[g_])
                    b_ = Bt.get()
                    c.op("dve", lambda e: e.scalar_tensor_tensor(out=b_[:], in0=g_[:, ::-1], scalar=8.0, in1=mds[:, i, :],
                                                                 op0=ALU.mult, op1=ALU.add), reads=[g_, mds], writes=[b_])
                    c.op("act", lambda e: e.activation(out=BH[h][i][:], in_=b_[:], func=AF.Copy), reads=[b_], writes=[BH[h][i]])
                    c.op("dve", lambda e: e.tensor_tensor(out=BL[h][i][:], in0=b_[:], in1=BH[h][i][:], op=ALU.subtract),
                         reads=[b_, BH[h][i]], writes=[BL[h][i]])

        c.barrier()
        with ExitStack() as es2:
            wb = c.sb([128, 8, 768], BF16, "wb", es=es2)
            w3 = wcat[:].rearrange("(k p) n -> p k n", p=128)
            for k0 in range(0, 8, 2):
                load_cast(c, wb, wb[:, k0:k0 + 2, :], w3[:, k0:k0 + 2, :])
            ub = Rot([c.sb([128, 8, 512], BF16, "ub%d" % i, es=es2) for i in range(2)])
            pbig = Rot([c.ps([128, 512], F32, "pb%d" % i, es=es2) for i in range(6)])
            u3 = uT[:].rearrange("(k p) n -> p k n", p=128)
            for (b0, ntl) in DN_BLOCKS:
                n = ntl * 128
                u = ub.get()
                c.dma("sp", u[:, :, :n], u3[:, :, b0:b0 + n], writes=[u])
                for j in range(4):
                    p = pbig.get()
                    for k in range(8):
                        c.op("pe", lambda e: e.matmul(p[:, :n], lhsT=wb[:, k, j * 128:(j + 1) * 128], rhs=u[:, k, :n],
                                                      start=(k == 0), stop=(k == 7)), reads=[wb, u], writes=[p], inc=(k == 7))
                    dst = (QT[j] if j < 2 else KT[j - 2])
                    if j % 2 == 0:
                        c.op("act", lambda e: e.activation(out=dst[:, b0:b0 + n], in_=p[:, :n], func=AF.Copy), reads=[p], writes=[dst])
                    else:
                        c.op("dve", lambda e: e.tensor_copy(out=dst[:, b0:b0 + n], in_=p[:, :n]), reads=[p], writes=[dst])
                for tt in range(ntl):
                    p = pbig.get()
                    for k in range(8):
                        c.op("pe", lambda e: e.matmul(p[:, 0:256], lhsT=u[:, k, tt * 128:(tt + 1) * 128], rhs=wb[:, k, 512:768],
                                                      start=(k == 0), stop=(k == 7)), reads=[wb, u], writes=[p], inc=(k == 7))
                    kt = b0 // 128 + tt
                    if tt % 2 == 0:
                        c.op("act", lambda e: e.activation(out=V[:, kt, :], in_=p[:, 0:256], func=AF.Copy), reads=[p], writes=[V])
                    else:
                        c.op("dve", lambda e: e.tensor_copy(out=V[:, kt, :], in_=p[:, 0:256]), reads=[p], writes=[V])

        c.barrier()
        with ExitStack() as es3:
            sps2 = Rot([c.ps([128, 1024], F32, "sps%d" % i, es=es3) for i in range(2)])
            oacc = [c.ps([128, 512], F32, "oacc%d" % i, es=es3) for i in range(2)]
            dacc = [c.ps([128, 512], F32, "dacc%d" % i, es=es3) for i in range(2)]
            ptb2 = Rot([c.sb([128, 1024], BF16, "pt%d" % i, es=es3) for i in range(3)])
            rr = [c.sb([128, 512], F32, "rr%d" % i, es=es3) for i in range(2)]
            aa = [c.sb([128, 512], F32, "aa%d" % i, es=es3) for i in range(2)]
            sqb = Rot([c.sb([128, 512], F32, "sq%d" % i, es=es3) for i in range(2)])
            rstd = c.sb([128, 512], F32, "rstd", es=es3)
            obr = Rot([c.sb([128, 512], BF16, "ob%d" % i, es=es3) for i in range(2)])
            dsr2 = Rot([c.sb([128, 1024], F32, "dsum%d" % i, es=es3) for i in range(2)])

            def both(t, nq):
                return t[:, :].rearrange("p (c n) -> p c n", c=2)[:, :, :nq]

            for (q0, nq) in qtiles:
                ktmax = (q0 + nq) // 128 - 1
                for h in range(2):
                    dsum = dsr2.get()

                    def emit_s(kt):
                        ps = sps2.get()
                        d = kt * 128 - q0
                        near = d in DA_DS
                        for cc in range(2):
                            rs = slice(cc * 64, cc * 64 + 64)
                            o_ = ps[:, cc * 512:cc * 512 + nq]
                            c.op("pe", lambda e: e.matmul(o_, lhsT=KT[h][rs, kt * 128:(kt + 1) * 128], rhs=QT[h][rs, q0:q0 + nq],
                                                          start=True, stop=not near), reads=[KT[h], QT[h]], writes=[ps], inc=not near)
                            if near:
                                i = DA_DS.index(d)
                                c.op("pe", lambda e: e.matmul(o_, lhsT=identb[:], rhs=BH[h][i][:, :nq], start=False, stop=False),
                                     reads=[identb, BH[h][i]], writes=[ps], inc=False)
                                c.op("pe", lambda e: e.matmul(o_, lhsT=identb[:], rhs=BL[h][i][:, :nq], start=False, stop=True),
                                     reads=[identb, BL[h][i]], writes=[ps])
                        return ps

                    ps_cur = emit_s(0)
                    for kt in range(ktmax + 1):
                        ps_next = emit_s(kt + 1) if kt < ktmax else None
                        pt = ptb2.get()
                        bsrc = bias0 if kt == 0 else biasc
                        c.op("act", lambda e: e.activation(out=both(pt, nq), in_=both(ps_cur, nq), func=AF.Exp,
                                                           bias=bsrc[:, h:h + 1], scale=0.125),
                             reads=[ps_cur, bsrc], writes=[pt])
                        for cc in range(2):
                            c.op("pe", lambda e: e.matmul(oacc[cc][:, :nq], lhsT=V[:, kt, h * 128:(h + 1) * 128],
                                                          rhs=pt[:, cc * 512:cc * 512 + nq],
                                                          start=(kt == 0), stop=(kt == ktmax)), reads=[V, pt], writes=[oacc[cc]])
                        if kt == 0:
                            c.op("dve", lambda e: e.tensor_copy(out=both(dsum, nq), in_=both(pt, nq)), reads=[pt], writes=[dsum])
                        else:
                            c.op("dve", lambda e: e.tensor_tensor(out=both(dsum, nq), in0=both(dsum, nq), in1=both(pt, nq), op=ALU.add),
                                 reads=[pt, dsum], writes=[dsum])
                        ps_cur = ps_next
                    for cc in range(2):
                        c.op("pe", lambda e: e.matmul(dacc[cc][:, :nq], lhsT=c.ones[:], rhs=dsum[:, cc * 512:cc * 512 + nq], start=True, stop=True),
                             reads=[c.ones, dsum], writes=[dacc[cc]])
                    for cc in range(2):
                        if q0 == 0:
                            c.op("dve", lambda e: e.tensor_scalar(out=rr[cc][:, :nq], in0=dacc[cc][:, :nq], scalar1=1e-30, scalar2=None,
                                                                  op0=ALU.max), reads=[dacc[cc]], writes=[rr[cc]])
                            c.op("dve", lambda e: e.reciprocal(out=rr[cc][:, :nq], in_=rr[cc][:, :nq]), reads=[rr[cc]], writes=[rr[cc]])
                        else:
                            c.op("dve", lambda e: e.reciprocal(out=rr[cc][:, :nq], in_=dacc[cc][:, :nq]), reads=[dacc[cc]], writes=[rr[cc]])
                        c.op("dve", lambda e: e.tensor_tensor(out=aa[cc][:, :nq], in0=oacc[cc][:, :nq], in1=rr[cc][:, :nq], op=ALU.mult),
                             reads=[oacc[cc], rr[cc]], writes=[aa[cc]])
                    c.op("dve", lambda e: e.scalar_tensor_tensor(out=aa[0][:, :nq], in0=aa[1][:, :nq], scalar=neglam[:, 0:1],
                                                                 in1=aa[0][:, :nq], op0=ALU.mult, op1=ALU.add),
                         reads=[aa[0], aa[1], neglam], writes=[aa[0]])
                    pstat_ = sps2.get()
                    rms_stats(c, c.ones, lambda k: (aa[0][:, :nq], aa[0]), 1, nq, pstat_.view(pstat_.t[:, 0:512]), sqb, rstd, 128.0)
                    ob = obr.get()
                    c.op("dve", lambda e: e.scalar_tensor_tensor(out=ob[:, :nq], in0=aa[0][:, :nq], scalar=swp[:, 0:1],
                                                                 in1=rstd[:, :nq], op0=ALU.mult, op1=ALU.mult),
                         reads=[aa[0], swp, rstd], writes=[ob])
                    if q0 == 0:
                        c.op("dve", lambda e: e.memset(ob[:, 0:112], 0.0), writes=[ob])
                    c.dma("sp", oT[h * 128:(h + 1) * 128, q0:q0 + nq], ob[:, :nq], reads=[ob], writes=[oT])


def build_pre():
    nc = bass.Bass("TRN2", target_bir_lowering=False)
    with ExitStack() as es:
        c = Ctx(nc, es)
        io = {}
        io["hT"] = c.dram("hT", [D, NTC], F32, "ExternalInput")
        io["nw"] = c.dram("nw", [128, 8], F32, "ExternalInput")
        io["uout"] = c.dram("uout", [D, NTC], BF16, "ExternalOutput")
        make_consts(c)
        emit_pre(c, io)
        c.finish()
    return nc


def emit_pre(c, io):
    if True:
        hT, nw, uout = io["hT"], io["nw"], io["uout"]
        nws = c.sb([128, 8], F32, "nws")
        c.dma("sp", nws[:], nw[:], writes=[nws])
        H = Rot([c.sb([128, 8, GS], F32, "H%d" % g) for g in range(2)])
        U = Rot([c.sb([128, 8, GS], BF16, "U%d" % g) for g in range(2)])
        sqb = Rot([c.sb([128, GS], F32, "sq%d" % i) for i in range(2)])
        rstd = c.sb([128, GS], F32, "rstd")
        pstat = c.ps([128, 512], F32, "pstat")
        hT3 = hT[:].rearrange("(k p) n -> p k n", p=128)
        uo3 = uout[:].rearrange("(k p) n -> p k n", p=128)
        for g in range(NG):
            h = H.get()
            c.dma("sp", h[:], hT3[:, :, g * GS:(g + 1) * GS], writes=[h])
            rms_stats(c, c.ones, lambda k: (h[:, k, :], h), 8, GS, pstat, sqb, rstd, float(D))
            u = U.get()
            for m in range(8):
                c.op("dve", lambda e: e.scalar_tensor_tensor(out=u[:, m, :], in0=h[:, m, :], scalar=nws[:, m:m + 1],
                                                             in1=rstd[:], op0=ALU.mult, op1=ALU.mult),
                     reads=[h, nws, rstd], writes=[u])
            c.dma("sp", uo3[:, :, g * GS:(g + 1) * GS], u[:], reads=[u], writes=[uout])


def _run(nc, in_maps):
    res = run_bass_kernel_spmd(nc, in_maps, core_ids=list(range(NCORES)))
    return res.results


def _nwcols(w):
    return np.ascontiguousarray(np.asarray(w, np.float32).reshape(8, 128).T)


def _tok_shards(fullT):
    out = []
    for b in range(B):
        for j in range(4):
            out.append(np.ascontiguousarray(fullT[b][:, j * NTC:(j + 1) * NTC]))
    return out


def _gather_tok(shards):
    return [np.concatenate([shards[4 * b + j] for j in range(4)], axis=1) for b in range(B)]


def kernel_unfused(x, meta_tokens, rel_bias, norm_mix_w, norm_mlp_w, final_norm_w,
           dn_w_in, dn_conv_w, dn_a_log, dn_dt_bias, dn_norm_w, dn_w_out,
           da_w_in, da_lam_q1, da_lam_k1, da_lam_q2, da_lam_k2, da_subln_w, da_w_out,
           lru_w_in, lru_conv_w, lru_conv_b, lru_w_rgate, lru_b_rgate, lru_w_igate,
           lru_b_igate, lru_lambda, lru_w_out, mlp_w1, mlp_w2):
    f32 = np.float32
    x = np.asarray(x, f32)
    meta = np.asarray(meta_tokens, f32)
    bf = ml_dtypes.bfloat16
    hT_full = []
    for b in range(B):
        seq = np.concatenate([np.zeros((PADF, D), f32), meta, x[b]], axis=0)
        hT_full.append(np.ascontiguousarray(seq.T))
    h_sh = _tok_shards(hT_full)
    nc = build_pre()
    r = _run(nc, [{"hT": h_sh[c], "nw": _nwcols(norm_mix_w[0])} for c in range(NCORES)])
    u_sh = [r[c]["uout"] for c in range(NCORES)]
    depth = 4
    zpad = np.zeros((D, 64), bf)
    for layer in range(depth):
        kind = layer % 3
        slot = layer // 3
        u_full = _gather_tok(u_sh)
        ims = []
        if kind == 0:
            w_in = np.asarray(dn_w_in[slot], f32)
            cw = np.asarray(dn_conv_w[slot], f32)
            cst = dn_consts()
            for c in range(NCORES):
                b, g = divmod(c, 4)
                h0 = 2 * g
                s = slice(h0 * 128, h0 * 128 + 256)
                wcat = np.concatenate([w_in[:, 0:1024][:, s], w_in[:, 1024:2048][:, s], w_in[:, 2048:3072][:, s],
                                       w_in[:, 3072:4096][:, s], w_in[:, 4096 + h0:4096 + h0 + 2],
                                       w_in[:, 4104 + h0:4104 + h0 + 2]], axis=1)
                cw6 = np.concatenate([cw[:, 0:1024][:, s], cw[:, 1024:2048][:, s], cw[:, 2048:3072][:, s]], axis=1)
                prm = dn_params(cw6, np.asarray(dn_a_log[slot], f32)[h0:h0 + 2],
                                np.asarray(dn_dt_bias[slot], f32)[h0:h0 + 2], np.asarray(dn_norm_w[slot], f32))
                ims.append({"uT": np.ascontiguousarray(np.concatenate([zpad, u_full[b]], axis=1)),
                            "wcat": np.ascontiguousarray(wcat), "cst": cst, "prm": prm})
            r = _run(build_dn(), ims)
            o_full = [np.concatenate([r[4 * b + g]["oT"][:, 64:] for g in range(4)], axis=0) for b in range(B)]
            wo = np.asarray(dn_w_out[slot], f32)
        elif kind == 1:
            w_in = np.asarray(da_w_in[slot], f32)
            oh, md, pm = da_consts()
            rbt = np.asarray(rel_bias, f32)
            lamv = np.stack([np.asarray(v[slot], f32) for v in (da_lam_q1, da_lam_k1, da_lam_q2, da_lam_k2)], axis=0)
            lamv = np.ascontiguousarray(np.broadcast_to(lamv[None], (128, 4, 64)))
            sw = np.ascontiguousarray(np.asarray(da_subln_w[slot], f32)[:, None])
            ident = np.eye(128, dtype=f32)
            for c in range(NCORES):
                b, g = divmod(c, 4)
                h0 = 2 * g
                s = slice(h0 * 128, h0 * 128 + 256)
                wcat = np.concatenate([w_in[:, 0:1024][:, s], w_in[:, 1024:2048][:, s], w_in[:, 2048:3072][:, s]], axis=1)
                ims.append({"uT": np.ascontiguousarray(np.concatenate([zpad, u_full[b]], axis=1)),
                            "wcat": np.ascontiguousarray(wcat), "oh": oh, "md": md, "pm": pm,
                            "rb": np.ascontiguousarray(rbt[:, h0:h0 + 2]),
                            "rb15": np.ascontiguousarray(np.broadcast_to(rbt[15:16, h0:h0 + 2], (128, 2))),
                            "lamv": lamv, "sw": sw, "identf": ident})
            lam_init = 0.8 - 0.6 * math.exp(-0.3 * layer)
            r = _run(build_da(lam_init), ims)
            o_full = [np.concatenate([r[4 * b + g]["oT"][:, 64:] for g in range(4)], axis=0) for b in range(B)]
            wo = np.asarray(da_w_out[slot], f32)
        else:
            w_in = np.asarray(lru_w_in[slot], f32)
            cw = np.asarray(lru_conv_w[slot], f32)
            for c in range(NCORES):
                b, g = divmod(c, 4)
                s = slice(g * 256, (g + 1) * 256)
                prm = lru_params(cw[:, s], np.asarray(lru_conv_b[slot], f32)[s], np.asarray(lru_b_rgate[slot], f32)[s],
                                 np.asarray(lru_b_igate[slot], f32)[s], np.asarray(lru_lambda[slot], f32)[s])
                ims.append({"uT": u_full[b], "wg": np.ascontiguousarray(w_in[:, 0:1024][:, s]),
                            "wx": np.ascontiguousarray(w_in[:, 1024:2048][:, s]),
                            "wr": np.ascontiguousarray(np.asarray(lru_w_rgate[slot], f32)[g]),
                            "wi": np.ascontiguousarray(np.asarray(lru_w_igate[slot], f32)[g]), "prm": prm})
            r = _run(build_lru(), ims)
            o_full = [np.concatenate([r[4 * b + g]["yT"] for g in range(4)], axis=0) for b in range(B)]
            wo = np.asarray(lru_w_out[slot], f32)
        o_sh = _tok_shards(o_full)
        final = (layer == depth - 1)
        nxt = final_norm_w if final else norm_mix_w[layer + 1]
        nw = np.ascontiguousarray(np.concatenate([_nwcols(norm_mlp_w[layer]), _nwcols(nxt)], axis=1))
        w1 = np.asarray(mlp_w1[layer], f32)
        w2 = np.asarray(mlp_w2[layer], f32)
        r = _run(build_post(final), [{"hT": h_sh[c], "oT": o_sh[c], "wo": wo, "w1": w1, "w2": w2, "nw": nw}
                                     for c in range(NCORES)])
        h_sh = [r[c]["hout"] for c in range(NCORES)]
        u_sh = [r[c]["uout"] for c in range(NCORES)]
    out_full = _gather_tok(u_sh)
    out = np.stack([np.ascontiguousarray(out_full[b][:, PADF + NMETA:].T) for b in range(B)], axis=0)
    return out.astype(f32)


DEPTH = 4


def _phase(c, fn):
    base = c.es
    with ExitStack() as pes:
        c.es = pes
        fn()
        c.barrier()
    c.es = base


def build_fused():
    nc = bass.Bass("TRN2", target_bir_lowering=False)
    with ExitStack() as es:
        c = Ctx(nc, es)
        h0T = c.dram("h0T", [D, T], F32, "ExternalInput")
        outT = c.dram("outT", [D, T], F32, "ExternalOutput")
        HT = c.dram("HT", [D, T], F32, "Internal")
        UT = c.dram("UT", [D, TD], BF16, "Internal")
        OT = c.dram("OT", [D, TD], BF16, "Internal")
        nw0 = c.dram("nw0", [128, 8], F32, "ExternalInput")
        W = {}
        for l in range(DEPTH):
            kind = l % 3
            p = "L%d_" % l
            if kind == 0:
                W[p + "wcat"] = c.dram(p + "wcat", [4, D, 1028], F32, "ExternalInput")
                W[p + "prm"] = c.dram(p + "prm", [4, 128, 32], F32, "ExternalInput")
            elif kind == 1:
                W[p + "wcat"] = c.dram(p + "wcat", [4, D, 768], F32, "ExternalInput")
                W[p + "rb"] = c.dram(p + "rb", [4, 32, 2], F32, "ExternalInput")
                W[p + "rb15"] = c.dram(p + "rb15", [4, 128, 2], F32, "ExternalInput")
                W[p + "lamv"] = c.dram(p + "lamv", [128, 4, 64], F32, "ExternalInput")
                W[p + "sw"] = c.dram(p + "sw", [128, 1], F32, "ExternalInput")
            else:
                W[p + "wg"] = c.dram(p + "wg", [4, D, 256], F32, "ExternalInput")
                W[p + "wx"] = c.dram(p + "wx", [4, D, 256], F32, "ExternalInput")
                W[p + "wr"] = c.dram(p + "wr", [4, 256, 256], F32, "ExternalInput")
                W[p + "wi"] = c.dram(p + "wi", [4, 256, 256], F32, "ExternalInput")
                W[p + "prm"] = c.dram(p + "prm", [4, 128, 16], F32, "ExternalInput")
            W[p + "wo"] = c.dram(p + "wo", [D, D], F32, "ExternalInput")
            W[p + "w1"] = c.dram(p + "w1", [D, DFF], F32, "ExternalInput")
            W[p + "w2"] = c.dram(p + "w2", [DFF, D], F32, "ExternalInput")
            W[p + "nw"] = c.dram(p + "nw", [128, 16], F32, "ExternalInput")
        dn_cst = c.dram("dn_cst", [128, 6, 128], F32, "ExternalInput")
        da_oh = c.dram("da_oh", [32, 1152], F32, "ExternalInput")
        da_md = c.dram("da_md", [128, 5, 512], F32, "ExternalInput")
        da_pm = c.dram("da_pm", [128, 1], F32, "ExternalInput")
        identf = c.dram("identf", [128, 128], F32, "ExternalInput")
        make_consts(c)

        def sh(t, j, off=0):
            return t.view(t.t[:, off + j * NTC:off + (j + 1) * NTC])

        def zero_front():
            z = c.sb([128, 8, 64], BF16, "zfront")
            c.op("pool", lambda e: e.memset(z[:], 0.0), writes=[z])
            c.dma("sp", UT[:].rearrange("(k p) n -> p k n", p=128)[:, :, 0:64], z[:], reads=[z], writes=[UT])
        _phase(c, zero_front)
        for j in range(4):
            _phase(c, lambda j=j: emit_pre(c, {"hT": sh(h0T, j), "nw": nw0, "uout": sh(UT, j, 64)}))
        for l in range(DEPTH):
            kind = l % 3
            p = "L%d_" % l
            for g in range(4):
                rows = slice(g * 256, (g + 1) * 256)
                if kind == 0:
                    io = {"uT": UT, "wcat": W[p + "wcat"].view(W[p + "wcat"].t[g]), "cst": dn_cst,
                          "prm": W[p + "prm"].view(W[p + "prm"].t[g]), "oT": OT.view(OT.t[rows, :])}
                    _phase(c, lambda io=io: emit_dn(c, io))
                elif kind == 1:
                    io = {"uT": UT, "wcat": W[p + "wcat"].view(W[p + "wcat"].t[g]), "oh": da_oh, "md": da_md, "pm": da_pm,
                          "rb": W[p + "rb"].view(W[p + "rb"].t[g]), "rb15": W[p + "rb15"].view(W[p + "rb15"].t[g]),
                          "lamv": W[p + "lamv"], "sw": W[p + "sw"], "identf": identf, "oT": OT.view(OT.t[rows, :])}
                    lam_init = 0.8 - 0.6 * math.exp(-0.3 * l)
                    _phase(c, lambda io=io, lam_init=lam_init, tag="_%d_%d" % (l, g): emit_da(c, io, lam_init, tag))
                else:
                    io = {"uT": UT.view(UT.t[:, 64:64 + T]), "wg": W[p + "wg"].view(W[p + "wg"].t[g]),
                          "wx": W[p + "wx"].view(W[p + "wx"].t[g]), "wr": W[p + "wr"].view(W[p + "wr"].t[g]),
                          "wi": W[p + "wi"].view(W[p + "wi"].t[g]), "prm": W[p + "prm"].view(W[p + "prm"].t[g]),
                          "yT": OT.view(OT.t[rows, 64:64 + T])}
                    _phase(c, lambda io=io: emit_lru(c, io))
            final = (l == DEPTH - 1)
            for j in range(4):
                io = {"hT": sh(h0T if l == 0 else HT, j), "oT": sh(OT, j, 64), "wo": W[p + "wo"], "w1": W[p + "w1"],
                      "w2": W[p + "w2"], "nw": W[p + "nw"], "hout": sh(HT, j),
                      "uout": sh(outT, j) if final else sh(UT, j, 64)}
                _phase(c, lambda io=io, final=final: emit_post(c, io, final))
            if l == 1:
                c.switch_sem("pe")
        c.finish()
    return nc


def fused_inputs(b, x, meta_tokens, rel_bias, norm_mix_w, norm_mlp_w, final_norm_w,
                 dn_w_in, dn_conv_w, dn_a_log, dn_dt_bias, dn_norm_w, dn_w_out,
                 da_w_in, da_lam_q1, da_lam_k1, da_lam_q2, da_lam_k2, da_subln_w, da_w_out,
                 lru_w_in, lru_conv_w, lru_conv_b, lru_w_rgate, lru_b_rgate, lru_w_igate,
                 lru_b_igate, lru_lambda, lru_w_out, mlp_w1, mlp_w2, shared=None):
    f32 = np.float32
    im = {}
    seq = np.concatenate([np.zeros((PADF, D), f32), np.asarray(meta_tokens, f32), np.asarray(x[b], f32)], axis=0)
    im["h0T"] = np.ascontiguousarray(seq.T)
    if shared is not None:
        im.update(shared)
        return im
    sh = {}
    sh["nw0"] = _nwcols(norm_mix_w[0])
    oh, md, pm = da_consts()
    sh["dn_cst"] = dn_consts()
    sh["da_oh"], sh["da_md"], sh["da_pm"] = oh, md, pm
    sh["identf"] = np.eye(128, dtype=f32)
    rbt = np.asarray(rel_bias, f32)
    for l in range(DEPTH):
        kind, slot = l % 3, l // 3
        p = "L%d_" % l
        if kind == 0:
            w_in = np.asarray(dn_w_in[slot], f32)
            cw = np.asarray(dn_conv_w[slot], f32)
            wc, pr = [], []
            for g in range(4):
                h0 = 2 * g
                s = slice(h0 * 128, h0 * 128 + 256)
                wc.append(np.concatenate([w_in[:, 0:1024][:, s], w_in[:, 1024:2048][:, s], w_in[:, 2048:3072][:, s],
                                          w_in[:, 3072:4096][:, s], w_in[:, 4096 + h0:4096 + h0 + 2],
                                          w_in[:, 4104 + h0:4104 + h0 + 2]], axis=1))
                cw6 = np.concatenate([cw[:, 0:1024][:, s], cw[:, 1024:2048][:, s], cw[:, 2048:3072][:, s]], axis=1)
                pr.append(dn_params(cw6, np.asarray(dn_a_log[slot], f32)[h0:h0 + 2],
                                    np.asarray(dn_dt_bias[slot], f32)[h0:h0 + 2], np.asarray(dn_norm_w[slot], f32)))
            sh[p + "wcat"] = np.ascontiguousarray(np.stack(wc))
            sh[p + "prm"] = np.ascontiguousarray(np.stack(pr))
            wo = dn_w_out[slot]
        elif kind == 1:
            w_in = np.asarray(da_w_in[slot], f32)
            wc, rb, rb15 = [], [], []
            for g in range(4):
                h0 = 2 * g
                s = slice(h0 * 128, h0 * 128 + 256)
                wc.append(np.concatenate([w_in[:, 0:1024][:, s], w_in[:, 1024:2048][:, s], w_in[:, 2048:3072][:, s]], axis=1))
                rb.append(rbt[:, h0:h0 + 2])
                rb15.append(np.broadcast_to(rbt[15:16, h0:h0 + 2], (128, 2)))
            sh[p + "wcat"] = np.ascontiguousarray(np.stack(wc))
            sh[p + "rb"] = np.ascontiguousarray(np.stack(rb))
            sh[p + "rb15"] = np.ascontiguousarray(np.stack(rb15))
            lamv = np.stack([np.asarray(v[slot], f32) for v in (da_lam_q1, da_lam_k1, da_lam_q2, da_lam_k2)], axis=0)
            sh[p + "lamv"] = np.ascontiguousarray(np.broadcast_to(lamv[None], (128, 4, 64)))
            sh[p + "sw"] = np.ascontiguousarray(np.asarray(da_subln_w[slot], f32)[:, None])
            wo = da_w_out[slot]
        else:
            w_in = np.asarray(lru_w_in[slot], f32)
            cw = np.asarray(lru_conv_w[slot], f32)
            wg, wx, pr = [], [], []
            for g in range(4):
                s = slice(g * 256, (g + 1) * 256)
                wg.append(w_in[:, 0:1024][:, s])
                wx.append(w_in[:, 1024:2048][:, s])
                pr.append(lru_params(cw[:, s], np.asarray(lru_conv_b[slot], f32)[s], np.asarray(lru_b_rgate[slot], f32)[s],
                                     np.asarray(lru_b_igate[slot], f32)[s], np.asarray(lru_lambda[slot], f32)[s]))
            sh[p + "wg"] = np.ascontiguousarray(np.stack(wg))
            sh[p + "wx"] = np.ascontiguousarray(np.stack(wx))
            sh[p + "wr"] = np.ascontiguousarray(np.asarray(lru_w_rgate[slot], f32))
            sh[p + "wi"] = np.ascontiguousarray(np.asarray(lru_w_igate[slot], f32))
            sh[p + "prm"] = np.ascontiguousarray(np.stack(pr))
            wo = lru_w_out[slot]
        final = (l == DEPTH - 1)
        nxt = final_norm_w if final else norm_mix_w[l + 1]
        sh[p + "wo"] = np.ascontiguousarray(np.asarray(wo, f32))
        sh[p + "w1"] = np.ascontiguousarray(np.asarray(mlp_w1[l], f32))
        sh[p + "w2"] = np.ascontiguousarray(np.asarray(mlp_w2[l], f32))
        sh[p + "nw"] = np.ascontiguousarray(np.concatenate([_nwcols(norm_mlp_w[l]), _nwcols(nxt)], axis=1))
    im.update(sh)
    im["_shared"] = sh
    return im


def kernel(**inputs):
    x = inputs["x"]
    im0 = fused_inputs(0, **inputs)
    shared = im0.pop("_shared")
    im1 = fused_inputs(1, **inputs, shared=shared)
    nc = build_fused()
    res = run_bass_kernel_spmd(nc, [im0, im1], core_ids=[0, 1])
    out = np.stack([np.ascontiguousarray(res.results[b]["outT"][:, PADF + NMETA:].T) for b in range(B)], axis=0)
    return out.astype(np.float32)
```

```python
import math
from contextlib import ExitStack

import numpy as np
import ml_dtypes
import concourse.bass as bass
import concourse.mybir as mybir
from concourse.bass_utils import run_bass_kernel_spmd

F32 = mybir.dt.float32
BF16 = mybir.dt.bfloat16
AF = mybir.ActivationFunctionType
ALU = mybir.AluOpType
AX = mybir.AxisListType

D = 1024
B = 2
SEQ = 8192
NMETA = 16
PADF = 48
T = PADF + NMETA + SEQ
NTC = T // 4
EPS = 1e-6
DFF = 4096
NCORES = 8


class Tl:
    def __init__(self, t, name, st=None):
        self.t = t
        self.name = name
        self.st = st if st is not None else [None, {}]

    @property
    def w(self):
        return self.st[0]

    @w.setter
    def w(self, v):
        self.st[0] = v

    @property
    def r(self):
        return self.st[1]

    @r.setter
    def r(self, v):
        self.st[1] = v

    def view(self, ap):
        return Tl(ap, self.name, self.st)

    def __getitem__(self, idx):
        return self.t[idx]


class Ctx:
    NDMA = 8

    def __init__(self, nc, es):
        self.nc = nc
        self.es = es
        self.eng = {"pe": nc.tensor, "act": nc.scalar, "dve": nc.vector,
                    "pool": nc.gpsimd, "sp": nc.sync}
        self.sem = {k: es.enter_context(nc.semaphore("s_" + k)) for k in self.eng}
        self.cnt = {k: 0 for k in self.eng}
        self.seen = {k: {} for k in self.eng}
        self.dsem = {}
        self.dcnt = {}
        for q in ("sp", "pool"):
            self.dsem[q] = [es.enter_context(nc.semaphore("d_%s%d" % (q, i)))
                            for i in range(self.NDMA)]
            self.dcnt[q] = 0
        self.ntile = 0

    def sb(self, shape, dt=F32, name=None, es=None):
        self.ntile += 1
        name = "%s_%d" % (name or "t", self.ntile)
        t = (es or self.es).enter_context(self.nc.sbuf_tensor(name, list(shape), dt))
        return Tl(t, name)

    def ps(self, shape, dt=F32, name=None, es=None):
        self.ntile += 1
        name = "%s_%d" % (name or "p", self.ntile)
        t = (es or self.es).enter_context(self.nc.psum_tensor(name, list(shape), dt))
        return Tl(t, name)

    def dram(self, name, shape, dt, kind):
        t = self.nc.dram_tensor(name, list(shape), dt, kind=kind)
        return Tl(t.ap(), name)

    def _wait(self, e, sem, val):
        key = id(sem)
        if self.seen[e].get(key, 0) >= val:
            return
        self.eng[e].wait_ge(sem, val)
        self.seen[e][key] = val

    def _deps(self, e, reads, writes):
        deps = {}

        def add(d):
            if d is None:
                return
            s, v = d
            if deps.get(id(s), (None, 0))[1] < v:
                deps[id(s)] = (s, v)
        for t in reads:
            add(t.w)
        for t in writes:
            add(t.w)
            for s_v in t.r.values():
                add(s_v)
        for s, v in deps.values():
            if e == "pe" and s is self.sem["pe"]:
                continue
            self._wait(e, s, v)

    def _mark(self, token, reads, writes):
        s, v = token
        for t in reads:
            t.r[id(s)] = (s, v)
        for t in writes:
            t.w = (s, v)
            t.r = {}

    def op(self, e, fn, reads=(), writes=(), inc=True):
        self._deps(e, reads, writes)
        ins = fn(self.eng[e])
        if inc:
            self.cnt[e] += 1
            ins.then_inc(self.sem[e], 1)
            tok = (self.sem[e], self.cnt[e])
        else:
            tok = (self.sem[e], self.cnt[e] + 1)
        self._mark(tok, reads, writes)
        return ins

    def dma(self, q, out, in_, reads=(), writes=(), **kw):
        self._deps(q, reads, writes)
        i = self.dcnt[q]
        s = self.dsem[q][i % self.NDMA]
        prev = 16 * (i // self.NDMA)
        if prev:
            self._wait(q, s, prev)
        ins = self.eng[q].dma_start(out=out, in_=in_, **kw)
        ins.then_inc(s, 16)
        self.dcnt[q] += 1
        self._mark((s, prev + 16), reads, writes)

    def switch_sem(self, e):
        self.nsw = getattr(self, "nsw", 0) + 1
        self.sem[e] = self.es.enter_context(self.nc.semaphore("s_%s_%d" % (e, self.nsw)))
        self.cnt[e] = 0

    def barrier(self):
        for e in self.eng:
            for e2 in self.eng:
                if e2 != e and self.cnt[e2] > 0:
                    self._wait(e, self.sem[e2], self.cnt[e2])
            for q in self.dsem:
                n = self.dcnt[q]
                for j, s in enumerate(self.dsem[q]):
                    k = (n - j + self.NDMA - 1) // self.NDMA
                    if k > 0:
                        self._wait(e, s, 16 * k)

    def finish(self):
        for q in self.dsem:
            n = self.dcnt[q]
            for j, s in enumerate(self.dsem[q]):
                k = (n - j + self.NDMA - 1) // self.NDMA
                if k > 0:
                    self._wait("sp", s, 16 * k)


class Rot:
    def __init__(self, tiles):
        self.tiles = tiles
        self.i = 0

    def get(self):
        t = self.tiles[self.i % len(self.tiles)]
        self.i += 1
        return t


def rms_stats(c, ones, src_tiles_fn, nk, n, pstat, sqrot, rstd, scale_div):
    for k in range(nk):
        src, src_t = src_tiles_fn(k)
        sq = sqrot.get()
        c.op("act", lambda e: e.activation(out=sq[:, :n], in_=src, func=AF.Square),
             reads=[src_t], writes=[sq])
        c.op("pe", lambda e: e.matmul(pstat[:, :n], lhsT=ones[:], rhs=sq[:, :n],
                                      start=(k == 0), stop=(k == nk - 1)),
             reads=[ones, sq], writes=[pstat])
    c.op("act", lambda e: e.activation(out=rstd[:, :n], in_=pstat[:, :n], func=AF.Sqrt,
                                       bias=c.eps_t[:, 0:1], scale=1.0 / scale_div),
         reads=[pstat, c.eps_t], writes=[rstd])
    c.op("dve", lambda e: e.reciprocal(out=rstd[:, :n], in_=rstd[:, :n]),
         reads=[rstd], writes=[rstd])


def make_consts(c):
    c.eps_t = c.sb([128, 1], F32, "eps_t")
    c.op("pool", lambda e: e.memset(c.eps_t[:], EPS), writes=[c.eps_t])
    c.ones = c.sb([128, 128], F32, "ones")
    c.op("pool", lambda e: e.memset(c.ones[:], 1.0), writes=[c.ones])


GS = 344
NG = NTC // GS


def build_post(final):
    nc = bass.Bass("TRN2", target_bir_lowering=False)
    with ExitStack() as es:
        c = Ctx(nc, es)
        io = {}
        io["hT"] = c.dram("hT", [D, NTC], F32, "ExternalInput")
        io["oT"] = c.dram("oT", [D, NTC], BF16, "ExternalInput")
        io["wo"] = c.dram("wo", [D, D], F32, "ExternalInput")
        io["w1"] = c.dram("w1", [D, DFF], F32, "ExternalInput")
        io["w2"] = c.dram("w2", [DFF, D], F32, "ExternalInput")
        io["nw"] = c.dram("nw", [128, 16], F32, "ExternalInput")
        io["hout"] = c.dram("hout", [D, NTC], F32, "ExternalOutput")
        io["uout"] = c.dram("uout", [D, NTC], F32 if final else BF16, "ExternalOutput")
        make_consts(c)
        emit_post(c, io, final)
        c.finish()
    return nc


def emit_post(c, io, final):
    if True:
        hT, oT, wo, w1, w2, nw, hout, uout = (io[k] for k in ("hT", "oT", "wo", "w1", "w2", "nw", "hout", "uout"))
        nws = c.sb([128, 16], F32, "nws")
        c.dma("sp", nws[:], nw[:], writes=[nws])

        H = [c.sb([128, 8, GS], F32, "H%d" % g) for g in range(NG)]
        XB = [c.sb([128, 8, GS], BF16, "XB%d" % g) for g in range(NG)]
        wbuf = Rot([c.sb([128, 8192], BF16, "wb%d" % i) for i in range(2)])
        abuf = Rot([c.sb([128, 4, GS], BF16, "ab%d" % i) for i in range(2)])
        rbuf = Rot([c.sb([128, GS], F32, "rb%d" % i) for i in range(3)])
        sqb = Rot([c.sb([128, GS], F32, "sq%d" % i) for i in range(2)])
        rstd = c.sb([128, GS], F32, "rstd")
        uo = Rot([c.sb([128, 8, GS], F32 if final else BF16, "uo%d" % i) for i in range(2)])
        pa = Rot([c.ps([128, 512], F32, "pa%d" % i) for i in range(4)])
        py = Rot([c.ps([128, 512], F32, "py%d" % i) for i in range(3)])
        pstat = c.ps([128, 512], F32, "pstat")

        hT3 = hT[:].rearrange("(k p) n -> p k n", p=128)
        oT3 = oT[:].rearrange("(k p) n -> p k n", p=128)
        ho3 = hout[:].rearrange("(k p) n -> p k n", p=128)
        uo3 = uout[:].rearrange("(k p) n -> p k n", p=128)

        wob = wbuf.get()
        wo3 = wo[:].rearrange("(k p) n -> p k n", p=128)
        for k0 in range(0, 8, 2):
            c.dma("pool", wob[:, k0 * 1024:(k0 + 2) * 1024].rearrange("p (k n) -> p k n", k=2), wo3[:, k0:k0 + 2, :], writes=[wob])
        for g in range(NG):
            c.dma("sp", XB[g][:], oT3[:, :, g * GS:(g + 1) * GS], writes=[XB[g]])
            c.dma("sp", H[g][:], hT3[:, :, g * GS:(g + 1) * GS], writes=[H[g]])

        def load_eighth(e8):
            wb = wbuf.get()
            w13 = w1[:].rearrange("(k p) n -> p k n", p=128)
            for k0 in range(0, 8, 4):
                c.dma("pool", wb[:, k0 * 512:(k0 + 4) * 512].rearrange("p (k n) -> p k n", k=4),
                      w13[:, k0:k0 + 4, e8 * 512:(e8 + 1) * 512], writes=[wb])
            w23 = w2[:].rearrange("(f p) n -> p f n", p=128)
            for f0 in range(0, 4, 2):
                c.dma("pool", wb[:, 4096 + f0 * 1024:4096 + (f0 + 2) * 1024].rearrange("p (f n) -> p f n", f=2),
                      w23[:, e8 * 4 + f0:e8 * 4 + f0 + 2, :], writes=[wb])
            return wb

        def norm_to(g, col, dst, dst_is_f32):
            rms_stats(c, c.ones, lambda k: (H[g][:, k, :], H[g]), 8, GS, pstat, sqb, rstd, float(D))
            for m in range(8):
                c.op("dve", lambda e: e.scalar_tensor_tensor(
                    out=dst[:, m, :], in0=H[g][:, m, :], scalar=nws[:, col + m:col + m + 1],
                    in1=rstd[:], op0=ALU.mult, op1=ALU.mult),
                    reads=[H[g], nws, rstd], writes=[dst])

        wnext = load_eighth(0)
        for g in range(NG):
            for m in range(8):
                p = py.get()
                for k in range(8):
                    c.op("pe", lambda e: e.matmul(p[:, :GS], lhsT=wob[:, k * 1024 + m * 128:k * 1024 + (m + 1) * 128],
                                                  rhs=XB[g][:, k, :], start=(k == 0), stop=(k == 7)),
                         reads=[wob, XB[g]], writes=[p], inc=(k == 7))
                c.op("dve", lambda e: e.tensor_tensor(out=H[g][:, m, :], in0=p[:, :GS], in1=H[g][:, m, :], op=ALU.add),
                     reads=[p, H[g]], writes=[H[g]])
            if g > 0:
                norm_to(g - 1, 0, XB[g - 1], False)
        norm_to(NG - 1, 0, XB[NG - 1], False)
        def mlp_up(e8, g):
            wb = wbs[e8]
            ab = abuf.get()
            for f in range(4):
                p = pa.get()
                for k in range(8):
                    c.op("pe", lambda e: e.matmul(p[:, :GS], lhsT=wb[:, k * 512 + f * 128:k * 512 + (f + 1) * 128],
                                                  rhs=XB[g][:, k, :], start=(k == 0), stop=(k == 7)),
                         reads=[wb, XB[g]], writes=[p], inc=(k == 7))
                r = rbuf.get()
                c.op("act", lambda e: e.activation(out=r[:], in_=p[:, :GS], func=AF.Relu),
                     reads=[p], writes=[r])
                c.op("dve", lambda e: e.tensor_tensor(out=ab[:, f, :], in0=r[:], in1=r[:], op=ALU.mult),
                     reads=[r], writes=[ab])
            return ab

        def mlp_down(e8, g, ab):
            wb = wbs[e8]
            for m in range(8):
                p = py.get()
                for f in range(4):
                    c.op("pe", lambda e: e.matmul(p[:, :GS], lhsT=wb[:, 4096 + f * 1024 + m * 128:4096 + f * 1024 + (m + 1) * 128],
                                                  rhs=ab[:, f, :], start=(f == 0), stop=(f == 3)),
                         reads=[wb, ab], writes=[p], inc=(f == 3))
                c.op("dve", lambda e: e.tensor_tensor(out=H[g][:, m, :], in0=p[:, :GS], in1=H[g][:, m, :], op=ALU.add),
                     reads=[p, H[g]], writes=[H[g]])

        wbs = {0: wnext}
        wbs[1] = load_eighth(1)
        steps = [(e8, g) for e8 in range(8) for g in range(NG)]
        ab_cur = mlp_up(0, 0)
        for i, (e8, g) in enumerate(steps):
            ab_next = mlp_up(*steps[i + 1]) if i + 1 < len(steps) else None
            mlp_down(e8, g, ab_cur)
            ab_cur = ab_next
            if g == NG - 1 and e8 + 2 < 8:
                wbs[e8 + 2] = load_eighth(e8 + 2)
            if True:
                if e8 == 7:
                    c.dma("sp", ho3[:, :, g * GS:(g + 1) * GS], H[g][:], reads=[H[g]], writes=[hout])
                    u = uo.get()
                    norm_to(g, 8, u, final)
                    c.dma("sp", uo3[:, :, g * GS:(g + 1) * GS], u[:], reads=[u], writes=[uout])


def load_cast(c, dst_t, dst_ap, src_ap):
    c.dma("pool", dst_ap, src_ap, writes=[dst_t])


LB = 342
LNB = (T - PADF) // LB


def build_lru():
    nc = bass.Bass("TRN2", target_bir_lowering=False)
    with ExitStack() as es:
        c = Ctx(nc, es)
        io = {}
        io["uT"] = c.dram("uT", [D, T], BF16, "ExternalInput")
        io["wg"] = c.dram("wg", [D, 256], F32, "ExternalInput")
        io["wx"] = c.dram("wx", [D, 256], F32, "ExternalInput")
        io["wr"] = c.dram("wr", [256, 256], F32, "ExternalInput")
        io["wi"] = c.dram("wi", [256, 256], F32, "ExternalInput")
        io["prm"] = c.dram("prm", [128, 16], F32, "ExternalInput")
        io["yT"] = c.dram("yT", [256, T], BF16, "ExternalOutput")
        make_consts(c)
        emit_lru(c, io)
        c.finish()
    return nc


def emit_lru(c, io):
    if True:
        uT, wg, wx, wr, wi, prm, yT = (io[k] for k in ("uT", "wg", "wx", "wr", "wi", "prm", "yT"))
        one_t = c.sb([128, 1], F32, "one_t")
        c.op("pool", lambda e: e.memset(one_t[:], 1.0), writes=[one_t])
        ps_ = c.sb([128, 16], F32, "prm_s")
        c.dma("sp", ps_[:], prm[:], writes=[ps_])
        wgb = c.sb([128, 8, 256], BF16, "wgb")
        wxb = c.sb([128, 8, 256], BF16, "wxb")
        wrb = c.sb([128, 2, 256], BF16, "wrb")
        wib = c.sb([128, 2, 256], BF16, "wib")
        for (dst, src, nk) in ((wgb, wg, 8), (wxb, wx, 8), (wrb, wr, 2), (wib, wi, 2)):
            load_cast(c, dst, dst[:], src[:].rearrange("(k p) n -> p k n", p=128))
        cch = c.sb([128, 2], F32, "cch")
        ee = c.sb([128, 2], F32, "ee")
        acc = c.sb([128, 2], F32, "lacc")
        lam_ap = lambda: ps_[:, 7:16:8]
        c.op("act", lambda e: e.activation(out=ee[:], in_=lam_ap(), func=AF.Exp, scale=-1.0),
             reads=[ps_], writes=[ee])
        c.op("dve", lambda e: e.tensor_scalar(out=acc[:], in0=ee[:], scalar1=-1.0 / 6, scalar2=1.0 / 5,
                                              op0=ALU.mult, op1=ALU.add), reads=[ee], writes=[acc])
        for coef in (1.0 / 4, 1.0 / 3, 1.0 / 2, 1.0):
            c.op("dve", lambda e: e.tensor_tensor(out=acc[:], in0=acc[:], in1=ee[:], op=ALU.mult),
                 reads=[acc, ee], writes=[acc])
            c.op("dve", lambda e: e.tensor_scalar(out=acc[:], in0=acc[:], scalar1=-1.0, scalar2=coef,
                                                  op0=ALU.mult, op1=ALU.add), reads=[acc], writes=[acc])
        c.op("dve", lambda e: e.tensor_tensor(out=acc[:], in0=acc[:], in1=ee[:], op=ALU.mult),
             reads=[acc, ee], writes=[acc])
        c.op("dve", lambda e: e.tensor_scalar(out=cch[:], in0=acc[:], scalar1=-8.0, scalar2=None,
                                              op0=ALU.mult), reads=[acc], writes=[cch])

        ub = Rot([c.sb([128, 8, LB], BF16, "ub%d" % i) for i in range(3)])
        xc = [c.sb([128, LB + 3], F32, "xc%d" % t) for t in range(2)]
        for t in range(2):
            c.op("pool", lambda e: e.memset(xc[t][:], 0.0), writes=[xc[t]])
        xr = Rot([c.sb([128, LB], F32, "xr%d" % i) for i in range(2)])
        xrb = Rot([c.sb([128, 2, LB], BF16, "xrb%d" % i) for i in range(2)])
        gt = Rot([c.sb([128, LB], F32, "gt%d" % i) for i in range(4)])
        xrs = Rot([c.sb([128, LB], F32, "xrs%d" % i) for i in range(4)])
        tmp = Rot([c.sb([128, LB], F32, "tmp%d" % i) for i in range(6)])
        hs = [Rot([c.sb([128, LB], F32, "hs%d_%d" % (t, i)) for i in range(2)]) for t in range(2)]
        yb = Rot([c.sb([128, 2, LB], BF16, "yb%d" % i) for i in range(2)])
        pp = Rot([c.ps([128, 512], F32, "pp%d" % i) for i in range(4)])
        pg_ = Rot([c.ps([128, 512], F32, "pq%d" % i) for i in range(4)])
        zz = c.sb([128, 2, PADF], BF16, "zz")
        c.op("pool", lambda e: e.memset(zz[:], 0.0), writes=[zz])
        y3 = yT[:].rearrange("(t p) n -> p t n", p=128)
        c.dma("sp", y3[:, :, 0:PADF], zz[:], reads=[zz], writes=[yT])
        u3 = uT[:].rearrange("(k p) n -> p k n", p=128)
        prev_hs = [None, None]
        P = lambda t, j: ps_[:, t * 8 + j:t * 8 + j + 1]
        n = LB
        for blk in range(LNB):
            t0 = PADF + blk * LB
            u = ub.get()
            c.dma("sp", u[:], u3[:, :, t0:t0 + n], writes=[u])
            gts, xrf = [], []
            xb_ = xrb.get()
            for t in range(2):
                pgt = pp.get()
                for k in range(8):
                    c.op("pe", lambda e: e.matmul(pgt[:, :n], lhsT=wgb[:, k, t * 128:(t + 1) * 128], rhs=u[:, k, :],
                                                  start=(k == 0), stop=(k == 7)), reads=[wgb, u], writes=[pgt], inc=(k == 7))
                pxt = pp.get()
                for k in range(8):
                    c.op("pe", lambda e: e.matmul(pxt[:, :n], lhsT=wxb[:, k, t * 128:(t + 1) * 128], rhs=u[:, k, :],
                                                  start=(k == 0), stop=(k == 7)), reads=[wxb, u], writes=[pxt], inc=(k == 7))
                s = tmp.get()
                c.op("act", lambda e: e.activation(out=s[:], in_=pgt[:, :n], func=AF.Square), reads=[pgt], writes=[s])
                c.op("dve", lambda e: e.tensor_scalar(out=s[:], in0=s[:], scalar1=0.044715, scalar2=1.0,
                                                      op0=ALU.mult, op1=ALU.add), reads=[s], writes=[s])
                c.op("dve", lambda e: e.tensor_tensor(out=s[:], in0=s[:], in1=pgt[:, :n], op=ALU.mult),
                     reads=[s, pgt], writes=[s])
                c.op("act", lambda e: e.activation(out=s[:], in_=s[:], func=AF.Sigmoid, scale=1.5957691216057308),
                     reads=[s], writes=[s])
                g_ = gt.get()
                c.op("dve", lambda e: e.tensor_tensor(out=g_[:], in0=s[:], in1=pgt[:, :n], op=ALU.mult),
                     reads=[s, pgt], writes=[g_])
                gts.append(g_)
                c.op("act", lambda e: e.activation(out=xc[t][:, 3:3 + n], in_=pxt[:, :n], func=AF.Copy),
                     reads=[pxt], writes=[xc[t]])
                x_ = xrs.get()
                c.op("dve", lambda e: e.tensor_scalar(out=x_[:], in0=xc[t][:, 0:n], scalar1=P(t, 0), scalar2=P(t, 4),
                                                      op0=ALU.mult, op1=ALU.add), reads=[xc[t], ps_], writes=[x_])
                for j in range(1, 4):
                    c.op("dve", lambda e: e.scalar_tensor_tensor(out=x_[:], in0=xc[t][:, j:j + n], scalar=P(t, j),
                                                                 in1=x_[:], op0=ALU.mult, op1=ALU.add),
                         reads=[xc[t], ps_, x_], writes=[x_])
                c.op("pool", lambda e: e.tensor_copy(out=xc[t][:, 0:3], in_=xc[t][:, n:n + 3]),
                     reads=[xc[t]], writes=[xc[t]])
                c.op("pool", lambda e: e.tensor_copy(out=xb_[:, t, :], in_=x_[:]), reads=[x_], writes=[xb_])
                xrf.append(x_)
            y_ = yb.get()
            for t in range(2):
                pr = pg_.get()
                for k in range(2):
                    c.op("pe", lambda e: e.matmul(pr[:, :n], lhsT=wrb[:, k, t * 128:(t + 1) * 128], rhs=xb_[:, k, :],
                                                  start=(k == 0), stop=(k == 1)), reads=[wrb, xb_], writes=[pr], inc=(k == 1))
                pi_ = pg_.get()
                for k in range(2):
                    c.op("pe", lambda e: e.matmul(pi_[:, :n], lhsT=wib[:, k, t * 128:(t + 1) * 128], rhs=xb_[:, k, :],
                                                  start=(k == 0), stop=(k == 1)), reads=[wib, xb_], writes=[pi_], inc=(k == 1))
                a_ = tmp.get()
                c.op("act", lambda e: e.activation(out=a_[:], in_=pr[:, :n], func=AF.Sigmoid, bias=P(t, 5)),
                     reads=[pr, ps_], writes=[a_])
                c.op("act", lambda e: e.activation(out=a_[:], in_=a_[:], func=AF.Exp, scale=cch[:, t:t + 1]),
                     reads=[a_, cch], writes=[a_])
                i_ = tmp.get()
                c.op("act", lambda e: e.activation(out=i_[:], in_=pi_[:, :n], func=AF.Sigmoid, bias=P(t, 6)),
                     reads=[pi_, ps_], writes=[i_])
                m_ = tmp.get()
                c.op("dve", lambda e: e.tensor_tensor(out=m_[:], in0=a_[:], in1=a_[:], op=ALU.mult), reads=[a_], writes=[m_])
                c.op("act", lambda e: e.activation(out=m_[:], in_=m_[:], func=AF.Sqrt, bias=one_t[:, 0:1], scale=-1.0),
                     reads=[m_, one_t], writes=[m_])
                c.op("dve", lambda e: e.tensor_tensor(out=i_[:], in0=i_[:], in1=xrf[t][:], op=ALU.mult),
                     reads=[i_, xrf[t]], writes=[i_])
                c.op("dve", lambda e: e.tensor_tensor(out=i_[:], in0=i_[:], in1=m_[:], op=ALU.mult),
                     reads=[i_, m_], writes=[i_])
                h_ = hs[t].get()
                if prev_hs[t] is None:
                    c.op("dve", lambda e: e.tensor_tensor_scan(out=h_[:], data0=a_[:], data1=i_[:], initial=0.0,
                                                               op0=ALU.mult, op1=ALU.add), reads=[a_, i_], writes=[h_])
                else:
                    ph = prev_hs[t]
                    c.op("dve", lambda e: e.tensor_tensor_scan(out=h_[:], data0=a_[:], data1=i_[:], initial=ph[:, n - 1:n],
                                                               op0=ALU.mult, op1=ALU.add), reads=[a_, i_, ph], writes=[h_])
                prev_hs[t] = h_
                c.op("pool", lambda e: e.tensor_tensor(out=y_[:, t, :], in0=h_[:], in1=gts[t][:], op=ALU.mult),
                     reads=[h_, gts[t]], writes=[y_])
            c.dma("sp", y3[:, :, t0:t0 + n], y_[:], reads=[y_], writes=[yT])


def lru_params(cw, cb, br, bi, lam):
    prm = np.zeros((128, 2, 8), np.float32)
    for t in range(2):
        sl = slice(t * 128, (t + 1) * 128)
        prm[:, t, 0:4] = cw[:, sl].T
        prm[:, t, 4] = cb[sl]
        prm[:, t, 5] = br[sl]
        prm[:, t, 6] = bi[sl]
        prm[:, t, 7] = lam[sl]
    return np.ascontiguousarray(prm.reshape(128, 16))


TD = T + 64
DN_BLOCKS = [(i * 512, 4) for i in range(16)] + [(8192, 1)]


def dn_consts():
    i = np.arange(128)
    same = (i[:, None] // 64) == (i[None, :] // 64)
    cst = np.zeros((128, 6, 128), np.float32)
    cst[:, 0, :] = np.eye(128)
    cst[:, 1, :] = (same & (i[:, None] <= i[None, :]))
    cst[:, 2, :] = (same & (i[:, None] > i[None, :]))
    cst[:, 3, :] = np.where(same & (i[:, None] >= i[None, :]), 0.0, -30000.0)
    cst[:, 4, :] = (same & (i[:, None] > i[None, :]))
    cst[:, 5, :] = -1.0
    return cst


def build_dn(blocks=None, TD=TD, dbg=99):
    blocks = blocks or DN_BLOCKS
    nc = bass.Bass("TRN2", target_bir_lowering=False)
    with ExitStack() as es:
        c = Ctx(nc, es)
        io = {}
        io["uT"] = c.dram("uT", [D, TD], BF16, "ExternalInput")
        io["wcat"] = c.dram("wcat", [D, 1028], F32, "ExternalInput")
        io["cst"] = c.dram("cst", [128, 6, 128], F32, "ExternalInput")
        io["prm"] = c.dram("prm", [128, 32], F32, "ExternalInput")
        io["oT"] = c.dram("oT", [256, TD], BF16, "ExternalOutput")
        make_consts(c)
        emit_dn(c, io, blocks, dbg)
        c.finish()
    return nc


def emit_dn(c, io, blocks=None, dbg=99):
    blocks = blocks or DN_BLOCKS
    if True:
        uT, wcat, cstd, prm, oT = (io[k] for k in ("uT", "wcat", "cst", "prm", "oT"))
        one_t = c.sb([128, 1], F32, "one_t")
        c.op("pool", lambda e: e.memset(one_t[:], 1.0), writes=[one_t])
        cst = c.sb([128, 6, 128], F32, "cst_s")
        c.dma("sp", cst[:], cstd[:], writes=[cst])
        ident = cst[:, 0, :]
        U2 = cst[:, 1, :]
        R2 = cst[:, 2, :]
        negmask = cst[:, 3, :]
        smask = cst[:, 4, :]
        negones = cst[:, 5, :]
        ps_ = c.sb([128, 32], F32, "prm_s")
        c.dma("sp", ps_[:], prm[:], writes=[ps_])
        nea = c.sb([128, 2], F32, "nea")
        c.op("act", lambda e: e.activation(out=nea[:], in_=ps_[:, 24:26], func=AF.Exp), reads=[ps_], writes=[nea])
        c.op("dve", lambda e: e.tensor_scalar(out=nea[:], in0=nea[:], scalar1=-1.0, scalar2=None, op0=ALU.mult),
             reads=[nea], writes=[nea])
        wb = c.sb([128, 8, 1028], BF16, "wb")
        w3 = wcat[:].rearrange("(k p) n -> p k n", p=128)
        for k0 in range(0, 8, 2):
            load_cast(c, wb, wb[:, k0:k0 + 2, :], w3[:, k0:k0 + 2, :])

        ub = Rot([c.sb([128, 8, 512], BF16, "ub%d" % i) for i in range(2)])
        xc = [c.sb([128, 515], F32, "xc%d" % j) for j in range(6)]
        for j in range(6):
            c.op("pool", lambda e: e.memset(xc[j][:], 0.0), writes=[xc[j]])
        ft = [Rot([c.sb([128, 512], F32, "ft%d_%d" % (j, i)) for i in range(2)]) for j in range(6)]
        szr = [Rot([c.sb([128, 512], F32, "sz%d_%d" % (h, i)) for i in range(2)]) for h in range(2)]
        sq4 = [c.sb([128, 512], F32, "sq%d" % i) for i in range(4)]
        rstd4 = [c.sb([128, 512], F32, "rstd%d" % i) for i in range(4)]
        obr = Rot([c.sb([128, 2, 512], BF16, "ob%d" % i) for i in range(2)])
        S = [c.sb([128, 128], F32, "S%d" % h) for h in range(2)]
        for h in range(2):
            c.op("pool", lambda e: e.memset(S[h][:], 0.0), writes=[S[h]])
        bt = Rot([c.sb([128, 4, 2], F32, "bt%d" % i) for i in range(2)])
        gg = Rot([c.sb([128, 4, 2], F32, "gg%d" % i) for i in range(2)])
        gtmp = Rot([c.sb([128, 4, 2], F32, "gtmp%d" % i) for i in range(6)])
        sm2 = Rot([c.sb([128, 2], F32, "sm2_%d" % i) for i in range(24)])
        sm1 = Rot([c.sb([128, 1], F32, "sm1_%d" % i) for i in range(8)])
        NSQ = 96
        sq128 = Rot([c.sb([128, 128], F32, "m%d" % i) for i in range(NSQ)])
        LL = {nm: [c.sb([128, 128], F32, "%s%d" % (nm, i)) for i in range(8)]
              for nm in ("dec", "egrow", "kd", "attT", "qg", "usb", "wTs", "otok")}
        pbig = Rot([c.ps([128, 512], F32, "pbig%d" % i) for i in range(2)])
        banks = [c.ps([128, 512], F32, "pbank%d" % i) for i in range(6)]
        psm = Rot([banks[i].view(banks[i].t[:, 0:128]) for i in range(6)])

        u3 = uT[:].rearrange("(k p) n -> p k n", p=128)
        o3 = oT[:].rearrange("(h p) n -> p h n", p=128)

        def mm(out_t, out_ap, lhsT, rhs, reads, start=True, stop=True):
            c.op("pe", lambda e: e.matmul(out_ap, lhsT=lhsT, rhs=rhs, start=start, stop=stop),
                 reads=reads, writes=[out_t], inc=stop)

        def evac(eng, dst_t, dst_ap, src_t, src_ap, scale=None, extra=()):
            if eng == "act":
                if scale is None:
                    c.op("act", lambda e: e.activation(out=dst_ap, in_=src_ap, func=AF.Copy),
                         reads=[src_t], writes=[dst_t])
                else:
                    c.op("act", lambda e: e.activation(out=dst_ap, in_=src_ap, func=AF.Copy, scale=scale),
                         reads=[src_t] + list(extra), writes=[dst_t])
            else:
                c.op("dve", lambda e: e.tensor_copy(out=dst_ap, in_=src_ap), reads=[src_t], writes=[dst_t])

        for (b0, ntl) in blocks:
            n = ntl * 128
            u = ub.get()
            c.dma("sp", u[:, :, :n], u3[:, :, b0:b0 + n], writes=[u])
            pba_t = psm.get()
            for tt in range(ntl):
                for k in range(8):
                    mm(pba_t, pba_t[:, tt * 4:(tt + 1) * 4], u[:, k, tt * 128:(tt + 1) * 128], wb[:, k, 1024:1028],
                       [u, wb], start=(k == 0), stop=(k == 7))
            pba = pba_t[:, 0:4 * ntl].rearrange("p (t f) -> p t f", f=4)
            btb = bt.get()
            ggb = gg.get()
            c.op("act", lambda e: e.activation(out=btb[:, :ntl, :], in_=pba[:, :, 0:2], func=AF.Sigmoid),
                 reads=[pba_t], writes=[btb])
            x_ = gtmp.get(); ax = gtmp.get(); rl = gtmp.get()
            for h in range(2):
                c.op("dve", lambda e: e.tensor_scalar(out=x_[:, :ntl, h:h + 1], in0=pba[:, :, 2 + h:3 + h],
                                                      scalar1=ps_[:, 26 + h:27 + h], scalar2=None, op0=ALU.add),
                     reads=[pba_t, ps_], writes=[x_])
            c.op("act", lambda e: e.activation(out=ax[:, :ntl, :], in_=x_[:, :ntl, :], func=AF.Abs),
                 reads=[x_], writes=[ax])
            c.op("act", lambda e: e.activation(out=ax[:, :ntl, :], in_=ax[:, :ntl, :], func=AF.Exp, scale=-1.0),
                 reads=[ax], writes=[ax])
            c.op("act", lambda e: e.activation(out=ax[:, :ntl, :], in_=ax[:, :ntl, :], func=AF.Ln, bias=one_t[:, 0:1]),
                 reads=[ax, one_t], writes=[ax])
            c.op("dve", lambda e: e.tensor_scalar(out=rl[:, :ntl, :], in0=x_[:, :ntl, :], scalar1=0.0, scalar2=None,
                                                  op0=ALU.max), reads=[x_], writes=[rl])
            c.op("dve", lambda e: e.tensor_tensor(out=rl[:, :ntl, :], in0=rl[:, :ntl, :], in1=ax[:, :ntl, :], op=ALU.add),
                 reads=[rl, ax], writes=[rl])
            for h in range(2):
                c.op("dve", lambda e: e.tensor_scalar(out=ggb[:, :ntl, h:h + 1], in0=rl[:, :ntl, h:h + 1],
                                                      scalar1=nea[:, h:h + 1], scalar2=None, op0=ALU.mult),
                     reads=[rl, nea], writes=[ggb])
            F = []
            pend_silu = None
            for j in range(8):
                p = pbig.get()
                for k in range(8):
                    mm(p, p[:, :n], wb[:, k, j * 128:(j + 1) * 128], u[:, k, :n], [wb, u], start=(k == 0), stop=(k == 7))
                if j < 6:
                    c.op("act", lambda e: e.activation(out=xc[j][:, 3:3 + n], in_=p[:, :n], func=AF.Copy),
                         reads=[p], writes=[xc[j]])
                    if pend_silu is not None:
                        pend_silu()
                        pend_silu = None
                    f = ft[j].get()
                    c.op("dve", lambda e: e.tensor_scalar(out=f[:, :n], in0=xc[j][:, 0:n], scalar1=ps_[:, j * 4:j * 4 + 1],
                                                          scalar2=None, op0=ALU.mult), reads=[xc[j], ps_], writes=[f])
                    for tp in range(1, 4):
                        c.op("dve", lambda e: e.scalar_tensor_tensor(out=f[:, :n], in0=xc[j][:, tp:tp + n],
                                                                     scalar=ps_[:, j * 4 + tp:j * 4 + tp + 1], in1=f[:, :n],
                                                                     op0=ALU.mult, op1=ALU.add),
                             reads=[xc[j], ps_, f], writes=[f])
                    c.op("pool", lambda e: e.tensor_copy(out=xc[j][:, 0:3], in_=xc[j][:, n:n + 3]),
                         reads=[xc[j]], writes=[xc[j]])
                    pend_silu = (lambda f=f: c.op("act", lambda e: e.activation(out=f[:, :n], in_=f[:, :n], func=AF.Silu),
                                                  reads=[f], writes=[f]))
                    F.append(f)
                else:
                    if pend_silu is not None:
                        pend_silu()
                        pend_silu = None
                    z = szr[j - 6].get()
                    c.op("act", lambda e: e.activation(out=z[:, :n], in_=p[:, :n], func=AF.Silu), reads=[p], writes=[z])
                    F.append(z)
            pst4 = [pbig.get(), pbig.get(), banks[4], banks[5]]
            for j in range(4):
                c.op("act", lambda e: e.activation(out=sq4[j][:, :n], in_=F[j][:, :n], func=AF.Square),
                     reads=[F[j]], writes=[sq4[j]])
            for j in range(4):
                c.op("pe", lambda e: e.matmul(pst4[j][:, :n], lhsT=c.ones[:], rhs=sq4[j][:, :n], start=True, stop=True),
                     reads=[c.ones, sq4[j]], writes=[pst4[j]])
            for j in range(4):
                c.op("act", lambda e: e.activation(out=rstd4[j][:, :n], in_=pst4[j][:, :n], func=AF.Sqrt,
                                                   bias=c.eps_t[:, 0:1], scale=1.0),
                     reads=[pst4[j], c.eps_t], writes=[rstd4[j]])
            for j in range(4):
                c.op("dve", lambda e: e.reciprocal(out=rstd4[j][:, :n], in_=rstd4[j][:, :n]), reads=[rstd4[j]], writes=[rstd4[j]])
            for j in range(4):
                if j < 2:
                    c.op("dve", lambda e: e.scalar_tensor_tensor(out=F[j][:, :n], in0=F[j][:, :n], scalar=128.0 ** -0.5,
                                                                 in1=rstd4[j][:, :n], op0=ALU.mult, op1=ALU.mult),
                         reads=[F[j], rstd4[j]], writes=[F[j]])
                else:
                    c.op("dve", lambda e: e.tensor_tensor(out=F[j][:, :n], in0=F[j][:, :n], in1=rstd4[j][:, :n], op=ALU.mult),
                         reads=[F[j], rstd4[j]], writes=[F[j]])
            ob = obr.get()
            TT = list(range(ntl))
            CH = [(tt, h) for tt in TT for h in range(2)]
            ci = {ch: i for i, ch in enumerate(CH)}
            csl = {tt: slice(tt * 128, (tt + 1) * 128) for tt in TT}
            egc, ed, be = {}, {}, {}
            for tt in TT:
                pg1 = psm.get(); pg2 = psm.get()
                mm(pg1, pg1[:, 0:2], U2, ggb[:, tt, :], [cst, ggb])
                mm(pg2, pg2[:, 0:2], R2, ggb[:, tt, :], [cst, ggb])
                egc[tt] = sm2.get(); ed[tt] = sm2.get(); be[tt] = sm2.get()
                c.op("act", lambda e: e.activation(out=egc[tt][:], in_=pg1[:, 0:2], func=AF.Exp), reads=[pg1], writes=[egc[tt]])
                c.op("act", lambda e: e.activation(out=ed[tt][:], in_=pg2[:, 0:2], func=AF.Exp), reads=[pg2], writes=[ed[tt]])
                c.op("dve", lambda e: e.tensor_tensor(out=be[tt][:], in0=egc[tt][:], in1=btb[:, tt, :], op=ALU.mult),
                     reads=[egc[tt], btb], writes=[be[tt]])
            Ug, dec, decs, egrow, P, Q, Y = {}, {}, {}, {}, {}, {}, {}
            for ch in CH:
                tt, h = ch
                Ug[ch] = sq128.get()
                c.op("dve", lambda e: e.tensor_scalar(out=Ug[ch][:], in0=U2, scalar1=ggb[:, tt, h:h + 1], scalar2=None,
                                                      op0=ALU.mult), reads=[cst, ggb], writes=[Ug[ch]])
            for ch in CH:
                tt, h = ch
                pd = psm.get()
                mm(pd, pd[:], Ug[ch][:], c.ones[:], [Ug[ch], c.ones], start=True, stop=False)
                mm(pd, pd[:], negones, Ug[ch][:], [cst, Ug[ch]], start=False, stop=True)
                dec[ch] = LL["dec"][ci[ch]]
                c.op("dve", lambda e: e.tensor_tensor(out=dec[ch][:], in0=pd[:], in1=negmask, op=ALU.add),
                     reads=[pd, cst], writes=[dec[ch]])
                c.op("act", lambda e: e.activation(out=dec[ch][:], in_=dec[ch][:], func=AF.Exp),
                     reads=[dec[ch]], writes=[dec[ch]])
                decs[ch] = sq128.get()
                c.op("pool", lambda e: e.tensor_tensor(out=decs[ch][:], in0=dec[ch][:], in1=smask, op=ALU.mult),
                     reads=[dec[ch], cst], writes=[decs[ch]])
                pe_ = psm.get()
                mm(pe_, pe_[:], c.ones[:], Ug[ch][:], [c.ones, Ug[ch]])
                egrow[ch] = LL["egrow"][ci[ch]]
                c.op("act", lambda e: e.activation(out=egrow[ch][:], in_=pe_[:], func=AF.Exp),
                     reads=[pe_], writes=[egrow[ch]])
            for ch in CH:
                tt, h = ch
                kT = F[2 + h][:, csl[tt]]
                pk = psm.get()
                mm(pk, pk[:], kT, kT, [F[2 + h]])
                P[ch] = sq128.get()
                c.op("dve", lambda e: e.scalar_tensor_tensor(out=P[ch][:], in0=pk[:], scalar=btb[:, tt, h:h + 1],
                                                             in1=decs[ch][:], op0=ALU.mult, op1=ALU.mult),
                     reads=[pk, btb, decs[ch]], writes=[P[ch]])
            for ch in CH:
                pb = psm.get()
                mm(pb, pb[:], P[ch][:], ident, [P[ch], cst])
                Q[ch] = sq128.get(); Y[ch] = sq128.get()
                evac("act", Q[ch], Q[ch][:], pb, pb[:])
                c.op("pool", lambda e: e.tensor_tensor(out=Y[ch][:], in0=ident, in1=Q[ch][:], op=ALU.subtract),
                     reads=[Q[ch], cst], writes=[Y[ch]])
            for s in range(5):
                Pn, Qn = {}, {}
                for ch in CH:
                    pp_ = psm.get()
                    mm(pp_, pp_[:], Q[ch][:], P[ch][:], [Q[ch], P[ch]])
                    Pn[ch] = sq128.get()
                    evac("act", Pn[ch], Pn[ch][:], pp_, pp_[:])
                    if s < 4:
                        pq = psm.get()
                        mm(pq, pq[:], P[ch][:], Q[ch][:], [P[ch], Q[ch]])
                        Qn[ch] = sq128.get()
                        evac("dve", Qn[ch], Qn[ch][:], pq, pq[:])
                for ch in CH:
                    py_ = psm.get()
                    mm(py_, py_[:], Pn[ch][:], Y[ch][:], [Pn[ch], Y[ch]])
                    Yn = sq128.get()
                    c.op("dve", lambda e: e.tensor_tensor(out=Yn[:], in0=py_[:], in1=Y[ch][:], op=ALU.add),
                         reads=[py_, Y[ch]], writes=[Yn])
                    Y[ch] = Yn
                    P[ch] = Pn[ch]
                    if s < 4:
                        Q[ch] = Qn[ch]
            kbg, kd, vb, usb, wTs, attT, qg = {}, {}, {}, {}, {}, {}, {}
            for ch in CH:
                tt, h = ch
                kT = F[2 + h][:, csl[tt]]; vT = F[4 + h][:, csl[tt]]; qT = F[h][:, csl[tt]]
                pkt = psm.get()
                mm(pkt, pkt[:], kT, ident, [F[2 + h], cst])
                kbg[ch] = sq128.get(); kd[ch] = LL["kd"][ci[ch]]
                evac("act", kbg[ch], kbg[ch][:], pkt, pkt[:], scale=be[tt][:, h:h + 1], extra=[be[tt]])
                evac("act", kd[ch], kd[ch][:], pkt, pkt[:], scale=ed[tt][:, h:h + 1], extra=[ed[tt]])
                pvt = psm.get()
                mm(pvt, pvt[:], vT, ident, [F[4 + h], cst])
                vb[ch] = sq128.get()
                evac("act", vb[ch], vb[ch][:], pvt, pvt[:], scale=btb[:, tt, h:h + 1], extra=[btb])
                pqk = psm.get()
                mm(pqk, pqk[:], qT, kT, [F[h], F[2 + h]])
                att = sq128.get()
                c.op("dve", lambda e: e.tensor_tensor(out=att[:], in0=pqk[:], in1=dec[ch][:], op=ALU.mult),
                     reads=[pqk, dec[ch]], writes=[att])
                pat = psm.get()
                mm(pat, pat[:], att[:], ident, [att, cst])
                attT[ch] = LL["attT"][ci[ch]]
                evac("dve", attT[ch], attT[ch][:], pat, pat[:])
                qg[ch] = LL["qg"][ci[ch]]
                c.op("pool", lambda e: e.tensor_tensor(out=qg[ch][:], in0=qT, in1=egrow[ch][:], op=ALU.mult),
                     reads=[F[h], egrow[ch]], writes=[qg[ch]])
            for ch in CH:
                pu = psm.get()
                mm(pu, pu[:], Y[ch][:], vb[ch][:], [Y[ch], vb[ch]])
                usb[ch] = LL["usb"][ci[ch]]
                evac("act", usb[ch], usb[ch][:], pu, pu[:])
                pw = psm.get()
                mm(pw, pw[:], kbg[ch][:], Y[ch][:], [kbg[ch], Y[ch]])
                wTs[ch] = LL["wTs"][ci[ch]]
                evac("dve", wTs[ch], wTs[ch][:], pw, pw[:])
            otok = {ch: LL["otok"][ci[ch]] for ch in CH}
            for tt in TT:
                vnew = {h: sq128.get() for h in range(2)}
                for half in range(2):
                    r = slice(half * 64, half * 64 + 64)
                    for h in range(2):
                        ch = (tt, h)
                        pws = psm.get()
                        mm(pws, pws[:], wTs[ch][:], S[h][:], [wTs[ch], S[h]])
                        c.op("dve", lambda e: e.tensor_tensor(out=vnew[h][r, :], in0=usb[ch][r, :], in1=pws[r, :], op=ALU.subtract),
                             reads=[usb[ch], pws], writes=[vnew[h]])
                    for h in range(2):
                        ch = (tt, h)
                        po = psm.get()
                        mm(po, po[:], qg[ch][:], S[h][:], [qg[ch], S[h]], start=True, stop=False)
                        mm(po, po[:], attT[ch][r, :], vnew[h][r, :], [attT[ch], vnew[h]], start=False, stop=True)
                        evac("act", otok[ch], otok[ch][r, :], po, po[r, :])
                        pst = psm.get()
                        mm(pst, pst[:], kd[ch][r, :], vnew[h][r, :], [kd[ch], vnew[h]])
                        c.op("dve", lambda e: e.scalar_tensor_tensor(out=S[h][:], in0=S[h][:],
                                                                     scalar=egrow[ch][:, half * 64 + 63:half * 64 + 64],
                                                                     in1=pst[:], op0=ALU.mult, op1=ALU.add),
                             reads=[S[h], egrow[ch], pst], writes=[S[h]])
            ssd, ond, potd = {}, {}, {}
            for ch in CH:
                junk = sq128.get()
                ssd[ch] = sm1.get()
                c.op("act", lambda e: e.activation(out=junk[:], in_=otok[ch][:], func=AF.Square, accum_out=ssd[ch][:]),
                     reads=[otok[ch]], writes=[junk, ssd[ch]])
            for ch in CH:
                c.op("act", lambda e: e.activation(out=ssd[ch][:], in_=ssd[ch][:], func=AF.Sqrt, bias=c.eps_t[:, 0:1], scale=1.0 / 128),
                     reads=[ssd[ch], c.eps_t], writes=[ssd[ch]])
            for ch in CH:
                c.op("dve", lambda e: e.reciprocal(out=ssd[ch][:], in_=ssd[ch][:]), reads=[ssd[ch]], writes=[ssd[ch]])
            for ch in CH:
                ond[ch] = sq128.get()
                c.op("dve", lambda e: e.tensor_scalar(out=ond[ch][:], in0=otok[ch][:], scalar1=ssd[ch][:, 0:1], scalar2=None, op0=ALU.mult),
                     reads=[otok[ch], ssd[ch]], writes=[ond[ch]])
            for ch in CH:
                tt, h = ch
                pot = psm.get()
                mm(pot, pot[:], ond[ch][:], ident, [ond[ch], cst])
                c.op("dve", lambda e: e.scalar_tensor_tensor(out=ob[:, h, csl[tt]], in0=pot[:], scalar=ps_[:, 28:29],
                                                             in1=F[6 + h][:, csl[tt]], op0=ALU.mult, op1=ALU.mult),
                     reads=[pot, ps_, F[6 + h]], writes=[ob])
            c.dma("sp", o3[:, :, b0:b0 + n], ob[:, :, :n], reads=[ob], writes=[oT])


def dn_params(conv_w6, a_log2, dt_bias2, norm_w):
    prm = np.zeros((128, 32), np.float32)
    for j in range(6):
        prm[:, j * 4:(j + 1) * 4] = conv_w6[:, j * 128:(j + 1) * 128].T
    prm[:, 24:26] = a_log2[None, :]
    prm[:, 26:28] = dt_bias2[None, :]
    prm[:, 28] = norm_w
    return prm


DA_DS = (-128, 0, 128, 256, 384)
DA_NEGM = -240000.0
LAMBDA_INIT_L1 = 0.8 - 0.6 * math.exp(-0.3 * 1)


def _t5_bucket_np(rel):
    import jax
    import jax.numpy as jnp
    with jax.default_device(jax.devices("cpu")[0]):
        rel = jnp.asarray(rel, jnp.int32)
        nb = 16
        ret = jnp.where(rel > 0, nb, 0)
        n = jnp.abs(rel)
        max_exact = nb // 2
        nf = jnp.maximum(n, 1).astype(jnp.float32)
        large = max_exact + (jnp.log(nf / max_exact) / math.log(128 / max_exact)
                             * (nb - max_exact)).astype(jnp.int32)
        large = jnp.minimum(large, nb - 1)
        return np.asarray(ret + jnp.where(n < max_exact, n, large))


def da_consts():
    r = np.arange(-639, 513)
    bk = _t5_bucket_np(r)
    oh = np.zeros((32, 1152), np.float32)
    oh[bk, np.arange(1152)] = 1.0
    oh[15, :] -= 1.0
    kk = np.arange(128)[:, None]
    qq = np.arange(512)[None, :]
    md = np.zeros((128, 5, 512), np.float32)
    for i, d in enumerate(DA_DS):
        allowed = ((d + kk) // 64) <= (qq // 64)
        md[:, i, :] = np.where(allowed, 0.0, DA_NEGM)
    pm = np.zeros((128, 1), np.float32)
    pm[:112] = -30000.0
    return oh, md, pm


def build_da(lambda_init=LAMBDA_INIT_L1):
    NKT = TD // 128
    qtiles = [(0, 128)] + [(128 + 512 * i, 512) for i in range(16)]
    nc = bass.Bass("TRN2", target_bir_lowering=False)
    with ExitStack() as es:
        c = Ctx(nc, es)
        io = {}
        io["uT"] = c.dram("uT", [D, TD], BF16, "ExternalInput")
        io["wcat"] = c.dram("wcat", [D, 768], F32, "ExternalInput")
        io["oh"] = c.dram("oh", [32, 1152], F32, "ExternalInput")
        io["md"] = c.dram("md", [128, 5, 512], F32, "ExternalInput")
        io["pm"] = c.dram("pm", [128, 1], F32, "ExternalInput")
        io["rb"] = c.dram("rb", [32, 2], F32, "ExternalInput")
        io["rb15"] = c.dram("rb15", [128, 2], F32, "ExternalInput")
        io["lamv"] = c.dram("lamv", [128, 4, 64], F32, "ExternalInput")
        io["sw"] = c.dram("sw", [128, 1], F32, "ExternalInput")
        io["identf"] = c.dram("identf", [128, 128], F32, "ExternalInput")
        io["oT"] = c.dram("oT", [256, TD], BF16, "ExternalOutput")
        make_consts(c)
        emit_da(c, io, lambda_init, "")
        c.finish()
    return nc


def emit_da(c, io, lambda_init, tag):
    NKT = TD // 128
    qtiles = [(0, 128)] + [(128 + 512 * i, 512) for i in range(16)]
    if True:
        uT, wcat, ohd, mdd, pmd, rb, rb15, lamv, sw, idd, oT = (io[k] for k in (
            "uT", "wcat", "oh", "md", "pm", "rb", "rb15", "lamv", "sw", "identf", "oT"))
        tvd = c.dram("tvscr" + tag, [2, 1152], F32, "Internal")
        identb = c.sb([128, 128], BF16, "identb")
        onesb = c.sb([128, 128], BF16, "onesb")
        c.op("pool", lambda e: e.memset(onesb[:], 1.0), writes=[onesb])
        QT = [c.sb([128, TD], BF16, "QT%d" % h) for h in range(2)]
        KT = [c.sb([128, TD], BF16, "KT%d" % h) for h in range(2)]
        V = c.sb([128, NKT, 256], BF16, "V")
        BH = [[c.sb([128, 512], BF16, "BH%d_%d" % (h, i)) for i in range(5)] for h in range(2)]
        BL = [[c.sb([128, 512], BF16, "BL%d_%d" % (h, i)) for i in range(5)] for h in range(2)]
        biasc = c.sb([128, 2], F32, "biasc")
        bias0 = c.sb([128, 2], F32, "bias0")
        neglam = c.sb([128, 1], F32, "neglam")
        swp = c.sb([128, 1], F32, "swp")

        with ExitStack() as es1:
            c.dma("sp", biasc[:], rb15[:], writes=[biasc])
            pms = c.sb([128, 1], F32, "pms", es=es1)
            c.dma("sp", pms[:], pmd[:], writes=[pms])
            c.op("dve", lambda e: e.tensor_scalar(out=bias0[:], in0=biasc[:], scalar1=pms[:, 0:1], scalar2=None, op0=ALU.add),
                 reads=[biasc, pms], writes=[bias0])
            sws = c.sb([128, 1], F32, "sws", es=es1)
            c.dma("sp", sws[:], sw[:], writes=[sws])
            c.op("dve", lambda e: e.tensor_scalar(out=swp[:], in0=sws[:], scalar1=1.0 - lambda_init, scalar2=None, op0=ALU.mult),
                 reads=[sws], writes=[swp])
            lv = c.sb([128, 4, 64], F32, "lv", es=es1)
            c.dma("sp", lv[:], lamv[:], writes=[lv])
            pr = c.sb([128, 2, 64], F32, "lpr", es=es1)
            sm = c.sb([128, 2], F32, "lsm", es=es1)
            for i in range(2):
                c.op("dve", lambda e: e.tensor_tensor(out=pr[:, i, :], in0=lv[:, 2 * i, :], in1=lv[:, 2 * i + 1, :], op=ALU.mult),
                     reads=[lv], writes=[pr])
                c.op("dve", lambda e: e.reduce_sum(out=sm[:, i:i + 1], in_=pr[:, i, :], axis=AX.X), reads=[pr], writes=[sm])
            c.op("act", lambda e: e.activation(out=sm[:], in_=sm[:], func=AF.Exp), reads=[sm], writes=[sm])
            c.op("dve", lambda e: e.tensor_tensor(out=neglam[:], in0=sm[:, 1:2], in1=sm[:, 0:1], op=ALU.subtract),
                 reads=[sm], writes=[neglam])
            c.op("dve", lambda e: e.tensor_scalar(out=neglam[:], in0=neglam[:], scalar1=-lambda_init, scalar2=None, op0=ALU.add),
                 reads=[neglam], writes=[neglam])
            idf = c.sb([128, 128], F32, "idf", es=es1)
            c.dma("sp", idf[:], idd[:], writes=[idf])
            c.op("dve", lambda e: e.tensor_copy(out=identb[:], in_=idf[:]), reads=[idf], writes=[identb])
            ohs = c.sb([32, 1152], F32, "ohs", es=es1)
            c.dma("sp", ohs[:], ohd[:], writes=[ohs])
            rbs = c.sb([32, 2], F32, "rbs", es=es1)
            c.dma("sp", rbs[:], rb[:], writes=[rbs])
            tvs = c.sb([2, 1152], F32, "tvs", es=es1)
            ptv = c.ps([128, 512], F32, "ptv", es=es1)
            for j in range(3):
                c.op("pe", lambda e: e.matmul(ptv[0:2, 0:384], lhsT=rbs[:], rhs=ohs[:, j * 384:(j + 1) * 384], start=True, stop=True),
                     reads=[rbs, ohs], writes=[ptv])
                c.op("act", lambda e: e.activation(out=tvs[:, j * 384:(j + 1) * 384], in_=ptv[0:2, 0:384], func=AF.Copy),
                     reads=[ptv], writes=[tvs])
            c.dma("sp", tvd[:], tvs[:], reads=[tvs], writes=[tvd])
            mds = c.sb([128, 5, 512], F32, "mds", es=es1)
            c.dma("sp", mds[:], mdd[:], writes=[mds])
            G = Rot([c.sb([128, 512], F32, "G%d" % i, es=es1) for i in range(2)])
            Bt = Rot([c.sb([128, 512], F32, "Bt%d" % i, es=es1) for i in range(2)])
            for h in range(2):
                for i, d in enumerate(DA_DS):
                    g_ = G.get()
                    src = bass.AP(tensor=tvd.t.tensor, offset=h * 1152 + d + 128, ap=[[1, 128], [1, 512]])
                    c.dma("sp", g_[:], src, reads=[tvd], writes=[g_])
                    b_ = Bt.get()
                    c.op("dve", lambda e: e.scalar_tensor_tensor(out=b_[:], in0=g_[:, ::-1], scalar=8.0, in1=mds[:, i, :],
                                                                 op0=ALU.mult, op1=ALU.add), reads=[g_, mds], writes=[b_])
                    c.op("act", lambda e: e.activation(out=BH[h][i][:], in_=b_[:], func=AF.Copy), reads=[b_], writes=[BH[h][i]])
                    c.op("dve", lambda e: e.tensor_tensor(out=BL[h][i][:], in0=b_[:], in1=BH[h][i][:], op=ALU.subtract),
                         reads=[b_, BH[h][i]], writes=[BL[h][i]])

        c.barrier()
        with ExitStack() as es2:
            wb = c.sb([128, 8, 768], BF16, "wb", es=es2)
            w3 = wcat[:].rearrange("(k p) n -> p k n", p=128)
            for k0 in range(0, 8, 2):
                load_cast(c, wb, wb[:, k0:k0 + 2, :], w3[:, k0:k0 + 2, :])
            ub = Rot([c.sb([128, 8, 512], BF16, "ub%d" % i, es=es2) for i in range(2)])
            pbig = Rot([c.ps([128, 512], F32, "pb%d" % i, es=es2) for i in range(6)])
            u3 = uT[:].rearrange("(k p) n -> p k n", p=128)
            for (b0, ntl) in DN_BLOCKS:
                n = ntl * 128
                u = ub.get()
                c.dma("sp", u[:, :, :n], u3[:, :, b0:b0 + n], writes=[u])
                for j in range(4):
                    p = pbig.get()
                    for k in range(8):
                        c.op("pe", lambda e: e.matmul(p[:, :n], lhsT=wb[:, k, j * 128:(j + 1) * 128], rhs=u[:, k, :n],
                                                      start=(k == 0), stop=(k == 7)), reads=[wb, u], writes=[p], inc=(k == 7))
                    dst = (QT[j] if j < 2 else KT[j - 2])
                    if j % 2 == 0:
                        c.op("act", lambda e: e.activation(out=dst[:, b0:b0 + n], in_=p[:, :n], func=AF.Copy), reads=[p], writes=[dst])
                    else:
                        c.op("dve", lambda e: e.tensor_copy(out=dst[:, b0:b0 + n], in_=p[:, :n]), reads=[p], writes=[dst])
                for tt in range(ntl):
                    p = pbig.get()
                    for k in range(8):
                        c.op("pe", lambda e: e.matmul(p[:, 0:256], lhsT=u[:, k, tt * 128:(tt + 1) * 128], rhs=wb[:, k, 512:768],
                                                      start=(k == 0), stop=(k == 7)), reads=[wb, u], writes=[p], inc=(k == 7))
                    kt = b0 // 128 + tt
                    if tt % 2 == 0:
                        c.op("act", lambda e: e.activation(out=V[:, kt, :], in_=p[:, 0:256], func=AF.Copy), reads=[p], writes=[V])
                    else:
                        c.op("dve", lambda e: e.tensor_copy(out=V[:, kt, :], in_=p[:, 0:256]), reads=[p], writes=[V])

        c.barrier()
        with ExitStack() as es3:
            sps2 = Rot([c.ps([128, 1024], F32, "sps%d" % i, es=es3) for i in range(2)])
            oacc = [c.ps([128, 512], F32, "oacc%d" % i, es=es3) for i in range(2)]
            dacc = [c.ps([128, 512], F32, "dacc%d" % i, es=es3) for i in range(2)]
            ptb2 = Rot([c.sb([128, 1024], BF16, "pt%d" % i, es=es3) for i in range(3)])
            rr = [c.sb([128, 512], F32, "rr%d" % i, es=es3) for i in range(2)]
            aa = [c.sb([128, 512], F32, "aa%d" % i, es=es3) for i in range(2)]
            sqb = Rot([c.sb([128, 512], F32, "sq%d" % i, es=es3) for i in range(2)])
            rstd = c.sb([128, 512], F32, "rstd", es=es3)
            obr = Rot([c.sb([128, 512], BF16, "ob%d" % i, es=es3) for i in range(2)])
            dsr2 = Rot([c.sb([128, 1024], F32, "dsum%d" % i, es=es3) for i in range(2)])

            def both(t, nq):
                return t[:, :].rearrange("p (c n) -> p c n", c=2)[:, :, :nq]

            for (q0, nq) in qtiles:
                ktmax = (q0 + nq) // 128 - 1
                for h in range(2):
                    dsum = dsr2.get()

                    def emit_s(kt):
                        ps = sps2.get()
                        d = kt * 128 - q0
                        near = d in DA_DS
                        for cc in range(2):
                            rs = slice(cc * 64, cc * 64 + 64)
                            o_ = ps[:, cc * 512:cc * 512 + nq]
                            c.op("pe", lambda e: e.matmul(o_, lhsT=KT[h][rs, kt * 128:(kt + 1) * 128], rhs=QT[h][rs, q0:q0 + nq],
                                                          start=True, stop=not near), reads=[KT[h], QT[h]], writes=[ps], inc=not near)
                            if near:
                                i = DA_DS.index(d)
                                c.op("pe", lambda e: e.matmul(o_, lhsT=identb[:], rhs=BH[h][i][:, :nq], start=False, stop=False),
                                     reads=[identb, BH[h][i]], writes=[ps], inc=False)
                                c.op("pe", lambda e: e.matmul(o_, lhsT=identb[:], rhs=BL[h][i][:, :nq], start=False, stop=True),
                                     reads=[identb, BL[h][i]], writes=[ps])
                        return ps

                    ps_cur = emit_s(0)
                    for kt in range(ktmax + 1):
                        ps_next = emit_s(kt + 1) if kt < ktmax else None
                        pt = ptb2.get()
                        bsrc = bias0 if kt == 0 else biasc
                        c.op("act", lambda e: e.activation(out=both(pt, nq), in_=both(ps_cur, nq), func=AF.Exp,
                                                           bias=bsrc[:, h:h + 1], scale=0.125),
                             reads=[ps_cur, bsrc], writes=[pt])
                        for cc in range(2):
                            c.op("pe", lambda e: e.matmul(oacc[cc][:, :nq], lhsT=V[:, kt, h * 128:(h + 1) * 128],
                                                          rhs=pt[:, cc * 512:cc * 512 + nq],
                                                          start=(kt == 0), stop=(kt == ktmax)), reads=[V, pt], writes=[oacc[cc]])
                        if kt == 0:
                            c.op("dve", lambda e: e.tensor_copy(out=both(dsum, nq), in_=both(pt, nq)), reads=[pt], writes=[dsum])
                        else:
                            c.op("dve", lambda e: e.tensor_tensor(out=both(dsum, nq), in0=both(dsum, nq), in1=both(pt, nq), op=ALU.add),
                                 reads=[pt, dsum], writes=[dsum])
                        ps_cur = ps_next
                    for cc in range(2):
                        c.op("pe", lambda e: e.matmul(dacc[cc][:, :nq], lhsT=c.ones[:], rhs=dsum[:, cc * 512:cc * 512 + nq], start=True, stop=True),
                             reads=[c.ones, dsum], writes=[dacc[cc]])
                    for cc in range(2):
                        if q0 == 0:
                            c.op("dve", lambda e: e.tensor_scalar(out=rr[cc][:, :nq], in0=dacc[cc][:, :nq], scalar1=1e-30, scalar2=None,
                                                                  op0=ALU.max), reads=[dacc[cc]], writes=[rr[cc]])
                            c.op("dve", lambda e: e.reciprocal(out=rr[cc][:, :nq], in_=rr[cc][:, :nq]), reads=[rr[cc]], writes=[rr[cc]])
                        else:
                            c.op("dve", lambda e: e.reciprocal(out=rr[cc][:, :nq], in_=dacc[cc][:, :nq]), reads=[dacc[cc]], writes=[rr[cc]])
                        c.op("dve", lambda e: e.tensor_tensor(out=aa[cc][:, :nq], in0=oacc[cc][:, :nq], in1=rr[cc][:, :nq], op=ALU.mult),
                             reads=[oacc[cc], rr[cc]], writes=[aa[cc]])
                    c.op("dve", lambda e: e.scalar_tensor_tensor(out=aa[0][:, :nq], in0=aa[1][:, :nq], scalar=neglam[:, 0:1],
                                                                 in1=aa[0][:, :nq], op0=ALU.mult, op1=ALU.add),
                         reads=[aa[0], aa[1], neglam], writes=[aa[0]])
                    pstat_ = sps2.get()
                    rms_stats(c, c.ones, lambda k: (aa[0][:, :nq], aa[0]), 1, nq, pstat_.view(pstat_.t[:, 0:512]), sqb, rstd, 128.0)
                    ob = obr.get()
                    c.op("dve", lambda e: e.scalar_tensor_tensor(out=ob[:, :nq], in0=aa[0][:, :nq], scalar=swp[:, 0:1],
                                                                 in1=rstd[:, :nq], op0=ALU.mult, op1=ALU.mult),
                         reads=[aa[0], swp, rstd], writes=[ob])
                    if q0 == 0:
                        c.op("dve", lambda e: e.memset(ob[:, 0:112], 0.0), writes=[ob])
                    c.dma("sp", oT[h * 128:(h + 1) * 128, q0:q0 + nq], ob[:, :nq], reads=[ob], writes=[oT])


def build_pre():
    nc = bass.Bass("TRN2", target_bir_lowering=False)
    with ExitStack() as es:
        c = Ctx(nc, es)
        io = {}
        io["hT"] = c.dram("hT", [D, NTC], F32, "ExternalInput")
        io["nw"] = c.dram("nw", [128, 8], F32, "ExternalInput")
        io["uout"] = c.dram("uout", [D, NTC], BF16, "ExternalOutput")
        make_consts(c)
        emit_pre(c, io)
        c.finish()
    return nc


def emit_pre(c, io):
    if True:
        hT, nw, uout = io["hT"], io["nw"], io["uout"]
        nws = c.sb([128, 8], F32, "nws")
        c.dma("sp", nws[:], nw[:], writes=[nws])
        H = Rot([c.sb([128, 8, GS], F32, "H%d" % g) for g in range(2)])
        U = Rot([c.sb([128, 8, GS], BF16, "U%d" % g) for g in range(2)])
        sqb = Rot([c.sb([128, GS], F32, "sq%d" % i) for i in range(2)])
        rstd = c.sb([128, GS], F32, "rstd")
        pstat = c.ps([128, 512], F32, "pstat")
        hT3 = hT[:].rearrange("(k p) n -> p k n", p=128)
        uo3 = uout[:].rearrange("(k p) n -> p k n", p=128)
        for g in range(NG):
            h = H.get()
            c.dma("sp", h[:], hT3[:, :, g * GS:(g + 1) * GS], writes=[h])
            rms_stats(c, c.ones, lambda k: (h[:, k, :], h), 8, GS, pstat, sqb, rstd, float(D))
            u = U.get()
            for m in range(8):
                c.op("dve", lambda e: e.scalar_tensor_tensor(out=u[:, m, :], in0=h[:, m, :], scalar=nws[:, m:m + 1],
                                                             in1=rstd[:], op0=ALU.mult, op1=ALU.mult),
                     reads=[h, nws, rstd], writes=[u])
            c.dma("sp", uo3[:, :, g * GS:(g + 1) * GS], u[:], reads=[u], writes=[uout])


def _run(nc, in_maps):
    res = run_bass_kernel_spmd(nc, in_maps, core_ids=list(range(NCORES)))
    return res.results


def _nwcols(w):
    return np.ascontiguousarray(np.asarray(w, np.float32).reshape(8, 128).T)


def _tok_shards(fullT):
    out = []
    for b in range(B):
        for j in range(4):
            out.append(np.ascontiguousarray(fullT[b][:, j * NTC:(j + 1) * NTC]))
    return out


def _gather_tok(shards):
    return [np.concatenate([shards[4 * b + j] for j in range(4)], axis=1) for b in range(B)]


def kernel_unfused(x, meta_tokens, rel_bias, norm_mix_w, norm_mlp_w, final_norm_w,
           dn_w_in, dn_conv_w, dn_a_log, dn_dt_bias, dn_norm_w, dn_w_out,
           da_w_in, da_lam_q1, da_lam_k1, da_lam_q2, da_lam_k2, da_subln_w, da_w_out,
           lru_w_in, lru_conv_w, lru_conv_b, lru_w_rgate, lru_b_rgate, lru_w_igate,
           lru_b_igate, lru_lambda, lru_w_out, mlp_w1, mlp_w2):
    f32 = np.float32
    x = np.asarray(x, f32)
    meta = np.asarray(meta_tokens, f32)
    bf = ml_dtypes.bfloat16
    hT_full = []
    for b in range(B):
        seq = np.concatenate([np.zeros((PADF, D), f32), meta, x[b]], axis=0)
        hT_full.append(np.ascontiguousarray(seq.T))
    h_sh = _tok_shards(hT_full)
    nc = build_pre()
    r = _run(nc, [{"hT": h_sh[c], "nw": _nwcols(norm_mix_w[0])} for c in range(NCORES)])
    u_sh = [r[c]["uout"] for c in range(NCORES)]
    depth = 4
    zpad = np.zeros((D, 64), bf)
    for layer in range(depth):
        kind = layer % 3
        slot = layer // 3
        u_full = _gather_tok(u_sh)
        ims = []
        if kind == 0:
            w_in = np.asarray(dn_w_in[slot], f32)
            cw = np.asarray(dn_conv_w[slot], f32)
            cst = dn_consts()
            for c in range(NCORES):
                b, g = divmod(c, 4)
                h0 = 2 * g
                s = slice(h0 * 128, h0 * 128 + 256)
                wcat = np.concatenate([w_in[:, 0:1024][:, s], w_in[:, 1024:2048][:, s], w_in[:, 2048:3072][:, s],
                                       w_in[:, 3072:4096][:, s], w_in[:, 4096 + h0:4096 + h0 + 2],
                                       w_in[:, 4104 + h0:4104 + h0 + 2]], axis=1)
                cw6 = np.concatenate([cw[:, 0:1024][:, s], cw[:, 1024:2048][:, s], cw[:, 2048:3072][:, s]], axis=1)
                prm = dn_params(cw6, np.asarray(dn_a_log[slot], f32)[h0:h0 + 2],
                                np.asarray(dn_dt_bias[slot], f32)[h0:h0 + 2], np.asarray(dn_norm_w[slot], f32))
                ims.append({"uT": np.ascontiguousarray(np.concatenate([zpad, u_full[b]], axis=1)),
                            "wcat": np.ascontiguousarray(wcat), "cst": cst, "prm": prm})
            r = _run(build_dn(), ims)
            o_full = [np.concatenate([r[4 * b + g]["oT"][:, 64:] for g in range(4)], axis=0) for b in range(B)]
            wo = np.asarray(dn_w_out[slot], f32)
        elif kind == 1:
            w_in = np.asarray(da_w_in[slot], f32)
            oh, md, pm = da_consts()
            rbt = np.asarray(rel_bias, f32)
            lamv = np.stack([np.asarray(v[slot], f32) for v in (da_lam_q1, da_lam_k1, da_lam_q2, da_lam_k2)], axis=0)
            lamv = np.ascontiguousarray(np.broadcast_to(lamv[None], (128, 4, 64)))
            sw = np.ascontiguousarray(np.asarray(da_subln_w[slot], f32)[:, None])
            ident = np.eye(128, dtype=f32)
            for c in range(NCORES):
                b, g = divmod(c, 4)
                h0 = 2 * g
                s = slice(h0 * 128, h0 * 128 + 256)
                wcat = np.concatenate([w_in[:, 0:1024][:, s], w_in[:, 1024:2048][:, s], w_in[:, 2048:3072][:, s]], axis=1)
                ims.append({"uT": np.ascontiguousarray(np.concatenate([zpad, u_full[b]], axis=1)),
                            "wcat": np.ascontiguousarray(wcat), "oh": oh, "md": md, "pm": pm,
                            "rb": np.ascontiguousarray(rbt[:, h0:h0 + 2]),
                            "rb15": np.ascontiguousarray(np.broadcast_to(rbt[15:16, h0:h0 + 2], (128, 2))),
                            "lamv": lamv, "sw": sw, "identf": ident})
            lam_init = 0.8 - 0.6 * math.exp(-0.3 * layer)
            r = _run(build_da(lam_init), ims)
            o_full = [np.concatenate([r[4 * b + g]["oT"][:, 64:] for g in range(4)], axis=0) for b in range(B)]
            wo = np.asarray(da_w_out[slot], f32)
        else:
            w_in = np.asarray(lru_w_in[slot], f32)
            cw = np.asarray(lru_conv_w[slot], f32)
            for c in range(NCORES):
                b, g = divmod(c, 4)
                s = slice(g * 256, (g + 1) * 256)
                prm = lru_params(cw[:, s], np.asarray(lru_conv_b[slot], f32)[s], np.asarray(lru_b_rgate[slot], f32)[s],
                                 np.asarray(lru_b_igate[slot], f32)[s], np.asarray(lru_lambda[slot], f32)[s])
                ims.append({"uT": u_full[b], "wg": np.ascontiguousarray(w_in[:, 0:1024][:, s]),
                            "wx": np.ascontiguousarray(w_in[:, 1024:2048][:, s]),
                            "wr": np.ascontiguousarray(np.asarray(lru_w_rgate[slot], f32)[g]),
                            "wi": np.ascontiguousarray(np.asarray(lru_w_igate[slot], f32)[g]), "prm": prm})
            r = _run(build_lru(), ims)
            o_full = [np.concatenate([r[4 * b + g]["yT"] for g in range(4)], axis=0) for b in range(B)]
            wo = np.asarray(lru_w_out[slot], f32)
        o_sh = _tok_shards(o_full)
        final = (layer == depth - 1)
        nxt = final_norm_w if final else norm_mix_w[layer + 1]
        nw = np.ascontiguousarray(np.concatenate([_nwcols(norm_mlp_w[layer]), _nwcols(nxt)], axis=1))
        w1 = np.asarray(mlp_w1[layer], f32)
        w2 = np.asarray(mlp_w2[layer], f32)
        r = _run(build_post(final), [{"hT": h_sh[c], "oT": o_sh[c], "wo": wo, "w1": w1, "w2": w2, "nw": nw}
                                     for c in range(NCORES)])
        h_sh = [r[c]["hout"] for c in range(NCORES)]
        u_sh = [r[c]["uout"] for c in range(NCORES)]
    out_full = _gather_tok(u_sh)
    out = np.stack([np.ascontiguousarray(out_full[b][:, PADF + NMETA:].T) for b in range(B)], axis=0)
    return out.astype(f32)


DEPTH = 4


def _phase(c, fn):
    base = c.es
    with ExitStack() as pes:
        c.es = pes
        fn()
        c.barrier()
    c.es = base


def build_fused():
    nc = bass.Bass("TRN2", target_bir_lowering=False)
    with ExitStack() as es:
        c = Ctx(nc, es)
        h0T = c.dram("h0T", [D, T], F32, "ExternalInput")
        outT = c.dram("outT", [D, T], F32, "ExternalOutput")
        HT = c.dram("HT", [D, T], F32, "Internal")
        UT = c.dram("UT", [D, TD], BF16, "Internal")
        OT = c.dram("OT", [D, TD], BF16, "Internal")
        nw0 = c.dram("nw0", [128, 8], F32, "ExternalInput")
        W = {}
        for l in range(DEPTH):
            kind = l % 3
            p = "L%d_" % l
            if kind == 0:
                W[p + "wcat"] = c.dram(p + "wcat", [4, D, 1028], F32, "ExternalInput")
                W[p + "prm"] = c.dram(p + "prm", [4, 128, 32], F32, "ExternalInput")
            elif kind == 1:
                W[p + "wcat"] = c.dram(p + "wcat", [4, D, 768], F32, "ExternalInput")
                W[p + "rb"] = c.dram(p + "rb", [4, 32, 2], F32, "ExternalInput")
                W[p + "rb15"] = c.dram(p + "rb15", [4, 128, 2], F32, "ExternalInput")
                W[p + "lamv"] = c.dram(p + "lamv", [128, 4, 64], F32, "ExternalInput")
                W[p + "sw"] = c.dram(p + "sw", [128, 1], F32, "ExternalInput")
            else:
                W[p + "wg"] = c.dram(p + "wg", [4, D, 256], F32, "ExternalInput")
                W[p + "wx"] = c.dram(p + "wx", [4, D, 256], F32, "ExternalInput")
                W[p + "wr"] = c.dram(p + "wr", [4, 256, 256], F32, "ExternalInput")
                W[p + "wi"] = c.dram(p + "wi", [4, 256, 256], F32, "ExternalInput")
                W[p + "prm"] = c.dram(p + "prm", [4, 128, 16], F32, "ExternalInput")
            W[p + "wo"] = c.dram(p + "wo", [D, D], F32, "ExternalInput")
            W[p + "w1"] = c.dram(p + "w1", [D, DFF], F32, "ExternalInput")
            W[p + "w2"] = c.dram(p + "w2", [DFF, D], F32, "ExternalInput")
            W[p + "nw"] = c.dram(p + "nw", [128, 16], F32, "ExternalInput")
        dn_cst = c.dram("dn_cst", [128, 6, 128], F32, "ExternalInput")
        da_oh = c.dram("da_oh", [32, 1152], F32, "ExternalInput")
        da_md = c.dram("da_md", [128, 5, 512], F32, "ExternalInput")
        da_pm = c.dram("da_pm", [128, 1], F32, "ExternalInput")
        identf = c.dram("identf", [128, 128], F32, "ExternalInput")
        make_consts(c)

        def sh(t, j, off=0):
            return t.view(t.t[:, off + j * NTC:off + (j + 1) * NTC])

        def zero_front():
            z = c.sb([128, 8, 64], BF16, "zfront")
            c.op("pool", lambda e: e.memset(z[:], 0.0), writes=[z])
            c.dma("sp", UT[:].rearrange("(k p) n -> p k n", p=128)[:, :, 0:64], z[:], reads=[z], writes=[UT])
        _phase(c, zero_front)
        for j in range(4):
            _phase(c, lambda j=j: emit_pre(c, {"hT": sh(h0T, j), "nw": nw0, "uout": sh(UT, j, 64)}))
        for l in range(DEPTH):
            kind = l % 3
            p = "L%d_" % l
            for g in range(4):
                rows = slice(g * 256, (g + 1) * 256)
                if kind == 0:
                    io = {"uT": UT, "wcat": W[p + "wcat"].view(W[p + "wcat"].t[g]), "cst": dn_cst,
                          "prm": W[p + "prm"].view(W[p + "prm"].t[g]), "oT": OT.view(OT.t[rows, :])}
                    _phase(c, lambda io=io: emit_dn(c, io))
                elif kind == 1:
                    io = {"uT": UT, "wcat": W[p + "wcat"].view(W[p + "wcat"].t[g]), "oh": da_oh, "md": da_md, "pm": da_pm,
                          "rb": W[p + "rb"].view(W[p + "rb"].t[g]), "rb15": W[p + "rb15"].view(W[p + "rb15"].t[g]),
                          "lamv": W[p + "lamv"], "sw": W[p + "sw"], "identf": identf, "oT": OT.view(OT.t[rows, :])}
                    lam_init = 0.8 - 0.6 * math.exp(-0.3 * l)
                    _phase(c, lambda io=io, lam_init=lam_init, tag="_%d_%d" % (l, g): emit_da(c, io, lam_init, tag))
                else:
                    io = {"uT": UT.view(UT.t[:, 64:64 + T]), "wg": W[p + "wg"].view(W[p + "wg"].t[g]),
                          "wx": W[p + "wx"].view(W[p + "wx"].t[g]), "wr": W[p + "wr"].view(W[p + "wr"].t[g]),
                          "wi": W[p + "wi"].view(W[p + "wi"].t[g]), "prm": W[p + "prm"].view(W[p + "prm"].t[g]),
                          "yT": OT.view(OT.t[rows, 64:64 + T])}
                    _phase(c, lambda io=io: emit_lru(c, io))
            final = (l == DEPTH - 1)
            for j in range(4):
                io = {"hT": sh(h0T if l == 0 else HT, j), "oT": sh(OT, j, 64), "wo": W[p + "wo"], "w1": W[p + "w1"],
                      "w2": W[p + "w2"], "nw": W[p + "nw"], "hout": sh(HT, j),
                      "uout": sh(outT, j) if final else sh(UT, j, 64)}
                _phase(c, lambda io=io, final=final: emit_post(c, io, final))
            if l == 1:
                c.switch_sem("pe")
        c.finish()
    return nc


def fused_inputs(b, x, meta_tokens, rel_bias, norm_mix_w, norm_mlp_w, final_norm_w,
                 dn_w_in, dn_conv_w, dn_a_log, dn_dt_bias, dn_norm_w, dn_w_out,
                 da_w_in, da_lam_q1, da_lam_k1, da_lam_q2, da_lam_k2, da_subln_w, da_w_out,
                 lru_w_in, lru_conv_w, lru_conv_b, lru_w_rgate, lru_b_rgate, lru_w_igate,
                 lru_b_igate, lru_lambda, lru_w_out, mlp_w1, mlp_w2, shared=None):
    f32 = np.float32
    im = {}
    seq = np.concatenate([np.zeros((PADF, D), f32), np.asarray(meta_tokens, f32), np.asarray(x[b], f32)], axis=0)
    im["h0T"] = np.ascontiguousarray(seq.T)
    if shared is not None:
        im.update(shared)
        return im
    sh = {}
    sh["nw0"] = _nwcols(norm_mix_w[0])
    oh, md, pm = da_consts()
    sh["dn_cst"] = dn_consts()
    sh["da_oh"], sh["da_md"], sh["da_pm"] = oh, md, pm
    sh["identf"] = np.eye(128, dtype=f32)
    rbt = np.asarray(rel_bias, f32)
    for l in range(DEPTH):
        kind, slot = l % 3, l // 3
        p = "L%d_" % l
        if kind == 0:
            w_in = np.asarray(dn_w_in[slot], f32)
            cw = np.asarray(dn_conv_w[slot], f32)
            wc, pr = [], []
            for g in range(4):
                h0 = 2 * g
                s = slice(h0 * 128, h0 * 128 + 256)
                wc.append(np.concatenate([w_in[:, 0:1024][:, s], w_in[:, 1024:2048][:, s], w_in[:, 2048:3072][:, s],
                                          w_in[:, 3072:4096][:, s], w_in[:, 4096 + h0:4096 + h0 + 2],
                                          w_in[:, 4104 + h0:4104 + h0 + 2]], axis=1))
                cw6 = np.concatenate([cw[:, 0:1024][:, s], cw[:, 1024:2048][:, s], cw[:, 2048:3072][:, s]], axis=1)
                pr.append(dn_params(cw6, np.asarray(dn_a_log[slot], f32)[h0:h0 + 2],
                                    np.asarray(dn_dt_bias[slot], f32)[h0:h0 + 2], np.asarray(dn_norm_w[slot], f32)))
            sh[p + "wcat"] = np.ascontiguousarray(np.stack(wc))
            sh[p + "prm"] = np.ascontiguousarray(np.stack(pr))
            wo = dn_w_out[slot]
        elif kind == 1:
            w_in = np.asarray(da_w_in[slot], f32)
            wc, rb, rb15 = [], [], []
            for g in range(4):
                h0 = 2 * g
                s = slice(h0 * 128, h0 * 128 + 256)
                wc.append(np.concatenate([w_in[:, 0:1024][:, s], w_in[:, 1024:2048][:, s], w_in[:, 2048:3072][:, s]], axis=1))
                rb.append(rbt[:, h0:h0 + 2])
                rb15.append(np.broadcast_to(rbt[15:16, h0:h0 + 2], (128, 2)))
            sh[p + "wcat"] = np.ascontiguousarray(np.stack(wc))
            sh[p + "rb"] = np.ascontiguousarray(np.stack(rb))
            sh[p + "rb15"] = np.ascontiguousarray(np.stack(rb15))
            lamv = np.stack([np.asarray(v[slot], f32) for v in (da_lam_q1, da_lam_k1, da_lam_q2, da_lam_k2)], axis=0)
            sh[p + "lamv"] = np.ascontiguousarray(np.broadcast_to(lamv[None], (128, 4, 64)))
            sh[p + "sw"] = np.ascontiguousarray(np.asarray(da_subln_w[slot], f32)[:, None])
            wo = da_w_out[slot]
        else:
            w_in = np.asarray(lru_w_in[slot], f32)
            cw = np.asarray(lru_conv_w[slot], f32)
            wg, wx, pr = [], [], []
            for g in range(4):
                s = slice(g * 256, (g + 1) * 256)
                wg.append(w_in[:, 0:1024][:, s])
                wx.append(w_in[:, 1024:2048][:, s])
                pr.append(lru_params(cw[:, s], np.asarray(lru_conv_b[slot], f32)[s], np.asarray(lru_b_rgate[slot], f32)[s],
                                     np.asarray(lru_b_igate[slot], f32)[s], np.asarray(lru_lambda[slot], f32)[s]))
            sh[p + "wg"] = np.ascontiguousarray(np.stack(wg))
            sh[p + "wx"] = np.ascontiguousarray(np.stack(wx))
            sh[p + "wr"] = np.ascontiguousarray(np.asarray(lru_w_rgate[slot], f32))
            sh[p + "wi"] = np.ascontiguousarray(np.asarray(lru_w_igate[slot], f32))
            sh[p + "prm"] = np.ascontiguousarray(np.stack(pr))
            wo = lru_w_out[slot]
        final = (l == DEPTH - 1)
        nxt = final_norm_w if final else norm_mix_w[l + 1]
        sh[p + "wo"] = np.ascontiguousarray(np.asarray(wo, f32))
        sh[p + "w1"] = np.ascontiguousarray(np.asarray(mlp_w1[l], f32))
        sh[p + "w2"] = np.ascontiguousarray(np.asarray(mlp_w2[l], f32))
        sh[p + "nw"] = np.ascontiguousarray(np.concatenate([_nwcols(norm_mlp_w[l]), _nwcols(nxt)], axis=1))
    im.update(sh)
    im["_shared"] = sh
    return im


def kernel(**inputs):
    x = inputs["x"]
    im0 = fused_inputs(0, **inputs)
    shared = im0.pop("_shared")
    im1 = fused_inputs(1, **inputs, shared=shared)
    nc = build_fused()
    res = run_bass_kernel_spmd(nc, [im0, im1], core_ids=[0, 1])
    out = np.stack([np.ascontiguousarray(res.results[b]["outT"][:, PADF + NMETA:].T) for b in range(B)], axis=0)
    return out.astype(np.float32)
```

```python
import math
from contextlib import ExitStack

import numpy as np
import ml_dtypes
import concourse.bass as bass
import concourse.mybir as mybir
from concourse.bass_utils import run_bass_kernel_spmd

F32 = mybir.dt.float32
BF16 = mybir.dt.bfloat16
AF = mybir.ActivationFunctionType
ALU = mybir.AluOpType
AX = mybir.AxisListType

D = 1024
B = 2
SEQ = 8192
NMETA = 16
PADF = 48
T = PADF + NMETA + SEQ
NTC = T // 4
EPS = 1e-6
DFF = 4096
NCORES = 8


class Tl:
    def __init__(self, t, name, st=None):
        self.t = t
        self.name = name
        self.st = st if st is not None else [None, {}]

    @property
    def w(self):
        return self.st[0]

    @w.setter
    def w(self, v):
        self.st[0] = v

    @property
    def r(self):
        return self.st[1]

    @r.setter
    def r(self, v):
        self.st[1] = v

    def view(self, ap):
        return Tl(ap, self.name, self.st)

    def __getitem__(self, idx):
        return self.t[idx]


class Ctx:
    NDMA = 8

    def __init__(self, nc, es):
        self.nc = nc
        self.es = es
        self.eng = {"pe": nc.tensor, "act": nc.scalar, "dve": nc.vector,
                    "pool": nc.gpsimd, "sp": nc.sync}
        self.sem = {k: es.enter_context(nc.semaphore("s_" + k)) for k in self.eng}
        self.cnt = {k: 0 for k in self.eng}
        self.seen = {k: {} for k in self.eng}
        self.dsem = {}
        self.dcnt = {}
        for q in ("sp", "pool"):
            self.dsem[q] = [es.enter_context(nc.semaphore("d_%s%d" % (q, i)))
                            for i in range(self.NDMA)]
            self.dcnt[q] = 0
        self.ntile = 0

    def sb(self, shape, dt=F32, name=None, es=None):
        self.ntile += 1
        name = "%s_%d" % (name or "t", self.ntile)
        t = (es or self.es).enter_context(self.nc.sbuf_tensor(name, list(shape), dt))
        return Tl(t, name)

    def ps(self, shape, dt=F32, name=None, es=None):
        self.ntile += 1
        name = "%s_%d" % (name or "p", self.ntile)
        t = (es or self.es).enter_context(self.nc.psum_tensor(name, list(shape), dt))
        return Tl(t, name)

    def dram(self, name, shape, dt, kind):
        t = self.nc.dram_tensor(name, list(shape), dt, kind=kind)
        return Tl(t.ap(), name)

    def _wait(self, e, sem, val):
        key = id(sem)
        if self.seen[e].get(key, 0) >= val:
            return
        self.eng[e].wait_ge(sem, val)
        self.seen[e][key] = val

    def _deps(self, e, reads, writes):
        deps = {}

        def add(d):
            if d is None:
                return
            s, v = d
            if deps.get(id(s), (None, 0))[1] < v:
                deps[id(s)] = (s, v)
        for t in reads:
            add(t.w)
        for t in writes:
            add(t.w)
            for s_v in t.r.values():
                add(s_v)
        for s, v in deps.values():
            if e == "pe" and s is self.sem["pe"]:
                continue
            self._wait(e, s, v)

    def _mark(self, token, reads, writes):
        s, v = token
        for t in reads:
            t.r[id(s)] = (s, v)
        for t in writes:
            t.w = (s, v)
            t.r = {}

    def op(self, e, fn, reads=(), writes=(), inc=True):
        self._deps(e, reads, writes)
        ins = fn(self.eng[e])
        if inc:
            self.cnt[e] += 1
            ins.then_inc(self.sem[e], 1)
            tok = (self.sem[e], self.cnt[e])
        else:
            tok = (self.sem[e], self.cnt[e] + 1)
        self._mark(tok, reads, writes)
        return ins

    def dma(self, q, out, in_, reads=(), writes=(), **kw):
        self._deps(q, reads, writes)
        i = self.dcnt[q]
        s = self.dsem[q][i % self.NDMA]
        prev = 16 * (i // self.NDMA)
        if prev:
            self._wait(q, s, prev)
        ins = self.eng[q].dma_start(out=out, in_=in_, **kw)
        ins.then_inc(s, 16)
        self.dcnt[q] += 1
        self._mark((s, prev + 16), reads, writes)

    def switch_sem(self, e):
        self.nsw = getattr(self, "nsw", 0) + 1
        self.sem[e] = self.es.enter_context(self.nc.semaphore("s_%s_%d" % (e, self.nsw)))
        self.cnt[e] = 0

    def barrier(self):
        for e in self.eng:
            for e2 in self.eng:
                if e2 != e and self.cnt[e2] > 0:
                    self._wait(e, self.sem[e2], self.cnt[e2])
            for q in self.dsem:
                n = self.dcnt[q]
                for j, s in enumerate(self.dsem[q]):
                    k = (n - j + self.NDMA - 1) // self.NDMA
                    if k > 0:
                        self._wait(e, s, 16 * k)

    def finish(self):
        for q in self.dsem:
            n = self.dcnt[q]
            for j, s in enumerate(self.dsem[q]):
                k = (n - j + self.NDMA - 1) // self.NDMA
                if k > 0:
                    self._wait("sp", s, 16 * k)


class Rot:
    def __init__(self, tiles):
        self.tiles = tiles
        self.i = 0

    def get(self):
        t = self.tiles[self.i % len(self.tiles)]
        self.i += 1
        return t


def rms_stats(c, ones, src_tiles_fn, nk, n, pstat, sqrot, rstd, scale_div):
    for k in range(nk):
        src, src_t = src_tiles_fn(k)
        sq = sqrot.get()
        c.op("act", lambda e: e.activation(out=sq[:, :n], in_=src, func=AF.Square),
             reads=[src_t], writes=[sq])
        c.op("pe", lambda e: e.matmul(pstat[:, :n], lhsT=ones[:], rhs=sq[:, :n],
                                      start=(k == 0), stop=(k == nk - 1)),
             reads=[ones, sq], writes=[pstat])
    c.op("act", lambda e: e.activation(out=rstd[:, :n], in_=pstat[:, :n], func=AF.Sqrt,
                                       bias=c.eps_t[:, 0:1], scale=1.0 / scale_div),
         reads=[pstat, c.eps_t], writes=[rstd])
    c.op("dve", lambda e: e.reciprocal(out=rstd[:, :n], in_=rstd[:, :n]),
         reads=[rstd], writes=[rstd])


def make_consts(c):
    c.eps_t = c.sb([128, 1], F32, "eps_t")
    c.op("pool", lambda e: e.memset(c.eps_t[:], EPS), writes=[c.eps_t])
    c.ones = c.sb([128, 128], F32, "ones")
    c.op("pool", lambda e: e.memset(c.ones[:], 1.0), writes=[c.ones])


GS = 344
NG = NTC // GS


def build_post(final):
    nc = bass.Bass("TRN2", target_bir_lowering=False)
    with ExitStack() as es:
        c = Ctx(nc, es)
        io = {}
        io["hT"] = c.dram("hT", [D, NTC], F32, "ExternalInput")
        io["oT"] = c.dram("oT", [D, NTC], BF16, "ExternalInput")
        io["wo"] = c.dram("wo", [D, D], F32, "ExternalInput")
        io["w1"] = c.dram("w1", [D, DFF], F32, "ExternalInput")
        io["w2"] = c.dram("w2", [DFF, D], F32, "ExternalInput")
        io["nw"] = c.dram("nw", [128, 16], F32, "ExternalInput")
        io["hout"] = c.dram("hout", [D, NTC], F32, "ExternalOutput")
        io["uout"] = c.dram("uout", [D, NTC], F32 if final else BF16, "ExternalOutput")
        make_consts(c)
        emit_post(c, io, final)
        c.finish()
    return nc


def emit_post(c, io, final):
    if True:
        hT, oT, wo, w1, w2, nw, hout, uout = (io[k] for k in ("hT", "oT", "wo", "w1", "w2", "nw", "hout", "uout"))
        nws = c.sb([128, 16], F32, "nws")
        c.dma("sp", nws[:], nw[:], writes=[nws])

        H = [c.sb([128, 8, GS], F32, "H%d" % g) for g in range(NG)]
        XB = [c.sb([128, 8, GS], BF16, "XB%d" % g) for g in range(NG)]
        wbuf = Rot([c.sb([128, 8192], BF16, "wb%d" % i) for i in range(2)])
        abuf = Rot([c.sb([128, 4, GS], BF16, "ab%d" % i) for i in range(2)])
        rbuf = Rot([c.sb([128, GS], F32, "rb%d" % i) for i in range(3)])
        sqb = Rot([c.sb([128, GS], F32, "sq%d" % i) for i in range(2)])
        rstd = c.sb([128, GS], F32, "rstd")
        uo = Rot([c.sb([128, 8, GS], F32 if final else BF16, "uo%d" % i) for i in range(2)])
        pa = Rot([c.ps([128, 512], F32, "pa%d" % i) for i in range(4)])
        py = Rot([c.ps([128, 512], F32, "py%d" % i) for i in range(3)])
        pstat = c.ps([128, 512], F32, "pstat")

        hT3 = hT[:].rearrange("(k p) n -> p k n", p=128)
        oT3 = oT[:].rearrange("(k p) n -> p k n", p=128)
        ho3 = hout[:].rearrange("(k p) n -> p k n", p=128)
        uo3 = uout[:].rearrange("(k p) n -> p k n", p=128)

        wob = wbuf.get()
        wo3 = wo[:].rearrange("(k p) n -> p k n", p=128)
        for k0 in range(0, 8, 2):
            c.dma("pool", wob[:, k0 * 1024:(k0 + 2) * 1024].rearrange("p (k n) -> p k n", k=2), wo3[:, k0:k0 + 2, :], writes=[wob])
        for g in range(NG):
            c.dma("sp", XB[g][:], oT3[:, :, g * GS:(g + 1) * GS], writes=[XB[g]])
            c.dma("sp", H[g][:], hT3[:, :, g * GS:(g + 1) * GS], writes=[H[g]])

        def load_eighth(e8):
            wb = wbuf.get()
            w13 = w1[:].rearrange("(k p) n -> p k n", p=128)
            for k0 in range(0, 8, 4):
                c.dma("pool", wb[:, k0 * 512:(k0 + 4) * 512].rearrange("p (k n) -> p k n", k=4),
                      w13[:, k0:k0 + 4, e8 * 512:(e8 + 1) * 512], writes=[wb])
            w23 = w2[:].rearrange("(f p) n -> p f n", p=128)
            for f0 in range(0, 4, 2):
                c.dma("pool", wb[:, 4096 + f0 * 1024:4096 + (f0 + 2) * 1024].rearrange("p (f n) -> p f n", f=2),
                      w23[:, e8 * 4 + f0:e8 * 4 + f0 + 2, :], writes=[wb])
            return wb

        def norm_to(g, col, dst, dst_is_f32):
            rms_stats(c, c.ones, lambda k: (H[g][:, k, :], H[g]), 8, GS, pstat, sqb, rstd, float(D))
            for m in range(8):
                c.op("dve", lambda e: e.scalar_tensor_tensor(
                    out=dst[:, m, :], in0=H[g][:, m, :], scalar=nws[:, col + m:col + m + 1],
                    in1=rstd[:], op0=ALU.mult, op1=ALU.mult),
                    reads=[H[g], nws, rstd], writes=[dst])

        wnext = load_eighth(0)
        for g in range(NG):
            for m in range(8):
                p = py.get()
                for k in range(8):
                    c.op("pe", lambda e: e.matmul(p[:, :GS], lhsT=wob[:, k * 1024 + m * 128:k * 1024 + (m + 1) * 128],
                                                  rhs=XB[g][:, k, :], start=(k == 0), stop=(k == 7)),
                         reads=[wob, XB[g]], writes=[p], inc=(k == 7))
                c.op("dve", lambda e: e.tensor_tensor(out=H[g][:, m, :], in0=p[:, :GS], in1=H[g][:, m, :], op=ALU.add),
                     reads=[p, H[g]], writes=[H[g]])
            if g > 0:
                norm_to(g - 1, 0, XB[g - 1], False)
        norm_to(NG - 1, 0, XB[NG - 1], False)
        def mlp_up(e8, g):
            wb = wbs[e8]
            ab = abuf.get()
            for f in range(4):
                p = pa.get()
                for k in range(8):
                    c.op("pe", lambda e: e.matmul(p[:, :GS], lhsT=wb[:, k * 512 + f * 128:k * 512 + (f + 1) * 128],
                                                  rhs=XB[g][:, k, :], start=(k == 0), stop=(k == 7)),
                         reads=[wb, XB[g]], writes=[p], inc=(k == 7))
                r = rbuf.get()
                c.op("act", lambda e: e.activation(out=r[:], in_=p[:, :GS], func=AF.Relu),
                     reads=[p], writes=[r])
                c.op("dve", lambda e: e.tensor_tensor(out=ab[:, f, :], in0=r[:], in1=r[:], op=ALU.mult),
                     reads=[r], writes=[ab])
            return ab

        def mlp_down(e8, g, ab):
            wb = wbs[e8]
            for m in range(8):
                p = py.get()
                for f in range(4):
                    c.op("pe", lambda e: e.matmul(p[:, :GS], lhsT=wb[:, 4096 + f * 1024 + m * 128:4096 + f * 1024 + (m + 1) * 128],
                                                  rhs=ab[:, f, :], start=(f == 0), stop=(f == 3)),
                         reads=[wb, ab], writes=[p], inc=(f == 3))
                c.op("dve", lambda e: e.tensor_tensor(out=H[g][:, m, :], in0=p[:, :GS], in1=H[g][:, m, :], op=ALU.add),
                     reads=[p, H[g]], writes=[H[g]])

        wbs = {0: wnext}
        wbs[1] = load_eighth(1)
        steps = [(e8, g) for e8 in range(8) for g in range(NG)]
        ab_cur = mlp_up(0, 0)
        for i, (e8, g) in enumerate(steps):
            ab_next = mlp_up(*steps[i + 1]) if i + 1 < len(steps) else None
            mlp_down(e8, g, ab_cur)
            ab_cur = ab_next
            if g == NG - 1 and e8 + 2 < 8:
                wbs[e8 + 2] = load_eighth(e8 + 2)
            if True:
                if e8 == 7:
                    c.dma("sp", ho3[:, :, g * GS:(g + 1) * GS], H[g][:], reads=[H[g]], writes=[hout])
                    u = uo.get()
                    norm_to(g, 8, u, final)
                    c.dma("sp", uo3[:, :, g * GS:(g + 1) * GS], u[:], reads=[u], writes=[uout])


def load_cast(c, dst_t, dst_ap, src_ap):
    c.dma("pool", dst_ap, src_ap, writes=[dst_t])


LB = 342
LNB = (T - PADF) // LB


def build_lru():
    nc = bass.Bass("TRN2", target_bir_lowering=False)
    with ExitStack() as es:
        c = Ctx(nc, es)
        io = {}
        io["uT"] = c.dram("uT", [D, T], BF16, "ExternalInput")
        io["wg"] = c.dram("wg", [D, 256], F32, "ExternalInput")
        io["wx"] = c.dram("wx", [D, 256], F32, "ExternalInput")
        io["wr"] = c.dram("wr", [256, 256], F32, "ExternalInput")
        io["wi"] = c.dram("wi", [256, 256], F32, "ExternalInput")
        io["prm"] = c.dram("prm", [128, 16], F32, "ExternalInput")
        io["yT"] = c.dram("yT", [256, T], BF16, "ExternalOutput")
        make_consts(c)
        emit_lru(c, io)
        c.finish()
    return nc


def emit_lru(c, io):
    if True:
        uT, wg, wx, wr, wi, prm, yT = (io[k] for k in ("uT", "wg", "wx", "wr", "wi", "prm", "yT"))
        one_t = c.sb([128, 1], F32, "one_t")
        c.op("pool", lambda e: e.memset(one_t[:], 1.0), writes=[one_t])
        ps_ = c.sb([128, 16], F32, "prm_s")
        c.dma("sp", ps_[:], prm[:], writes=[ps_])
        wgb = c.sb([128, 8, 256], BF16, "wgb")
        wxb = c.sb([128, 8, 256], BF16, "wxb")
        wrb = c.sb([128, 2, 256], BF16, "wrb")
        wib = c.sb([128, 2, 256], BF16, "wib")
        for (dst, src, nk) in ((wgb, wg, 8), (wxb, wx, 8), (wrb, wr, 2), (wib, wi, 2)):
            load_cast(c, dst, dst[:], src[:].rearrange("(k p) n -> p k n", p=128))
        cch = c.sb([128, 2], F32, "cch")
        ee = c.sb([128, 2], F32, "ee")
        acc = c.sb([128, 2], F32, "lacc")
        lam_ap = lambda: ps_[:, 7:16:8]
        c.op("act", lambda e: e.activation(out=ee[:], in_=lam_ap(), func=AF.Exp, scale=-1.0),
             reads=[ps_], writes=[ee])
        c.op("dve", lambda e: e.tensor_scalar(out=acc[:], in0=ee[:], scalar1=-1.0 / 6, scalar2=1.0 / 5,
                                              op0=ALU.mult, op1=ALU.add), reads=[ee], writes=[acc])
        for coef in (1.0 / 4, 1.0 / 3, 1.0 / 2, 1.0):
            c.op("dve", lambda e: e.tensor_tensor(out=acc[:], in0=acc[:], in1=ee[:], op=ALU.mult),
                 reads=[acc, ee], writes=[acc])
            c.op("dve", lambda e: e.tensor_scalar(out=acc[:], in0=acc[:], scalar1=-1.0, scalar2=coef,
                                                  op0=ALU.mult, op1=ALU.add), reads=[acc], writes=[acc])
        c.op("dve", lambda e: e.tensor_tensor(out=acc[:], in0=acc[:], in1=ee[:], op=ALU.mult),
             reads=[acc, ee], writes=[acc])
        c.op("dve", lambda e: e.tensor_scalar(out=cch[:], in0=acc[:], scalar1=-8.0, scalar2=None,
                                              op0=ALU.mult), reads=[acc], writes=[cch])

        ub = Rot([c.sb([128, 8, LB], BF16, "ub%d" % i) for i in range(3)])
        xc = [c.sb([128, LB + 3], F32, "xc%d" % t) for t in range(2)]
        for t in range(2):
            c.op("pool", lambda e: e.memset(xc[t][:], 0.0), writes=[xc[t]])
        xr = Rot([c.sb([128, LB], F32, "xr%d" % i) for i in range(2)])
        xrb = Rot([c.sb([128, 2, LB], BF16, "xrb%d" % i) for i in range(2)])
        gt = Rot([c.sb([128, LB], F32, "gt%d" % i) for i in range(4)])
        xrs = Rot([c.sb([128, LB], F32, "xrs%d" % i) for i in range(4)])
        tmp = Rot([c.sb([128, LB], F32, "tmp%d" % i) for i in range(10)])
        hs = [Rot([c.sb([128, LB], F32, "hs%d_%d" % (t, i)) for i in range(2)]) for t in range(2)]
        yb = Rot([c.sb([128, 2, LB], BF16, "yb%d" % i) for i in range(2)])
        pp = Rot([c.ps([128, 512], F32, "pp%d" % i) for i in range(4)])
        pg_ = Rot([c.ps([128, 512], F32, "pq%d" % i) for i in range(4)])
        zz = c.sb([128, 2, PADF], BF16, "zz")
        c.op("pool", lambda e: e.memset(zz[:], 0.0), writes=[zz])
        y3 = yT[:].rearrange("(t p) n -> p t n", p=128)
        c.dma("sp", y3[:, :, 0:PADF], zz[:], reads=[zz], writes=[yT])
        u3 = uT[:].rearrange("(k p) n -> p k n", p=128)
        prev_hs = [None, None]
        P = lambda t, j: ps_[:, t * 8 + j:t * 8 + j + 1]
        n = LB
        for blk in range(LNB):
            t0 = PADF + blk * LB
            u = ub.get()
            c.dma("sp", u[:], u3[:, :, t0:t0 + n], writes=[u])
            xb_ = xrb.get()
            y_ = yb.get()
            TS = range(2)
            pgt, pxt, s_, g_, x_, pr, pi_, a_, i_, m_, h_ = ({} for _ in range(11))
            for t in TS:
                pgt[t] = pp.get()
                for k in range(8):
                    c.op("pe", lambda e: e.matmul(pgt[t][:, :n], lhsT=wgb[:, k, t * 128:(t + 1) * 128], rhs=u[:, k, :],
                                                  start=(k == 0), stop=(k == 7)), reads=[wgb, u], writes=[pgt[t]], inc=(k == 7))
                pxt[t] = pp.get()
                for k in range(8):
                    c.op("pe", lambda e: e.matmul(pxt[t][:, :n], lhsT=wxb[:, k, t * 128:(t + 1) * 128], rhs=u[:, k, :],
                                                  start=(k == 0), stop=(k == 7)), reads=[wxb, u], writes=[pxt[t]], inc=(k == 7))
            for t in TS:
                c.op("act", lambda e: e.activation(out=xc[t][:, 3:3 + n], in_=pxt[t][:, :n], func=AF.Copy),
                     reads=[pxt[t]], writes=[xc[t]])
            for t in TS:
                s_[t] = tmp.get()
                c.op("act", lambda e: e.activation(out=s_[t][:], in_=pgt[t][:, :n], func=AF.Square), reads=[pgt[t]], writes=[s_[t]])
            for t in TS:
                x_[t] = xrs.get()
                c.op("dve", lambda e: e.tensor_scalar(out=x_[t][:], in0=xc[t][:, 0:n], scalar1=P(t, 0), scalar2=P(t, 4),
                                                      op0=ALU.mult, op1=ALU.add), reads=[xc[t], ps_], writes=[x_[t]])
                for j in range(1, 4):
                    c.op("dve", lambda e: e.scalar_tensor_tensor(out=x_[t][:], in0=xc[t][:, j:j + n], scalar=P(t, j),
                                                                 in1=x_[t][:], op0=ALU.mult, op1=ALU.add),
                         reads=[xc[t], ps_, x_[t]], writes=[x_[t]])
                c.op("pool", lambda e: e.tensor_copy(out=xc[t][:, 0:3], in_=xc[t][:, n:n + 3]),
                     reads=[xc[t]], writes=[xc[t]])
                c.op("pool", lambda e: e.tensor_copy(out=xb_[:, t, :], in_=x_[t][:]), reads=[x_[t]], writes=[xb_])
            for t in TS:
                c.op("dve", lambda e: e.tensor_scalar(out=s_[t][:], in0=s_[t][:], scalar1=0.044715, scalar2=1.0,
                                                      op0=ALU.mult, op1=ALU.add), reads=[s_[t]], writes=[s_[t]])
                c.op("dve", lambda e: e.tensor_tensor(out=s_[t][:], in0=s_[t][:], in1=pgt[t][:, :n], op=ALU.mult),
                     reads=[s_[t], pgt[t]], writes=[s_[t]])
            for t in TS:
                pr[t] = pg_.get()
                for k in range(2):
                    c.op("pe", lambda e: e.matmul(pr[t][:, :n], lhsT=wrb[:, k, t * 128:(t + 1) * 128], rhs=xb_[:, k, :],
                                                  start=(k == 0), stop=(k == 1)), reads=[wrb, xb_], writes=[pr[t]], inc=(k == 1))
                pi_[t] = pg_.get()
                for k in range(2):
                    c.op("pe", lambda e: e.matmul(pi_[t][:, :n], lhsT=wib[:, k, t * 128:(t + 1) * 128], rhs=xb_[:, k, :],
                                                  start=(k == 0), stop=(k == 1)), reads=[wib, xb_], writes=[pi_[t]], inc=(k == 1))
            for t in TS:
                c.op("act", lambda e: e.activation(out=s_[t][:], in_=s_[t][:], func=AF.Sigmoid, scale=1.5957691216057308),
                     reads=[s_[t]], writes=[s_[t]])
            for t in TS:
                a_[t] = tmp.get()
                c.op("act", lambda e: e.activation(out=a_[t][:], in_=pr[t][:, :n], func=AF.Sigmoid, bias=P(t, 5)),
                     reads=[pr[t], ps_], writes=[a_[t]])
                c.op("act", lambda e: e.activation(out=a_[t][:], in_=a_[t][:], func=AF.Exp, scale=cch[:, t:t + 1]),
                     reads=[a_[t], cch], writes=[a_[t]])
                i_[t] = tmp.get()
                c.op("act", lambda e: e.activation(out=i_[t][:], in_=pi_[t][:, :n], func=AF.Sigmoid, bias=P(t, 6)),
                     reads=[pi_[t], ps_], writes=[i_[t]])
            for t in TS:
                g_[t] = gt.get()
                c.op("dve", lambda e: e.tensor_tensor(out=g_[t][:], in0=s_[t][:], in1=pgt[t][:, :n], op=ALU.mult),
                     reads=[s_[t], pgt[t]], writes=[g_[t]])
            for t in TS:
                m_[t] = tmp.get()
                c.op("dve", lambda e: e.tensor_tensor(out=m_[t][:], in0=a_[t][:], in1=a_[t][:], op=ALU.mult), reads=[a_[t]], writes=[m_[t]])
                c.op("dve", lambda e: e.tensor_tensor(out=i_[t][:], in0=i_[t][:], in1=x_[t][:], op=ALU.mult),
                     reads=[i_[t], x_[t]], writes=[i_[t]])
            for t in TS:
                c.op("act", lambda e: e.activation(out=m_[t][:], in_=m_[t][:], func=AF.Sqrt, bias=one_t[:, 0:1], scale=-1.0),
                     reads=[m_[t], one_t], writes=[m_[t]])
            for t in TS:
                c.op("dve", lambda e: e.tensor_tensor(out=i_[t][:], in0=i_[t][:], in1=m_[t][:], op=ALU.mult),
                     reads=[i_[t], m_[t]], writes=[i_[t]])
                h_[t] = hs[t].get()
                if prev_hs[t] is None:
                    c.op("dve", lambda e: e.tensor_tensor_scan(out=h_[t][:], data0=a_[t][:], data1=i_[t][:], initial=0.0,
                                                               op0=ALU.mult, op1=ALU.add), reads=[a_[t], i_[t]], writes=[h_[t]])
                else:
                    ph = prev_hs[t]
                    c.op("dve", lambda e: e.tensor_tensor_scan(out=h_[t][:], data0=a_[t][:], data1=i_[t][:], initial=ph[:, n - 1:n],
                                                               op0=ALU.mult, op1=ALU.add), reads=[a_[t], i_[t], ph], writes=[h_[t]])
                prev_hs[t] = h_[t]
                c.op("pool", lambda e: e.tensor_tensor(out=y_[:, t, :], in0=h_[t][:], in1=g_[t][:], op=ALU.mult),
                     reads=[h_[t], g_[t]], writes=[y_])
            c.dma("sp", y3[:, :, t0:t0 + n], y_[:], reads=[y_], writes=[yT])


def lru_params(cw, cb, br, bi, lam):
    prm = np.zeros((128, 2, 8), np.float32)
    for t in range(2):
        sl = slice(t * 128, (t + 1) * 128)
        prm[:, t, 0:4] = cw[:, sl].T
        prm[:, t, 4] = cb[sl]
        prm[:, t, 5] = br[sl]
        prm[:, t, 6] = bi[sl]
        prm[:, t, 7] = lam[sl]
    return np.ascontiguousarray(prm.reshape(128, 16))


TD = T + 64
DN_BLOCKS = [(i * 512, 4) for i in range(16)] + [(8192, 1)]


def dn_consts():
    i = np.arange(128)
    same = (i[:, None] // 64) == (i[None, :] // 64)
    cst = np.zeros((128, 6, 128), np.float32)
    cst[:, 0, :] = np.eye(128)
    cst[:, 1, :] = (same & (i[:, None] <= i[None, :]))
    cst[:, 2, :] = (same & (i[:, None] > i[None, :]))
    cst[:, 3, :] = np.where(same & (i[:, None] >= i[None, :]), 0.0, -30000.0)
    cst[:, 4, :] = (same & (i[:, None] > i[None, :]))
    cst[:, 5, :] = -1.0
    return cst


def build_dn(blocks=None, TD=TD, dbg=99):
    blocks = blocks or DN_BLOCKS
    nc = bass.Bass("TRN2", target_bir_lowering=False)
    with ExitStack() as es:
        c = Ctx(nc, es)
        io = {}
        io["uT"] = c.dram("uT", [D, TD], BF16, "ExternalInput")
        io["wcat"] = c.dram("wcat", [D, 1028], F32, "ExternalInput")
        io["cst"] = c.dram("cst", [128, 6, 128], F32, "ExternalInput")
        io["prm"] = c.dram("prm", [128, 32], F32, "ExternalInput")
        io["oT"] = c.dram("oT", [256, TD], BF16, "ExternalOutput")
        make_consts(c)
        emit_dn(c, io, blocks, dbg)
        c.finish()
    return nc


def emit_dn(c, io, blocks=None, dbg=99):
    blocks = blocks or DN_BLOCKS
    if True:
        uT, wcat, cstd, prm, oT = (io[k] for k in ("uT", "wcat", "cst", "prm", "oT"))
        one_t = c.sb([128, 1], F32, "one_t")
        c.op("pool", lambda e: e.memset(one_t[:], 1.0), writes=[one_t])
        cst = c.sb([128, 6, 128], F32, "cst_s")
        c.dma("sp", cst[:], cstd[:], writes=[cst])
        ident = cst[:, 0, :]
        U2 = cst[:, 1, :]
        R2 = cst[:, 2, :]
        negmask = cst[:, 3, :]
        smask = cst[:, 4, :]
        negones = cst[:, 5, :]
        ps_ = c.sb([128, 32], F32, "prm_s")
        c.dma("sp", ps_[:], prm[:], writes=[ps_])
        nea = c.sb([128, 2], F32, "nea")
        c.op("act", lambda e: e.activation(out=nea[:], in_=ps_[:, 24:26], func=AF.Exp), reads=[ps_], writes=[nea])
        c.op("dve", lambda e: e.tensor_scalar(out=nea[:], in0=nea[:], scalar1=-1.0, scalar2=None, op0=ALU.mult),
             reads=[nea], writes=[nea])
        wb = c.sb([128, 8, 1028], BF16, "wb")
        w3 = wcat[:].rearrange("(k p) n -> p k n", p=128)
        for k0 in range(0, 8, 2):
            load_cast(c, wb, wb[:, k0:k0 + 2, :], w3[:, k0:k0 + 2, :])

        ub = Rot([c.sb([128, 8, 512], BF16, "ub%d" % i) for i in range(2)])
        xc = [c.sb([128, 515], F32, "xc%d" % j) for j in range(6)]
        for j in range(6):
            c.op("pool", lambda e: e.memset(xc[j][:], 0.0), writes=[xc[j]])
        ft = [Rot([c.sb([128, 512], F32, "ft%d_%d" % (j, i)) for i in range(2)]) for j in range(6)]
        szr = [Rot([c.sb([128, 512], F32, "sz%d_%d" % (h, i)) for i in range(2)]) for h in range(2)]
        sq4 = [c.sb([128, 512], F32, "sq%d" % i) for i in range(4)]
        rstd4 = [c.sb([128, 512], F32, "rstd%d" % i) for i in range(4)]
        obr = Rot([c.sb([128, 2, 512], BF16, "ob%d" % i) for i in range(2)])
        S = [c.sb([128, 128], F32, "S%d" % h) for h in range(2)]
        for h in range(2):
            c.op("pool", lambda e: e.memset(S[h][:], 0.0), writes=[S[h]])
        bt = Rot([c.sb([128, 4, 2], F32, "bt%d" % i) for i in range(2)])
        gg = Rot([c.sb([128, 4, 2], F32, "gg%d" % i) for i in range(2)])
        gtmp = Rot([c.sb([128, 4, 2], F32, "gtmp%d" % i) for i in range(6)])
        sm2 = Rot([c.sb([128, 2], F32, "sm2_%d" % i) for i in range(24)])
        sm1 = Rot([c.sb([128, 1], F32, "sm1_%d" % i) for i in range(8)])
        NSQ = 96
        sq128 = Rot([c.sb([128, 128], F32, "m%d" % i) for i in range(NSQ)])
        LL = {nm: [c.sb([128, 128], F32, "%s%d" % (nm, i)) for i in range(8)]
              for nm in ("dec", "egrow", "kd", "attT", "qg", "usb", "wTs", "otok")}
        pbig = Rot([c.ps([128, 512], F32, "pbig%d" % i) for i in range(2)])
        banks = [c.ps([128, 512], F32, "pbank%d" % i) for i in range(6)]
        psm = Rot([banks[i].view(banks[i].t[:, 0:128]) for i in range(6)])

        u3 = uT[:].rearrange("(k p) n -> p k n", p=128)
        o3 = oT[:].rearrange("(h p) n -> p h n", p=128)

        def mm(out_t, out_ap, lhsT, rhs, reads, start=True, stop=True):
            c.op("pe", lambda e: e.matmul(out_ap, lhsT=lhsT, rhs=rhs, start=start, stop=stop),
                 reads=reads, writes=[out_t], inc=stop)

        def evac(eng, dst_t, dst_ap, src_t, src_ap, scale=None, extra=()):
            if eng == "act":
                if scale is None:
                    c.op("act", lambda e: e.activation(out=dst_ap, in_=src_ap, func=AF.Copy),
                         reads=[src_t], writes=[dst_t])
                else:
                    c.op("act", lambda e: e.activation(out=dst_ap, in_=src_ap, func=AF.Copy, scale=scale),
                         reads=[src_t] + list(extra), writes=[dst_t])
            else:
                c.op("dve", lambda e: e.tensor_copy(out=dst_ap, in_=src_ap), reads=[src_t], writes=[dst_t])

        for (b0, ntl) in blocks:
            n = ntl * 128
            u = ub.get()
            c.dma("sp", u[:, :, :n], u3[:, :, b0:b0 + n], writes=[u])
            pba_t = psm.get()
            for tt in range(ntl):
                for k in range(8):
                    mm(pba_t, pba_t[:, tt * 4:(tt + 1) * 4], u[:, k, tt * 128:(tt + 1) * 128], wb[:, k, 1024:1028],
                       [u, wb], start=(k == 0), stop=(k == 7))
            pba = pba_t[:, 0:4 * ntl].rearrange("p (t f) -> p t f", f=4)
            btb = bt.get()
            ggb = gg.get()
            c.op("act", lambda e: e.activation(out=btb[:, :ntl, :], in_=pba[:, :, 0:2], func=AF.Sigmoid),
                 reads=[pba_t], writes=[btb])
            x_ = gtmp.get(); ax = gtmp.get(); rl = gtmp.get()
            for h in range(2):
                c.op("dve", lambda e: e.tensor_scalar(out=x_[:, :ntl, h:h + 1], in0=pba[:, :, 2 + h:3 + h],
                                                      scalar1=ps_[:, 26 + h:27 + h], scalar2=None, op0=ALU.add),
                     reads=[pba_t, ps_], writes=[x_])
            c.op("act", lambda e: e.activation(out=ax[:, :ntl, :], in_=x_[:, :ntl, :], func=AF.Abs),
                 reads=[x_], writes=[ax])
            c.op("act", lambda e: e.activation(out=ax[:, :ntl, :], in_=ax[:, :ntl, :], func=AF.Exp, scale=-1.0),
                 reads=[ax], writes=[ax])
            c.op("act", lambda e: e.activation(out=ax[:, :ntl, :], in_=ax[:, :ntl, :], func=AF.Ln, bias=one_t[:, 0:1]),
                 reads=[ax, one_t], writes=[ax])
            c.op("dve", lambda e: e.tensor_scalar(out=rl[:, :ntl, :], in0=x_[:, :ntl, :], scalar1=0.0, scalar2=None,
                                                  op0=ALU.max), reads=[x_], writes=[rl])
            c.op("dve", lambda e: e.tensor_tensor(out=rl[:, :ntl, :], in0=rl[:, :ntl, :], in1=ax[:, :ntl, :], op=ALU.add),
                 reads=[rl, ax], writes=[rl])
            for h in range(2):
                c.op("dve", lambda e: e.tensor_scalar(out=ggb[:, :ntl, h:h + 1], in0=rl[:, :ntl, h:h + 1],
                                                      scalar1=nea[:, h:h + 1], scalar2=None, op0=ALU.mult),
                     reads=[rl, nea], writes=[ggb])
            F = []
            pend_silu = None
            for j in range(8):
                p = pbig.get()
                for k in range(8):
                    mm(p, p[:, :n], wb[:, k, j * 128:(j + 1) * 128], u[:, k, :n], [wb, u], start=(k == 0), stop=(k == 7))
                if j < 6:
                    c.op("act", lambda e: e.activation(out=xc[j][:, 3:3 + n], in_=p[:, :n], func=AF.Copy),
                         reads=[p], writes=[xc[j]])
                    if pend_silu is not None:
                        pend_silu()
                        pend_silu = None
                    f = ft[j].get()
                    c.op("dve", lambda e: e.tensor_scalar(out=f[:, :n], in0=xc[j][:, 0:n], scalar1=ps_[:, j * 4:j * 4 + 1],
                                                          scalar2=None, op0=ALU.mult), reads=[xc[j], ps_], writes=[f])
                    for tp in range(1, 4):
                        c.op("dve", lambda e: e.scalar_tensor_tensor(out=f[:, :n], in0=xc[j][:, tp:tp + n],
                                                                     scalar=ps_[:, j * 4 + tp:j * 4 + tp + 1], in1=f[:, :n],
                                                                     op0=ALU.mult, op1=ALU.add),
                             reads=[xc[j], ps_, f], writes=[f])
                    c.op("pool", lambda e: e.tensor_copy(out=xc[j][:, 0:3], in_=xc[j][:, n:n + 3]),
                         reads=[xc[j]], writes=[xc[j]])
                    pend_silu = (lambda f=f: c.op("act", lambda e: e.activation(out=f[:, :n], in_=f[:, :n], func=AF.Silu),
                                                  reads=[f], writes=[f]))
                    F.append(f)
                else:
                    if pend_silu is not None:
                        pend_silu()
                        pend_silu = None
                    z = szr[j - 6].get()
                    c.op("act", lambda e: e.activation(out=z[:, :n], in_=p[:, :n], func=AF.Silu), reads=[p], writes=[z])
                    F.append(z)
            pst4 = [pbig.get(), pbig.get(), banks[4], banks[5]]
            for j in range(4):
                c.op("act", lambda e: e.activation(out=sq4[j][:, :n], in_=F[j][:, :n], func=AF.Square),
                     reads=[F[j]], writes=[sq4[j]])
            for j in range(4):
                c.op("pe", lambda e: e.matmul(pst4[j][:, :n], lhsT=c.ones[:], rhs=sq4[j][:, :n], start=True, stop=True),
                     reads=[c.ones, sq4[j]], writes=[pst4[j]])
            for j in range(4):
                c.op("act", lambda e: e.activation(out=rstd4[j][:, :n], in_=pst4[j][:, :n], func=AF.Sqrt,
                                                   bias=c.eps_t[:, 0:1], scale=1.0),
                     reads=[pst4[j], c.eps_t], writes=[rstd4[j]])
            for j in range(4):
                c.op("dve", lambda e: e.reciprocal(out=rstd4[j][:, :n], in_=rstd4[j][:, :n]), reads=[rstd4[j]], writes=[rstd4[j]])
            for j in range(4):
                if j < 2:
                    c.op("dve", lambda e: e.scalar_tensor_tensor(out=F[j][:, :n], in0=F[j][:, :n], scalar=128.0 ** -0.5,
                                                                 in1=rstd4[j][:, :n], op0=ALU.mult, op1=ALU.mult),
                         reads=[F[j], rstd4[j]], writes=[F[j]])
                else:
                    c.op("dve", lambda e: e.tensor_tensor(out=F[j][:, :n], in0=F[j][:, :n], in1=rstd4[j][:, :n], op=ALU.mult),
                         reads=[F[j], rstd4[j]], writes=[F[j]])
            ob = obr.get()
            TT = list(range(ntl))
            CH = [(tt, h) for tt in TT for h in range(2)]
            ci = {ch: i for i, ch in enumerate(CH)}
            csl = {tt: slice(tt * 128, (tt + 1) * 128) for tt in TT}
            egc, ed, be = {}, {}, {}
            for tt in TT:
                pg1 = psm.get(); pg2 = psm.get()
                mm(pg1, pg1[:, 0:2], U2, ggb[:, tt, :], [cst, ggb])
                mm(pg2, pg2[:, 0:2], R2, ggb[:, tt, :], [cst, ggb])
                egc[tt] = sm2.get(); ed[tt] = sm2.get(); be[tt] = sm2.get()
                c.op("act", lambda e: e.activation(out=egc[tt][:], in_=pg1[:, 0:2], func=AF.Exp), reads=[pg1], writes=[egc[tt]])
                c.op("act", lambda e: e.activation(out=ed[tt][:], in_=pg2[:, 0:2], func=AF.Exp), reads=[pg2], writes=[ed[tt]])
                c.op("dve", lambda e: e.tensor_tensor(out=be[tt][:], in0=egc[tt][:], in1=btb[:, tt, :], op=ALU.mult),
                     reads=[egc[tt], btb], writes=[be[tt]])
            Ug, dec, decs, egrow, P, Q, Y = {}, {}, {}, {}, {}, {}, {}
            for ch in CH:
                tt, h = ch
                Ug[ch] = sq128.get()
                c.op("dve", lambda e: e.tensor_scalar(out=Ug[ch][:], in0=U2, scalar1=ggb[:, tt, h:h + 1], scalar2=None,
                                                      op0=ALU.mult), reads=[cst, ggb], writes=[Ug[ch]])
            for ch in CH:
                tt, h = ch
                pd = psm.get()
                mm(pd, pd[:], Ug[ch][:], c.ones[:], [Ug[ch], c.ones], start=True, stop=False)
                mm(pd, pd[:], negones, Ug[ch][:], [cst, Ug[ch]], start=False, stop=True)
                dec[ch] = LL["dec"][ci[ch]]
                c.op("dve", lambda e: e.tensor_tensor(out=dec[ch][:], in0=pd[:], in1=negmask, op=ALU.add),
                     reads=[pd, cst], writes=[dec[ch]])
                c.op("act", lambda e: e.activation(out=dec[ch][:], in_=dec[ch][:], func=AF.Exp),
                     reads=[dec[ch]], writes=[dec[ch]])
                decs[ch] = sq128.get()
                c.op("pool", lambda e: e.tensor_tensor(out=decs[ch][:], in0=dec[ch][:], in1=smask, op=ALU.mult),
                     reads=[dec[ch], cst], writes=[decs[ch]])
                pe_ = psm.get()
                mm(pe_, pe_[:], c.ones[:], Ug[ch][:], [c.ones, Ug[ch]])
                egrow[ch] = LL["egrow"][ci[ch]]
                c.op("act", lambda e: e.activation(out=egrow[ch][:], in_=pe_[:], func=AF.Exp),
                     reads=[pe_], writes=[egrow[ch]])
            for ch in CH:
                tt, h = ch
                kT = F[2 + h][:, csl[tt]]
                pk = psm.get()
                mm(pk, pk[:], kT, kT, [F[2 + h]])
                P[ch] = sq128.get()
                c.op("dve", lambda e: e.scalar_tensor_tensor(out=P[ch][:], in0=pk[:], scalar=btb[:, tt, h:h + 1],
                                                             in1=decs[ch][:], op0=ALU.mult, op1=ALU.mult),
                     reads=[pk, btb, decs[ch]], writes=[P[ch]])
            for ch in CH:
                pb = psm.get()
                mm(pb, pb[:], P[ch][:], ident, [P[ch], cst])
                Q[ch] = sq128.get(); Y[ch] = sq128.get()
                evac("act", Q[ch], Q[ch][:], pb, pb[:])
                c.op("pool", lambda e: e.tensor_tensor(out=Y[ch][:], in0=ident, in1=Q[ch][:], op=ALU.subtract),
                     reads=[Q[ch], cst], writes=[Y[ch]])
            for s in range(5):
                Pn, Qn = {}, {}
                for ch in CH:
                    pp_ = psm.get()
                    mm(pp_, pp_[:], Q[ch][:], P[ch][:], [Q[ch], P[ch]])
                    Pn[ch] = sq128.get()
                    evac("act", Pn[ch], Pn[ch][:], pp_, pp_[:])
                    if s < 4:
                        pq = psm.get()
                        mm(pq, pq[:], P[ch][:], Q[ch][:], [P[ch], Q[ch]])
                        Qn[ch] = sq128.get()
                        evac("dve", Qn[ch], Qn[ch][:], pq, pq[:])
                for ch in CH:
                    py_ = psm.get()
                    mm(py_, py_[:], Pn[ch][:], Y[ch][:], [Pn[ch], Y[ch]])
                    Yn = sq128.get()
                    c.op("dve", lambda e: e.tensor_tensor(out=Yn[:], in0=py_[:], in1=Y[ch][:], op=ALU.add),
                         reads=[py_, Y[ch]], writes=[Yn])
                    Y[ch] = Yn
                    P[ch] = Pn[ch]
                    if s < 4:
                        Q[ch] = Qn[ch]
            kbg, kd, vb, usb, wTs, attT, qg = {}, {}, {}, {}, {}, {}, {}
            for ch in CH:
                tt, h = ch
                kT = F[2 + h][:, csl[tt]]; vT = F[4 + h][:, csl[tt]]; qT = F[h][:, csl[tt]]
                pkt = psm.get()
                mm(pkt, pkt[:], kT, ident, [F[2 + h], cst])
                kbg[ch] = sq128.get(); kd[ch] = LL["kd"][ci[ch]]
                evac("act", kbg[ch], kbg[ch][:], pkt, pkt[:], scale=be[tt][:, h:h + 1], extra=[be[tt]])
                evac("act", kd[ch], kd[ch][:], pkt, pkt[:], scale=ed[tt][:, h:h + 1], extra=[ed[tt]])
                pvt = psm.get()
                mm(pvt, pvt[:], vT, ident, [F[4 + h], cst])
                vb[ch] = sq128.get()
                evac("act", vb[ch], vb[ch][:], pvt, pvt[:], scale=btb[:, tt, h:h + 1], extra=[btb])
                pqk = psm.get()
                mm(pqk, pqk[:], qT, kT, [F[h], F[2 + h]])
                att = sq128.get()
                c.op("dve", lambda e: e.tensor_tensor(out=att[:], in0=pqk[:], in1=dec[ch][:], op=ALU.mult),
                     reads=[pqk, dec[ch]], writes=[att])
                pat = psm.get()
                mm(pat, pat[:], att[:], ident, [att, cst])
                attT[ch] = LL["attT"][ci[ch]]
                evac("dve", attT[ch], attT[ch][:], pat, pat[:])
                qg[ch] = LL["qg"][ci[ch]]
                c.op("pool", lambda e: e.tensor_tensor(out=qg[ch][:], in0=qT, in1=egrow[ch][:], op=ALU.mult),
                     reads=[F[h], egrow[ch]], writes=[qg[ch]])
            for ch in CH:
                pu = psm.get()
                mm(pu, pu[:], Y[ch][:], vb[ch][:], [Y[ch], vb[ch]])
                usb[ch] = LL["usb"][ci[ch]]
                evac("act", usb[ch], usb[ch][:], pu, pu[:])
                pw = psm.get()
                mm(pw, pw[:], kbg[ch][:], Y[ch][:], [kbg[ch], Y[ch]])
                wTs[ch] = LL["wTs"][ci[ch]]
                evac("dve", wTs[ch], wTs[ch][:], pw, pw[:])
            otok = {ch: LL["otok"][ci[ch]] for ch in CH}
            for tt in TT:
                vnew = {h: sq128.get() for h in range(2)}
                for half in range(2):
                    r = slice(half * 64, half * 64 + 64)
                    for h in range(2):
                        ch = (tt, h)
                        pws = psm.get()
                        mm(pws, pws[:], wTs[ch][:], S[h][:], [wTs[ch], S[h]])
                        c.op("dve", lambda e: e.tensor_tensor(out=vnew[h][r, :], in0=usb[ch][r, :], in1=pws[r, :], op=ALU.subtract),
                             reads=[usb[ch], pws], writes=[vnew[h]])
                    for h in range(2):
                        ch = (tt, h)
                        po = psm.get()
                        mm(po, po[:], qg[ch][:], S[h][:], [qg[ch], S[h]], start=True, stop=False)
                        mm(po, po[:], attT[ch][r, :], vnew[h][r, :], [attT[ch], vnew[h]], start=False, stop=True)
                        evac("act", otok[ch], otok[ch][r, :], po, po[r, :])
                        pst = psm.get()
                        mm(pst, pst[:], kd[ch][r, :], vnew[h][r, :], [kd[ch], vnew[h]])
                        c.op("dve", lambda e: e.scalar_tensor_tensor(out=S[h][:], in0=S[h][:],
                                                                     scalar=egrow[ch][:, half * 64 + 63:half * 64 + 64],
                                                                     in1=pst[:], op0=ALU.mult, op1=ALU.add),
                             reads=[S[h], egrow[ch], pst], writes=[S[h]])
            ssd, ond, potd = {}, {}, {}
            for ch in CH:
                junk = sq128.get()
                ssd[ch] = sm1.get()
                c.op("act", lambda e: e.activation(out=junk[:], in_=otok[ch][:], func=AF.Square, accum_out=ssd[ch][:]),
                     reads=[otok[ch]], writes=[junk, ssd[ch]])
            for ch in CH:
                c.op("act", lambda e: e.activation(out=ssd[ch][:], in_=ssd[ch][:], func=AF.Sqrt, bias=c.eps_t[:, 0:1], scale=1.0 / 128),
                     reads=[ssd[ch], c.eps_t], writes=[ssd[ch]])
            for ch in CH:
                c.op("dve", lambda e: e.reciprocal(out=ssd[ch][:], in_=ssd[ch][:]), reads=[ssd[ch]], writes=[ssd[ch]])
            for ch in CH:
                ond[ch] = sq128.get()
                c.op("dve", lambda e: e.tensor_scalar(out=ond[ch][:], in0=otok[ch][:], scalar1=ssd[ch][:, 0:1], scalar2=None, op0=ALU.mult),
                     reads=[otok[ch], ssd[ch]], writes=[ond[ch]])
            for ch in CH:
                tt, h = ch
                pot = psm.get()
                mm(pot, pot[:], ond[ch][:], ident, [ond[ch], cst])
                c.op("dve", lambda e: e.scalar_tensor_tensor(out=ob[:, h, csl[tt]], in0=pot[:], scalar=ps_[:, 28:29],
                                                             in1=F[6 + h][:, csl[tt]], op0=ALU.mult, op1=ALU.mult),
                     reads=[pot, ps_, F[6 + h]], writes=[ob])
            c.dma("sp", o3[:, :, b0:b0 + n], ob[:, :, :n], reads=[ob], writes=[oT])


def dn_params(conv_w6, a_log2, dt_bias2, norm_w):
    prm = np.zeros((128, 32), np.float32)
    for j in range(6):
        prm[:, j * 4:(j + 1) * 4] = conv_w6[:, j * 128:(j + 1) * 128].T
    prm[:, 24:26] = a_log2[None, :]
    prm[:, 26:28] = dt_bias2[None, :]
    prm[:, 28] = norm_w
    return prm


DA_DS = (-128, 0, 128, 256, 384)
DA_NEGM = -240000.0
LAMBDA_INIT_L1 = 0.8 - 0.6 * math.exp(-0.3 * 1)


def _t5_bucket_np(rel):
    import jax
    import jax.numpy as jnp
    with jax.default_device(jax.devices("cpu")[0]):
        rel = jnp.asarray(rel, jnp.int32)
        nb = 16
        ret = jnp.where(rel > 0, nb, 0)
        n = jnp.abs(rel)
        max_exact = nb // 2
        nf = jnp.maximum(n, 1).astype(jnp.float32)
        large = max_exact + (jnp.log(nf / max_exact) / math.log(128 / max_exact)
                             * (nb - max_exact)).astype(jnp.int32)
        large = jnp.minimum(large, nb - 1)
        return np.asarray(ret + jnp.where(n < max_exact, n, large))


def da_consts():
    r = np.arange(-639, 513)
    bk = _t5_bucket_np(r)
    oh = np.zeros((32, 1152), np.float32)
    oh[bk, np.arange(1152)] = 1.0
    oh[15, :] -= 1.0
    kk = np.arange(128)[:, None]
    qq = np.arange(512)[None, :]
    md = np.zeros((128, 5, 512), np.float32)
    for i, d in enumerate(DA_DS):
        allowed = ((d + kk) // 64) <= (qq // 64)
        md[:, i, :] = np.where(allowed, 0.0, DA_NEGM)
    pm = np.zeros((128, 1), np.float32)
    pm[:112] = -30000.0
    return oh, md, pm


def build_da(lambda_init=LAMBDA_INIT_L1):
    NKT = TD // 128
    qtiles = [(0, 128)] + [(128 + 512 * i, 512) for i in range(16)]
    nc = bass.Bass("TRN2", target_bir_lowering=False)
    with ExitStack() as es:
        c = Ctx(nc, es)
        io = {}
        io["uT"] = c.dram("uT", [D, TD], BF16, "ExternalInput")
        io["wcat"] = c.dram("wcat", [D, 768], F32, "ExternalInput")
        io["oh"] = c.dram("oh", [32, 1152], F32, "ExternalInput")
        io["md"] = c.dram("md", [128, 5, 512], F32, "ExternalInput")
        io["pm"] = c.dram("pm", [128, 1], F32, "ExternalInput")
        io["rb"] = c.dram("rb", [32, 2], F32, "ExternalInput")
        io["rb15"] = c.dram("rb15", [128, 2], F32, "ExternalInput")
        io["lamv"] = c.dram("lamv", [128, 4, 64], F32, "ExternalInput")
        io["sw"] = c.dram("sw", [128, 1], F32, "ExternalInput")
        io["identf"] = c.dram("identf", [128, 128], F32, "ExternalInput")
        io["oT"] = c.dram("oT", [256, TD], BF16, "ExternalOutput")
        make_consts(c)
        emit_da(c, io, lambda_init, "")
        c.finish()
    return nc


def emit_da(c, io, lambda_init, tag):
    NKT = TD // 128
    qtiles = [(0, 128)] + [(128 + 512 * i, 512) for i in range(16)]
    if True:
        uT, wcat, ohd, mdd, pmd, rb, rb15, lamv, sw, idd, oT = (io[k] for k in (
            "uT", "wcat", "oh", "md", "pm", "rb", "rb15", "lamv", "sw", "identf", "oT"))
        tvd = c.dram("tvscr" + tag, [2, 1152], F32, "Internal")
        identb = c.sb([128, 128], BF16, "identb")
        onesb = c.sb([128, 128], BF16, "onesb")
        c.op("pool", lambda e: e.memset(onesb[:], 1.0), writes=[onesb])
        QT = [c.sb([128, TD], BF16, "QT%d" % h) for h in range(2)]
        KT = [c.sb([128, TD], BF16, "KT%d" % h) for h in range(2)]
        V = c.sb([128, NKT, 256], BF16, "V")
        BH = [[c.sb([128, 512], BF16, "BH%d_%d" % (h, i)) for i in range(5)] for h in range(2)]
        BL = [[c.sb([128, 512], BF16, "BL%d_%d" % (h, i)) for i in range(5)] for h in range(2)]
        biasc = c.sb([128, 2], F32, "biasc")
        bias0 = c.sb([128, 2], F32, "bias0")
        neglam = c.sb([128, 1], F32, "neglam")
        swp = c.sb([128, 1], F32, "swp")

        with ExitStack() as es1:
            c.dma("sp", biasc[:], rb15[:], writes=[biasc])
            pms = c.sb([128, 1], F32, "pms", es=es1)
            c.dma("sp", pms[:], pmd[:], writes=[pms])
            c.op("dve", lambda e: e.tensor_scalar(out=bias0[:], in0=biasc[:], scalar1=pms[:, 0:1], scalar2=None, op0=ALU.add),
                 reads=[biasc, pms], writes=[bias0])
            sws = c.sb([128, 1], F32, "sws", es=es1)
            c.dma("sp", sws[:], sw[:], writes=[sws])
            c.op("dve", lambda e: e.tensor_scalar(out=swp[:], in0=sws[:], scalar1=1.0 - lambda_init, scalar2=None, op0=ALU.mult),
                 reads=[sws], writes=[swp])
            lv = c.sb([128, 4, 64], F32, "lv", es=es1)
            c.dma("sp", lv[:], lamv[:], writes=[lv])
            pr = c.sb([128, 2, 64], F32, "lpr", es=es1)
            sm = c.sb([128, 2], F32, "lsm", es=es1)
            for i in range(2):
                c.op("dve", lambda e: e.tensor_tensor(out=pr[:, i, :], in0=lv[:, 2 * i, :], in1=lv[:, 2 * i + 1, :], op=ALU.mult),
                     reads=[lv], writes=[pr])
                c.op("dve", lambda e: e.reduce_sum(out=sm[:, i:i + 1], in_=pr[:, i, :], axis=AX.X), reads=[pr], writes=[sm])
            c.op("act", lambda e: e.activation(out=sm[:], in_=sm[:], func=AF.Exp), reads=[sm], writes=[sm])
            c.op("dve", lambda e: e.tensor_tensor(out=neglam[:], in0=sm[:, 1:2], in1=sm[:, 0:1], op=ALU.subtract),
                 reads=[sm], writes=[neglam])
            c.op("dve", lambda e: e.tensor_scalar(out=neglam[:], in0=neglam[:], scalar1=-lambda_init, scalar2=None, op0=ALU.add),
                 reads=[neglam], writes=[neglam])
            idf = c.sb([128, 128], F32, "idf", es=es1)
            c.dma("sp", idf[:], idd[:], writes=[idf])
            c.op("dve", lambda e: e.tensor_copy(out=identb[:], in_=idf[:]), reads=[idf], writes=[identb])
            ohs = c.sb([32, 1152], F32, "ohs", es=es1)
            c.dma("sp", ohs[:], ohd[:], writes=[ohs])
            rbs = c.sb([32, 2], F32, "rbs", es=es1)
            c.dma("sp", rbs[:], rb[:], writes=[rbs])
            tvs = c.sb([2, 1152], F32, "tvs", es=es1)
            ptv = c.ps([128, 512], F32, "ptv", es=es1)
            for j in range(3):
                c.op("pe", lambda e: e.matmul(ptv[0:2, 0:384], lhsT=rbs[:], rhs=ohs[:, j * 384:(j + 1) * 384], start=True, stop=True),
                     reads=[rbs, ohs], writes=[ptv])
                c.op("act", lambda e: e.activation(out=tvs[:, j * 384:(j + 1) * 384], in_=ptv[0:2, 0:384], func=AF.Copy),
                     reads=[ptv], writes=[tvs])
            c.dma("sp", tvd[:], tvs[:], reads=[tvs], writes=[tvd])
            mds = c.sb([128, 5, 512], F32, "mds", es=es1)
            c.dma("sp", mds[:], mdd[:], writes=[mds])
            G = Rot([c.sb([128, 512], F32, "G%d" % i, es=es1) for i in range(2)])
            Bt = Rot([c.sb([128, 512], F32, "Bt%d" % i, es=es1) for i in range(2)])
            for h in range(2):
                for i, d in enumerate(DA_DS):
                    g_ = G.get()
                    src = bass.AP(tensor=tvd.t.tensor, offset=h * 1152 + d + 128, ap=[[1, 128], [1, 512]])
                    c.dma("sp", g_[:], src, reads=[tvd], writes=[g_])
                    b_ = Bt.get()
                    c.op("dve", lambda e: e.scalar_tensor_tensor(out=b_[:], in0=g_[:, ::-1], scalar=8.0, in1=mds[:, i, :],
                                                                 op0=ALU.mult, op1=ALU.add), reads=[g_, mds], writes=[b_])
                    c.op("act", lambda e: e.activation(out=BH[h][i][:], in_=b_[:], func=AF.Copy), reads=[b_], writes=[BH[h][i]])
                    c.op("dve", lambda e: e.tensor_tensor(out=BL[h][i][:], in0=b_[:], in1=BH[h][i][:], op=ALU.subtract),
                         reads=[b_, BH[h][i]], writes=[BL[h][i]])

        c.barrier()
        with ExitStack() as es2:
            wb = c.sb([128, 8, 768], BF16, "wb", es=es2)
            w3 = wcat[:].rearrange("(k p) n -> p k n", p=128)
            for k0 in range(0, 8, 2):
                load_cast(c, wb, wb[:, k0:k0 + 2, :], w3[:, k0:k0 + 2, :])
            ub = Rot([c.sb([128, 8, 512], BF16, "ub%d" % i, es=es2) for i in range(2)])
            pbig = Rot([c.ps([128, 512], F32, "pb%d" % i, es=es2) for i in range(6)])
            u3 = uT[:].rearrange("(k p) n -> p k n", p=128)
            for (b0, ntl) in DN_BLOCKS:
                n = ntl * 128
                u = ub.get()
                c.dma("sp", u[:, :, :n], u3[:, :, b0:b0 + n], writes=[u])
                for j in range(4):
                    p = pbig.get()
                    for k in range(8):
                        c.op("pe", lambda e: e.matmul(p[:, :n], lhsT=wb[:, k, j * 128:(j + 1) * 128], rhs=u[:, k, :n],
                                                      start=(k == 0), stop=(k == 7)), reads=[wb, u], writes=[p], inc=(k == 7))
                    dst = (QT[j] if j < 2 else KT[j - 2])
                    if j % 2 == 0:
                        c.op("act", lambda e: e.activation(out=dst[:, b0:b0 + n], in_=p[:, :n], func=AF.Copy), reads=[p], writes=[dst])
                    else:
                        c.op("dve", lambda e: e.tensor_copy(out=dst[:, b0:b0 + n], in_=p[:, :n]), reads=[p], writes=[dst])
                for tt in range(ntl):
                    p = pbig.get()
                    for k in range(8):
                        c.op("pe", lambda e: e.matmul(p[:, 0:256], lhsT=u[:, k, tt * 128:(tt + 1) * 128], rhs=wb[:, k, 512:768],
                                                      start=(k == 0), stop=(k == 7)), reads=[wb, u], writes=[p], inc=(k == 7))
                    kt = b0 // 128 + tt
                    if tt % 2 == 0:
                        c.op("act", lambda e: e.activation(out=V[:, kt, :], in_=p[:, 0:256], func=AF.Copy), reads=[p], writes=[V])
                    else:
                        c.op("dve", lambda e: e.tensor_copy(out=V[:, kt, :], in_=p[:, 0:256]), reads=[p], writes=[V])

        c.barrier()
        with ExitStack() as es3:
            sps2 = Rot([c.ps([128, 1024], F32, "sps%d" % i, es=es3) for i in range(2)])
            oacc = [c.ps([128, 512], F32, "oacc%d" % i, es=es3) for i in range(2)]
            dacc = [c.ps([128, 512], F32, "dacc%d" % i, es=es3) for i in range(2)]
            ptb2 = Rot([c.sb([128, 1024], BF16, "pt%d" % i, es=es3) for i in range(3)])
            rr = [c.sb([128, 512], F32, "rr%d" % i, es=es3) for i in range(2)]
            aa = [c.sb([128, 512], F32, "aa%d" % i, es=es3) for i in range(2)]
            sqb = Rot([c.sb([128, 512], F32, "sq%d" % i, es=es3) for i in range(2)])
            rstd = c.sb([128, 512], F32, "rstd", es=es3)
            obr = Rot([c.sb([128, 512], BF16, "ob%d" % i, es=es3) for i in range(2)])
            dsr2 = Rot([c.sb([128, 1024], F32, "dsum%d" % i, es=es3) for i in range(2)])

            def both(t, nq):
                return t[:, :].rearrange("p (c n) -> p c n", c=2)[:, :, :nq]

            for (q0, nq) in qtiles:
                ktmax = (q0 + nq) // 128 - 1
                for h in range(2):
                    dsum = dsr2.get()

                    def emit_s(kt):
                        ps = sps2.get()
                        d = kt * 128 - q0
                        near = d in DA_DS
                        for cc in range(2):
                            rs = slice(cc * 64, cc * 64 + 64)
                            o_ = ps[:, cc * 512:cc * 512 + nq]
                            c.op("pe", lambda e: e.matmul(o_, lhsT=KT[h][rs, kt * 128:(kt + 1) * 128], rhs=QT[h][rs, q0:q0 + nq],
                                                          start=True, stop=not near), reads=[KT[h], QT[h]], writes=[ps], inc=not near)
                            if near:
                                i = DA_DS.index(d)
                                c.op("pe", lambda e: e.matmul(o_, lhsT=identb[:], rhs=BH[h][i][:, :nq], start=False, stop=False),
                                     reads=[identb, BH[h][i]], writes=[ps], inc=False)
                                c.op("pe", lambda e: e.matmul(o_, lhsT=identb[:], rhs=BL[h][i][:, :nq], start=False, stop=True),
                                     reads=[identb, BL[h][i]], writes=[ps])
                        return ps

                    ps_cur = emit_s(0)
                    for kt in range(ktmax + 1):
                        ps_next = emit_s(kt + 1) if kt < ktmax else None
                        pt = ptb2.get()
                        bsrc = bias0 if kt == 0 else biasc
                        c.op("act", lambda e: e.activation(out=both(pt, nq), in_=both(ps_cur, nq), func=AF.Exp,
                                                           bias=bsrc[:, h:h + 1], scale=0.125),
                             reads=[ps_cur, bsrc], writes=[pt])
                        for cc in range(2):
                            c.op("pe", lambda e: e.matmul(oacc[cc][:, :nq], lhsT=V[:, kt, h * 128:(h + 1) * 128],
                                                          rhs=pt[:, cc * 512:cc * 512 + nq],
                                                          start=(kt == 0), stop=(kt == ktmax)), reads=[V, pt], writes=[oacc[cc]])
                        if kt == 0:
                            c.op("dve", lambda e: e.tensor_copy(out=both(dsum, nq), in_=both(pt, nq)), reads=[pt], writes=[dsum])
                        else:
                            c.op("dve", lambda e: e.tensor_tensor(out=both(dsum, nq), in0=both(dsum, nq), in1=both(pt, nq), op=ALU.add),
                                 reads=[pt, dsum], writes=[dsum])
                        ps_cur = ps_next
                    for cc in range(2):
                        c.op("pe", lambda e: e.matmul(dacc[cc][:, :nq], lhsT=c.ones[:], rhs=dsum[:, cc * 512:cc * 512 + nq], start=True, stop=True),
                             reads=[c.ones, dsum], writes=[dacc[cc]])
                    for cc in range(2):
                        if q0 == 0:
                            c.op("dve", lambda e: e.tensor_scalar(out=rr[cc][:, :nq], in0=dacc[cc][:, :nq], scalar1=1e-30, scalar2=None,
                                                                  op0=ALU.max), reads=[dacc[cc]], writes=[rr[cc]])
                            c.op("dve", lambda e: e.reciprocal(out=rr[cc][:, :nq], in_=rr[cc][:, :nq]), reads=[rr[cc]], writes=[rr[cc]])
                        else:
                            c.op("dve", lambda e: e.reciprocal(out=rr[cc][:, :nq], in_=dacc[cc][:, :nq]), reads=[dacc[cc]], writes=[rr[cc]])
                        c.op("dve", lambda e: e.tensor_tensor(out=aa[cc][:, :nq], in0=oacc[cc][:, :nq], in1=rr[cc][:, :nq], op=ALU.mult),
                             reads=[oacc[cc], rr[cc]], writes=[aa[cc]])
                    c.op("dve", lambda e: e.scalar_tensor_tensor(out=aa[0][:, :nq], in0=aa[1][:, :nq], scalar=neglam[:, 0:1],
                                                                 in1=aa[0][:, :nq], op0=ALU.mult, op1=ALU.add),
                         reads=[aa[0], aa[1], neglam], writes=[aa[0]])
                    pstat_ = sps2.get()
                    rms_stats(c, c.ones, lambda k: (aa[0][:, :nq], aa[0]), 1, nq, pstat_.view(pstat_.t[:, 0:512]), sqb, rstd, 128.0)
                    ob = obr.get()
                    c.op("dve", lambda e: e.scalar_tensor_tensor(out=ob[:, :nq], in0=aa[0][:, :nq], scalar=swp[:, 0:1],
                                                                 in1=rstd[:, :nq], op0=ALU.mult, op1=ALU.mult),
                         reads=[aa[0], swp, rstd], writes=[ob])
                    if q0 == 0:
                        c.op("dve", lambda e: e.memset(ob[:, 0:112], 0.0), writes=[ob])
                    c.dma("sp", oT[h * 128:(h + 1) * 128, q0:q0 + nq], ob[:, :nq], reads=[ob], writes=[oT])


def build_pre():
    nc = bass.Bass("TRN2", target_bir_lowering=False)
    with ExitStack() as es:
        c = Ctx(nc, es)
        io = {}
        io["hT"] = c.dram("hT", [D, NTC], F32, "ExternalInput")
        io["nw"] = c.dram("nw", [128, 8], F32, "ExternalInput")
        io["uout"] = c.dram("uout", [D, NTC], BF16, "ExternalOutput")
        make_consts(c)
        emit_pre(c, io)
        c.finish()
    return nc


def emit_pre(c, io):
    if True:
        hT, nw, uout = io["hT"], io["nw"], io["uout"]
        nws = c.sb([128, 8], F32, "nws")
        c.dma("sp", nws[:], nw[:], writes=[nws])
        H = Rot([c.sb([128, 8, GS], F32, "H%d" % g) for g in range(2)])
        U = Rot([c.sb([128, 8, GS], BF16, "U%d" % g) for g in range(2)])
        sqb = Rot([c.sb([128, GS], F32, "sq%d" % i) for i in range(2)])
        rstd = c.sb([128, GS], F32, "rstd")
        pstat = c.ps([128, 512], F32, "pstat")
        hT3 = hT[:].rearrange("(k p) n -> p k n", p=128)
        uo3 = uout[:].rearrange("(k p) n -> p k n", p=128)
        for g in range(NG):
            h = H.get()
            c.dma("sp", h[:], hT3[:, :, g * GS:(g + 1) * GS], writes=[h])
            rms_stats(c, c.ones, lambda k: (h[:, k, :], h), 8, GS, pstat, sqb, rstd, float(D))
            u = U.get()
            for m in range(8):
                c.op("dve", lambda e: e.scalar_tensor_tensor(out=u[:, m, :], in0=h[:, m, :], scalar=nws[:, m:m + 1],
                                                             in1=rstd[:], op0=ALU.mult, op1=ALU.mult),
                     reads=[h, nws, rstd], writes=[u])
            c.dma("sp", uo3[:, :, g * GS:(g + 1) * GS], u[:], reads=[u], writes=[uout])


def _run(nc, in_maps):
    res = run_bass_kernel_spmd(nc, in_maps, core_ids=list(range(NCORES)))
    return res.results


def _nwcols(w):
    return np.ascontiguousarray(np.asarray(w, np.float32).reshape(8, 128).T)


def _tok_shards(fullT):
    out = []
    for b in range(B):
        for j in range(4):
            out.append(np.ascontiguousarray(fullT[b][:, j * NTC:(j + 1) * NTC]))
    return out


def _gather_tok(shards):
    return [np.concatenate([shards[4 * b + j] for j in range(4)], axis=1) for b in range(B)]


def kernel_unfused(x, meta_tokens, rel_bias, norm_mix_w, norm_mlp_w, final_norm_w,
           dn_w_in, dn_conv_w, dn_a_log, dn_dt_bias, dn_norm_w, dn_w_out,
           da_w_in, da_lam_q1, da_lam_k1, da_lam_q2, da_lam_k2, da_subln_w, da_w_out,
           lru_w_in, lru_conv_w, lru_conv_b, lru_w_rgate, lru_b_rgate, lru_w_igate,
           lru_b_igate, lru_lambda, lru_w_out, mlp_w1, mlp_w2):
    f32 = np.float32
    x = np.asarray(x, f32)
    meta = np.asarray(meta_tokens, f32)
    bf = ml_dtypes.bfloat16
    hT_full = []
    for b in range(B):
        seq = np.concatenate([np.zeros((PADF, D), f32), meta, x[b]], axis=0)
        hT_full.append(np.ascontiguousarray(seq.T))
    h_sh = _tok_shards(hT_full)
    nc = build_pre()
    r = _run(nc, [{"hT": h_sh[c], "nw": _nwcols(norm_mix_w[0])} for c in range(NCORES)])
    u_sh = [r[c]["uout"] for c in range(NCORES)]
    depth = 4
    zpad = np.zeros((D, 64), bf)
    for layer in range(depth):
        kind = layer % 3
        slot = layer // 3
        u_full = _gather_tok(u_sh)
        ims = []
        if kind == 0:
            w_in = np.asarray(dn_w_in[slot], f32)
            cw = np.asarray(dn_conv_w[slot], f32)
            cst = dn_consts()
            for c in range(NCORES):
                b, g = divmod(c, 4)
                h0 = 2 * g
                s = slice(h0 * 128, h0 * 128 + 256)
                wcat = np.concatenate([w_in[:, 0:1024][:, s], w_in[:, 1024:2048][:, s], w_in[:, 2048:3072][:, s],
                                       w_in[:, 3072:4096][:, s], w_in[:, 4096 + h0:4096 + h0 + 2],
                                       w_in[:, 4104 + h0:4104 + h0 + 2]], axis=1)
                cw6 = np.concatenate([cw[:, 0:1024][:, s], cw[:, 1024:2048][:, s], cw[:, 2048:3072][:, s]], axis=1)
                prm = dn_params(cw6, np.asarray(dn_a_log[slot], f32)[h0:h0 + 2],
                                np.asarray(dn_dt_bias[slot], f32)[h0:h0 + 2], np.asarray(dn_norm_w[slot], f32))
                ims.append({"uT": np.ascontiguousarray(np.concatenate([zpad, u_full[b]], axis=1)),
                            "wcat": np.ascontiguousarray(wcat), "cst": cst, "prm": prm})
            r = _run(build_dn(), ims)
            o_full = [np.concatenate([r[4 * b + g]["oT"][:, 64:] for g in range(4)], axis=0) for b in range(B)]
            wo = np.asarray(dn_w_out[slot], f32)
        elif kind == 1:
            w_in = np.asarray(da_w_in[slot], f32)
            oh, md, pm = da_consts()
            rbt = np.asarray(rel_bias, f32)
            lamv = np.stack([np.asarray(v[slot], f32) for v in (da_lam_q1, da_lam_k1, da_lam_q2, da_lam_k2)], axis=0)
            lamv = np.ascontiguousarray(np.broadcast_to(lamv[None], (128, 4, 64)))
            sw = np.ascontiguousarray(np.asarray(da_subln_w[slot], f32)[:, None])
            ident = np.eye(128, dtype=f32)
            for c in range(NCORES):
                b, g = divmod(c, 4)
                h0 = 2 * g
                s = slice(h0 * 128, h0 * 128 + 256)
                wcat = np.concatenate([w_in[:, 0:1024][:, s], w_in[:, 1024:2048][:, s], w_in[:, 2048:3072][:, s]], axis=1)
                ims.append({"uT": np.ascontiguousarray(np.concatenate([zpad, u_full[b]], axis=1)),
                            "wcat": np.ascontiguousarray(wcat), "oh": oh, "md": md, "pm": pm,
                            "rb": np.ascontiguousarray(rbt[:, h0:h0 + 2]),
                            "rb15": np.ascontiguousarray(np.broadcast_to(rbt[15:16, h0:h0 + 2], (128, 2))),
                            "lamv": lamv, "sw": sw, "identf": ident})
            lam_init = 0.8 - 0.6 * math.exp(-0.3 * layer)
            r = _run(build_da(lam_init), ims)
            o_full = [np.concatenate([r[4 * b + g]["oT"][:, 64:] for g in range(4)], axis=0) for b in range(B)]
            wo = np.asarray(da_w_out[slot], f32)
        else:
            w_in = np.asarray(lru_w_in[slot], f32)
            cw = np.asarray(lru_conv_w[slot], f32)
            for c in range(NCORES):
                b, g = divmod(c, 4)
                s = slice(g * 256, (g + 1) * 256)
                prm = lru_params(cw[:, s], np.asarray(lru_conv_b[slot], f32)[s], np.asarray(lru_b_rgate[slot], f32)[s],
                                 np.asarray(lru_b_igate[slot], f32)[s], np.asarray(lru_lambda[slot], f32)[s])
                ims.append({"uT": u_full[b], "wg": np.ascontiguousarray(w_in[:, 0:1024][:, s]),
                            "wx": np.ascontiguousarray(w_in[:, 1024:2048][:, s]),
                            "wr": np.ascontiguousarray(np.asarray(lru_w_rgate[slot], f32)[g]),
                            "wi": np.ascontiguousarray(np.asarray(lru_w_igate[slot], f32)[g]), "prm": prm})
            r = _run(build_lru(), ims)
            o_full = [np.concatenate([r[4 * b + g]["yT"] for g in range(4)], axis=0) for b in range(B)]
            wo = np.asarray(lru_w_out[slot], f32)
        o_sh = _tok_shards(o_full)
        final = (layer == depth - 1)
        nxt = final_norm_w if final else norm_mix_w[layer + 1]
        nw = np.ascontiguousarray(np.concatenate([_nwcols(norm_mlp_w[layer]), _nwcols(nxt)], axis=1))
        w1 = np.asarray(mlp_w1[layer], f32)
        w2 = np.asarray(mlp_w2[layer], f32)
        r = _run(build_post(final), [{"hT": h_sh[c], "oT": o_sh[c], "wo": wo, "w1": w1, "w2": w2, "nw": nw}
                                     for c in range(NCORES)])
        h_sh = [r[c]["hout"] for c in range(NCORES)]
        u_sh = [r[c]["uout"] for c in range(NCORES)]
    out_full = _gather_tok(u_sh)
    out = np.stack([np.ascontiguousarray(out_full[b][:, PADF + NMETA:].T) for b in range(B)], axis=0)
    return out.astype(f32)


DEPTH = 4


def _phase(c, fn):
    base = c.es
    with ExitStack() as pes:
        c.es = pes
        fn()
        c.barrier()
    c.es = base


def build_fused():
    nc = bass.Bass("TRN2", target_bir_lowering=False)
    with ExitStack() as es:
        c = Ctx(nc, es)
        h0T = c.dram("h0T", [D, T], F32, "ExternalInput")
        outT = c.dram("outT", [D, T], F32, "ExternalOutput")
        HT = c.dram("HT", [D, T], F32, "Internal")
        UT = c.dram("UT", [D, TD], BF16, "Internal")
        OT = c.dram("OT", [D, TD], BF16, "Internal")
        nw0 = c.dram("nw0", [128, 8], F32, "ExternalInput")
        W = {}
        for l in range(DEPTH):
            kind = l % 3
            p = "L%d_" % l
            if kind == 0:
                W[p + "wcat"] = c.dram(p + "wcat", [4, D, 1028], F32, "ExternalInput")
                W[p + "prm"] = c.dram(p + "prm", [4, 128, 32], F32, "ExternalInput")
            elif kind == 1:
                W[p + "wcat"] = c.dram(p + "wcat", [4, D, 768], F32, "ExternalInput")
                W[p + "rb"] = c.dram(p + "rb", [4, 32, 2], F32, "ExternalInput")
                W[p + "rb15"] = c.dram(p + "rb15", [4, 128, 2], F32, "ExternalInput")
                W[p + "lamv"] = c.dram(p + "lamv", [128, 4, 64], F32, "ExternalInput")
                W[p + "sw"] = c.dram(p + "sw", [128, 1], F32, "ExternalInput")
            else:
                W[p + "wg"] = c.dram(p + "wg", [4, D, 256], F32, "ExternalInput")
                W[p + "wx"] = c.dram(p + "wx", [4, D, 256], F32, "ExternalInput")
                W[p + "wr"] = c.dram(p + "wr", [4, 256, 256], F32, "ExternalInput")
                W[p + "wi"] = c.dram(p + "wi", [4, 256, 256], F32, "ExternalInput")
                W[p + "prm"] = c.dram(p + "prm", [4, 128, 16], F32, "ExternalInput")
            W[p + "wo"] = c.dram(p + "wo", [D, D], F32, "ExternalInput")
            W[p + "w1"] = c.dram(p + "w1", [D, DFF], F32, "ExternalInput")
            W[p + "w2"] = c.dram(p + "w2", [DFF, D], F32, "ExternalInput")
            W[p + "nw"] = c.dram(p + "nw", [128, 16], F32, "ExternalInput")
        dn_cst = c.dram("dn_cst", [128, 6, 128], F32, "ExternalInput")
        da_oh = c.dram("da_oh", [32, 1152], F32, "ExternalInput")
        da_md = c.dram("da_md", [128, 5, 512], F32, "ExternalInput")
        da_pm = c.dram("da_pm", [128, 1], F32, "ExternalInput")
        identf = c.dram("identf", [128, 128], F32, "ExternalInput")
        make_consts(c)

        def sh(t, j, off=0):
            return t.view(t.t[:, off + j * NTC:off + (j + 1) * NTC])

        def zero_front():
            z = c.sb([128, 8, 64], BF16, "zfront")
            c.op("pool", lambda e: e.memset(z[:], 0.0), writes=[z])
            c.dma("sp", UT[:].rearrange("(k p) n -> p k n", p=128)[:, :, 0:64], z[:], reads=[z], writes=[UT])
        _phase(c, zero_front)
        for j in range(4):
            _phase(c, lambda j=j: emit_pre(c, {"hT": sh(h0T, j), "nw": nw0, "uout": sh(UT, j, 64)}))
        for l in range(DEPTH):
            kind = l % 3
            p = "L%d_" % l
            for g in range(4):
                rows = slice(g * 256, (g + 1) * 256)
                if kind == 0:
                    io = {"uT": UT, "wcat": W[p + "wcat"].view(W[p + "wcat"].t[g]), "cst": dn_cst,
                          "prm": W[p + "prm"].view(W[p + "prm"].t[g]), "oT": OT.view(OT.t[rows, :])}
                    _phase(c, lambda io=io: emit_dn(c, io))
                elif kind == 1:
                    io = {"uT": UT, "wcat": W[p + "wcat"].view(W[p + "wcat"].t[g]), "oh": da_oh, "md": da_md, "pm": da_pm,
                          "rb": W[p + "rb"].view(W[p + "rb"].t[g]), "rb15": W[p + "rb15"].view(W[p + "rb15"].t[g]),
                          "lamv": W[p + "lamv"], "sw": W[p + "sw"], "identf": identf, "oT": OT.view(OT.t[rows, :])}
                    lam_init = 0.8 - 0.6 * math.exp(-0.3 * l)
                    _phase(c, lambda io=io, lam_init=lam_init, tag="_%d_%d" % (l, g): emit_da(c, io, lam_init, tag))
                else:
                    io = {"uT": UT.view(UT.t[:, 64:64 + T]), "wg": W[p + "wg"].view(W[p + "wg"].t[g]),
                          "wx": W[p + "wx"].view(W[p + "wx"].t[g]), "wr": W[p + "wr"].view(W[p + "wr"].t[g]),
                          "wi": W[p + "wi"].view(W[p + "wi"].t[g]), "prm": W[p + "prm"].view(W[p + "prm"].t[g]),
                          "yT": OT.view(OT.t[rows, 64:64 + T])}
                    _phase(c, lambda io=io: emit_lru(c, io))
            final = (l == DEPTH - 1)
            for j in range(4):
                io = {"hT": sh(h0T if l == 0 else HT, j), "oT": sh(OT, j, 64), "wo": W[p + "wo"], "w1": W[p + "w1"],
                      "w2": W[p + "w2"], "nw": W[p + "nw"], "hout": sh(HT, j),
                      "uout": sh(outT, j) if final else sh(UT, j, 64)}
                _phase(c, lambda io=io, final=final: emit_post(c, io, final))
            if l == 1:
                c.switch_sem("pe")
        c.finish()
    return nc


def fused_inputs(b, x, meta_tokens, rel_bias, norm_mix_w, norm_mlp_w, final_norm_w,
                 dn_w_in, dn_conv_w, dn_a_log, dn_dt_bias, dn_norm_w, dn_w_out,
                 da_w_in, da_lam_q1, da_lam_k1, da_lam_q2, da_lam_k2, da_subln_w, da_w_out,
                 lru_w_in, lru_conv_w, lru_conv_b, lru_w_rgate, lru_b_rgate, lru_w_igate,
                 lru_b_igate, lru_lambda, lru_w_out, mlp_w1, mlp_w2, shared=None):
    f32 = np.float32
    im = {}
    seq = np.concatenate([np.zeros((PADF, D), f32), np.asarray(meta_tokens, f32), np.asarray(x[b], f32)], axis=0)
    im["h0T"] = np.ascontiguousarray(seq.T)
    if shared is not None:
        im.update(shared)
        return im
    sh = {}
    sh["nw0"] = _nwcols(norm_mix_w[0])
    oh, md, pm = da_consts()
    sh["dn_cst"] = dn_consts()
    sh["da_oh"], sh["da_md"], sh["da_pm"] = oh, md, pm
    sh["identf"] = np.eye(128, dtype=f32)
    rbt = np.asarray(rel_bias, f32)
    for l in range(DEPTH):
        kind, slot = l % 3, l // 3
        p = "L%d_" % l
        if kind == 0:
            w_in = np.asarray(dn_w_in[slot], f32)
            cw = np.asarray(dn_conv_w[slot], f32)
            wc, pr = [], []
            for g in range(4):
                h0 = 2 * g
                s = slice(h0 * 128, h0 * 128 + 256)
                wc.append(np.concatenate([w_in[:, 0:1024][:, s], w_in[:, 1024:2048][:, s], w_in[:, 2048:3072][:, s],
                                          w_in[:, 3072:4096][:, s], w_in[:, 4096 + h0:4096 + h0 + 2],
                                          w_in[:, 4104 + h0:4104 + h0 + 2]], axis=1))
                cw6 = np.concatenate([cw[:, 0:1024][:, s], cw[:, 1024:2048][:, s], cw[:, 2048:3072][:, s]], axis=1)
                pr.append(dn_params(cw6, np.asarray(dn_a_log[slot], f32)[h0:h0 + 2],
                                    np.asarray(dn_dt_bias[slot], f32)[h0:h0 + 2], np.asarray(dn_norm_w[slot], f32)))
            sh[p + "wcat"] = np.ascontiguousarray(np.stack(wc))
            sh[p + "prm"] = np.ascontiguousarray(np.stack(pr))
            wo = dn_w_out[slot]
        elif kind == 1:
            w_in = np.asarray(da_w_in[slot], f32)
            wc, rb, rb15 = [], [], []
            for g in range(4):
                h0 = 2 * g
                s = slice(h0 * 128, h0 * 128 + 256)
                wc.append(np.concatenate([w_in[:, 0:1024][:, s], w_in[:, 1024:2048][:, s], w_in[:, 2048:3072][:, s]], axis=1))
                rb.append(rbt[:, h0:h0 + 2])
                rb15.append(np.broadcast_to(rbt[15:16, h0:h0 + 2], (128, 2)))
            sh[p + "wcat"] = np.ascontiguousarray(np.stack(wc))
            sh[p + "rb"] = np.ascontiguousarray(np.stack(rb))
            sh[p + "rb15"] = np.ascontiguousarray(np.stack(rb15))
            lamv = np.stack([np.asarray(v[slot], f32) for v in (da_lam_q1, da_lam_k1, da_lam_q2, da_lam_k2)], axis=0)
            sh[p + "lamv"] = np.ascontiguousarray(np.broadcast_to(lamv[None], (128, 4, 64)))
            sh[p + "sw"] = np.ascontiguousarray(np.asarray(da_subln_w[slot], f32)[:, None])
            wo = da_w_out[slot]
        else:
            w_in = np.asarray(lru_w_in[slot], f32)
            cw = np.asarray(lru_conv_w[slot], f32)
            wg, wx, pr = [], [], []
            for g in range(4):
                s = slice(g * 256, (g + 1) * 256)
                wg.append(w_in[:, 0:1024][:, s])
                wx.append(w_in[:, 1024:2048][:, s])
                pr.append(lru_params(cw[:, s], np.asarray(lru_conv_b[slot], f32)[s], np.asarray(lru_b_rgate[slot], f32)[s],
                                     np.asarray(lru_b_igate[slot], f32)[s], np.asarray(lru_lambda[slot], f32)[s]))
            sh[p + "wg"] = np.ascontiguousarray(np.stack(wg))
            sh[p + "wx"] = np.ascontiguousarray(np.stack(wx))
            sh[p + "wr"] = np.ascontiguousarray(np.asarray(lru_w_rgate[slot], f32))
            sh[p + "wi"] = np.ascontiguousarray(np.asarray(lru_w_igate[slot], f32))
            sh[p + "prm"] = np.ascontiguousarray(np.stack(pr))
            wo = lru_w_out[slot]
        final = (l == DEPTH - 1)
        nxt = final_norm_w if final else norm_mix_w[l + 1]
        sh[p + "wo"] = np.ascontiguousarray(np.asarray(wo, f32))
        sh[p + "w1"] = np.ascontiguousarray(np.asarray(mlp_w1[l], f32))
        sh[p + "w2"] = np.ascontiguousarray(np.asarray(mlp_w2[l], f32))
        sh[p + "nw"] = np.ascontiguousarray(np.concatenate([_nwcols(norm_mlp_w[l]), _nwcols(nxt)], axis=1))
    im.update(sh)
    im["_shared"] = sh
    return im


def kernel(**inputs):
    x = inputs["x"]
    im0 = fused_inputs(0, **inputs)
    shared = im0.pop("_shared")
    im1 = fused_inputs(1, **inputs, shared=shared)
    nc = build_fused()
    res = run_bass_kernel_spmd(nc, [im0, im1], core_ids=[0, 1])
    out = np.stack([np.ascontiguousarray(res.results[b]["outT"][:, PADF + NMETA:].T) for b in range(B)], axis=0)
    return out.astype(np.float32)
```
